# Optimizing a Trainium2 kernel written in Bass

```python
import math
import jax, jax.numpy as jnp
from jax import lax
import numpy as np

D_MODEL = 1024
BATCH = 8
SEQ = 2048
DEPTH = 4
DEC_BATCH = 128
DEC_SEQ = 4
PAST_LEN = 16384
PAGE_SIZE = 128

HEAD_DIM = 64
N_HEADS = D_MODEL // HEAD_DIM
N_GLA = (5 * N_HEADS) // 16
N_RET = (5 * N_HEADS) // 16
N_RWKV = N_HEADS - N_GLA - N_RET
MIX_WIDTH = N_HEADS * HEAD_DIM
RWKV_W = N_RWKV * HEAD_DIM
GLA_DK = HEAD_DIM // 2
GLA_K = N_GLA * GLA_DK
GLA_V = N_GLA * HEAD_DIM
RET_W = N_RET * HEAD_DIM
LORA_W = max(32, int(round(1.8 * D_MODEL ** 0.5 / 32)) * 32)
LORA_A = max(32, int(round(1.8 * D_MODEL ** 0.5 / 32)) * 32)
LORA_G = max(32, int(round(0.6 * D_MODEL ** 0.8 / 32)) * 32)
GLA_LORA = 16
GLA_GATE_NORM = 16.0
CHUNK = 64
D_FF = ((8 * D_MODEL // 3 + 127) // 128) * 128
CONV_W = 3
ALPHA = (2 * DEPTH) ** 0.25
BETA = (8 * DEPTH) ** -0.25
RWKV_GN_EPS = 64e-5
LN_EPS = 1e-5
RMS_EPS = 1e-6
ROPE_BASE = 10000.0

RWKV_SIZES = (RWKV_W, RWKV_W, RWKV_W, LORA_W, LORA_A, LORA_G)
RWKV_COLS = sum(RWKV_SIZES)
GLA_SIZES = (GLA_K, GLA_K, GLA_V, GLA_V, GLA_LORA)
GLA_COLS = sum(GLA_SIZES)
RET_SIZES = (RET_W, RET_W, RET_W, RET_W)
RET_COLS = sum(RET_SIZES)
IN_COLS = RWKV_COLS + GLA_COLS + RET_COLS

kernel_name = 'hybrid_rwkv7_gla_retnet_convffn_step'


def _split(a, sizes):
    return jnp.split(a, np.cumsum(sizes)[:-1].tolist(), axis=-1)


def _heads(a, h):
    return a.reshape(a.shape[:-1] + (h, a.shape[-1] // h))


def _layer_norm(x, g, b):
    xf = x.astype(jnp.float32)
    mu = xf.mean(-1, keepdims=True)
    var = jnp.mean(jnp.square(xf - mu), -1, keepdims=True)
    return ((xf - mu) * lax.rsqrt(var + LN_EPS) * g + b).astype(x.dtype)


def _rms_heads(o):
    return o * lax.rsqrt(jnp.mean(jnp.square(o), -1, keepdims=True) + RMS_EPS)


def _to_chunks(a, c):
    b, t, h, d = a.shape
    return a.reshape(b, t // c, c, h, d).transpose(1, 0, 3, 2, 4)


def _from_chunks(o):
    n, b, h, c, d = o.shape
    return o.transpose(1, 0, 3, 2, 4).reshape(b, n * c, h, d)


def _rwkv7_mix(p, shift0, s0, mu, w0, bw, a0, ba, bg, k_k, k_a, r_k, ln_w, ln_b):
    f32 = jnp.float32
    b_, t_, _ = p.shape
    prev = jnp.concatenate([shift0[:, None].astype(p.dtype), p[:, :-1]], axis=1)
    pm = p + (prev - p) * mu
    r, k, v, xw, xa, xg = _split(pm, RWKV_SIZES)
    w_log = -jax.nn.softplus(-(w0 + jnp.tanh(xw) @ bw).astype(f32)) - 0.5
    decay = jnp.exp(-jnp.exp(w_log))
    a = jax.nn.sigmoid((a0 + xa @ ba).astype(f32))
    g = (jax.nn.sigmoid(xg) @ bg).astype(f32)
    kk = _heads(k.astype(f32) * k_k, N_RWKV)
    kk = kk / jnp.maximum(jnp.sqrt(jnp.sum(kk * kk, -1, keepdims=True)), 1e-12)
    k = k.astype(f32) * (1.0 + (a - 1.0) * k_a)
    r_h, k_h, v_h, w_h, a_h = [_heads(z.astype(f32), N_RWKV) for z in (r, k, v, decay, a)]
    seq = [jnp.moveaxis(z, 1, 0) for z in (r_h, w_h, -kk, kk * a_h, k_h, v_h)]

    def step(s, inp):
        r_t, w_t, a_t, b_t, k_t, v_t = inp
        sa = jnp.einsum('bhvk,bhk->bhv', s, a_t)
        s = s * w_t[:, :, None, :] + sa[..., None] * b_t[:, :, None, :] + v_t[..., None] * k_t[:, :, None, :]
        return s, jnp.einsum('bhvk,bhk->bhv', s, r_t)

    s, y = lax.scan(step, s0.astype(f32), seq)
    y = jnp.moveaxis(y, 0, 1)
    ym = y.mean(-1, keepdims=True)
    yv = jnp.mean(jnp.square(y - ym), -1, keepdims=True)
    y = ((y - ym) * lax.rsqrt(yv + RWKV_GN_EPS)).reshape(b_, t_, RWKV_W) * ln_w + ln_b
    bonus = jnp.sum(r_h * k_h * r_k, -1, keepdims=True) * v_h
    y = (y + bonus.reshape(b_, t_, RWKV_W)) * g
    return y.astype(p.dtype), p[:, -1], s.astype(s0.dtype)


def _gla_chunked(q, k, v, log_g, s0):
    c = math.gcd(q.shape[1], CHUNK)
    q, k, v, log_g = (_to_chunks(z, c) for z in (q, k, v, log_g))
    b = jnp.cumsum(log_g, axis=-2)
    b_last = b[..., -1:, :]
    q_in = q * jnp.exp(b)
    k_in = k * jnp.exp(-b)
    k_st = k * jnp.exp(b_last - b)
    causal = jnp.tril(jnp.ones((c, c), bool))

    def step(s, inp):
        qi, ki, ks, vi, bl = inp
        att = jnp.where(causal, jnp.einsum('bhik,bhjk->bhij', qi, ki), 0.0)
        o = jnp.einsum('bhij,bhjv->bhiv', att, vi) + jnp.einsum('bhik,bhkv->bhiv', qi, s)
        s = s * jnp.exp(bl)[:, :, 0, :, None] + jnp.einsum('bhjk,bhjv->bhkv', ks, vi)
        return s, o

    s, o = lax.scan(step, s0, (q_in, k_in, k_st, v, b_last))
    return _from_chunks(o), s


def _gla_mix(p, s0, bgk, bgk_b, norm_w):
    f32 = jnp.float32
    b_, t_, _ = p.shape
    q, k, v, g, gk = _split(p, GLA_SIZES)
    log_g = jax.nn.log_sigmoid((gk @ bgk + bgk_b).astype(f32)) / GLA_GATE_NORM
    qh = _heads(q.astype(f32), N_GLA) * GLA_DK ** -0.5
    kh, vh, gh = (_heads(z.astype(f32), N_GLA) for z in (k, v, log_g))
    o, s = _gla_chunked(qh, kh, vh, gh, s0.astype(f32))
    o = (_rms_heads(o) * norm_w).reshape(b_, t_, GLA_V) * jax.nn.silu(g.astype(f32))
    return o.astype(p.dtype), s.astype(s0.dtype)


def _rotary(x, pos):
    half = x.shape[-1] // 2
    inv = 1.0 / (ROPE_BASE ** jnp.linspace(0.0, 1.0, half, dtype=jnp.float32))
    ang = pos.astype(jnp.float32)[:, None] * inv[None]
    cos, sin = jnp.cos(ang)[None, :, None], jnp.sin(ang)[None, :, None]
    x1, x2 = x[..., :half], x[..., half:]
    return jnp.concatenate([x1 * cos - x2 * sin, x1 * sin + x2 * cos], -1)


def _retention_chunked(q, k, v, log_gamma, s0):
    c = math.gcd(q.shape[1], CHUNK)
    q, k, v = (_to_chunks(z, c) for z in (q, k, v))
    i = jnp.arange(c, dtype=jnp.float32)
    diff = i[:, None] - i[None, :]
    causal = diff >= 0
    dmask = jnp.where(causal, jnp.exp(jnp.where(causal, diff, 0.0) * log_gamma[:, None, None]), 0.0)
    q_decay = jnp.exp((i + 1.0) * log_gamma[:, None])[..., None]
    k_decay = jnp.exp((c - 1.0 - i) * log_gamma[:, None])[..., None]
    chunk_decay = jnp.exp(c * log_gamma)[:, None, None]

    def step(s, inp):
        qi, ki, vi = inp
        att = jnp.einsum('bhik,bhjk->bhij', qi, ki) * dmask
        o = jnp.einsum('bhij,bhjv->bhiv', att, vi) + jnp.einsum('bhik,bhkv->bhiv', qi * q_decay, s)
        s = s * chunk_decay + jnp.einsum('bhjk,bhjv->bhkv', ki * k_decay, vi)
        return s, o

    s, o = lax.scan(step, s0, (q, k, v))
    return _from_chunks(o), s


def _ret_mix(p, s0, pos):
    f32 = jnp.float32
    b_, t_, _ = p.shape
    q, k, v, g = _split(p, RET_SIZES)
    qh = _rotary(_heads(q.astype(f32), N_RET), pos)
    kh = _rotary(_heads(k.astype(f32), N_RET), pos) * HEAD_DIM ** -0.5
    vh = _heads(v.astype(f32), N_RET)
    log_gamma = jnp.log(1.0 - jnp.exp2(-5.0 - jnp.arange(N_RET, dtype=f32)))
    o, s = _retention_chunked(qh, kh, vh, log_gamma, s0.astype(f32))
    o = _rms_heads(o).reshape(b_, t_, RET_W) * jax.nn.silu(g.astype(f32))
    return o.astype(p.dtype), s.astype(s0.dtype)


def _conv_ffn(x, conv0, w_up, conv_w, conv_b, w_down):
    t_ = x.shape[1]
    u = x @ w_up
    buf = jnp.concatenate([conv0.astype(u.dtype), u], axis=1)
    c = conv_b + sum(conv_w[j] * buf[:, j:j + t_] for j in range(CONV_W))
    a, b = jnp.split(c, 2, axis=-1)
    return (jax.nn.gelu(a) * b) @ w_down, buf[:, -(CONV_W - 1):]


def _trunk(x, pos, s_rwkv, s_shift, s_gla, s_ret, s_conv, params):
    (w_in, rwkv_mu, rwkv_w0, rwkv_bw, rwkv_a0, rwkv_ba, rwkv_bg, rwkv_kk, rwkv_ka, rwkv_rk,
     rwkv_lnw, rwkv_lnb, gla_bgk, gla_bgk_b, gla_norm_w, w_out, ln1_g, ln1_b, ln2_g, ln2_b,
     ffn_up, ffn_conv_w, ffn_conv_b, ffn_down) = params
    n_rw, n_sh, n_gl, n_rt, n_cv = [], [], [], [], []
    for l in range(DEPTH):
        p = x @ w_in[l]
        p_rw, p_gl, p_rt = _split(p, (RWKV_COLS, GLA_COLS, RET_COLS))
        y_a, sh, s_a = _rwkv7_mix(p_rw, s_shift[l], s_rwkv[l], rwkv_mu[l], rwkv_w0[l], rwkv_bw[l],
                                  rwkv_a0[l], rwkv_ba[l], rwkv_bg[l], rwkv_kk[l], rwkv_ka[l],
                                  rwkv_rk[l], rwkv_lnw[l], rwkv_lnb[l])
        y_b, s_b = _gla_mix(p_gl, s_gla[l], gla_bgk[l], gla_bgk_b[l], gla_norm_w[l])
        y_c, s_c = _ret_mix(p_rt, s_ret[l], pos)
        mix = jnp.concatenate([y_a, y_b, y_c], axis=-1) @ w_out[l]
        x = _layer_norm(ALPHA * x + mix, ln1_g[l], ln1_b[l])
        f, cv = _conv_ffn(x, s_conv[l], ffn_up[l], ffn_conv_w[l], ffn_conv_b[l], ffn_down[l])
        x = _layer_norm(ALPHA * x + f, ln2_g[l], ln2_b[l])
        n_rw.append(s_a); n_sh.append(sh); n_gl.append(s_b); n_rt.append(s_c); n_cv.append(cv)
    return x, (jnp.stack(n_rw), jnp.stack(n_sh), jnp.stack(n_gl), jnp.stack(n_rt), jnp.stack(n_cv))


def setup_inputs(seed: int = 0) -> dict:
    key = jax.random.key(seed)
    ks = iter(jax.random.split(key, 40))
    f32 = jnp.float32
    L = DEPTH

    def nrm(shape, s):
        return jax.random.normal(next(ks), shape, f32) * s

    def uni(shape, lo, hi):
        return jax.random.uniform(next(ks), shape, f32, lo, hi)

    return {
        'x_prompt': nrm((BATCH, SEQ, D_MODEL), 1.0),
        'x_sample': nrm((DEC_BATCH, DEC_SEQ, D_MODEL), 1.0),
        'state_rwkv': nrm((L, DEC_BATCH, N_RWKV, HEAD_DIM, HEAD_DIM), 0.3),
        'state_shift': nrm((L, DEC_BATCH, RWKV_COLS), 1.0),
        'state_gla': nrm((L, DEC_BATCH, N_GLA, GLA_DK, HEAD_DIM), 0.3),
        'state_ret': nrm((L, DEC_BATCH, N_RET, HEAD_DIM, HEAD_DIM), 0.3),
        'state_conv': nrm((L, DEC_BATCH, CONV_W - 1, 2 * D_FF), BETA),
        'w_in': nrm((L, D_MODEL, IN_COLS), D_MODEL ** -0.5),
        'rwkv_mu': uni((L, RWKV_COLS), 0.0, 1.0),
        'rwkv_w0': uni((L, RWKV_W), -6.0, 0.0),
        'rwkv_bw': nrm((L, LORA_W, RWKV_W), 0.1 * LORA_W ** -0.5),
        'rwkv_a0': nrm((L, RWKV_W), 0.1),
        'rwkv_ba': nrm((L, LORA_A, RWKV_W), 0.1 * LORA_A ** -0.5),
        'rwkv_bg': nrm((L, LORA_G, RWKV_W), LORA_G ** -0.5),
        'rwkv_kk': 0.85 + nrm((L, RWKV_W), 0.02),
        'rwkv_ka': 1.0 + nrm((L, RWKV_W), 0.02),
        'rwkv_rk': nrm((L, N_RWKV, HEAD_DIM), 0.1),
        'rwkv_lnw': 1.0 + nrm((L, RWKV_W), 0.02),
        'rwkv_lnb': nrm((L, RWKV_W), 0.02),
        'gla_bgk': nrm((L, GLA_LORA, GLA_K), GLA_LORA ** -0.5),
        'gla_bgk_b': nrm((L, GLA_K), 0.1),
        'gla_norm_w': 1.0 + nrm((L, HEAD_DIM), 0.02),
        'w_out': nrm((L, MIX_WIDTH, D_MODEL), BETA * MIX_WIDTH ** -0.5),
        'ln1_g': 1.0 + nrm((L, D_MODEL), 0.02),
        'ln1_b': nrm((L, D_MODEL), 0.02),
        'ln2_g': 1.0 + nrm((L, D_MODEL), 0.02),
        'ln2_b': nrm((L, D_MODEL), 0.02),
        'ffn_up': nrm((L, D_MODEL, 2 * D_FF), BETA * D_MODEL ** -0.5),
        'ffn_conv_w': nrm((L, CONV_W, 2 * D_FF), CONV_W ** -0.5),
        'ffn_conv_b': nrm((L, 2 * D_FF), 0.02),
        'ffn_down': nrm((L, D_FF, D_MODEL), BETA * D_FF ** -0.5),
    }


def reference(x_prompt, x_sample, state_rwkv, state_shift, state_gla, state_ret, state_conv,
              w_in, rwkv_mu, rwkv_w0, rwkv_bw, rwkv_a0, rwkv_ba, rwkv_bg, rwkv_kk, rwkv_ka,
              rwkv_rk, rwkv_lnw, rwkv_lnb, gla_bgk, gla_bgk_b, gla_norm_w, w_out,
              ln1_g, ln1_b, ln2_g, ln2_b, ffn_up, ffn_conv_w, ffn_conv_b, ffn_down):
    params = (w_in, rwkv_mu, rwkv_w0, rwkv_bw, rwkv_a0, rwkv_ba, rwkv_bg, rwkv_kk, rwkv_ka,
              rwkv_rk, rwkv_lnw, rwkv_lnb, gla_bgk, gla_bgk_b, gla_norm_w, w_out,
              ln1_g, ln1_b, ln2_g, ln2_b, ffn_up, ffn_conv_w, ffn_conv_b, ffn_down)
    bp = x_prompt.shape[0]

    def zeros_like_state(s):
        return jnp.zeros((s.shape[0], bp) + s.shape[2:], x_prompt.dtype)

    y_p, (rw_p, sh_p, gl_p, rt_p, cv_p) = _trunk(
        x_prompt, jnp.arange(x_prompt.shape[1]),
        zeros_like_state(state_rwkv), zeros_like_state(state_shift), zeros_like_state(state_gla),
        zeros_like_state(state_ret), zeros_like_state(state_conv), params)
    y_s, (rw_s, sh_s, gl_s, rt_s, cv_s) = _trunk(
        x_sample, PAST_LEN + jnp.arange(x_sample.shape[1]),
        state_rwkv, state_shift, state_gla, state_ret, state_conv, params)
    return (y_p, y_s, rw_p, sh_p, gl_p, rt_p, cv_p, rw_s, sh_s, gl_s, rt_s, cv_s)
```

```python
import bisect
import math
from contextlib import ExitStack

import numpy as np
import concourse.bass as bass
import concourse.mybir as mybir
from concourse.bass_utils import run_bass_kernel_spmd

F32 = mybir.dt.float32
F32R = mybir.dt.float32r
BF16 = mybir.dt.bfloat16
AF = mybir.ActivationFunctionType
ALU = mybir.AluOpType
AX = mybir.AxisListType

NL = 4
D = 1024
SEQ = 2048
NSS = 16
LS = 4
TT = 256
PAST = 16384
DFF = 2816
NJ = 22
ALPHA = (2 * NL) ** 0.25
NBLK = 38
NPIECE_IN = 19
SLOT = 2816
NSLOT = 4
RW0, GL0, RT0 = 0, 1440, 2416


class _Stop(Exception):
    pass


class Buf:
    __slots__ = ("name", "last_write", "reads")

    def __init__(self, name):
        self.name = name
        self.last_write = None
        self.reads = []


class Op:
    __slots__ = ("eng", "fn", "deps", "idx", "needs_inc", "cnt", "chan", "dma_cnt")

    def __init__(self, eng, fn, deps, idx, chan=None):
        self.eng = eng
        self.fn = fn
        self.deps = deps
        self.idx = idx
        self.needs_inc = False
        self.cnt = None
        self.chan = chan
        self.dma_cnt = None


COMPUTE = ("pe", "act", "dve", "pool")
DMAQ = ("sp", "actq", "poolq")
STREAM = {"pe": "pe", "act": "act", "dve": "dve", "pool": "pool", "sp": "sp", "actq": "act", "poolq": "pool"}
SAME_SYNC = {"pe": False, "act": True, "dve": True, "pool": True}


class Sched:
    def __init__(self, nc):
        self.nc = nc
        self.ops = []

    def op(self, eng, fn, reads=(), writes=(), chan=None):
        idx = len(self.ops)
        deps = set()
        for b in reads:
            if b.last_write is not None:
                deps.add(b.last_write)
        for b in writes:
            if b.last_write is not None:
                deps.add(b.last_write)
            deps.update(b.reads)
        o = Op(eng, fn, deps, idx, chan)
        self.ops.append(o)
        for b in reads:
            b.reads.append(idx)
        for b in writes:
            b.last_write = idx
            b.reads = []
        return o

    def pe(self, fn, reads=(), writes=()):
        return self.op("pe", fn, reads, writes)

    def act(self, fn, reads=(), writes=()):
        return self.op("act", fn, reads, writes)

    def dve(self, fn, reads=(), writes=()):
        return self.op("dve", fn, reads, writes)

    def pool(self, fn, reads=(), writes=()):
        return self.op("pool", fn, reads, writes)

    def dma(self, fn, reads=(), writes=(), chan=None, q="sp"):
        return self.op(q, fn, reads, writes, chan=chan)

    def _skip(self, p, so):
        sp = STREAM[p.eng]
        return sp == so and p.eng in COMPUTE and not SAME_SYNC[p.eng]

    def run_block(self):
        nc = self.nc
        ops = self.ops
        for o in ops:
            so = STREAM[o.eng]
            for d in o.deps:
                p = ops[d]
                if p.eng in DMAQ or self._skip(p, so):
                    continue
                p.needs_inc = True
        cnt = {e: 0 for e in COMPUTE}
        chan_cnt = {}
        chan_hist = {}
        for o in ops:
            if o.eng in DMAQ:
                c = chan_cnt.get(o.chan, 0) + 16
                chan_cnt[o.chan] = c
                o.dma_cnt = c
                chan_hist.setdefault(o.chan, []).append((o.idx, c))
            elif o.needs_inc:
                cnt[o.eng] += 1
                o.cnt = cnt[o.eng]
        chan_idx = {c: [i for i, _ in h] for c, h in chan_hist.items()}
        streams = {s: [] for s in ("pe", "act", "dve", "pool", "sp")}
        waited = {s: {} for s in streams}
        for o in ops:
            so = STREAM[o.eng]
            need = {}
            for d in o.deps:
                p = ops[d]
                if p.eng in DMAQ:
                    h = chan_hist[p.chan]
                    k = bisect.bisect_left(chan_idx[p.chan], o.idx) - 1
                    key = ("chan", p.chan)
                    need[key] = max(need.get(key, 0), h[k][1])
                else:
                    if self._skip(p, so):
                        continue
                    key = ("eng", p.eng)
                    need[key] = max(need.get(key, 0), p.cnt)
            w = []
            for key, c in need.items():
                if waited[so].get(key, 0) >= c:
                    continue
                waited[so][key] = c
                w.append((key, c))
            streams[so].append((o, w))
        with ExitStack() as es:
            sems = {}
            for e in COMPUTE:
                sems[("eng", e)] = es.enter_context(nc.semaphore("s_" + e))
            for c in chan_cnt:
                sems[("chan", c)] = es.enter_context(nc.semaphore("c_" + str(c)))
            block = es.enter_context(nc.Block())

            def mk(sname):
                def body(eng):
                    for o, w in streams[sname]:
                        for key, c in w:
                            eng.wait_ge(sems[key], c)
                        ins = o.fn(eng)
                        if o.eng in DMAQ:
                            ins.then_inc(sems[("chan", o.chan)], 16)
                        elif o.needs_inc:
                            ins.then_inc(sems[("eng", o.eng)], 1)
                    if sname == "sp":
                        for c, v in chan_cnt.items():
                            eng.wait_ge(sems[("chan", c)], v)
                        for e in COMPUTE:
                            if cnt[e] > 0:
                                eng.wait_ge(sems[("eng", e)], cnt[e])
                return body

            block.tensor(mk("pe"))
            block.scalar(mk("act"))
            block.vector(mk("dve"))
            block.gpsimd(mk("pool"))
            block.sync(mk("sp"))


def _colmap():
    cm = -np.ones(NBLK * 128, np.int64)
    for g, (rw_off, gl_off, rt_off, pad) in enumerate(((0, 0, 0, True), (384, 160, 320, True), (768, 320, 640, False))):
        base = g * 1024
        cm[base:base + 384] = RW0 + rw_off + np.arange(384)
        for j in range(5):
            if pad:
                cm[base + 384 + j * 64: base + 384 + j * 64 + 32] = GL0 + gl_off + j * 32 + np.arange(32)
            else:
                cm[base + 384 + j * 64: base + 384 + j * 64 + 64] = GL0 + gl_off + j * 64 + np.arange(64)
        cm[base + 704: base + 1024] = RT0 + rt_off + np.arange(320)
    b = 24 * 128
    cm[b:b + 320] = GL0 + 640 + np.arange(320)
    cm[b + 320:b + 640] = RT0 + 960 + np.arange(320)
    b = 29 * 128
    cm[b:b + 128] = RW0 + 1152 + np.arange(128)
    cm[b + 128:b + 256] = RW0 + 1280 + np.arange(128)
    cm[b + 256:b + 288] = RW0 + 1408 + np.arange(32)
    cm[b + 288:b + 304] = GL0 + 960 + np.arange(16)
    for gi, off in enumerate((0, 320)):
        b = (32 + 3 * gi) * 128
        for r in range(704, 1024):
            i = r - 704
            h, d = divmod(i, 64)
            cm[b + (r - 640)] = RT0 + off + h * 64 + ((d + 32) % 64)
    return cm


def _rows_layout(v, nblk):
    return np.ascontiguousarray(v.reshape(nblk, 128).T)


VEC = {}


def _vec_layout():
    off = 0
    for name, n in (("mu", 12), ("w0", 3), ("a0", 3), ("kkv", 3), ("ka", 3), ("rk", 3), ("nw", 8), ("nb", 8),
                    ("bgkb", 3), ("ln1g", 8), ("ln1b", 8), ("ln2g", 8), ("ln2b", 8), ("cw0", 44), ("cw1", 44),
                    ("cw2", 44), ("cb", 44), ("rscale", 8), ("kscale", 8), ("ldret", 8)):
        VEC[name] = (off, n)
        off += n
    return off


NV = _vec_layout()
NM = 4 * 384


def _prep_layer_params(inp, l):
    f = np.float32
    vec = np.zeros((128, NV), f)

    def put(name, arr):
        o, n = VEC[name]
        assert arr.shape == (128, n), (name, arr.shape)
        vec[:, o:o + n] = arr

    mu = inp["rwkv_mu"][l]
    mup = np.zeros(12 * 128, f)
    mup[0:384] = mu[0:384]
    mup[384:768] = mu[384:768]
    mup[768:1152] = mu[768:1152]
    mup[1152:1152 + 128] = mu[1152:1280]
    mup[1280:1408] = mu[1280:1408]
    mup[1408:1440] = mu[1408:1440]
    put("mu", _rows_layout(mup, 12))
    put("w0", _rows_layout(inp["rwkv_w0"][l], 3))
    put("a0", _rows_layout(inp["rwkv_a0"][l], 3))
    put("kkv", _rows_layout(inp["rwkv_kk"][l], 3))
    put("ka", _rows_layout(inp["rwkv_ka"][l], 3))
    put("rk", _rows_layout(inp["rwkv_rk"][l].reshape(-1), 3))
    nw = np.ones(1024, f)
    nb = np.zeros(1024, f)
    nw[0:384] = inp["rwkv_lnw"][l]
    nb[0:384] = inp["rwkv_lnb"][l]
    nw[384:704] = np.tile(inp["gla_norm_w"][l], 5)
    put("nw", _rows_layout(nw, 8))
    put("nb", _rows_layout(nb, 8))
    bb = np.zeros(384, f)
    for j in range(5):
        bb[j * 64:j * 64 + 32] = inp["gla_bgk_b"][l][j * 32:(j + 1) * 32]
    put("bgkb", _rows_layout(bb, 3))
    put("ln1g", _rows_layout(inp["ln1_g"][l], 8))
    put("ln1b", _rows_layout(inp["ln1_b"][l], 8))
    put("ln2g", _rows_layout(inp["ln2_g"][l], 8))
    put("ln2b", _rows_layout(inp["ln2_b"][l], 8))
    cw = inp["ffn_conv_w"][l]
    put("cw0", _rows_layout(cw[0], 44))
    put("cw1", _rows_layout(cw[1], 44))
    put("cw2", _rows_layout(cw[2], 44))
    put("cb", _rows_layout(inp["ffn_conv_b"][l], 44))
    rs = np.ones(1024, f)
    rs[384:704] = 32.0 ** -0.5
    ks = np.ones(1024, f)
    ks[704:1024] = 64.0 ** -0.5
    put("rscale", _rows_layout(rs, 8))
    put("kscale", _rows_layout(ks, 8))
    ldr = np.zeros(1024, np.float64)
    for h in range(5):
        ldr[704 + h * 64:704 + (h + 1) * 64] = math.log(1.0 - 2.0 ** (-5.0 - h))
    put("ldret", _rows_layout(ldr.astype(f), 8))
    mats = np.zeros((128, NM), f)
    mats[0:64, 0:384] = inp["rwkv_bw"][l]
    mats[64:128, 0:384] = inp["rwkv_ba"][l]
    mats[:, 384:768] = inp["rwkv_bg"][l][0:128]
    mats[0:32, 768:1152] = inp["rwkv_bg"][l][128:160]
    bgk = inp["gla_bgk"][l]
    for j in range(5):
        mats[32:48, 1152 + j * 64:1152 + j * 64 + 32] = bgk[:, j * 32:(j + 1) * 32]
    return vec, mats


def _prep_weights(inp):
    cm = _colmap()
    npieces = NPIECE_IN + 4 + NJ + 8
    W = np.zeros((NL, npieces, 128, SLOT), np.float32)
    for l in range(NL):
        win = inp["w_in"][l]
        wp = np.zeros((1024, NBLK * 128), np.float32)
        ok = cm >= 0
        wp[:, ok] = win[:, cm[ok]]
        p = 0
        for i in range(NPIECE_IN):
            blkc = wp[:, i * 256:(i + 1) * 256].reshape(8, 128, 256).transpose(1, 0, 2)
            W[l, p, :, :2048] = blkc.reshape(128, 2048)
            p += 1
        wo = inp["w_out"][l]
        for i in range(4):
            blkc = wo[:, i * 256:(i + 1) * 256].reshape(8, 128, 256).transpose(1, 0, 2)
            W[l, p, :, :2048] = blkc.reshape(128, 2048)
            p += 1
        up = inp["ffn_up"][l]
        for j in range(NJ):
            ab = np.concatenate([up[:, j * 128:(j + 1) * 128], up[:, DFF + j * 128:DFF + (j + 1) * 128]], axis=1)
            blkc = ab.reshape(8, 128, 256).transpose(1, 0, 2)
            W[l, p, :, :2048] = blkc.reshape(128, 2048)
            p += 1
        dn = inp["ffn_down"][l]
        for cb in range(8):
            blkc = dn[:, cb * 128:(cb + 1) * 128].reshape(NJ, 128, 128).transpose(1, 0, 2)
            W[l, p, :, :NJ * 128] = blkc.reshape(128, NJ * 128)
            p += 1
        assert p == npieces
    return W


def _consts():
    ident = np.eye(128, dtype=np.float32)
    sI = np.arange(128)[:, None]
    tI = np.arange(128)[None, :]
    bones = np.zeros((128, 128), np.float32)
    bones[0:64, 0:64] = 1
    bones[64:128, 64:128] = 1
    cst = np.zeros((128, 672), np.float32)
    cst[:, 0:128] = ident
    cst[:, 128:256] = bones
    cst[:, 256:384] = (tI >= sI)
    cst[:, 384:512] = (tI > sI)
    cst[:, 512:640] = (tI < sI)
    tok = np.zeros((128, 32), np.float32)
    tok[:, 0:6] = 1.0 / 64
    tok[:, 16:22] = 64e-5
    tok[:, 22:32] = 1e-6
    cst[:, 640:672] = tok
    return cst


def _rope_tables():
    ntok = SEQ + NSS * LS
    pos = np.concatenate([np.arange(SEQ), np.tile(PAST + np.arange(LS), NSS)]).astype(np.float32)
    inv = (1.0 / (10000.0 ** np.linspace(0.0, 1.0, 32, dtype=np.float32))).astype(np.float32)
    ang = pos[None, :] * inv[:, None]
    cos = np.cos(ang).astype(np.float32)
    sin = np.sin(ang).astype(np.float32)
    C = np.ones((384, ntok), np.float32)
    S = np.zeros((384, ntok), np.float32)
    for h in range(5):
        r0 = 64 + h * 64
        C[r0:r0 + 32] = cos
        C[r0 + 32:r0 + 64] = cos
        S[r0:r0 + 32] = -sin
        S[r0 + 32:r0 + 64] = sin
    C = C.reshape(3, 128, ntok).transpose(1, 0, 2)
    S = S.reshape(3, 128, ntok).transpose(1, 0, 2)
    return np.ascontiguousarray(C), np.ascontiguousarray(S)


CFG_FULL = dict(tiles=[("p", i) for i in range(SEQ // TT)] + [("s", 0)], nl=NL)


def build(cfg):
    nc = bass.Bass("TRN2", target_bir_lowering=False)
    S = Sched(nc)
    NTOK = SEQ + NSS * LS
    npieces = NPIECE_IN + 4 + NJ + 8
    nl = cfg["nl"]
    tiles = cfg["tiles"]

    def din(name, shape, dt=F32):
        return nc.dram_tensor(name, list(shape), dt, kind="ExternalInput").ap()

    def dout(name, shape):
        return nc.dram_tensor(name, list(shape), F32, kind="ExternalOutput").ap()

    xT_d = din("xT", [128, 8, NTOK])
    wts_d = din("wts", [NL, npieces, 128, SLOT])
    vec_d = din("vecs", [NL, 128, NV])
    mat_d = din("mats", [NL, 128, NM])
    cst_d = din("cst", [128, 672])
    cos_d = din("ropec", [128, 3, NTOK])
    sin_d = din("ropes", [128, 3, NTOK])
    sst_d = din("sst", [NL, NSS, 128, 512])
    ssh_d = din("ssh", [NL, 128, 12 * NSS])
    scv_d = din("scv", [NL, 128, 44 * NSS * 2])
    yT_d = dout("yT", [128, 8, NTOK])
    sto_d = dout("sto", [NL, 1 + NSS, 128, 512])
    sho_d = dout("sho", [NL, 128, 12 * (1 + NSS)])
    cvo_d = dout("cvo", [NL, 128, 44 * (1 + NSS) * 2])
    dbg_d = dout("dbg", [128, 8192]) if cfg.get("dbg") else None
    dbg_state = {"col": 0, "items": []}

    def dbg(name, ap, bufs, np_=128):
        if dbg_d is None:
            return
        n = 1
        for d_ in ap.shape[1:]:
            n *= d_
        c0 = dbg_state["col"]
        if c0 + n > 8192:
            return
        dbg_state["col"] = c0 + n
        dbg_state["items"].append((name, c0, n, np_, tuple(ap.shape)))
        flat = dbg_d[0:np_, c0:c0 + n]
        if len(ap.shape) == 3:
            flat = flat.rearrange("p (a b) -> p a b", b=ap.shape[2])
        S.dma(E("dma_start", out=flat, in_=ap), reads=bufs, chan="dbg")
    cfg["_dbg_items"] = dbg_state["items"]

    def sb(name, shape, dt=F32):
        return nc.alloc_sbuf_tensor("sb_" + name, list(shape), dt)

    TM = TT
    x32 = sb("x32", [128, 8, TM]); b_x32 = Buf("x32")
    xTb = sb("xTb", [128, 8, TM], BF16); b_xTb = Buf("xTb")
    un1 = sb("un1", [128, 12 * (TM + NSS)]); b_PRW = Buf("PRW")
    PRW = un1[:].rearrange("p (a b) -> p a b", b=TM + NSS)
    Rg = sb("Rg", [128, 8, TM]); b_R = Buf("R")
    Kg = sb("Kg", [128, 8, TM]); b_K = Buf("K")
    Vg = sb("Vg", [128, 8, TM]); b_V = Buf("V")
    Gg = sb("Gg", [128, 8, TM], BF16); b_G = Buf("G")

    XS = sb("XS", [128, 3, TM]); b_XS = Buf("XS")
    ATt = sb("ATt", [128, 3, TM]); b_AT = Buf("AT")
    BTt = sb("BTt", [128, 3, TM]); b_BT = Buf("BT")
    bonus = sb("bonus", [128, 3, TM]); b_bonus = Buf("bonus")
    ynT = sb("ynT", [128, 8, TM]); b_ynT = Buf("ynT")
    SW = ynT; b_SW = b_ynT
    mixT = sb("mixT", [128, 8, TM], BF16); b_mixT = Buf("mixT")
    hT = un1[:].bitcast(BF16)[:, 0:NJ * TM].rearrange("p (a b) -> p a b", b=TM); b_hT = b_PRW
    WC = sb("WC", [128, 8, 16]); b_WC = Buf("WC")
    NTMP = 9
    tmp = [sb("tmp%d" % i, [128, TM]) for i in range(NTMP)]
    b_tmp = [Buf("tmp%d" % i) for i in range(NTMP)]
    tmpb = [sb("tmpb%d" % i, [128, TM], BF16) for i in range(4)]
    b_tmpb = [Buf("tmpb%d" % i) for i in range(4)]
    ubuf = [sb("ubuf%d" % i, [128, TM + 2 * NSS]) for i in range(2)]
    b_ubuf = [Buf("ubuf%d" % i) for i in range(2)]
    cosT = sb("cosT", [128, 3, TM]); sinT = sb("sinT", [128, 3, TM]); b_rope = Buf("rope")
    cst = sb("cst", [128, 672]); b_cst = Buf("cst")
    onesb = sb("onesb", [128, 128], BF16); b_ones = Buf("onesb")
    bonesr = sb("bonesr", [128, 128]); b_bones = Buf("bonesr")
    rmask = sb("rmask", [128, TM]); b_rmask = Buf("rmask")
    vecs = [sb("vecs%d" % i, [128, NV]) for i in range(2)]; b_vecs = [Buf("vecs%d" % i) for i in range(2)]
    vaux = [sb("vaux%d" % i, [128, 8]) for i in range(2)]; b_vaux = [Buf("vaux%d" % i) for i in range(2)]
    _m0 = sb("mats0", [128, NM], BF16); _bm0 = Buf("mats0")
    mats = [_m0, _m0]; b_mats = [_bm0, _bm0]
    hpw = [sb("hpw%d" % i, [128, 1024]) for i in range(2)]; b_hpw = [Buf("hpw%d" % i) for i in range(2)]
    mix32 = sb("mix32", [128, 8, 2]); b_mix32 = Buf("mix32")
    h32 = sb("h32", [128, NJ, 2]); b_h32 = Buf("h32")
    slots = [sb("slot%d" % i, [128, SLOT], BF16) for i in range(NSLOT)]
    b_slots = [Buf("slot%d" % i) for i in range(NSLOT)]
    Sst = [sb("Sst%d" % l, [128, 8, 64]) for l in range(NL)]; b_Sst = [Buf("Sst%d" % l) for l in range(NL)]
    _ss = sb("Ssm0", [128, 8, 64]); _bss = Buf("Ssm0")
    Ssm = [_ss, _ss]; b_Ssm = [_bss, _bss]
    _s0 = sb("Sld0", [128, 8, 64]); _bs0 = Buf("Sld0")
    Sld = [_s0, _s0]; b_Sld = [_bs0, _bs0]
    shc = sb("shc", [128, NL, 12]); b_shc = [Buf("shc%d" % l) for l in range(NL)]
    shst = sb("shst", [128, 12, NSS]); b_shst = Buf("shst")
    sho = sb("sho", [128, 12, NSS]); b_sho = Buf("sho")
    cvc = sb("cvc", [128, NL, 44, 2]); b_cvc = [Buf("cvc%d" % l) for l in range(NL)]
    cvst = sb("cvst", [128, 44, NSS, 2]); b_cvst = Buf("cvst")
    cvo = sb("cvo", [128, 44, NSS, 2]); b_cvo = Buf("cvo")
    CM = 128
    vtok = sb("vtok", [CM, 1024]); b_vtok = Buf("vtok")
    ktok = sb("ktok", [CM, 1024]); b_ktok = Buf("ktok")
    btok = sb("btok", [CM, 384]); b_btok = Buf("btok")
    ark = sb("ark", [CM, 16, CM]); b_ark = Buf("ark")
    aak = sb("aak", [CM, 6, CM]); b_aak = Buf("aak")
    arb = sb("arb", [CM, 6, CM]); b_arb = Buf("arb")
    _p = sb("Pm0", [CM, 6, CM]); _bp = Buf("Pm0")
    _q = sb("Qm0", [CM, 6, CM]); _bq = Buf("Qm0")
    Pm = [_p, _p]; b_Pm = [_bp, _bp]
    Qm = [_q, _q]; b_Qm = [_bq, _bq]
    Xm = sb("Xm", [CM, 6, CM]); b_Xm = Buf("Xm")
    rhs0 = sb("rhs0", [CM, 384]); b_rhs0 = Buf("rhs0")
    usb = sb("usb", [CM, 384]); b_usb = Buf("usb")
    ysb = sb("ysb", [CM, 1024]); b_ysb = Buf("ysb")
    ysq = sb("ysq", [CM, 1024]); b_ysq = Buf("ysq")
    yst = sb("yst", [CM, 8, 16]); b_yst = Buf("yst")
    ps = nc.alloc_psum_tensor("ps", [128, 8, 512], F32)
    b_ps = [Buf("ps%d" % i) for i in range(8)]
    bank_ctr = [0]

    def nbank():
        b = bank_ctr[0] % 8
        bank_ctr[0] += 1
        return b

    R_ = lambda ap: ap.bitcast(F32R)

    def E(name, *a, **k):
        return lambda e: getattr(e, name)(*a, **k)

    S.dma(E("dma_start", out=cst[:], in_=cst_d), writes=[b_cst], chan="cst")
    ident = cst[:, 0:128]
    m_incl = cst[:, 256:384]
    m_su = cst[:, 384:512]
    m_sl = cst[:, 512:640]
    eye64 = cst[:, 0:128]
    tokc = cst[:, 640:672]
    S.dve(E("memset", onesb[:], 1.0), writes=[b_ones])
    S.dve(E("tensor_copy", out=R_(bonesr[:]), in_=cst[:, 128:256]), reads=[b_cst], writes=[b_bones])
    S.dve(E("memset", Sld[0][:], 0.0), writes=[b_Sld[0]])
    for l in range(NL):
        S.dve(E("tensor_copy", out=R_(Sst[l][:]), in_=Sld[0][:]), reads=[b_Sld[0]], writes=[b_Sst[l]])
        S.dve(E("memset", shc[:, l, :], 0.0), writes=[b_shc[l]])
        S.dve(E("memset", cvc[:, l, :, :], 0.0), writes=[b_cvc[l]])

    plan = []
    for (kind, ti) in tiles:
        for l in range(nl):
            for p in range(npieces):
                plan.append((l, p))
    ws = {"issued": 0, "used": 0}

    wbf_d = nc.dram_tensor("wbf", [NL, npieces, 128, SLOT], BF16).ap()
    b_wbf = {}

    def ws_issue():
        i = ws["issued"]
        if i >= len(plan):
            return
        l, p = plan[i]
        s = i % NSLOT
        n = 2048 if p < NPIECE_IN + 4 + NJ else NJ * 128
        if (l, p) not in b_wbf:
            S.dma(E("dma_start", out=slots[s][:, 0:n], in_=wts_d[l, p, :, 0:n]),
                  writes=[b_slots[s]], chan="slot%d" % s, q="poolq")
            if len(tiles) > 1:
                b_wbf[(l, p)] = Buf("wbf%d_%d" % (l, p))
                S.dma(E("dma_start", out=wbf_d[l, p, :, 0:n], in_=slots[s][:, 0:n]), reads=[b_slots[s]],
                      writes=[b_wbf[(l, p)]], chan="wbfw")
        else:
            S.dma(E("dma_start", out=slots[s][:, 0:n], in_=wbf_d[l, p, :, 0:n]), reads=[b_wbf[(l, p)]],
                  writes=[b_slots[s]], chan="slot%d" % s)
        ws["issued"] += 1

    def ws_next(l, p):
        i = ws["used"]
        assert plan[i] == (l, p), (plan[i], l, p)
        while ws["issued"] < min(len(plan), i + NSLOT):
            ws_issue()
        ws["used"] += 1
        s = i % NSLOT
        return slots[s], b_slots[s]

    for _ in range(NSLOT - 1):
        ws_issue()

    pv_ctr = [0]
    hp_ctr = [0]

    def hp_load(l, p, part):
        i = hp_ctr[0] % 2
        hp_ctr[0] += 1
        if part[0] == "sub":
            src = wts_d[l, p, :, 0:2048].rearrange("p (kc c) -> p kc c", c=256)[:, :, part[1] * 128:(part[1] + 1) * 128]
            dst = hpw[i][:, 0:1024].rearrange("p (kc c) -> p kc c", c=128)
        else:
            src = wts_d[l, p, :, part[1] * 1024:part[1] * 1024 + part[2] * 128]
            dst = hpw[i][:, 0:part[2] * 128]
        S.dma(E("dma_start", out=dst, in_=src), writes=[b_hpw[i]], chan="hp%d" % i)
        return hpw[i], b_hpw[i]

    def hp_mm8(l, p, sub, rhs_fn, brhs):
        w, bw = hp_load(l, p, ("sub", sub))
        bkh = nbank()
        for kc in range(8):
            S.pe(E("matmul", ps[:, bkh, 0:2], lhsT=w[:, kc * 128:(kc + 1) * 128], rhs=rhs_fn(kc), start=(kc == 0), stop=(kc == 7)),
                 reads=[bw, brhs], writes=[b_ps[bkh]])
        return bkh

    def stage(n):
        if cfg.get("stop") == n:
            raise _Stop()

    def _tile_loop():
        for (kind, ti) in tiles:
            if kind == "p":
                T = TT; nseq = 1; L = TT; C = 128; tok0 = ti * TT; nsteps = 6
            else:
                T = NSS * LS; nseq = NSS; L = LS; C = LS; tok0 = SEQ; nsteps = 1
            nch = T // C
            last_prompt = (kind == "p" and ti == SEQ // TT - 1)
            hpf = (kind == "p" and ti == 0 and cfg.get("hp", True))
            clo = 2 if hpf else 0
            S.dma(E("dma_start", out=x32[:, :, 0:T], in_=xT_d[:, :, tok0:tok0 + T]),
                  writes=[b_x32], chan="x32")
            S.dma(E("dma_start", out=xTb[:, :, 0:T], in_=xT_d[:, :, tok0:tok0 + T]),
                  writes=[b_xTb], chan="xTb", q="poolq")
            S.dma(E("dma_start", out=cosT[:, :, 0:T], in_=cos_d[:, :, tok0:tok0 + T]),
                  writes=[b_rope], chan="rope")
            S.dma(E("dma_start", out=sinT[:, :, 0:T], in_=sin_d[:, :, tok0:tok0 + T]),
                  writes=[b_rope], chan="rope")
            S.dve(E("memset", rmask[:, 0:T], 1.0), writes=[b_rmask])
            S.dve(E("memset", rmask[:, 0:T].rearrange("p (c t) -> p c t", t=C)[:, :, 0:1], 0.0),
                  writes=[b_rmask])

            for l in range(nl):
                pv = pv_ctr[0] % 2
                pv_ctr[0] += 1
                vec, bvec, mat, bmat, vx, bvx = vecs[pv], b_vecs[pv], mats[pv], b_mats[pv], vaux[pv], b_vaux[pv]
                S.dma(E("dma_start", out=vec[:], in_=vec_d[l]), writes=[bvec], chan="vec%d" % pv)
                S.dma(E("dma_start", out=mat[:], in_=mat_d[l]), writes=[bmat], chan="mat0",
                      q="poolq")

                def V_(name, b=0, n=1, vec=vec):
                    o, _ = VEC[name]
                    return vec[:, o + b:o + b + n]

                o_ka = VEC["ka"][0]; o_bg = VEC["bgkb"][0]
                S.dve(E("tensor_scalar", out=vx[:, 0:3], in0=vec[:, o_ka:o_ka + 3], scalar1=-1.0,
                                                                        scalar2=1.0, op0=ALU.mult, op1=ALU.add),
                      reads=[bvec], writes=[bvx])
                S.dve(E("tensor_scalar", out=vx[:, 3:6], in0=vec[:, o_bg:o_bg + 3], scalar1=-1.0,
                                                                        scalar2=None, op0=ALU.mult),
                      reads=[bvec], writes=[bvx])

                if kind == "s":
                    S.dma(E("dma_start", out=shst[:].rearrange("p a b -> p (a b)"), in_=ssh_d[l]),
                          writes=[b_shst], chan="shst")
                    S.dve(E("tensor_copy", out=PRW[:, :, 0:NSS * (LS + 1)].rearrange("p a (s t) -> p a s t", t=LS + 1)[:, :, :, 0],
                                                  in_=shst[:]), reads=[b_shst], writes=[b_PRW])
                else:
                    S.dve(E("tensor_copy", out=PRW[:, :, 0], in_=shc[:, l, :]), reads=[b_shc[l]], writes=[b_PRW])

                def prw_dst(pb, T=T, nseq=nseq, L=L):
                    return PRW[:, pb, 0:nseq * (L + 1)].rearrange("p (s t) -> p s t", t=L + 1)[:, :, 1:L + 1]

                def prw_prev(pb, T=T, nseq=nseq, L=L):
                    return PRW[:, pb, 0:nseq * (L + 1)].rearrange("p (s t) -> p s t", t=L + 1)[:, :, 0:L]

                def t3(ap, nseq=nseq, L=L):
                    return ap.rearrange("p (s t) -> p s t", t=L)

                evi = [0]
                for piece in range(NPIECE_IN):
                    slot, bslot = ws_next(l, piece)
                    for sub in range(2):
                        blk = piece * 2 + sub
                        bk = nbank()
                        for kc in range(8):
                            S.pe(E("matmul",
                                ps[:, bk, 0:T], lhsT=slot[:, kc * 256 + sub * 128: kc * 256 + sub * 128 + 128],
                                rhs=xTb[:, kc, 0:T], start=(kc == 0), stop=(kc == 7)),
                                reads=[bslot, b_xTb], writes=[b_ps[bk]])
                        g, gb = divmod(blk, 8)

                        def proj_evac(srcap, lo, hi, bkb, blk=blk, g=g, gb=gb):
                            if blk < 24 and gb < 3:
                                pb = g * 3 + gb
                            elif 29 <= blk < 32:
                                pb = 9 + blk - 29
                            else:
                                pb = None
                            if pb is not None:
                                if kind == "p":
                                    S.act(E("copy", out=PRW[:, pb, 1 + lo:1 + hi], in_=srcap), reads=[bkb], writes=[b_PRW])
                                else:
                                    S.act(E("copy", out=prw_dst(pb), in_=t3(srcap)), reads=[bkb], writes=[b_PRW])
                            elif blk < 24:
                                tile_, btile = ((Rg, b_R), (Kg, b_K), (Vg, b_V))[g]
                                S.dve(E("tensor_copy", out=R_(tile_[:, gb, lo:hi]), in_=srcap), reads=[bkb], writes=[btile])
                            elif blk < 29:
                                S.act(E("activation", out=Gg[:, 3 + blk - 24, lo:hi], in_=srcap, func=AF.Silu), reads=[bkb],
                                      writes=[b_G])
                            else:
                                S.dve(E("tensor_copy", out=SW[:, blk - 32, lo:hi], in_=srcap), reads=[bkb], writes=[b_SW])

                        proj_evac(ps[:, bk, clo:T], clo, T, b_ps[bk])
                        if hpf:
                            bkh = hp_mm8(l, piece, sub, lambda kc: x32[:, kc, 0:2], b_x32)
                            proj_evac(ps[:, bkh, 0:2], 0, 2, b_ps[bkh])

                stage(1)
                for pb in range(12):
                    g, gb = divmod(pb, 3)
                    tile_, btile = ((Rg, b_R), (Kg, b_K), (Vg, b_V), (XS, b_XS))[g]
                    tb = pb % 8
                    d3 = t3(tmp[tb][:, 0:T])
                    S.dve(E("tensor_tensor", out=d3, in0=prw_prev(pb), in1=prw_dst(pb), op=ALU.subtract),
                          reads=[b_PRW], writes=[b_tmp[tb]])
                    dst3 = t3(tile_[:, gb, 0:T])
                    if g <= 2:
                        dst3 = R_(dst3)
                    S.dve(E("scalar_tensor_tensor",
                        out=dst3, in0=d3, scalar=V_("mu", pb), in1=prw_dst(pb), op0=ALU.mult, op1=ALU.add),
                        reads=[b_PRW, b_tmp[tb], bvec], writes=[btile])
                if l == 0:
                    dbg("PRW0", PRW[:, 0, 0:80], [b_PRW]); dbg("Rsh", Rg[:, 0, 0:T], [b_R]); dbg("shst", shst[:, 0, :], [b_shst])
                lastcol = PRW[:, :, 0:nseq * (L + 1)].rearrange("p a (s t) -> p a s t", t=L + 1)[:, :, :, L]
                if kind == "s":
                    S.act(E("copy", out=sho[:], in_=lastcol), reads=[b_PRW], writes=[b_sho])
                    S.dma(E("dma_start", out=sho_d[l][:, 12:12 * (1 + NSS)], in_=sho[:].rearrange("p a b -> p (a b)")),
                          reads=[b_sho], chan="shout")
                else:
                    S.act(E("copy", out=shc[:, l, :], in_=lastcol[:, :, 0]), reads=[b_PRW],
                          writes=[b_shc[l]])
                    if last_prompt:
                        S.dma(E("dma_start", out=sho_d[l][:, 0:12], in_=shc[:, l, :]), reads=[b_shc[l]],
                              chan="shout")

                stage(2)
                for gi, (tile_, btile) in enumerate(((Rg, b_R), (Kg, b_K))):
                    a = tile_[:, 5:8, 0:T]
                    sw = SW[:, gi * 3:gi * 3 + 3, 0:T]
                    S.dve(E("tensor_tensor", out=R_(a), in0=a, in1=cosT[:, :, 0:T], op=ALU.mult),
                          reads=[b_rope, btile], writes=[btile])
                    S.dve(E("tensor_tensor", out=sw, in0=sw, in1=sinT[:, :, 0:T], op=ALU.mult),
                          reads=[b_rope, b_SW], writes=[b_SW])
                    S.dve(E("tensor_tensor", out=R_(a), in0=a, in1=sw, op=ALU.add),
                          reads=[b_SW, btile], writes=[btile])

                thx, bthx = tmpb[0], b_tmpb[0]
                sgx, bsgx = tmpb[1], b_tmpb[1]
                sgx2, bsgx2 = tmpb[2], b_tmpb[2]
                gkb, bgkb_ = tmpb[3], b_tmpb[3]
                S.act(E("activation", out=thx[0:64, 0:T], in_=XS[0:64, 0, 0:T], func=AF.Tanh), reads=[b_XS], writes=[bthx])
                S.act(E("copy", out=thx[64:128, 0:T], in_=XS[64:128, 0, 0:T]), reads=[b_XS], writes=[bthx])
                S.act(E("activation", out=sgx[:, 0:T], in_=XS[:, 1, 0:T], func=AF.Sigmoid), reads=[b_XS], writes=[bsgx])
                S.act(E("activation", out=sgx2[0:32, 0:T], in_=XS[0:32, 2, 0:T], func=AF.Sigmoid), reads=[b_XS], writes=[bsgx2])
                S.act(E("copy", out=gkb[32:64, 0:T], in_=XS[32:64, 2, 0:T]), reads=[b_XS], writes=[bgkb_])

                for b in range(8):
                    t_ld, t_lw, t_W, t_IW, t_a, t_kk, t_x, t_y, t_r = tmp
                    bl_ld, bl_lw, bl_W, bl_IW, bl_a, bl_kk, bl_x, bl_y, bl_r = b_tmp
                    if b < 3:
                        c0 = b * 128
                        bk = nbank()
                        S.pe(E("matmul", ps[:, bk, 0:T], lhsT=mat[0:64, c0:c0 + 128], rhs=thx[0:64, 0:T],
                                                              start=True, stop=True), reads=[bmat, bthx], writes=[b_ps[bk]])
                        S.act(E("activation", out=t_ld[:, 0:T], in_=ps[:, bk, 0:T], func=AF.Sigmoid,
                                                                 bias=V_("w0", b), scale=1.0),
                              reads=[b_ps[bk], bvec], writes=[bl_ld])
                        S.dve(E("tensor_scalar", out=t_ld[:, 0:T], in0=t_ld[:, 0:T], scalar1=-math.exp(-0.5), scalar2=None,
                                                        op0=ALU.mult), reads=[bl_ld], writes=[bl_ld])
                        bk = nbank()
                        S.pe(E("matmul", ps[:, bk, 0:T], lhsT=mat[64:128, c0:c0 + 128], rhs=thx[64:128, 0:T],
                                                              start=True, stop=True), reads=[bmat, bthx], writes=[b_ps[bk]])
                        S.act(E("activation", out=t_a[:, 0:T], in_=ps[:, bk, 0:T], func=AF.Sigmoid,
                                                                 bias=V_("a0", b), scale=1.0),
                              reads=[b_ps[bk], bvec], writes=[bl_a])
                        bk = nbank()
                        S.pe(E("matmul", ps[:, bk, 0:T], lhsT=mat[:, 384 + c0:384 + c0 + 128], rhs=sgx[:, 0:T],
                                                              start=True, stop=False), reads=[bmat, bsgx], writes=[b_ps[bk]])
                        S.pe(E("matmul", ps[:, bk, 0:T], lhsT=mat[0:32, 768 + c0:768 + c0 + 128],
                                                              rhs=sgx2[0:32, 0:T], start=False, stop=True),
                             reads=[bmat, bsgx2], writes=[b_ps[bk]])
                        S.act(E("copy", out=Gg[:, b, 0:T], in_=ps[:, bk, 0:T]), reads=[b_ps[bk]], writes=[b_G])
                        S.dve(E("tensor_scalar", out=t_kk[:, 0:T], in0=Kg[:, b, 0:T], scalar1=V_("kkv", b), scalar2=None,
                                                             op0=ALU.mult), reads=[b_K, bvec], writes=[bl_kk])
                        S.act(E("activation", out=R_(t_r[:, 0:T]), in_=t_kk[:, 0:T], func=AF.Square), reads=[bl_kk],
                              writes=[bl_r])
                        bk = nbank()
                        S.pe(E("matmul", ps[:, bk, 0:T], lhsT=R_(bonesr[:]), rhs=R_(t_r[:, 0:T]), start=True, stop=True),
                             reads=[b_bones, bl_r], writes=[b_ps[bk]])
                        S.act(E("activation", out=t_x[:, 0:T], in_=ps[:, bk, 0:T], func=AF.Sqrt), reads=[b_ps[bk]],
                              writes=[bl_x])
                        S.dve(E("tensor_scalar", out=t_x[:, 0:T], in0=t_x[:, 0:T], scalar1=1e-12, scalar2=None, op0=ALU.max),
                              reads=[bl_x], writes=[bl_x])
                        S.dve(E("reciprocal", out=t_x[:, 0:T], in_=t_x[:, 0:T]), reads=[bl_x], writes=[bl_x])
                        S.dve(E("tensor_tensor", out=t_kk[:, 0:T], in0=t_kk[:, 0:T], in1=t_x[:, 0:T], op=ALU.mult),
                              reads=[bl_x, bl_kk], writes=[bl_kk])
                        S.dve(E("tensor_scalar", out=t_x[:, 0:T], in0=t_a[:, 0:T], scalar1=V_("ka", b),
                                                             scalar2=vx[:, b:b + 1], op0=ALU.mult, op1=ALU.add),
                              reads=[bl_a, bvec, bvx], writes=[bl_x])
                        S.dve(E("tensor_tensor", out=R_(Kg[:, b, 0:T]), in0=Kg[:, b, 0:T], in1=t_x[:, 0:T], op=ALU.mult),
                              reads=[bl_x, b_K], writes=[b_K])
                        S.dve(E("scalar_tensor_tensor", out=R_(t_r[:, 0:T]), in0=Rg[:, b, 0:T], scalar=V_("rk", b),
                                                                    in1=Kg[:, b, 0:T], op0=ALU.mult, op1=ALU.mult),
                              reads=[b_R, b_K, bvec], writes=[bl_r])
                        bk = nbank()
                        S.pe(E("matmul", ps[:, bk, 0:T], lhsT=R_(bonesr[:]), rhs=R_(t_r[:, 0:T]), start=True, stop=True),
                             reads=[b_bones, bl_r], writes=[b_ps[bk]])
                        S.dve(E("tensor_tensor", out=bonus[:, b, 0:T], in0=ps[:, bk, 0:T], in1=Vg[:, b, 0:T],
                                                                    op=ALU.mult), reads=[b_ps[bk], b_V], writes=[b_bonus])
                    elif b < 6:
                        c0 = (b - 3) * 128
                        np_ = 64 if b == 5 else 128
                        bk = nbank()
                        S.pe(E("matmul", ps[0:np_, bk, 0:T], lhsT=mat[32:48, 1152 + c0:1152 + c0 + np_],
                                                                       rhs=gkb[32:48, 0:T], start=True, stop=True),
                             reads=[bmat, bgkb_], writes=[b_ps[bk]])
                        S.act(E("activation", out=t_ld[0:np_, 0:T], in_=ps[0:np_, bk, 0:T], func=AF.Exp,
                                                                          bias=vx[0:np_, b:b + 1], scale=-1.0),
                              reads=[b_ps[bk], bvx], writes=[bl_ld])
                        S.act(E("activation", out=t_ld[0:np_, 0:T], in_=t_ld[0:np_, 0:T], func=AF.Ln, bias=1.0,
                                                              scale=1.0), reads=[bl_ld], writes=[bl_ld])
                        S.dve(E("tensor_scalar", out=t_ld[0:np_, 0:T], in0=t_ld[0:np_, 0:T], scalar1=-1.0 / 16.0,
                                                                 scalar2=None, op0=ALU.mult), reads=[bl_ld], writes=[bl_ld])
                        if b == 5:
                            S.dve(E("tensor_scalar", out=t_ld[64:128, 0:T], in0=rmask[64:128, 0:T], scalar1=0.0,
                                                                 scalar2=V_("ldret", b)[64:128], op0=ALU.mult, op1=ALU.add),
                                  reads=[b_rmask, bvec], writes=[bl_ld])
                    else:
                        S.dve(E("tensor_scalar", out=t_ld[:, 0:T], in0=rmask[:, 0:T], scalar1=0.0,
                                                             scalar2=V_("ldret", b), op0=ALU.mult, op1=ALU.add),
                              reads=[b_rmask, bvec], writes=[bl_ld])
                    S.dve(E("tensor_tensor_scan", out=t_lw[:, 0:T], data0=rmask[:, 0:T], data1=t_ld[:, 0:T], initial=0.0,
                                                         op0=ALU.mult, op1=ALU.add), reads=[bl_ld, b_rmask], writes=[bl_lw])
                    S.act(E("activation", out=t_W[:, 0:T], in_=t_lw[:, 0:T], func=AF.Exp), reads=[bl_lw], writes=[bl_W])
                    S.act(E("activation", out=t_IW[:, 0:T], in_=t_lw[:, 0:T], func=AF.Exp, scale=-1.0), reads=[bl_lw],
                          writes=[bl_IW])
                    S.act(E("copy", out=WC[:, b, 0:nch],
                                                              in_=t_W[:, 0:T].rearrange("p (c t) -> p c t", t=C)[:, :, C - 1]),
                          reads=[bl_W], writes=[b_WC])
                    if b < 3:
                        S.dve(E("tensor_tensor", out=t_y[:, 0:T], in0=t_lw[:, 0:T], in1=t_ld[:, 0:T], op=ALU.subtract),
                              reads=[bl_lw, bl_ld], writes=[bl_y])
                        S.act(E("activation", out=t_y[:, 0:T], in_=t_y[:, 0:T], func=AF.Exp), reads=[bl_y], writes=[bl_y])
                        S.dve(E("scalar_tensor_tensor", out=R_(ATt[:, b, 0:T]), in0=t_kk[:, 0:T], scalar=-1.0,
                                                                    in1=t_y[:, 0:T], op0=ALU.mult, op1=ALU.mult),
                              reads=[bl_kk, bl_y], writes=[b_AT])
                        S.dve(E("tensor_tensor", out=t_y[:, 0:T], in0=t_kk[:, 0:T], in1=t_a[:, 0:T], op=ALU.mult),
                              reads=[bl_kk, bl_a], writes=[bl_y])
                        S.dve(E("tensor_tensor", out=R_(BTt[:, b, 0:T]), in0=t_y[:, 0:T], in1=t_IW[:, 0:T], op=ALU.mult),
                              reads=[bl_y, bl_IW], writes=[b_BT])
                    S.dve(E("scalar_tensor_tensor", out=R_(Rg[:, b, 0:T]), in0=Rg[:, b, 0:T], scalar=V_("rscale", b),
                                                                in1=t_W[:, 0:T], op0=ALU.mult, op1=ALU.mult),
                          reads=[b_R, bl_W, bvec], writes=[b_R])
                    S.dve(E("scalar_tensor_tensor", out=R_(Kg[:, b, 0:T]), in0=Kg[:, b, 0:T], scalar=V_("kscale", b),
                                                                in1=t_IW[:, 0:T], op0=ALU.mult, op1=ALU.mult),
                          reads=[b_K, bl_IW, bvec], writes=[b_K])

                stage(3)
                for ci in range(nch):
                    t0 = ci * C
                    cs = slice(t0, t0 + C)
                    cfg.setdefault("_marks", []).append((l, ci, len(S.ops)))
                    if kind == "p":
                        St, bSt = Sst[l], b_Sst[l]
                    else:
                        sp_ = ci % 2
                        St, bSt = Ssm[sp_], b_Ssm[sp_]
                        S.dma(E("dma_start", out=Sld[sp_][:].rearrange("p a b -> p (a b)"),
                                                                         in_=sst_d[l, ci]),
                              writes=[b_Sld[sp_]], chan="sld0")
                        S.act(E("copy", out=R_(St[:]), in_=Sld[sp_][:]), reads=[b_Sld[sp_]], writes=[bSt])
                    for (src, bsrc, dst, bdst, nb_) in ((Vg, b_V, vtok, b_vtok, 8), (Kg, b_K, ktok, b_ktok, 8),
                                                        (BTt, b_BT, btok, b_btok, 3)):
                        for half in range((nb_ + 3) // 4):
                            bk = nbank()
                            n_in = min(4, nb_ - half * 4)
                            for bl in range(n_in):
                                blk = half * 4 + bl
                                S.pe(E("transpose",
                                    ps[0:C, bk, bl * 128:(bl + 1) * 128], src[:, blk, cs], ident),
                                    reads=[bsrc, b_cst], writes=[b_ps[bk]])
                            w = n_in * 128
                            S.act(E("copy", out=R_(dst[0:C, half * 512:half * 512 + w]),
                                                                                   in_=ps[0:C, bk, 0:w]),
                                  reads=[b_ps[bk]], writes=[bdst])
                    stage(31)
                    def hp(h):
                        return slice((h % 2) * 64, (h % 2) * 64 + 64), h // 2

                    def pv3(bk, n):
                        return ps[0:C, bk, 0:n * C].rearrange("p (h c) -> p h c", c=C)

                    def bc(m, n):
                        return m[0:C, 0:C].unsqueeze(1).to_broadcast([C, n, C])

                    def par3(t, par, n):
                        return t[0:C, :, 0:C].rearrange("p (a two) c -> p a two c", two=2)[:, :, par, :]

                    hpb = min(8, 512 // C)
                    for (lt, blt, rt_, brt, dst, bdst, msk, nh) in ((Kg, b_K, Rg, b_R, ark, b_ark, m_incl, 16),
                                                                   (Kg, b_K, ATt, b_AT, aak, b_aak, m_su, 6),
                                                                   (BTt, b_BT, Rg, b_R, arb, b_arb, m_incl, 6),
                                                                   (BTt, b_BT, ATt, b_AT, Pm[0], b_Pm[0], m_su, 6),
                                                                   (ATt, b_AT, BTt, b_BT, Qm[0], b_Qm[0], m_sl, 6)):
                        npar = nh // 2
                        ngrp = (npar + hpb - 1) // hpb
                        banks = {}
                        for g_ in range(ngrp):
                            for par in range(2):
                                banks[(par, g_)] = nbank()
                        for h in range(nh):
                            rows, blk = hp(h)
                            hl = h // 2
                            bk = banks[(h % 2, hl // hpb)]
                            hi = hl % hpb
                            S.pe(E("matmul", ps[0:C, bk, hi * C:(hi + 1) * C], lhsT=R_(lt[rows, blk, cs]),
                                   rhs=R_(rt_[rows, blk, cs]), start=True, stop=True), reads=[blt, brt], writes=[b_ps[bk]])
                        for g_ in range(ngrp):
                            n_ = min(hpb, npar - g_ * hpb)
                            for par in range(2):
                                bk = banks[(par, g_)]
                                S.dve(E("tensor_tensor", out=R_(par3(dst, par, nh)[:, g_ * hpb:g_ * hpb + n_, :]), in0=pv3(bk, n_),
                                        in1=bc(msk, n_), op=ALU.mult), reads=[b_ps[bk], b_cst], writes=[bdst])
                    if ci == cfg.get("dbgci", 0) and l == 0:
                        dbg("AT0", ATt[:, 0, cs], [b_AT]); dbg("BT0", BTt[:, 0, cs], [b_BT])
                        dbg("K0", Kg[:, 0, cs], [b_K]); dbg("R0", Rg[:, 0, cs], [b_R]); dbg("V0", Vg[:, 0, cs], [b_V])
                        dbg("WC", WC[:, :, 0:4], [b_WC])
                        dbg("P0", Pm[0][0:C, :, 0:C], [b_Pm[0]], C); dbg("Q0", Qm[0][0:C, :, 0:C], [b_Qm[0]], C)
                        dbg("aak", aak[0:C, :, 0:C], [b_aak], C); dbg("arb", arb[0:C, :, 0:C], [b_arb], C)
                        dbg("btok", btok[0:C, 0:128], [b_btok], C)
                    stage(32)
                    S.dve(E("tensor_tensor", out=R_(Xm[0:C, :, 0:C]), in0=Pm[0][0:C, :, 0:C], in1=bc(eye64, 6), op=ALU.add),
                          reads=[b_Pm[0], b_cst], writes=[b_Xm])
                    ng_ = 1 if 6 * C <= 512 else 2
                    hg_ = 6 // ng_

                    def inv_mm(lhs_t, blhs, rhs_t, brhs):
                        bks = []
                        for g_ in range(ng_):
                            bk = nbank()
                            bks.append(bk)
                            for hi in range(hg_):
                                h = g_ * hg_ + hi
                                S.pe(E("matmul", ps[0:C, bk, hi * C:(hi + 1) * C], lhsT=R_(lhs_t[0:C, h, 0:C]),
                                       rhs=R_(rhs_t[0:C, h, 0:C]), start=True, stop=True), reads=[blhs, brhs], writes=[b_ps[bk]])
                        return bks

                    def gsl(t, g_):
                        return t[0:C, g_ * hg_:(g_ + 1) * hg_, 0:C]

                    def par2(t, par, n):
                        return t[0:C, 0:n * 64].rearrange("p (a two v) -> p a two v", two=2, v=64)[:, :, par, :]

                    def pv64(bk, n):
                        return ps[0:C, bk, 0:n * 64].rearrange("p (h v) -> p h v", v=64)

                    def f_y_state():
                        bkp = (nbank(), nbank())
                        for h in range(16):
                            rows, blk = hp(h)
                            bk = bkp[h % 2]
                            hl = h // 2
                            S.pe(E("matmul", ps[0:C, bk, hl * 64:(hl + 1) * 64], lhsT=R_(Rg[rows, blk, cs]), rhs=R_(St[rows, blk, :]),
                                   start=True, stop=True), reads=[b_R, bSt], writes=[b_ps[bk]])
                        for par in range(2):
                            S.act(E("copy", out=par2(ysb, par, 16), in_=pv64(bkp[par], 8)), reads=[b_ps[bkp[par]]], writes=[b_ysb])

                    def f_rhs_state():
                        bkp = (nbank(), nbank())
                        for h in range(6):
                            rows, blk = hp(h)
                            bk = bkp[h % 2]
                            hl = h // 2
                            S.pe(E("matmul", ps[0:C, bk, hl * 64:(hl + 1) * 64], lhsT=R_(ATt[rows, blk, cs]), rhs=R_(St[rows, blk, :]),
                                   start=True, stop=True), reads=[b_AT, bSt], writes=[b_ps[bk]])
                        for par in range(2):
                            S.act(E("copy", out=R_(par2(rhs0, par, 6)), in_=pv64(bkp[par], 3)), reads=[b_ps[bkp[par]]], writes=[b_rhs0])

                    def f_rhs_tok():
                        bk = nbank()
                        for h in range(6):
                            S.pe(E("matmul", ps[0:C, bk, h * 64:(h + 1) * 64], lhsT=R_(aak[0:C, h, 0:C]),
                                   rhs=R_(vtok[0:C, h * 64:(h + 1) * 64]), start=True, stop=True),
                                 reads=[b_aak, b_vtok], writes=[b_ps[bk]])
                        S.dve(E("tensor_tensor", out=R_(rhs0[0:C, :]), in0=rhs0[0:C, :], in1=ps[0:C, bk, 0:384], op=ALU.add),
                              reads=[b_ps[bk], b_rhs0], writes=[b_rhs0])

                    def f_y_tok(half):
                        bk = nbank()
                        for hl in range(8):
                            h = half * 8 + hl
                            S.pe(E("matmul", ps[0:C, bk, hl * 64:(hl + 1) * 64], lhsT=R_(ark[0:C, h, 0:C]),
                                   rhs=R_(vtok[0:C, h * 64:(h + 1) * 64]), start=True, stop=(h >= 6)),
                                 reads=[b_ark, b_vtok], writes=[b_ps[bk]])
                            if h < 6:
                                S.pe(E("matmul", ps[0:C, bk, hl * 64:(hl + 1) * 64], lhsT=R_(arb[0:C, h, 0:C]),
                                       rhs=R_(usb[0:C, h * 64:(h + 1) * 64]), start=False, stop=True),
                                     reads=[b_arb, b_usb], writes=[b_ps[bk]])
                        S.dve(E("tensor_tensor", out=ysb[0:C, half * 512:(half + 1) * 512], in0=ysb[0:C, half * 512:(half + 1) * 512],
                                in1=ps[0:C, bk, :], op=ALU.add), reads=[b_ps[bk], b_ysb], writes=[b_ysb])

                    fillers = [f_y_state, f_rhs_state, f_rhs_tok, lambda: f_y_tok(1)]
                    for st in range(nsteps):
                        bq = inv_mm(Pm[0], b_Pm[0], Qm[0], b_Qm[0])
                        if st < nsteps - 1:
                            bp = inv_mm(Qm[0], b_Qm[0], Pm[0], b_Pm[0])
                        for g_ in range(ng_):
                            S.act(E("copy", out=R_(gsl(Qm[0], g_)), in_=pv3(bq[g_], hg_)), reads=[b_ps[bq[g_]]], writes=[b_Qm[0]])
                        if st < nsteps - 1:
                            for g_ in range(ng_):
                                S.dve(E("tensor_copy", out=R_(gsl(Pm[0], g_)), in_=pv3(bp[g_], hg_)), reads=[b_ps[bp[g_]]],
                                      writes=[b_Pm[0]])
                        if fillers:
                            fillers.pop(0)()
                        bx = inv_mm(Qm[0], b_Qm[0], Xm, b_Xm)
                        for g_ in range(ng_):
                            S.dve(E("tensor_tensor", out=R_(gsl(Xm, g_)), in0=gsl(Xm, g_), in1=pv3(bx[g_], hg_), op=ALU.add),
                                  reads=[b_ps[bx[g_]], b_Xm], writes=[b_Xm])
                    while fillers:
                        fillers.pop(0)()
                    stage(33)
                    bk = nbank()
                    for h in range(6):
                        S.pe(E("matmul", ps[0:C, bk, h * 64:(h + 1) * 64], lhsT=R_(Xm[0:C, h, 0:C]),
                               rhs=R_(rhs0[0:C, h * 64:(h + 1) * 64]), start=True, stop=True),
                             reads=[b_Xm, b_rhs0], writes=[b_ps[bk]])
                    S.act(E("copy", out=R_(usb[0:C, :]), in_=ps[0:C, bk, 0:384]), reads=[b_ps[bk]], writes=[b_usb])
                    stage(34)
                    f_y_tok(0)
                    stage(35)
                    for half in range(2):
                        bk = nbank()
                        for bl in range(4):
                            blk = half * 4 + bl
                            S.pe(E("matmul", ps[:, bk, bl * 128:(bl + 1) * 128],
                                                                           lhsT=R_(ktok[0:C, blk * 128:(blk + 1) * 128]),
                                                                           rhs=R_(vtok[0:C, blk * 128:(blk + 1) * 128]),
                                                                           start=True, stop=(blk >= 3)),
                                 reads=[b_ktok, b_vtok], writes=[b_ps[bk]])
                            if blk < 3:
                                S.pe(E("matmul", ps[:, bk, bl * 128:(bl + 1) * 128],
                                                                               lhsT=R_(btok[0:C, blk * 128:(blk + 1) * 128]),
                                                                               rhs=R_(usb[0:C, blk * 128:(blk + 1) * 128]),
                                                                               start=False, stop=True),
                                     reads=[b_btok, b_usb], writes=[b_ps[bk]])
                        for hh in range(2):
                            rows = slice(hh * 64, hh * 64 + 64)
                            S.dve(E("tensor_tensor",
                                out=R_(St[rows, half * 4:half * 4 + 4, :]), in0=St[rows, half * 4:half * 4 + 4, :],
                                in1=ps[rows, bk, :].rearrange("p (a b) -> p a b", b=128)[:, :, hh * 64:hh * 64 + 64], op=ALU.add),
                                reads=[b_ps[bk], bSt], writes=[bSt])
                    S.dve(E("tensor_tensor", out=R_(St[:]), in0=St[:], in1=WC[:, :, ci:ci + 1].to_broadcast([128, 8, 64]),
                                                           op=ALU.mult), reads=[b_WC, bSt], writes=[bSt])
                    if kind == "s":
                        S.dma(E("dma_start", out=sto_d[l, 1 + ci], in_=St[:].rearrange("p a b -> p (a b)")),
                              reads=[bSt], chan="stout")
                    elif last_prompt and ci == nch - 1:
                        S.dma(E("dma_start", out=sto_d[l, 0], in_=St[:].rearrange("p a b -> p (a b)")),
                              reads=[bSt], chan="stout")
                    if ci == cfg.get("dbgci", 0) and l == 0:
                        dbg("ysb", ysb[0:C, 0:128], [b_ysb], C); dbg("St", St[:, 0, :], [bSt])
                    stage(36)
                    y3 = ysb[0:C, :].rearrange("p (h v) -> p h v", v=64)
                    q3 = ysq[0:C, :].rearrange("p (h v) -> p h v", v=64)
                    s1, s2, mean, tmpv, rstd = (yst[0:C, i, :] for i in range(5))
                    S.dve(E("tensor_tensor", out=ysq[0:C, :], in0=ysb[0:C, :], in1=ysb[0:C, :], op=ALU.mult), reads=[b_ysb],
                          writes=[b_ysq])
                    S.dve(E("tensor_reduce", out=s1, in_=y3, axis=AX.X, op=ALU.add), reads=[b_ysb], writes=[b_yst])
                    S.dve(E("tensor_reduce", out=s2, in_=q3, axis=AX.X, op=ALU.add), reads=[b_ysq], writes=[b_yst])
                    S.dve(E("tensor_tensor", out=mean, in0=s1, in1=tokc[0:C, 0:16], op=ALU.mult),
                          reads=[b_yst, b_cst], writes=[b_yst])
                    S.dve(E("tensor_tensor", out=tmpv, in0=mean, in1=mean, op=ALU.mult), reads=[b_yst],
                          writes=[b_yst])
                    S.dve(E("scalar_tensor_tensor", out=tmpv, in0=s2, scalar=1.0 / 64, in1=tmpv, op0=ALU.mult,
                                                                             op1=ALU.subtract), reads=[b_yst], writes=[b_yst])
                    S.dve(E("tensor_tensor", out=tmpv, in0=tmpv, in1=tokc[0:C, 16:32], op=ALU.add),
                          reads=[b_yst, b_cst], writes=[b_yst])
                    S.act(E("activation", out=tmpv, in_=tmpv, func=AF.Sqrt), reads=[b_yst], writes=[b_yst])
                    S.dve(E("reciprocal", out=rstd, in_=tmpv), reads=[b_yst], writes=[b_yst])
                    S.dve(E("tensor_tensor", out=y3, in0=y3, in1=mean.unsqueeze(2).to_broadcast([C, 16, 64]),
                                                                      op=ALU.subtract), reads=[b_yst, b_ysb], writes=[b_ysb])
                    S.dve(E("tensor_tensor", out=y3, in0=y3, in1=rstd.unsqueeze(2).to_broadcast([C, 16, 64]),
                                                                      op=ALU.mult), reads=[b_yst, b_ysb], writes=[b_ysb])
                    bpb = min(8, 512 // C)
                    for g_ in range(8 // bpb):
                        bk = nbank()
                        for bi in range(bpb):
                            blk = g_ * bpb + bi
                            S.pe(E("transpose", ps[:, bk, bi * C:(bi + 1) * C], ysb[0:C, blk * 128:(blk + 1) * 128],
                                   cst[0:C, 0:C]), reads=[b_ysb, b_cst], writes=[b_ps[bk]])
                        S.act(E("copy", out=ynT[:, g_ * bpb:(g_ + 1) * bpb, cs],
                                in_=ps[:, bk, 0:bpb * C].rearrange("p (a c) -> p a c", c=C)), reads=[b_ps[bk]], writes=[b_ynT])

                stage(4)
                for b in range(8):
                    S.dve(E("tensor_scalar", out=ynT[:, b, 0:T], in0=ynT[:, b, 0:T], scalar1=V_("nw", b), scalar2=V_("nb", b),
                                                         op0=ALU.mult, op1=ALU.add), reads=[b_ynT, bvec], writes=[b_ynT])
                    if b < 3:
                        S.dve(E("tensor_tensor", out=ynT[:, b, 0:T], in0=ynT[:, b, 0:T], in1=bonus[:, b, 0:T], op=ALU.add),
                              reads=[b_ynT, b_bonus], writes=[b_ynT])
                    S.dve(E("tensor_tensor", out=mixT[:, b, 0:T], in0=ynT[:, b, 0:T], in1=Gg[:, b, 0:T], op=ALU.mult),
                          reads=[b_ynT, b_G], writes=[b_mixT])
                    if hpf:
                        S.dve(E("tensor_tensor", out=mix32[:, b, :], in0=ynT[:, b, 0:2], in1=Gg[:, b, 0:2], op=ALU.mult),
                              reads=[b_ynT, b_G], writes=[b_mix32])

                def ln_epilogue(gname, bname):
                    r16, br16 = hT[:, 0:8, :], b_hT
                    S.act(E("copy", out=hT[:, 0:8, 0:T], in_=x32[:, :, 0:T]), reads=[b_x32], writes=[b_hT])
                    S.act(E("activation", out=hT[:, 8:16, 0:T], in_=x32[:, :, 0:T], func=AF.Square), reads=[b_x32], writes=[b_hT])
                    bk1 = nbank(); bk2 = nbank()
                    for kc in range(8):
                        S.pe(E("matmul", ps[:, bk1, 0:T], lhsT=onesb[:], rhs=hT[:, kc, 0:T], start=(kc == 0), stop=(kc == 7)),
                             reads=[b_ones, b_hT], writes=[b_ps[bk1]])
                    for kc in range(8):
                        S.pe(E("matmul", ps[:, bk2, 0:T], lhsT=onesb[:], rhs=hT[:, 8 + kc, 0:T], start=(kc == 0),
                                                       stop=(kc == 7)), reads=[b_ones, b_hT], writes=[b_ps[bk2]])
                    mean, var = tmp[0], tmp[1]
                    S.act(E("mul", out=mean[:, 0:T], in_=ps[:, bk1, 0:T], mul=1.0 / D), reads=[b_ps[bk1]], writes=[b_tmp[0]])
                    S.dve(E("tensor_tensor", out=var[:, 0:T], in0=mean[:, 0:T], in1=mean[:, 0:T], op=ALU.mult), reads=[b_tmp[0]],
                          writes=[b_tmp[1]])
                    S.dve(E("scalar_tensor_tensor", out=var[:, 0:T], in0=ps[:, bk2, 0:T], scalar=1.0 / D, in1=var[:, 0:T],
                                                           op0=ALU.mult, op1=ALU.subtract), reads=[b_ps[bk2], b_tmp[1]], writes=[b_tmp[1]])
                    S.dve(E("tensor_scalar", out=var[:, 0:T], in0=var[:, 0:T], scalar1=1e-5, scalar2=None, op0=ALU.add),
                          reads=[b_tmp[1]], writes=[b_tmp[1]])
                    S.act(E("activation", out=var[:, 0:T], in_=var[:, 0:T], func=AF.Sqrt), reads=[b_tmp[1]], writes=[b_tmp[1]])
                    S.dve(E("reciprocal", out=var[:, 0:T], in_=var[:, 0:T]), reads=[b_tmp[1]], writes=[b_tmp[1]])
                    S.dve(E("tensor_tensor", out=x32[:, :, 0:T], in0=x32[:, :, 0:T],
                                                    in1=mean[:, 0:T].unsqueeze(1).to_broadcast([128, 8, T]), op=ALU.subtract),
                          reads=[b_tmp[0], b_x32], writes=[b_x32])
                    S.dve(E("tensor_tensor", out=x32[:, :, 0:T], in0=x32[:, :, 0:T],
                                                    in1=var[:, 0:T].unsqueeze(1).to_broadcast([128, 8, T]), op=ALU.mult),
                          reads=[b_tmp[1], b_x32], writes=[b_x32])
                    for b in range(8):
                        S.dve(E("tensor_scalar", out=x32[:, b, 0:T], in0=x32[:, b, 0:T], scalar1=V_(gname, b),
                                                             scalar2=V_(bname, b), op0=ALU.mult, op1=ALU.add),
                              reads=[b_x32, bvec], writes=[b_x32])
                    S.act(E("copy", out=xTb[:, :, 0:T], in_=x32[:, :, 0:T]), reads=[b_x32], writes=[b_xTb])

                stage(5)
                for piece in range(4):
                    slot, bslot = ws_next(l, NPIECE_IN + piece)
                    for sub in range(2):
                        cb = piece * 2 + sub
                        bk = nbank()
                        for kc in range(8):
                            S.pe(E("matmul",
                                ps[:, bk, 0:T], lhsT=slot[:, kc * 256 + sub * 128: kc * 256 + sub * 128 + 128],
                                rhs=mixT[:, kc, 0:T], start=(kc == 0), stop=(kc == 7)), reads=[bslot, b_mixT], writes=[b_ps[bk]])
                        if hpf:
                            bkh = hp_mm8(l, NPIECE_IN + piece, sub, lambda kc: mix32[:, kc, 0:2], b_mix32)
                            S.dve(E("scalar_tensor_tensor", out=x32[:, cb, 0:2], in0=x32[:, cb, 0:2], scalar=ALPHA,
                                    in1=ps[:, bkh, 0:2], op0=ALU.mult, op1=ALU.add), reads=[b_ps[bkh], b_x32], writes=[b_x32])
                        S.dve(E("scalar_tensor_tensor", out=x32[:, cb, clo:T], in0=x32[:, cb, clo:T], scalar=ALPHA,
                                                                             in1=ps[:, bk, clo:T], op0=ALU.mult, op1=ALU.add),
                              reads=[b_ps[bk], b_x32], writes=[b_x32])
                ln_epilogue("ln1g", "ln1b")

                stage(6)
                if kind == "s":
                    S.dma(E("dma_start", out=cvst[:].rearrange("p a b c -> p (a b c)"), in_=scv_d[l]), writes=[b_cvst],
                          chan="cvst")
                for j in range(NJ):
                    slot, bslot = ws_next(l, NPIECE_IN + 4 + j)
                    cv = []
                    for ab in range(2):
                        cbk = j + NJ * ab
                        bk = nbank()
                        for kc in range(8):
                            S.pe(E("matmul",
                                ps[:, bk, 0:T], lhsT=slot[:, kc * 256 + ab * 128: kc * 256 + ab * 128 + 128],
                                rhs=xTb[:, kc, 0:T], start=(kc == 0), stop=(kc == 7)), reads=[bslot, b_xTb], writes=[b_ps[bk]])
                        ub, bub = ubuf[ab], b_ubuf[ab]
                        u3 = ub[:, 0:nseq * (L + 2)].rearrange("p (s t) -> p s t", t=L + 2)
                        if kind == "s":
                            S.dve(E("tensor_copy", out=u3[:, :, 0:2], in_=cvst[:, cbk, :, :]), reads=[b_cvst],
                                  writes=[bub])
                        else:
                            S.dve(E("tensor_copy", out=u3[:, 0, 0:2], in_=cvc[:, l, cbk, :]),
                                  reads=[b_cvc[l]], writes=[bub])
                        if hpf:
                            bkh = hp_mm8(l, NPIECE_IN + 4 + j, ab, lambda kc: x32[:, kc, 0:2], b_x32)
                            S.act(E("copy", out=u3[:, 0, 2:4], in_=ps[:, bkh, 0:2]), reads=[b_ps[bkh]], writes=[bub])
                            S.act(E("copy", out=u3[:, 0, 4:L + 2], in_=ps[:, bk, 2:T]), reads=[b_ps[bk]], writes=[bub])
                        else:
                            S.act(E("copy", out=u3[:, :, 2:L + 2], in_=t3(ps[:, bk, 0:T])), reads=[b_ps[bk]], writes=[bub])
                        if kind == "s":
                            S.act(E("copy", out=cvo[:, cbk, :, :], in_=u3[:, :, L:L + 2]), reads=[bub], writes=[b_cvo])
                        else:
                            S.act(E("copy", out=cvc[:, l, cbk, :], in_=u3[:, 0, L:L + 2]), reads=[bub],
                                  writes=[b_cvc[l]])
                        ct, bct = tmp[2 + ab], b_tmp[2 + ab]
                        c3 = t3(ct[:, 0:T])
                        S.pool(E("tensor_scalar", out=c3, in0=u3[:, :, 0:L], scalar1=V_("cw0", cbk),
                                                                               scalar2=V_("cb", cbk), op0=ALU.mult, op1=ALU.add),
                              reads=[bub, bvec], writes=[bct])
                        S.dve(E("scalar_tensor_tensor", out=c3, in0=u3[:, :, 1:L + 1], scalar=V_("cw1", cbk),
                                                                                      in1=c3, op0=ALU.mult, op1=ALU.add),
                              reads=[bub, bvec, bct], writes=[bct])
                        S.dve(E("scalar_tensor_tensor", out=c3, in0=u3[:, :, 2:L + 2], scalar=V_("cw2", cbk),
                                                                                      in1=c3, op0=ALU.mult, op1=ALU.add),
                              reads=[bub, bvec, bct], writes=[bct])
                        cv.append((ct, bct))
                    (ca, bca), (cb_, bcb) = cv
                    S.act(E("activation", out=ca[:, 0:T], in_=ca[:, 0:T], func=AF.Gelu_apprx_tanh), reads=[bca], writes=[bca])
                    S.dve(E("tensor_tensor", out=hT[:, j, 0:T], in0=ca[:, 0:T], in1=cb_[:, 0:T], op=ALU.mult),
                          reads=[bca, bcb], writes=[b_hT])
                    if hpf:
                        S.dve(E("tensor_tensor", out=h32[:, j, :], in0=ca[:, 0:2], in1=cb_[:, 0:2], op=ALU.mult),
                              reads=[bca, bcb], writes=[b_h32])
                if kind == "s":
                    S.dma(E("dma_start", out=cvo_d[l][:, 88:88 * (1 + NSS)], in_=cvo[:].rearrange("p a b c -> p (a b c)")),
                          reads=[b_cvo], chan="cvout")
                elif last_prompt:
                    S.dma(E("dma_start", out=cvo_d[l][:, 0:88], in_=cvc[:, l, :, :].rearrange("p a c -> p (a c)")),
                          reads=[b_cvc[l]], chan="cvout")
                for cb in range(8):
                    slot, bslot = ws_next(l, NPIECE_IN + 4 + NJ + cb)
                    bk = nbank()
                    for j in range(NJ):
                        S.pe(E("matmul", ps[:, bk, 0:T], lhsT=slot[:, j * 128:(j + 1) * 128], rhs=hT[:, j, 0:T],
                                                                       start=(j == 0), stop=(j == NJ - 1)),
                             reads=[bslot, b_hT], writes=[b_ps[bk]])
                    if hpf:
                        bkh = nbank()
                        for jh in range(3):
                            njj = min(8, NJ - jh * 8)
                            w, bw = hp_load(l, NPIECE_IN + 4 + NJ + cb, ("jh", jh, njj))
                            for jj in range(njj):
                                j = jh * 8 + jj
                                S.pe(E("matmul", ps[:, bkh, 0:2], lhsT=w[:, jj * 128:(jj + 1) * 128], rhs=h32[:, j, 0:2],
                                       start=(j == 0), stop=(j == NJ - 1)), reads=[bw, b_h32], writes=[b_ps[bkh]])
                        S.dve(E("scalar_tensor_tensor", out=x32[:, cb, 0:2], in0=x32[:, cb, 0:2], scalar=ALPHA,
                                in1=ps[:, bkh, 0:2], op0=ALU.mult, op1=ALU.add), reads=[b_ps[bkh], b_x32], writes=[b_x32])
                    S.dve(E("scalar_tensor_tensor", out=x32[:, cb, clo:T], in0=x32[:, cb, clo:T], scalar=ALPHA,
                                                                         in1=ps[:, bk, clo:T], op0=ALU.mult, op1=ALU.add),
                          reads=[b_ps[bk], b_x32], writes=[b_x32])
                stage(7)
                ln_epilogue("ln2g", "ln2b")

            S.dma(E("dma_start", out=yT_d[:, :, tok0:tok0 + T], in_=x32[:, :, 0:T]), reads=[b_x32], chan="yout")


    try:
        _tile_loop()
        assert ws["used"] == len(plan)
    except _Stop:
        pass
    S.run_block()
    return nc, S


_CACHE = {}


def _prep_inputs(inp, cfg):
    f = np.float32
    W = _prep_weights(inp)
    vm = [_prep_layer_params(inp, l) for l in range(NL)]
    vecs = np.stack([v for v, _ in vm])
    mats = np.stack([m for _, m in vm])
    cst = _consts()
    cosT, sinT = _rope_tables()
    in_maps = []
    for c in range(8):
        xp = inp["x_prompt"][c]
        xs = inp["x_sample"][c * NSS:(c + 1) * NSS].reshape(NSS * LS, D)
        x = np.concatenate([xp, xs], axis=0)
        xT = np.ascontiguousarray(x.T.reshape(8, 128, -1).transpose(1, 0, 2))
        sst = np.zeros((NL, NSS, 16, 64, 64), f)
        sl = slice(c * NSS, (c + 1) * NSS)
        sst[:, :, 0:6] = inp["state_rwkv"][:, sl].transpose(0, 1, 2, 4, 3)
        sst[:, :, 6:11, 0:32] = inp["state_gla"][:, sl]
        sst[:, :, 11:16] = inp["state_ret"][:, sl]
        sst = sst.reshape(NL, NSS, 8, 2, 64, 64).transpose(0, 1, 3, 4, 2, 5).reshape(NL, NSS, 128, 512)
        sh = inp["state_shift"][:, sl]
        shp = np.zeros((NL, NSS, 12 * 128), f)
        shp[:, :, 0:1152] = sh[:, :, 0:1152]
        shp[:, :, 1152:1408] = sh[:, :, 1152:1408]
        shp[:, :, 1408:1440] = sh[:, :, 1408:1440]
        ssh = shp.reshape(NL, NSS, 12, 128).transpose(0, 3, 2, 1).reshape(NL, 128, 12 * NSS)
        cvs = inp["state_conv"][:, sl]
        scv = cvs.reshape(NL, NSS, 2, 44, 128).transpose(0, 4, 3, 1, 2).reshape(NL, 128, 44 * NSS * 2)
        in_maps.append({"xT": xT, "wts": W, "vecs": vecs, "mats": mats, "cst": cst, "ropec": cosT, "ropes": sinT,
                        "sst": np.ascontiguousarray(sst), "ssh": np.ascontiguousarray(ssh), "scv": np.ascontiguousarray(scv)})
    return in_maps


def _assemble(results):
    f = np.float32
    y_p = np.zeros((8, SEQ, D), f); y_s = np.zeros((8 * NSS, LS, D), f)
    rw_p = np.zeros((NL, 8, 6, 64, 64), f); rw_s = np.zeros((NL, 8 * NSS, 6, 64, 64), f)
    sh_p = np.zeros((NL, 8, 1440), f); sh_s = np.zeros((NL, 8 * NSS, 1440), f)
    gl_p = np.zeros((NL, 8, 5, 32, 64), f); gl_s = np.zeros((NL, 8 * NSS, 5, 32, 64), f)
    rt_p = np.zeros((NL, 8, 5, 64, 64), f); rt_s = np.zeros((NL, 8 * NSS, 5, 64, 64), f)
    cv_p = np.zeros((NL, 8, 2, 2 * DFF), f); cv_s = np.zeros((NL, 8 * NSS, 2, 2 * DFF), f)
    for c, r in enumerate(results):
        yT = r["yT"]
        y = yT.transpose(1, 0, 2).reshape(D, -1).T
        y_p[c] = y[0:SEQ]
        y_s[c * NSS:(c + 1) * NSS] = y[SEQ:].reshape(NSS, LS, D)
        st = r["sto"].reshape(NL, 1 + NSS, 2, 64, 8, 64).transpose(0, 1, 4, 2, 3, 5).reshape(NL, 1 + NSS, 16, 64, 64)
        rw = st[:, :, 0:6].transpose(0, 1, 2, 4, 3)
        gl = st[:, :, 6:11, 0:32]
        rt = st[:, :, 11:16]
        rw_p[:, c] = rw[:, 0]; gl_p[:, c] = gl[:, 0]; rt_p[:, c] = rt[:, 0]
        sl = slice(c * NSS, (c + 1) * NSS)
        rw_s[:, sl] = rw[:, 1:]; gl_s[:, sl] = gl[:, 1:]; rt_s[:, sl] = rt[:, 1:]
        sho = r["sho"].reshape(NL, 128, 12 * (1 + NSS))
        p = sho[:, :, 0:12].transpose(0, 2, 1).reshape(NL, 12 * 128)
        s = sho[:, :, 12:].reshape(NL, 128, 12, NSS).transpose(0, 3, 2, 1).reshape(NL, NSS, 12 * 128)
        sh_p[:, c] = p[:, 0:1440]
        sh_s[:, sl] = s[:, :, 0:1440]
        cvo = r["cvo"].reshape(NL, 128, 44 * (1 + NSS) * 2)
        p = cvo[:, :, 0:88].reshape(NL, 128, 44, 2).transpose(0, 3, 2, 1).reshape(NL, 2, 2 * DFF)
        s = cvo[:, :, 88:].reshape(NL, 128, 44, NSS, 2).transpose(0, 3, 4, 2, 1).reshape(NL, NSS, 2, 2 * DFF)
        cv_p[:, c] = p
        cv_s[:, sl] = s
    return (y_p, y_s, rw_p, sh_p, gl_p, rt_p, cv_p, rw_s, sh_s, gl_s, rt_s, cv_s)


def kernel(**inputs):
    inp = {k: np.asarray(v) for k, v in inputs.items()}
    cfg = CFG_FULL
    if "nc" not in _CACHE:
        _CACHE["nc"] = build(cfg)[0]
    nc = _CACHE["nc"]
    in_maps = _prep_inputs(inp, cfg)
    res = run_bass_kernel_spmd(nc, in_maps, core_ids=list(range(8)))
    return _assemble(res.results)
```

```python
import bisect
import math
from contextlib import ExitStack

import numpy as np
import concourse.bass as bass
import concourse.mybir as mybir
from concourse.bass_utils import run_bass_kernel_spmd

F32 = mybir.dt.float32
F32R = mybir.dt.float32r
BF16 = mybir.dt.bfloat16
AF = mybir.ActivationFunctionType
ALU = mybir.AluOpType
AX = mybir.AxisListType

NL = 4
D = 1024
SEQ = 2048
NSS = 16
LS = 4
TT = 256
PAST = 16384
DFF = 2816
NJ = 22
ALPHA = (2 * NL) ** 0.25
NBLK = 38
NPIECE_IN = 19
SLOT = 2816
NSLOT = 4
RW0, GL0, RT0 = 0, 1440, 2416


class _Stop(Exception):
    pass


class Buf:
    __slots__ = ("name", "last_write", "reads")

    def __init__(self, name):
        self.name = name
        self.last_write = None
        self.reads = []


class Op:
    __slots__ = ("eng", "fn", "deps", "idx", "needs_inc", "cnt", "chan", "dma_cnt")

    def __init__(self, eng, fn, deps, idx, chan=None):
        self.eng = eng
        self.fn = fn
        self.deps = deps
        self.idx = idx
        self.needs_inc = False
        self.cnt = None
        self.chan = chan
        self.dma_cnt = None


COMPUTE = ("pe", "act", "dve", "pool")
DMAQ = ("sp", "actq", "poolq")
STREAM = {"pe": "pe", "act": "act", "dve": "dve", "pool": "pool", "sp": "sp", "actq": "act", "poolq": "pool"}
SAME_SYNC = {"pe": False, "act": True, "dve": True, "pool": True}


class Sched:
    def __init__(self, nc):
        self.nc = nc
        self.ops = []

    def op(self, eng, fn, reads=(), writes=(), chan=None):
        idx = len(self.ops)
        deps = set()
        for b in reads:
            if b.last_write is not None:
                deps.add(b.last_write)
        for b in writes:
            if b.last_write is not None:
                deps.add(b.last_write)
            deps.update(b.reads)
        o = Op(eng, fn, deps, idx, chan)
        self.ops.append(o)
        for b in reads:
            b.reads.append(idx)
        for b in writes:
            b.last_write = idx
            b.reads = []
        return o

    def pe(self, fn, reads=(), writes=()):
        return self.op("pe", fn, reads, writes)

    def act(self, fn, reads=(), writes=()):
        return self.op("act", fn, reads, writes)

    def dve(self, fn, reads=(), writes=()):
        return self.op("dve", fn, reads, writes)

    def pool(self, fn, reads=(), writes=()):
        return self.op("pool", fn, reads, writes)

    def dma(self, fn, reads=(), writes=(), chan=None, q="sp"):
        return self.op(q, fn, reads, writes, chan=chan)

    def _skip(self, p, so):
        sp = STREAM[p.eng]
        return sp == so and p.eng in COMPUTE and not SAME_SYNC[p.eng]

    def run_block(self):
        nc = self.nc
        ops = self.ops
        for o in ops:
            so = STREAM[o.eng]
            for d in o.deps:
                p = ops[d]
                if p.eng in DMAQ or self._skip(p, so):
                    continue
                p.needs_inc = True
        cnt = {e: 0 for e in COMPUTE}
        chan_cnt = {}
        chan_hist = {}
        for o in ops:
            if o.eng in DMAQ:
                c = chan_cnt.get(o.chan, 0) + 16
                chan_cnt[o.chan] = c
                o.dma_cnt = c
                chan_hist.setdefault(o.chan, []).append((o.idx, c))
            elif o.needs_inc:
                cnt[o.eng] += 1
                o.cnt = cnt[o.eng]
        chan_idx = {c: [i for i, _ in h] for c, h in chan_hist.items()}
        streams = {s: [] for s in ("pe", "act", "dve", "pool", "sp")}
        waited = {s: {} for s in streams}
        for o in ops:
            so = STREAM[o.eng]
            need = {}
            for d in o.deps:
                p = ops[d]
                if p.eng in DMAQ:
                    h = chan_hist[p.chan]
                    k = bisect.bisect_left(chan_idx[p.chan], o.idx) - 1
                    key = ("chan", p.chan)
                    need[key] = max(need.get(key, 0), h[k][1])
                else:
                    if self._skip(p, so):
                        continue
                    key = ("eng", p.eng)
                    need[key] = max(need.get(key, 0), p.cnt)
            w = []
            for key, c in need.items():
                if waited[so].get(key, 0) >= c:
                    continue
                waited[so][key] = c
                w.append((key, c))
            streams[so].append((o, w))
        with ExitStack() as es:
            sems = {}
            for e in COMPUTE:
                sems[("eng", e)] = es.enter_context(nc.semaphore("s_" + e))
            for c in chan_cnt:
                sems[("chan", c)] = es.enter_context(nc.semaphore("c_" + str(c)))
            block = es.enter_context(nc.Block())

            def mk(sname):
                def body(eng):
                    for o, w in streams[sname]:
                        for key, c in w:
                            eng.wait_ge(sems[key], c)
                        ins = o.fn(eng)
                        if o.eng in DMAQ:
                            ins.then_inc(sems[("chan", o.chan)], 16)
                        elif o.needs_inc:
                            ins.then_inc(sems[("eng", o.eng)], 1)
                    if sname == "sp":
                        for c, v in chan_cnt.items():
                            eng.wait_ge(sems[("chan", c)], v)
                        for e in COMPUTE:
                            if cnt[e] > 0:
                                eng.wait_ge(sems[("eng", e)], cnt[e])
                return body

            block.tensor(mk("pe"))
            block.scalar(mk("act"))
            block.vector(mk("dve"))
            block.gpsimd(mk("pool"))
            block.sync(mk("sp"))


def _colmap():
    cm = -np.ones(NBLK * 128, np.int64)
    for g, (rw_off, gl_off, rt_off, pad) in enumerate(((0, 0, 0, True), (384, 160, 320, True), (768, 320, 640, False))):
        base = g * 1024
        cm[base:base + 384] = RW0 + rw_off + np.arange(384)
        for j in range(5):
            if pad:
                cm[base + 384 + j * 64: base + 384 + j * 64 + 32] = GL0 + gl_off + j * 32 + np.arange(32)
            else:
                cm[base + 384 + j * 64: base + 384 + j * 64 + 64] = GL0 + gl_off + j * 64 + np.arange(64)
        cm[base + 704: base + 1024] = RT0 + rt_off + np.arange(320)
    b = 24 * 128
    cm[b:b + 320] = GL0 + 640 + np.arange(320)
    cm[b + 320:b + 640] = RT0 + 960 + np.arange(320)
    b = 29 * 128
    cm[b:b + 128] = RW0 + 1152 + np.arange(128)
    cm[b + 128:b + 256] = RW0 + 1280 + np.arange(128)
    cm[b + 256:b + 288] = RW0 + 1408 + np.arange(32)
    cm[b + 288:b + 304] = GL0 + 960 + np.arange(16)
    for gi, off in enumerate((0, 320)):
        b = (32 + 3 * gi) * 128
        for r in range(704, 1024):
            i = r - 704
            h, d = divmod(i, 64)
            cm[b + (r - 640)] = RT0 + off + h * 64 + ((d + 32) % 64)
    return cm


def _rows_layout(v, nblk):
    return np.ascontiguousarray(v.reshape(nblk, 128).T)


VEC = {}


def _vec_layout():
    off = 0
    for name, n in (("mu", 12), ("w0", 3), ("a0", 3), ("kkv", 3), ("ka", 3), ("rk", 3), ("nw", 8), ("nb", 8),
                    ("bgkb", 3), ("ln1g", 8), ("ln1b", 8), ("ln2g", 8), ("ln2b", 8), ("cw0", 44), ("cw1", 44),
                    ("cw2", 44), ("cb", 44), ("rscale", 8), ("kscale", 8), ("ldret", 8)):
        VEC[name] = (off, n)
        off += n
    return off


NV = _vec_layout()
NM = 4 * 384


def _prep_layer_params(inp, l):
    f = np.float32
    vec = np.zeros((128, NV), f)

    def put(name, arr):
        o, n = VEC[name]
        assert arr.shape == (128, n), (name, arr.shape)
        vec[:, o:o + n] = arr

    mu = inp["rwkv_mu"][l]
    mup = np.zeros(12 * 128, f)
    mup[0:384] = mu[0:384]
    mup[384:768] = mu[384:768]
    mup[768:1152] = mu[768:1152]
    mup[1152:1152 + 128] = mu[1152:1280]
    mup[1280:1408] = mu[1280:1408]
    mup[1408:1440] = mu[1408:1440]
    put("mu", _rows_layout(mup, 12))
    put("w0", _rows_layout(inp["rwkv_w0"][l], 3))
    put("a0", _rows_layout(inp["rwkv_a0"][l], 3))
    put("kkv", _rows_layout(inp["rwkv_kk"][l], 3))
    put("ka", _rows_layout(inp["rwkv_ka"][l], 3))
    put("rk", _rows_layout(inp["rwkv_rk"][l].reshape(-1), 3))
    nw = np.ones(1024, f)
    nb = np.zeros(1024, f)
    nw[0:384] = inp["rwkv_lnw"][l]
    nb[0:384] = inp["rwkv_lnb"][l]
    nw[384:704] = np.tile(inp["gla_norm_w"][l], 5)
    put("nw", _rows_layout(nw, 8))
    put("nb", _rows_layout(nb, 8))
    bb = np.zeros(384, f)
    for j in range(5):
        bb[j * 64:j * 64 + 32] = inp["gla_bgk_b"][l][j * 32:(j + 1) * 32]
    put("bgkb", _rows_layout(bb, 3))
    put("ln1g", _rows_layout(inp["ln1_g"][l], 8))
    put("ln1b", _rows_layout(inp["ln1_b"][l], 8))
    put("ln2g", _rows_layout(inp["ln2_g"][l], 8))
    put("ln2b", _rows_layout(inp["ln2_b"][l], 8))
    cw = inp["ffn_conv_w"][l]
    put("cw0", _rows_layout(cw[0], 44))
    put("cw1", _rows_layout(cw[1], 44))
    put("cw2", _rows_layout(cw[2], 44))
    put("cb", _rows_layout(inp["ffn_conv_b"][l], 44))
    rs = np.ones(1024, f)
    rs[384:704] = 32.0 ** -0.5
    ks = np.ones(1024, f)
    ks[704:1024] = 64.0 ** -0.5
    put("rscale", _rows_layout(rs, 8))
    put("kscale", _rows_layout(ks, 8))
    ldr = np.zeros(1024, np.float64)
    for h in range(5):
        ldr[704 + h * 64:704 + (h + 1) * 64] = math.log(1.0 - 2.0 ** (-5.0 - h))
    put("ldret", _rows_layout(ldr.astype(f), 8))
    mats = np.zeros((128, NM), f)
    mats[0:64, 0:384] = inp["rwkv_bw"][l]
    mats[64:128, 0:384] = inp["rwkv_ba"][l]
    mats[:, 384:768] = inp["rwkv_bg"][l][0:128]
    mats[0:32, 768:1152] = inp["rwkv_bg"][l][128:160]
    bgk = inp["gla_bgk"][l]
    for j in range(5):
        mats[32:48, 1152 + j * 64:1152 + j * 64 + 32] = bgk[:, j * 32:(j + 1) * 32]
    return vec, mats


def _prep_weights(inp):
    cm = _colmap()
    npieces = NPIECE_IN + 4 + NJ + 8
    W = np.zeros((NL, npieces, 128, SLOT), np.float32)
    for l in range(NL):
        win = inp["w_in"][l]
        wp = np.zeros((1024, NBLK * 128), np.float32)
        ok = cm >= 0
        wp[:, ok] = win[:, cm[ok]]
        p = 0
        for i in range(NPIECE_IN):
            blkc = wp[:, i * 256:(i + 1) * 256].reshape(8, 128, 256).transpose(1, 0, 2)
            W[l, p, :, :2048] = blkc.reshape(128, 2048)
            p += 1
        wo = inp["w_out"][l]
        for i in range(4):
            blkc = wo[:, i * 256:(i + 1) * 256].reshape(8, 128, 256).transpose(1, 0, 2)
            W[l, p, :, :2048] = blkc.reshape(128, 2048)
            p += 1
        up = inp["ffn_up"][l]
        for j in range(NJ):
            ab = np.concatenate([up[:, j * 128:(j + 1) * 128], up[:, DFF + j * 128:DFF + (j + 1) * 128]], axis=1)
            blkc = ab.reshape(8, 128, 256).transpose(1, 0, 2)
            W[l, p, :, :2048] = blkc.reshape(128, 2048)
            p += 1
        dn = inp["ffn_down"][l]
        for cb in range(8):
            blkc = dn[:, cb * 128:(cb + 1) * 128].reshape(NJ, 128, 128).transpose(1, 0, 2)
            W[l, p, :, :NJ * 128] = blkc.reshape(128, NJ * 128)
            p += 1
        assert p == npieces
    return W


def _consts():
    ident = np.eye(128, dtype=np.float32)
    sI = np.arange(128)[:, None]
    tI = np.arange(128)[None, :]
    bones = np.zeros((128, 128), np.float32)
    bones[0:64, 0:64] = 1
    bones[64:128, 64:128] = 1
    cst = np.zeros((128, 672), np.float32)
    cst[:, 0:128] = ident
    cst[:, 128:256] = bones
    cst[:, 256:384] = (tI >= sI)
    cst[:, 384:512] = (tI > sI)
    cst[:, 512:640] = (tI < sI)
    tok = np.zeros((128, 32), np.float32)
    tok[:, 0:6] = 1.0 / 64
    tok[:, 16:22] = 64e-5
    tok[:, 22:32] = 1e-6
    cst[:, 640:672] = tok
    return cst


def _rope_tables():
    ntok = SEQ + NSS * LS
    pos = np.concatenate([np.arange(SEQ), np.tile(PAST + np.arange(LS), NSS)]).astype(np.float32)
    inv = (1.0 / (10000.0 ** np.linspace(0.0, 1.0, 32, dtype=np.float32))).astype(np.float32)
    ang = pos[None, :] * inv[:, None]
    cos = np.cos(ang).astype(np.float32)
    sin = np.sin(ang).astype(np.float32)
    C = np.ones((384, ntok), np.float32)
    S = np.zeros((384, ntok), np.float32)
    for h in range(5):
        r0 = 64 + h * 64
        C[r0:r0 + 32] = cos
        C[r0 + 32:r0 + 64] = cos
        S[r0:r0 + 32] = -sin
        S[r0 + 32:r0 + 64] = sin
    C = C.reshape(3, 128, ntok).transpose(1, 0, 2)
    S = S.reshape(3, 128, ntok).transpose(1, 0, 2)
    return np.ascontiguousarray(C), np.ascontiguousarray(S)


CFG_FULL = dict(tiles=[("p", i) for i in range(SEQ // TT)] + [("s", 0)], nl=NL)


def build(cfg):
    nc = bass.Bass("TRN2", target_bir_lowering=False)
    S = Sched(nc)
    NTOK = SEQ + NSS * LS
    npieces = NPIECE_IN + 4 + NJ + 8
    nl = cfg["nl"]
    tiles = cfg["tiles"]

    def din(name, shape, dt=F32):
        return nc.dram_tensor(name, list(shape), dt, kind="ExternalInput").ap()

    def dout(name, shape):
        return nc.dram_tensor(name, list(shape), F32, kind="ExternalOutput").ap()

    xT_d = din("xT", [128, 8, NTOK])
    wts_d = din("wts", [NL, npieces, 128, SLOT])
    vec_d = din("vecs", [NL, 128, NV])
    mat_d = din("mats", [NL, 128, NM])
    cst_d = din("cst", [128, 672])
    cos_d = din("ropec", [128, 3, NTOK])
    sin_d = din("ropes", [128, 3, NTOK])
    sst_d = din("sst", [NL, NSS, 128, 512])
    ssh_d = din("ssh", [NL, 128, 12 * NSS])
    scv_d = din("scv", [NL, 128, 44 * NSS * 2])
    yT_d = dout("yT", [128, 8, NTOK])
    sto_d = dout("sto", [NL, 1 + NSS, 128, 512])
    sho_d = dout("sho", [NL, 128, 12 * (1 + NSS)])
    cvo_d = dout("cvo", [NL, 128, 44 * (1 + NSS) * 2])
    dbg_d = dout("dbg", [128, 8192]) if cfg.get("dbg") else None
    dbg_state = {"col": 0, "items": []}

    def dbg(name, ap, bufs, np_=128):
        if dbg_d is None:
            return
        n = 1
        for d_ in ap.shape[1:]:
            n *= d_
        c0 = dbg_state["col"]
        if c0 + n > 8192:
            return
        dbg_state["col"] = c0 + n
        dbg_state["items"].append((name, c0, n, np_, tuple(ap.shape)))
        flat = dbg_d[0:np_, c0:c0 + n]
        if len(ap.shape) == 3:
            flat = flat.rearrange("p (a b) -> p a b", b=ap.shape[2])
        S.dma(E("dma_start", out=flat, in_=ap), reads=bufs, chan="dbg")
    cfg["_dbg_items"] = dbg_state["items"]

    def sb(name, shape, dt=F32):
        return nc.alloc_sbuf_tensor("sb_" + name, list(shape), dt)

    TM = TT
    x32 = sb("x32", [128, 8, TM]); b_x32 = Buf("x32")
    xTb = sb("xTb", [128, 8, TM], BF16); b_xTb = Buf("xTb")
    un1 = sb("un1", [128, 12 * (TM + NSS)]); b_PRW = Buf("PRW")
    PRW = un1[:].rearrange("p (a b) -> p a b", b=TM + NSS)
    Rg = sb("Rg", [128, 8, TM]); b_R = Buf("R")
    Kg = sb("Kg", [128, 8, TM]); b_K = Buf("K")
    Vg = sb("Vg", [128, 8, TM]); b_V = Buf("V")
    Gg = sb("Gg", [128, 8, TM], BF16); b_G = Buf("G")

    XS = sb("XS", [128, 3, TM]); b_XS = Buf("XS")
    ATt = sb("ATt", [128, 3, TM]); b_AT = Buf("AT")
    BTt = sb("BTt", [128, 3, TM]); b_BT = Buf("BT")
    bonus = sb("bonus", [128, 3, TM]); b_bonus = Buf("bonus")
    ynT = sb("ynT", [128, 8, TM]); b_ynT = Buf("ynT")
    SW = ynT; b_SW = b_ynT
    mixT = sb("mixT", [128, 8, TM], BF16); b_mixT = Buf("mixT")
    hT = un1[:].bitcast(BF16)[:, 0:NJ * TM].rearrange("p (a b) -> p a b", b=TM); b_hT = b_PRW
    WC = sb("WC", [128, 8, 16]); b_WC = Buf("WC")
    NTMP = 9
    tmp = [sb("tmp%d" % i, [128, TM]) for i in range(NTMP)]
    b_tmp = [Buf("tmp%d" % i) for i in range(NTMP)]
    tmpb = [sb("tmpb%d" % i, [128, TM], BF16) for i in range(4)]
    b_tmpb = [Buf("tmpb%d" % i) for i in range(4)]
    ubuf = [sb("ubuf%d" % i, [128, TM + 2 * NSS]) for i in range(4)]
    b_ubuf = [Buf("ubuf%d" % i) for i in range(4)]
    cosT = sb("cosT", [128, 3, TM]); sinT = sb("sinT", [128, 3, TM]); b_rope = Buf("rope")
    cst = sb("cst", [128, 672]); b_cst = Buf("cst")
    onesb = sb("onesb", [128, 128], BF16); b_ones = Buf("onesb")
    bonesr = sb("bonesr", [128, 128]); b_bones = Buf("bonesr")
    rmask = sb("rmask", [128, TM]); b_rmask = Buf("rmask")
    vecs = [sb("vecs%d" % i, [128, NV]) for i in range(2)]; b_vecs = [Buf("vecs%d" % i) for i in range(2)]
    vaux = [sb("vaux%d" % i, [128, 8]) for i in range(2)]; b_vaux = [Buf("vaux%d" % i) for i in range(2)]
    _m0 = sb("mats0", [128, NM], BF16); _bm0 = Buf("mats0")
    mats = [_m0, _m0]; b_mats = [_bm0, _bm0]
    hpw = [sb("hpw%d" % i, [128, 1024]) for i in range(2)]; b_hpw = [Buf("hpw%d" % i) for i in range(2)]
    mix32 = sb("mix32", [128, 8, 2]); b_mix32 = Buf("mix32")
    h32 = sb("h32", [128, NJ, 2]); b_h32 = Buf("h32")
    slots = [sb("slot%d" % i, [128, SLOT], BF16) for i in range(NSLOT)]
    b_slots = [Buf("slot%d" % i) for i in range(NSLOT)]
    Sst = [sb("Sst%d" % l, [128, 8, 64]) for l in range(NL)]; b_Sst = [Buf("Sst%d" % l) for l in range(NL)]
    _ss = sb("Ssm0", [128, 8, 64]); _bss = Buf("Ssm0")
    Ssm = [_ss, _ss]; b_Ssm = [_bss, _bss]
    _s0 = sb("Sld0", [128, 8, 64]); _bs0 = Buf("Sld0")
    Sld = [_s0, _s0]; b_Sld = [_bs0, _bs0]
    shc = sb("shc", [128, NL, 12]); b_shc = [Buf("shc%d" % l) for l in range(NL)]
    shst = sb("shst", [128, 12, NSS]); b_shst = Buf("shst")
    sho = sb("sho", [128, 12, NSS]); b_sho = Buf("sho")
    cvc = sb("cvc", [128, NL, 44, 2]); b_cvc = [Buf("cvc%d" % l) for l in range(NL)]
    cvst = sb("cvst", [128, 44, NSS, 2]); b_cvst = Buf("cvst")
    cvo = sb("cvo", [128, 44, NSS, 2]); b_cvo = Buf("cvo")
    CM = 128
    vtok = sb("vtok", [CM, 1024]); b_vtok = Buf("vtok")
    ktok = sb("ktok", [CM, 1024]); b_ktok = Buf("ktok")
    btok = sb("btok", [CM, 384]); b_btok = Buf("btok")
    ark = sb("ark", [CM, 16, CM]); b_ark = Buf("ark")
    aak = sb("aak", [CM, 6, CM]); b_aak = Buf("aak")
    arb = sb("arb", [CM, 6, CM]); b_arb = Buf("arb")
    _p = sb("Pm0", [CM, 6, CM]); _bp = Buf("Pm0")
    _q = sb("Qm0", [CM, 6, CM]); _bq = Buf("Qm0")
    Pm = [_p, _p]; b_Pm = [_bp, _bp]
    Qm = [_q, _q]; b_Qm = [_bq, _bq]
    Xm = sb("Xm", [CM, 6, CM]); b_Xm = Buf("Xm")
    rhs0 = sb("rhs0", [CM, 384]); b_rhs0 = Buf("rhs0")
    usb = sb("usb", [CM, 384]); b_usb = Buf("usb")
    ysb = sb("ysb", [CM, 1024]); b_ysb = Buf("ysb")
    ysq = ktok; b_ysq = b_ktok
    yst = sb("yst", [CM, 8, 16]); b_yst = Buf("yst")
    ps = nc.alloc_psum_tensor("ps", [128, 8, 512], F32)
    b_ps = [Buf("ps%d" % i) for i in range(8)]
    bank_ctr = [0]

    def nbank():
        b = bank_ctr[0] % 8
        bank_ctr[0] += 1
        return b

    R_ = lambda ap: ap.bitcast(F32R)

    def E(name, *a, **k):
        return lambda e: getattr(e, name)(*a, **k)

    S.dma(E("dma_start", out=cst[:], in_=cst_d), writes=[b_cst], chan="cst")
    ident = cst[:, 0:128]
    m_incl = cst[:, 256:384]
    m_su = cst[:, 384:512]
    m_sl = cst[:, 512:640]
    eye64 = cst[:, 0:128]
    tokc = cst[:, 640:672]
    S.dve(E("memset", onesb[:], 1.0), writes=[b_ones])
    S.dve(E("tensor_copy", out=R_(bonesr[:]), in_=cst[:, 128:256]), reads=[b_cst], writes=[b_bones])
    S.dve(E("memset", Sld[0][:], 0.0), writes=[b_Sld[0]])
    for l in range(NL):
        S.dve(E("tensor_copy", out=R_(Sst[l][:]), in_=Sld[0][:]), reads=[b_Sld[0]], writes=[b_Sst[l]])
        S.dve(E("memset", shc[:, l, :], 0.0), writes=[b_shc[l]])
        S.dve(E("memset", cvc[:, l, :, :], 0.0), writes=[b_cvc[l]])

    plan = []
    for (kind, ti) in tiles:
        for l in range(nl):
            for p in range(npieces):
                plan.append((l, p))
    ws = {"issued": 0, "used": 0, "pending_wr": {}}

    wbf_d = nc.dram_tensor("wbf", [NL, npieces, 128, SLOT], BF16).ap()
    b_wbf = {}

    def ws_issue():
        i = ws["issued"]
        if i >= len(plan):
            return
        l, p = plan[i]
        s = i % NSLOT
        n = 2048 if p < NPIECE_IN + 4 + NJ else NJ * 128
        if (l, p) not in b_wbf:
            S.dma(E("dma_start", out=slots[s][:, 0:n], in_=wts_d[l, p, :, 0:n]),
                  writes=[b_slots[s]], chan="slot%d" % s, q="poolq")
            if len(tiles) > 1:
                b_wbf[(l, p)] = Buf("wbf%d_%d" % (l, p))
                ws["pending_wr"][i] = (l, p, s, n)
        else:
            S.dma(E("dma_start", out=slots[s][:, 0:n], in_=wbf_d[l, p, :, 0:n]), reads=[b_wbf[(l, p)]],
                  writes=[b_slots[s]], chan="slot%d" % s)
        ws["issued"] += 1

    def ws_next(l, p):
        i = ws["used"]
        assert plan[i] == (l, p), (plan[i], l, p)
        while ws["issued"] < min(len(plan), i + NSLOT):
            ws_issue()
        if i in ws["pending_wr"]:
            l_, p_, s_, n_ = ws["pending_wr"].pop(i)
            S.dma(E("dma_start", out=wbf_d[l_, p_, :, 0:n_], in_=slots[s_][:, 0:n_]), reads=[b_slots[s_]],
                  writes=[b_wbf[(l_, p_)]], chan="wbfw")
        ws["used"] += 1
        s = i % NSLOT
        return slots[s], b_slots[s]

    for _ in range(NSLOT - 1):
        ws_issue()

    pv_ctr = [0]
    hp_ctr = [0]

    def hp_load(l, p, part):
        i = hp_ctr[0] % 2
        hp_ctr[0] += 1
        if part[0] == "sub":
            src = wts_d[l, p, :, 0:2048].rearrange("p (kc c) -> p kc c", c=256)[:, :, part[1] * 128:(part[1] + 1) * 128]
            dst = hpw[i][:, 0:1024].rearrange("p (kc c) -> p kc c", c=128)
        else:
            src = wts_d[l, p, :, part[1] * 1024:part[1] * 1024 + part[2] * 128]
            dst = hpw[i][:, 0:part[2] * 128]
        S.dma(E("dma_start", out=dst, in_=src), writes=[b_hpw[i]], chan="hp%d" % i)
        return hpw[i], b_hpw[i]

    def hp_mm8(l, p, sub, rhs_fn, brhs):
        w, bw = hp_load(l, p, ("sub", sub))
        bkh = nbank()
        for kc in range(8):
            S.pe(E("matmul", ps[:, bkh, 0:2], lhsT=w[:, kc * 128:(kc + 1) * 128], rhs=rhs_fn(kc), start=(kc == 0), stop=(kc == 7)),
                 reads=[bw, brhs], writes=[b_ps[bkh]])
        return bkh

    def stage(n):
        if cfg.get("stop") == n:
            raise _Stop()

    def _tile_loop():
        for (kind, ti) in tiles:
            if kind == "p":
                T = TT; nseq = 1; L = TT; C = 128; tok0 = ti * TT; nsteps = 6
            else:
                T = NSS * LS; nseq = NSS; L = LS; C = LS; tok0 = SEQ; nsteps = 1
            nch = T // C
            last_prompt = (kind == "p" and ti == SEQ // TT - 1)
            hpf = (kind == "p" and ti == 0 and cfg.get("hp", True))
            clo = 2 if hpf else 0
            S.dma(E("dma_start", out=x32[:, :, 0:T], in_=xT_d[:, :, tok0:tok0 + T]),
                  writes=[b_x32], chan="x32")
            S.dma(E("dma_start", out=xTb[:, :, 0:T], in_=xT_d[:, :, tok0:tok0 + T]),
                  writes=[b_xTb], chan="xTb", q="poolq")
            S.dma(E("dma_start", out=cosT[:, :, 0:T], in_=cos_d[:, :, tok0:tok0 + T]),
                  writes=[b_rope], chan="rope")
            S.dma(E("dma_start", out=sinT[:, :, 0:T], in_=sin_d[:, :, tok0:tok0 + T]),
                  writes=[b_rope], chan="rope")
            S.dve(E("memset", rmask[:, 0:T], 1.0), writes=[b_rmask])
            S.dve(E("memset", rmask[:, 0:T].rearrange("p (c t) -> p c t", t=C)[:, :, 0:1], 0.0),
                  writes=[b_rmask])

            for l in range(nl):
                pv = pv_ctr[0] % 2
                pv_ctr[0] += 1
                vec, bvec, mat, bmat, vx, bvx = vecs[pv], b_vecs[pv], mats[pv], b_mats[pv], vaux[pv], b_vaux[pv]
                S.dma(E("dma_start", out=vec[:], in_=vec_d[l]), writes=[bvec], chan="vec%d" % pv)
                S.dma(E("dma_start", out=mat[:], in_=mat_d[l]), writes=[bmat], chan="mat0",
                      q="poolq")

                def V_(name, b=0, n=1, vec=vec):
                    o, _ = VEC[name]
                    return vec[:, o + b:o + b + n]

                o_ka = VEC["ka"][0]; o_bg = VEC["bgkb"][0]
                S.dve(E("tensor_scalar", out=vx[:, 0:3], in0=vec[:, o_ka:o_ka + 3], scalar1=-1.0,
                                                                        scalar2=1.0, op0=ALU.mult, op1=ALU.add),
                      reads=[bvec], writes=[bvx])
                S.dve(E("tensor_scalar", out=vx[:, 3:6], in0=vec[:, o_bg:o_bg + 3], scalar1=-1.0,
                                                                        scalar2=None, op0=ALU.mult),
                      reads=[bvec], writes=[bvx])

                if kind == "s":
                    S.dma(E("dma_start", out=shst[:].rearrange("p a b -> p (a b)"), in_=ssh_d[l]),
                          writes=[b_shst], chan="shst")
                    S.dve(E("tensor_copy", out=PRW[:, :, 0:NSS * (LS + 1)].rearrange("p a (s t) -> p a s t", t=LS + 1)[:, :, :, 0],
                                                  in_=shst[:]), reads=[b_shst], writes=[b_PRW])
                else:
                    S.dve(E("tensor_copy", out=PRW[:, :, 0], in_=shc[:, l, :]), reads=[b_shc[l]], writes=[b_PRW])

                def prw_dst(pb, T=T, nseq=nseq, L=L):
                    return PRW[:, pb, 0:nseq * (L + 1)].rearrange("p (s t) -> p s t", t=L + 1)[:, :, 1:L + 1]

                def prw_prev(pb, T=T, nseq=nseq, L=L):
                    return PRW[:, pb, 0:nseq * (L + 1)].rearrange("p (s t) -> p s t", t=L + 1)[:, :, 0:L]

                def t3(ap, nseq=nseq, L=L):
                    return ap.rearrange("p (s t) -> p s t", t=L)

                evi = [0]
                for piece in range(NPIECE_IN):
                    slot, bslot = ws_next(l, piece)
                    for sub in range(2):
                        blk = piece * 2 + sub
                        bk = nbank()
                        for kc in range(8):
                            S.pe(E("matmul",
                                ps[:, bk, 0:T], lhsT=slot[:, kc * 256 + sub * 128: kc * 256 + sub * 128 + 128],
                                rhs=xTb[:, kc, 0:T], start=(kc == 0), stop=(kc == 7)),
                                reads=[bslot, b_xTb], writes=[b_ps[bk]])
                        g, gb = divmod(blk, 8)

                        def proj_evac(srcap, lo, hi, bkb, blk=blk, g=g, gb=gb):
                            if blk < 24 and gb < 3:
                                pb = g * 3 + gb
                            elif 29 <= blk < 32:
                                pb = 9 + blk - 29
                            else:
                                pb = None
                            if pb is not None:
                                if kind == "p":
                                    S.act(E("copy", out=PRW[:, pb, 1 + lo:1 + hi], in_=srcap), reads=[bkb], writes=[b_PRW])
                                else:
                                    S.act(E("copy", out=prw_dst(pb), in_=t3(srcap)), reads=[bkb], writes=[b_PRW])
                            elif blk < 24:
                                tile_, btile = ((Rg, b_R), (Kg, b_K), (Vg, b_V))[g]
                                S.dve(E("tensor_copy", out=R_(tile_[:, gb, lo:hi]), in_=srcap), reads=[bkb], writes=[btile])
                            elif blk < 29:
                                S.act(E("activation", out=Gg[:, 3 + blk - 24, lo:hi], in_=srcap, func=AF.Silu), reads=[bkb],
                                      writes=[b_G])
                            else:
                                S.dve(E("tensor_copy", out=SW[:, blk - 32, lo:hi], in_=srcap), reads=[bkb], writes=[b_SW])

                        proj_evac(ps[:, bk, clo:T], clo, T, b_ps[bk])
                        if hpf:
                            bkh = hp_mm8(l, piece, sub, lambda kc: x32[:, kc, 0:2], b_x32)
                            proj_evac(ps[:, bkh, 0:2], 0, 2, b_ps[bkh])

                stage(1)
                for pb in range(12):
                    g, gb = divmod(pb, 3)
                    tile_, btile = ((Rg, b_R), (Kg, b_K), (Vg, b_V), (XS, b_XS))[g]
                    tb = pb % 8
                    d3 = t3(tmp[tb][:, 0:T])
                    S.dve(E("tensor_tensor", out=d3, in0=prw_prev(pb), in1=prw_dst(pb), op=ALU.subtract),
                          reads=[b_PRW], writes=[b_tmp[tb]])
                    dst3 = t3(tile_[:, gb, 0:T])
                    if g <= 2:
                        dst3 = R_(dst3)
                    S.dve(E("scalar_tensor_tensor",
                        out=dst3, in0=d3, scalar=V_("mu", pb), in1=prw_dst(pb), op0=ALU.mult, op1=ALU.add),
                        reads=[b_PRW, b_tmp[tb], bvec], writes=[btile])
                if l == 0:
                    dbg("PRW0", PRW[:, 0, 0:80], [b_PRW]); dbg("Rsh", Rg[:, 0, 0:T], [b_R]); dbg("shst", shst[:, 0, :], [b_shst])
                lastcol = PRW[:, :, 0:nseq * (L + 1)].rearrange("p a (s t) -> p a s t", t=L + 1)[:, :, :, L]
                if kind == "s":
                    S.act(E("copy", out=sho[:], in_=lastcol), reads=[b_PRW], writes=[b_sho])
                    S.dma(E("dma_start", out=sho_d[l][:, 12:12 * (1 + NSS)], in_=sho[:].rearrange("p a b -> p (a b)")),
                          reads=[b_sho], chan="shout")
                else:
                    S.act(E("copy", out=shc[:, l, :], in_=lastcol[:, :, 0]), reads=[b_PRW],
                          writes=[b_shc[l]])
                    if last_prompt:
                        S.dma(E("dma_start", out=sho_d[l][:, 0:12], in_=shc[:, l, :]), reads=[b_shc[l]],
                              chan="shout")

                stage(2)
                for gi, (tile_, btile) in enumerate(((Rg, b_R), (Kg, b_K))):
                    a = tile_[:, 5:8, 0:T]
                    sw = SW[:, gi * 3:gi * 3 + 3, 0:T]
                    S.dve(E("tensor_tensor", out=R_(a), in0=a, in1=cosT[:, :, 0:T], op=ALU.mult),
                          reads=[b_rope, btile], writes=[btile])
                    S.dve(E("tensor_tensor", out=sw, in0=sw, in1=sinT[:, :, 0:T], op=ALU.mult),
                          reads=[b_rope, b_SW], writes=[b_SW])
                    S.dve(E("tensor_tensor", out=R_(a), in0=a, in1=sw, op=ALU.add),
                          reads=[b_SW, btile], writes=[btile])

                thx, bthx = tmpb[0], b_tmpb[0]
                sgx, bsgx = tmpb[1], b_tmpb[1]
                sgx2, bsgx2 = tmpb[2], b_tmpb[2]
                gkb, bgkb_ = tmpb[3], b_tmpb[3]
                S.act(E("activation", out=thx[0:64, 0:T], in_=XS[0:64, 0, 0:T], func=AF.Tanh), reads=[b_XS], writes=[bthx])
                S.act(E("copy", out=thx[64:128, 0:T], in_=XS[64:128, 0, 0:T]), reads=[b_XS], writes=[bthx])
                S.act(E("activation", out=sgx[:, 0:T], in_=XS[:, 1, 0:T], func=AF.Sigmoid), reads=[b_XS], writes=[bsgx])
                S.act(E("activation", out=sgx2[0:32, 0:T], in_=XS[0:32, 2, 0:T], func=AF.Sigmoid), reads=[b_XS], writes=[bsgx2])
                S.act(E("copy", out=gkb[32:64, 0:T], in_=XS[32:64, 2, 0:T]), reads=[b_XS], writes=[bgkb_])

                for b in range(8):
                    t_ld, t_lw, t_W, t_IW, t_a, t_kk, t_x, t_y, t_r = tmp
                    bl_ld, bl_lw, bl_W, bl_IW, bl_a, bl_kk, bl_x, bl_y, bl_r = b_tmp
                    if b < 3:
                        c0 = b * 128
                        bk = nbank()
                        S.pe(E("matmul", ps[:, bk, 0:T], lhsT=mat[0:64, c0:c0 + 128], rhs=thx[0:64, 0:T],
                                                              start=True, stop=True), reads=[bmat, bthx], writes=[b_ps[bk]])
                        S.act(E("activation", out=t_ld[:, 0:T], in_=ps[:, bk, 0:T], func=AF.Sigmoid,
                                                                 bias=V_("w0", b), scale=1.0),
                              reads=[b_ps[bk], bvec], writes=[bl_ld])
                        S.dve(E("tensor_scalar", out=t_ld[:, 0:T], in0=t_ld[:, 0:T], scalar1=-math.exp(-0.5), scalar2=None,
                                                        op0=ALU.mult), reads=[bl_ld], writes=[bl_ld])
                        bk = nbank()
                        S.pe(E("matmul", ps[:, bk, 0:T], lhsT=mat[64:128, c0:c0 + 128], rhs=thx[64:128, 0:T],
                                                              start=True, stop=True), reads=[bmat, bthx], writes=[b_ps[bk]])
                        S.act(E("activation", out=t_a[:, 0:T], in_=ps[:, bk, 0:T], func=AF.Sigmoid,
                                                                 bias=V_("a0", b), scale=1.0),
                              reads=[b_ps[bk], bvec], writes=[bl_a])
                        bk = nbank()
                        S.pe(E("matmul", ps[:, bk, 0:T], lhsT=mat[:, 384 + c0:384 + c0 + 128], rhs=sgx[:, 0:T],
                                                              start=True, stop=False), reads=[bmat, bsgx], writes=[b_ps[bk]])
                        S.pe(E("matmul", ps[:, bk, 0:T], lhsT=mat[0:32, 768 + c0:768 + c0 + 128],
                                                              rhs=sgx2[0:32, 0:T], start=False, stop=True),
                             reads=[bmat, bsgx2], writes=[b_ps[bk]])
                        S.act(E("copy", out=Gg[:, b, 0:T], in_=ps[:, bk, 0:T]), reads=[b_ps[bk]], writes=[b_G])
                        S.dve(E("tensor_scalar", out=t_kk[:, 0:T], in0=Kg[:, b, 0:T], scalar1=V_("kkv", b), scalar2=None,
                                                             op0=ALU.mult), reads=[b_K, bvec], writes=[bl_kk])
                        S.act(E("activation", out=R_(t_r[:, 0:T]), in_=t_kk[:, 0:T], func=AF.Square), reads=[bl_kk],
                              writes=[bl_r])
                        bk = nbank()
                        S.pe(E("matmul", ps[:, bk, 0:T], lhsT=R_(bonesr[:]), rhs=R_(t_r[:, 0:T]), start=True, stop=True),
                             reads=[b_bones, bl_r], writes=[b_ps[bk]])
                        S.act(E("activation", out=t_x[:, 0:T], in_=ps[:, bk, 0:T], func=AF.Sqrt), reads=[b_ps[bk]],
                              writes=[bl_x])
                        S.dve(E("tensor_scalar", out=t_x[:, 0:T], in0=t_x[:, 0:T], scalar1=1e-12, scalar2=None, op0=ALU.max),
                              reads=[bl_x], writes=[bl_x])
                        S.dve(E("reciprocal", out=t_x[:, 0:T], in_=t_x[:, 0:T]), reads=[bl_x], writes=[bl_x])
                        S.dve(E("tensor_tensor", out=t_kk[:, 0:T], in0=t_kk[:, 0:T], in1=t_x[:, 0:T], op=ALU.mult),
                              reads=[bl_x, bl_kk], writes=[bl_kk])
                        S.dve(E("tensor_scalar", out=t_x[:, 0:T], in0=t_a[:, 0:T], scalar1=V_("ka", b),
                                                             scalar2=vx[:, b:b + 1], op0=ALU.mult, op1=ALU.add),
                              reads=[bl_a, bvec, bvx], writes=[bl_x])
                        S.dve(E("tensor_tensor", out=R_(Kg[:, b, 0:T]), in0=Kg[:, b, 0:T], in1=t_x[:, 0:T], op=ALU.mult),
                              reads=[bl_x, b_K], writes=[b_K])
                        S.dve(E("scalar_tensor_tensor", out=R_(t_r[:, 0:T]), in0=Rg[:, b, 0:T], scalar=V_("rk", b),
                                                                    in1=Kg[:, b, 0:T], op0=ALU.mult, op1=ALU.mult),
                              reads=[b_R, b_K, bvec], writes=[bl_r])
                        bk = nbank()
                        S.pe(E("matmul", ps[:, bk, 0:T], lhsT=R_(bonesr[:]), rhs=R_(t_r[:, 0:T]), start=True, stop=True),
                             reads=[b_bones, bl_r], writes=[b_ps[bk]])
                        S.dve(E("tensor_tensor", out=bonus[:, b, 0:T], in0=ps[:, bk, 0:T], in1=Vg[:, b, 0:T],
                                                                    op=ALU.mult), reads=[b_ps[bk], b_V], writes=[b_bonus])
                    elif b < 6:
                        c0 = (b - 3) * 128
                        np_ = 64 if b == 5 else 128
                        bk = nbank()
                        S.pe(E("matmul", ps[0:np_, bk, 0:T], lhsT=mat[32:48, 1152 + c0:1152 + c0 + np_],
                                                                       rhs=gkb[32:48, 0:T], start=True, stop=True),
                             reads=[bmat, bgkb_], writes=[b_ps[bk]])
                        S.act(E("activation", out=t_ld[0:np_, 0:T], in_=ps[0:np_, bk, 0:T], func=AF.Exp,
                                                                          bias=vx[0:np_, b:b + 1], scale=-1.0),
                              reads=[b_ps[bk], bvx], writes=[bl_ld])
                        S.act(E("activation", out=t_ld[0:np_, 0:T], in_=t_ld[0:np_, 0:T], func=AF.Ln, bias=1.0,
                                                              scale=1.0), reads=[bl_ld], writes=[bl_ld])
                        S.dve(E("tensor_scalar", out=t_ld[0:np_, 0:T], in0=t_ld[0:np_, 0:T], scalar1=-1.0 / 16.0,
                                                                 scalar2=None, op0=ALU.mult), reads=[bl_ld], writes=[bl_ld])
                        if b == 5:
                            S.dve(E("tensor_scalar", out=t_ld[64:128, 0:T], in0=rmask[64:128, 0:T], scalar1=0.0,
                                                                 scalar2=V_("ldret", b)[64:128], op0=ALU.mult, op1=ALU.add),
                                  reads=[b_rmask, bvec], writes=[bl_ld])
                    else:
                        S.dve(E("tensor_scalar", out=t_ld[:, 0:T], in0=rmask[:, 0:T], scalar1=0.0,
                                                             scalar2=V_("ldret", b), op0=ALU.mult, op1=ALU.add),
                              reads=[b_rmask, bvec], writes=[bl_ld])
                    S.dve(E("tensor_tensor_scan", out=t_lw[:, 0:T], data0=rmask[:, 0:T], data1=t_ld[:, 0:T], initial=0.0,
                                                         op0=ALU.mult, op1=ALU.add), reads=[bl_ld, b_rmask], writes=[bl_lw])
                    S.act(E("activation", out=t_W[:, 0:T], in_=t_lw[:, 0:T], func=AF.Exp), reads=[bl_lw], writes=[bl_W])
                    S.act(E("activation", out=t_IW[:, 0:T], in_=t_lw[:, 0:T], func=AF.Exp, scale=-1.0), reads=[bl_lw],
                          writes=[bl_IW])
                    S.act(E("copy", out=WC[:, b, 0:nch],
                                                              in_=t_W[:, 0:T].rearrange("p (c t) -> p c t", t=C)[:, :, C - 1]),
                          reads=[bl_W], writes=[b_WC])
                    if b < 3:
                        S.dve(E("tensor_tensor", out=t_y[:, 0:T], in0=t_lw[:, 0:T], in1=t_ld[:, 0:T], op=ALU.subtract),
                              reads=[bl_lw, bl_ld], writes=[bl_y])
                        S.act(E("activation", out=t_y[:, 0:T], in_=t_y[:, 0:T], func=AF.Exp), reads=[bl_y], writes=[bl_y])
                        S.dve(E("scalar_tensor_tensor", out=R_(ATt[:, b, 0:T]), in0=t_kk[:, 0:T], scalar=-1.0,
                                                                    in1=t_y[:, 0:T], op0=ALU.mult, op1=ALU.mult),
                              reads=[bl_kk, bl_y], writes=[b_AT])
                        S.dve(E("tensor_tensor", out=t_y[:, 0:T], in0=t_kk[:, 0:T], in1=t_a[:, 0:T], op=ALU.mult),
                              reads=[bl_kk, bl_a], writes=[bl_y])
                        S.dve(E("tensor_tensor", out=R_(BTt[:, b, 0:T]), in0=t_y[:, 0:T], in1=t_IW[:, 0:T], op=ALU.mult),
                              reads=[bl_y, bl_IW], writes=[b_BT])
                    S.dve(E("scalar_tensor_tensor", out=R_(Rg[:, b, 0:T]), in0=Rg[:, b, 0:T], scalar=V_("rscale", b),
                                                                in1=t_W[:, 0:T], op0=ALU.mult, op1=ALU.mult),
                          reads=[b_R, bl_W, bvec], writes=[b_R])
                    S.dve(E("scalar_tensor_tensor", out=R_(Kg[:, b, 0:T]), in0=Kg[:, b, 0:T], scalar=V_("kscale", b),
                                                                in1=t_IW[:, 0:T], op0=ALU.mult, op1=ALU.mult),
                          reads=[b_K, bl_IW, bvec], writes=[b_K])

                stage(3)
                for ci in range(nch):
                    t0 = ci * C
                    cs = slice(t0, t0 + C)
                    cfg.setdefault("_marks", []).append((l, ci, len(S.ops)))
                    if kind == "p":
                        St, bSt = Sst[l], b_Sst[l]
                    else:
                        sp_ = ci % 2
                        St, bSt = Ssm[sp_], b_Ssm[sp_]
                        S.dma(E("dma_start", out=Sld[sp_][:].rearrange("p a b -> p (a b)"),
                                                                         in_=sst_d[l, ci]),
                              writes=[b_Sld[sp_]], chan="sld0")
                        S.act(E("copy", out=R_(St[:]), in_=Sld[sp_][:]), reads=[b_Sld[sp_]], writes=[bSt])
                    for (src, bsrc, dst, bdst, nb_) in ((Vg, b_V, vtok, b_vtok, 8), (Kg, b_K, ktok, b_ktok, 8),
                                                        (BTt, b_BT, btok, b_btok, 3)):
                        for half in range((nb_ + 3) // 4):
                            bk = nbank()
                            n_in = min(4, nb_ - half * 4)
                            for bl in range(n_in):
                                blk = half * 4 + bl
                                S.pe(E("transpose",
                                    ps[0:C, bk, bl * 128:(bl + 1) * 128], src[:, blk, cs], ident),
                                    reads=[bsrc, b_cst], writes=[b_ps[bk]])
                            w = n_in * 128
                            S.act(E("copy", out=R_(dst[0:C, half * 512:half * 512 + w]),
                                                                                   in_=ps[0:C, bk, 0:w]),
                                  reads=[b_ps[bk]], writes=[bdst])
                    stage(31)
                    def hp(h):
                        return slice((h % 2) * 64, (h % 2) * 64 + 64), h // 2

                    def pv3(bk, n):
                        return ps[0:C, bk, 0:n * C].rearrange("p (h c) -> p h c", c=C)

                    def bc(m, n):
                        return m[0:C, 0:C].unsqueeze(1).to_broadcast([C, n, C])

                    def par3(t, par, n):
                        return t[0:C, :, 0:C].rearrange("p (a two) c -> p a two c", two=2)[:, :, par, :]

                    hpb = min(8, 512 // C)
                    for (lt, blt, rt_, brt, dst, bdst, msk, nh) in ((Kg, b_K, Rg, b_R, ark, b_ark, m_incl, 16),
                                                                   (Kg, b_K, ATt, b_AT, aak, b_aak, m_su, 6),
                                                                   (BTt, b_BT, Rg, b_R, arb, b_arb, m_incl, 6),
                                                                   (BTt, b_BT, ATt, b_AT, Pm[0], b_Pm[0], m_su, 6),
                                                                   (ATt, b_AT, BTt, b_BT, Qm[0], b_Qm[0], m_sl, 6)):
                        npar = nh // 2
                        ngrp = (npar + hpb - 1) // hpb
                        banks = {}
                        for g_ in range(ngrp):
                            for par in range(2):
                                banks[(par, g_)] = nbank()
                        for h in range(nh):
                            rows, blk = hp(h)
                            hl = h // 2
                            bk = banks[(h % 2, hl // hpb)]
                            hi = hl % hpb
                            S.pe(E("matmul", ps[0:C, bk, hi * C:(hi + 1) * C], lhsT=R_(lt[rows, blk, cs]),
                                   rhs=R_(rt_[rows, blk, cs]), start=True, stop=True), reads=[blt, brt], writes=[b_ps[bk]])
                        for g_ in range(ngrp):
                            n_ = min(hpb, npar - g_ * hpb)
                            for par in range(2):
                                bk = banks[(par, g_)]
                                S.dve(E("tensor_tensor", out=R_(par3(dst, par, nh)[:, g_ * hpb:g_ * hpb + n_, :]), in0=pv3(bk, n_),
                                        in1=bc(msk, n_), op=ALU.mult), reads=[b_ps[bk], b_cst], writes=[bdst])
                    if ci == cfg.get("dbgci", 0) and l == 0:
                        dbg("AT0", ATt[:, 0, cs], [b_AT]); dbg("BT0", BTt[:, 0, cs], [b_BT])
                        dbg("K0", Kg[:, 0, cs], [b_K]); dbg("R0", Rg[:, 0, cs], [b_R]); dbg("V0", Vg[:, 0, cs], [b_V])
                        dbg("WC", WC[:, :, 0:4], [b_WC])
                        dbg("P0", Pm[0][0:C, :, 0:C], [b_Pm[0]], C); dbg("Q0", Qm[0][0:C, :, 0:C], [b_Qm[0]], C)
                        dbg("aak", aak[0:C, :, 0:C], [b_aak], C); dbg("arb", arb[0:C, :, 0:C], [b_arb], C)
                        dbg("btok", btok[0:C, 0:128], [b_btok], C)
                    stage(32)
                    S.dve(E("tensor_tensor", out=R_(Xm[0:C, :, 0:C]), in0=Pm[0][0:C, :, 0:C], in1=bc(eye64, 6), op=ALU.add),
                          reads=[b_Pm[0], b_cst], writes=[b_Xm])
                    ng_ = 1 if 6 * C <= 512 else 2
                    hg_ = 6 // ng_

                    def inv_mm(lhs_t, blhs, rhs_t, brhs):
                        bks = []
                        for g_ in range(ng_):
                            bk = nbank()
                            bks.append(bk)
                            for hi in range(hg_):
                                h = g_ * hg_ + hi
                                S.pe(E("matmul", ps[0:C, bk, hi * C:(hi + 1) * C], lhsT=R_(lhs_t[0:C, h, 0:C]),
                                       rhs=R_(rhs_t[0:C, h, 0:C]), start=True, stop=True), reads=[blhs, brhs], writes=[b_ps[bk]])
                        return bks

                    def gsl(t, g_):
                        return t[0:C, g_ * hg_:(g_ + 1) * hg_, 0:C]

                    def par2(t, par, n):
                        return t[0:C, 0:n * 64].rearrange("p (a two v) -> p a two v", two=2, v=64)[:, :, par, :]

                    def pv64(bk, n):
                        return ps[0:C, bk, 0:n * 64].rearrange("p (h v) -> p h v", v=64)

                    def f_y_state():
                        bkp = (nbank(), nbank())
                        for h in range(16):
                            rows, blk = hp(h)
                            bk = bkp[h % 2]
                            hl = h // 2
                            S.pe(E("matmul", ps[0:C, bk, hl * 64:(hl + 1) * 64], lhsT=R_(Rg[rows, blk, cs]), rhs=R_(St[rows, blk, :]),
                                   start=True, stop=True), reads=[b_R, bSt], writes=[b_ps[bk]])
                        for par in range(2):
                            S.act(E("copy", out=par2(ysb, par, 16), in_=pv64(bkp[par], 8)), reads=[b_ps[bkp[par]]], writes=[b_ysb])

                    def f_rhs_state():
                        bkp = (nbank(), nbank())
                        for h in range(6):
                            rows, blk = hp(h)
                            bk = bkp[h % 2]
                            hl = h // 2
                            S.pe(E("matmul", ps[0:C, bk, hl * 64:(hl + 1) * 64], lhsT=R_(ATt[rows, blk, cs]), rhs=R_(St[rows, blk, :]),
                                   start=True, stop=True), reads=[b_AT, bSt], writes=[b_ps[bk]])
                        for par in range(2):
                            S.act(E("copy", out=R_(par2(rhs0, par, 6)), in_=pv64(bkp[par], 3)), reads=[b_ps[bkp[par]]], writes=[b_rhs0])

                    def f_rhs_tok():
                        bk = nbank()
                        for h in range(6):
                            S.pe(E("matmul", ps[0:C, bk, h * 64:(h + 1) * 64], lhsT=R_(aak[0:C, h, 0:C]),
                                   rhs=R_(vtok[0:C, h * 64:(h + 1) * 64]), start=True, stop=True),
                                 reads=[b_aak, b_vtok], writes=[b_ps[bk]])
                        S.dve(E("tensor_tensor", out=R_(rhs0[0:C, :]), in0=rhs0[0:C, :], in1=ps[0:C, bk, 0:384], op=ALU.add),
                              reads=[b_ps[bk], b_rhs0], writes=[b_rhs0])

                    def f_y_tok(half):
                        bk = nbank()
                        for hl in range(8):
                            h = half * 8 + hl
                            S.pe(E("matmul", ps[0:C, bk, hl * 64:(hl + 1) * 64], lhsT=R_(ark[0:C, h, 0:C]),
                                   rhs=R_(vtok[0:C, h * 64:(h + 1) * 64]), start=True, stop=(h >= 6)),
                                 reads=[b_ark, b_vtok], writes=[b_ps[bk]])
                            if h < 6:
                                S.pe(E("matmul", ps[0:C, bk, hl * 64:(hl + 1) * 64], lhsT=R_(arb[0:C, h, 0:C]),
                                       rhs=R_(usb[0:C, h * 64:(h + 1) * 64]), start=False, stop=True),
                                     reads=[b_arb, b_usb], writes=[b_ps[bk]])
                        S.dve(E("tensor_tensor", out=ysb[0:C, half * 512:(half + 1) * 512], in0=ysb[0:C, half * 512:(half + 1) * 512],
                                in1=ps[0:C, bk, :], op=ALU.add), reads=[b_ps[bk], b_ysb], writes=[b_ysb])

                    fillers = [f_y_state, f_rhs_state, f_rhs_tok, lambda: f_y_tok(1)]
                    for st in range(nsteps):
                        bq = inv_mm(Pm[0], b_Pm[0], Qm[0], b_Qm[0])
                        if st < nsteps - 1:
                            bp = inv_mm(Qm[0], b_Qm[0], Pm[0], b_Pm[0])
                        for g_ in range(ng_):
                            S.act(E("copy", out=R_(gsl(Qm[0], g_)), in_=pv3(bq[g_], hg_)), reads=[b_ps[bq[g_]]], writes=[b_Qm[0]])
                        if st < nsteps - 1:
                            for g_ in range(ng_):
                                S.dve(E("tensor_copy", out=R_(gsl(Pm[0], g_)), in_=pv3(bp[g_], hg_)), reads=[b_ps[bp[g_]]],
                                      writes=[b_Pm[0]])
                        if fillers:
                            fillers.pop(0)()
                        bx = inv_mm(Qm[0], b_Qm[0], Xm, b_Xm)
                        for g_ in range(ng_):
                            S.dve(E("tensor_tensor", out=R_(gsl(Xm, g_)), in0=gsl(Xm, g_), in1=pv3(bx[g_], hg_), op=ALU.add),
                                  reads=[b_ps[bx[g_]], b_Xm], writes=[b_Xm])
                    while fillers:
                        fillers.pop(0)()
                    stage(33)
                    bk = nbank()
                    for h in range(6):
                        S.pe(E("matmul", ps[0:C, bk, h * 64:(h + 1) * 64], lhsT=R_(Xm[0:C, h, 0:C]),
                               rhs=R_(rhs0[0:C, h * 64:(h + 1) * 64]), start=True, stop=True),
                             reads=[b_Xm, b_rhs0], writes=[b_ps[bk]])
                    S.act(E("copy", out=R_(usb[0:C, :]), in_=ps[0:C, bk, 0:384]), reads=[b_ps[bk]], writes=[b_usb])
                    stage(34)
                    f_y_tok(0)
                    stage(35)
                    for half in range(2):
                        bk = nbank()
                        for bl in range(4):
                            blk = half * 4 + bl
                            S.pe(E("matmul", ps[:, bk, bl * 128:(bl + 1) * 128],
                                                                           lhsT=R_(ktok[0:C, blk * 128:(blk + 1) * 128]),
                                                                           rhs=R_(vtok[0:C, blk * 128:(blk + 1) * 128]),
                                                                           start=True, stop=(blk >= 3)),
                                 reads=[b_ktok, b_vtok], writes=[b_ps[bk]])
                            if blk < 3:
                                S.pe(E("matmul", ps[:, bk, bl * 128:(bl + 1) * 128],
                                                                               lhsT=R_(btok[0:C, blk * 128:(blk + 1) * 128]),
                                                                               rhs=R_(usb[0:C, blk * 128:(blk + 1) * 128]),
                                                                               start=False, stop=True),
                                     reads=[b_btok, b_usb], writes=[b_ps[bk]])
                        for hh in range(2):
                            rows = slice(hh * 64, hh * 64 + 64)
                            S.dve(E("tensor_tensor",
                                out=R_(St[rows, half * 4:half * 4 + 4, :]), in0=St[rows, half * 4:half * 4 + 4, :],
                                in1=ps[rows, bk, :].rearrange("p (a b) -> p a b", b=128)[:, :, hh * 64:hh * 64 + 64], op=ALU.add),
                                reads=[b_ps[bk], bSt], writes=[bSt])
                    S.dve(E("tensor_tensor", out=R_(St[:]), in0=St[:], in1=WC[:, :, ci:ci + 1].to_broadcast([128, 8, 64]),
                                                           op=ALU.mult), reads=[b_WC, bSt], writes=[bSt])
                    if kind == "s":
                        S.dma(E("dma_start", out=sto_d[l, 1 + ci], in_=St[:].rearrange("p a b -> p (a b)")),
                              reads=[bSt], chan="stout")
                    elif last_prompt and ci == nch - 1:
                        S.dma(E("dma_start", out=sto_d[l, 0], in_=St[:].rearrange("p a b -> p (a b)")),
                              reads=[bSt], chan="stout")
                    if ci == cfg.get("dbgci", 0) and l == 0:
                        dbg("ysb", ysb[0:C, 0:128], [b_ysb], C); dbg("St", St[:, 0, :], [bSt])
                    stage(36)
                    y3 = ysb[0:C, :].rearrange("p (h v) -> p h v", v=64)
                    q3 = ysq[0:C, :].rearrange("p (h v) -> p h v", v=64)
                    s1, s2, mean, tmpv, rstd = (yst[0:C, i, :] for i in range(5))
                    S.dve(E("tensor_tensor", out=R_(ysq[0:C, :]), in0=ysb[0:C, :], in1=ysb[0:C, :], op=ALU.mult), reads=[b_ysb],
                          writes=[b_ysq])
                    S.dve(E("tensor_reduce", out=s1, in_=y3, axis=AX.X, op=ALU.add), reads=[b_ysb], writes=[b_yst])
                    S.dve(E("tensor_reduce", out=s2, in_=q3, axis=AX.X, op=ALU.add), reads=[b_ysq], writes=[b_yst])
                    S.dve(E("tensor_tensor", out=mean, in0=s1, in1=tokc[0:C, 0:16], op=ALU.mult),
                          reads=[b_yst, b_cst], writes=[b_yst])
                    S.dve(E("tensor_tensor", out=tmpv, in0=mean, in1=mean, op=ALU.mult), reads=[b_yst],
                          writes=[b_yst])
                    S.dve(E("scalar_tensor_tensor", out=tmpv, in0=s2, scalar=1.0 / 64, in1=tmpv, op0=ALU.mult,
                                                                             op1=ALU.subtract), reads=[b_yst], writes=[b_yst])
                    S.dve(E("tensor_tensor", out=tmpv, in0=tmpv, in1=tokc[0:C, 16:32], op=ALU.add),
                          reads=[b_yst, b_cst], writes=[b_yst])
                    S.act(E("activation", out=tmpv, in_=tmpv, func=AF.Sqrt), reads=[b_yst], writes=[b_yst])
                    S.dve(E("reciprocal", out=rstd, in_=tmpv), reads=[b_yst], writes=[b_yst])
                    S.dve(E("tensor_tensor", out=y3, in0=y3, in1=mean.unsqueeze(2).to_broadcast([C, 16, 64]),
                                                                      op=ALU.subtract), reads=[b_yst, b_ysb], writes=[b_ysb])
                    S.dve(E("tensor_tensor", out=y3, in0=y3, in1=rstd.unsqueeze(2).to_broadcast([C, 16, 64]),
                                                                      op=ALU.mult), reads=[b_yst, b_ysb], writes=[b_ysb])
                    bpb = min(8, 512 // C)
                    for g_ in range(8 // bpb):
                        bk = nbank()
                        for bi in range(bpb):
                            blk = g_ * bpb + bi
                            S.pe(E("transpose", ps[:, bk, bi * C:(bi + 1) * C], ysb[0:C, blk * 128:(blk + 1) * 128],
                                   cst[0:C, 0:C]), reads=[b_ysb, b_cst], writes=[b_ps[bk]])
                        S.act(E("copy", out=ynT[:, g_ * bpb:(g_ + 1) * bpb, cs],
                                in_=ps[:, bk, 0:bpb * C].rearrange("p (a c) -> p a c", c=C)), reads=[b_ps[bk]], writes=[b_ynT])

                stage(4)
                for b in range(8):
                    S.dve(E("tensor_scalar", out=ynT[:, b, 0:T], in0=ynT[:, b, 0:T], scalar1=V_("nw", b), scalar2=V_("nb", b),
                                                         op0=ALU.mult, op1=ALU.add), reads=[b_ynT, bvec], writes=[b_ynT])
                    if b < 3:
                        S.dve(E("tensor_tensor", out=ynT[:, b, 0:T], in0=ynT[:, b, 0:T], in1=bonus[:, b, 0:T], op=ALU.add),
                              reads=[b_ynT, b_bonus], writes=[b_ynT])
                    S.dve(E("tensor_tensor", out=mixT[:, b, 0:T], in0=ynT[:, b, 0:T], in1=Gg[:, b, 0:T], op=ALU.mult),
                          reads=[b_ynT, b_G], writes=[b_mixT])
                    if hpf:
                        S.dve(E("tensor_tensor", out=mix32[:, b, :], in0=ynT[:, b, 0:2], in1=Gg[:, b, 0:2], op=ALU.mult),
                              reads=[b_ynT, b_G], writes=[b_mix32])

                def ln_epilogue(gname, bname):
                    r16, br16 = hT[:, 0:8, :], b_hT
                    S.act(E("copy", out=hT[:, 0:8, 0:T], in_=x32[:, :, 0:T]), reads=[b_x32], writes=[b_hT])
                    S.act(E("activation", out=hT[:, 8:16, 0:T], in_=x32[:, :, 0:T], func=AF.Square), reads=[b_x32], writes=[b_hT])
                    bk1 = nbank(); bk2 = nbank()
                    for kc in range(8):
                        S.pe(E("matmul", ps[:, bk1, 0:T], lhsT=onesb[:], rhs=hT[:, kc, 0:T], start=(kc == 0), stop=(kc == 7)),
                             reads=[b_ones, b_hT], writes=[b_ps[bk1]])
                    for kc in range(8):
                        S.pe(E("matmul", ps[:, bk2, 0:T], lhsT=onesb[:], rhs=hT[:, 8 + kc, 0:T], start=(kc == 0),
                                                       stop=(kc == 7)), reads=[b_ones, b_hT], writes=[b_ps[bk2]])
                    mean, var = tmp[0], tmp[1]
                    S.act(E("mul", out=mean[:, 0:T], in_=ps[:, bk1, 0:T], mul=1.0 / D), reads=[b_ps[bk1]], writes=[b_tmp[0]])
                    S.dve(E("tensor_tensor", out=var[:, 0:T], in0=mean[:, 0:T], in1=mean[:, 0:T], op=ALU.mult), reads=[b_tmp[0]],
                          writes=[b_tmp[1]])
                    S.dve(E("scalar_tensor_tensor", out=var[:, 0:T], in0=ps[:, bk2, 0:T], scalar=1.0 / D, in1=var[:, 0:T],
                                                           op0=ALU.mult, op1=ALU.subtract), reads=[b_ps[bk2], b_tmp[1]], writes=[b_tmp[1]])
                    S.dve(E("tensor_scalar", out=var[:, 0:T], in0=var[:, 0:T], scalar1=1e-5, scalar2=None, op0=ALU.add),
                          reads=[b_tmp[1]], writes=[b_tmp[1]])
                    S.act(E("activation", out=var[:, 0:T], in_=var[:, 0:T], func=AF.Sqrt), reads=[b_tmp[1]], writes=[b_tmp[1]])
                    S.dve(E("reciprocal", out=var[:, 0:T], in_=var[:, 0:T]), reads=[b_tmp[1]], writes=[b_tmp[1]])
                    S.dve(E("tensor_tensor", out=x32[:, :, 0:T], in0=x32[:, :, 0:T],
                                                    in1=mean[:, 0:T].unsqueeze(1).to_broadcast([128, 8, T]), op=ALU.subtract),
                          reads=[b_tmp[0], b_x32], writes=[b_x32])
                    S.dve(E("tensor_tensor", out=x32[:, :, 0:T], in0=x32[:, :, 0:T],
                                                    in1=var[:, 0:T].unsqueeze(1).to_broadcast([128, 8, T]), op=ALU.mult),
                          reads=[b_tmp[1], b_x32], writes=[b_x32])
                    for b in range(8):
                        S.dve(E("tensor_scalar", out=x32[:, b, 0:T], in0=x32[:, b, 0:T], scalar1=V_(gname, b),
                                                             scalar2=V_(bname, b), op0=ALU.mult, op1=ALU.add),
                              reads=[b_x32, bvec], writes=[b_x32])
                    S.act(E("copy", out=xTb[:, :, 0:T], in_=x32[:, :, 0:T]), reads=[b_x32], writes=[b_xTb])

                stage(5)
                for piece in range(4):
                    slot, bslot = ws_next(l, NPIECE_IN + piece)
                    for sub in range(2):
                        cb = piece * 2 + sub
                        bk = nbank()
                        for kc in range(8):
                            S.pe(E("matmul",
                                ps[:, bk, 0:T], lhsT=slot[:, kc * 256 + sub * 128: kc * 256 + sub * 128 + 128],
                                rhs=mixT[:, kc, 0:T], start=(kc == 0), stop=(kc == 7)), reads=[bslot, b_mixT], writes=[b_ps[bk]])
                        if hpf:
                            bkh = hp_mm8(l, NPIECE_IN + piece, sub, lambda kc: mix32[:, kc, 0:2], b_mix32)
                            S.dve(E("scalar_tensor_tensor", out=x32[:, cb, 0:2], in0=x32[:, cb, 0:2], scalar=ALPHA,
                                    in1=ps[:, bkh, 0:2], op0=ALU.mult, op1=ALU.add), reads=[b_ps[bkh], b_x32], writes=[b_x32])
                        S.dve(E("scalar_tensor_tensor", out=x32[:, cb, clo:T], in0=x32[:, cb, clo:T], scalar=ALPHA,
                                                                             in1=ps[:, bk, clo:T], op0=ALU.mult, op1=ALU.add),
                              reads=[b_ps[bk], b_x32], writes=[b_x32])
                ln_epilogue("ln1g", "ln1b")

                stage(6)
                if kind == "s":
                    S.dma(E("dma_start", out=cvst[:].rearrange("p a b c -> p (a b c)"), in_=scv_d[l]), writes=[b_cvst],
                          chan="cvst")
                for j in range(NJ):
                    slot, bslot = ws_next(l, NPIECE_IN + 4 + j)
                    cv = []
                    for ab in range(2):
                        cbk = j + NJ * ab
                        bk = nbank()
                        for kc in range(8):
                            S.pe(E("matmul",
                                ps[:, bk, 0:T], lhsT=slot[:, kc * 256 + ab * 128: kc * 256 + ab * 128 + 128],
                                rhs=xTb[:, kc, 0:T], start=(kc == 0), stop=(kc == 7)), reads=[bslot, b_xTb], writes=[b_ps[bk]])
                        ub, bub = ubuf[ab + 2 * (j % 2)], b_ubuf[ab + 2 * (j % 2)]
                        u3 = ub[:, 0:nseq * (L + 2)].rearrange("p (s t) -> p s t", t=L + 2)
                        if kind == "s":
                            S.dve(E("tensor_copy", out=u3[:, :, 0:2], in_=cvst[:, cbk, :, :]), reads=[b_cvst],
                                  writes=[bub])
                        else:
                            S.dve(E("tensor_copy", out=u3[:, 0, 0:2], in_=cvc[:, l, cbk, :]),
                                  reads=[b_cvc[l]], writes=[bub])
                        if hpf:
                            bkh = hp_mm8(l, NPIECE_IN + 4 + j, ab, lambda kc: x32[:, kc, 0:2], b_x32)
                            S.act(E("copy", out=u3[:, 0, 2:4], in_=ps[:, bkh, 0:2]), reads=[b_ps[bkh]], writes=[bub])
                            S.act(E("copy", out=u3[:, 0, 4:L + 2], in_=ps[:, bk, 2:T]), reads=[b_ps[bk]], writes=[bub])
                        else:
                            S.act(E("copy", out=u3[:, :, 2:L + 2], in_=t3(ps[:, bk, 0:T])), reads=[b_ps[bk]], writes=[bub])
                        if kind == "s":
                            S.act(E("copy", out=cvo[:, cbk, :, :], in_=u3[:, :, L:L + 2]), reads=[bub], writes=[b_cvo])
                        else:
                            S.act(E("copy", out=cvc[:, l, cbk, :], in_=u3[:, 0, L:L + 2]), reads=[bub],
                                  writes=[b_cvc[l]])
                        ct, bct = tmp[2 + ab + 2 * (j % 2)], b_tmp[2 + ab + 2 * (j % 2)]
                        c3 = t3(ct[:, 0:T])
                        S.pool(E("tensor_scalar", out=c3, in0=u3[:, :, 0:L], scalar1=V_("cw0", cbk),
                                                                               scalar2=V_("cb", cbk), op0=ALU.mult, op1=ALU.add),
                              reads=[bub, bvec], writes=[bct])
                        S.dve(E("scalar_tensor_tensor", out=c3, in0=u3[:, :, 1:L + 1], scalar=V_("cw1", cbk),
                                                                                      in1=c3, op0=ALU.mult, op1=ALU.add),
                              reads=[bub, bvec, bct], writes=[bct])
                        S.dve(E("scalar_tensor_tensor", out=c3, in0=u3[:, :, 2:L + 2], scalar=V_("cw2", cbk),
                                                                                      in1=c3, op0=ALU.mult, op1=ALU.add),
                              reads=[bub, bvec, bct], writes=[bct])
                        cv.append((ct, bct))
                    (ca, bca), (cb_, bcb) = cv
                    S.act(E("activation", out=ca[:, 0:T], in_=ca[:, 0:T], func=AF.Gelu_apprx_tanh), reads=[bca], writes=[bca])
                    S.dve(E("tensor_tensor", out=hT[:, j, 0:T], in0=ca[:, 0:T], in1=cb_[:, 0:T], op=ALU.mult),
                          reads=[bca, bcb], writes=[b_hT])
                    if hpf:
                        S.dve(E("tensor_tensor", out=h32[:, j, :], in0=ca[:, 0:2], in1=cb_[:, 0:2], op=ALU.mult),
                              reads=[bca, bcb], writes=[b_h32])
                if kind == "s":
                    S.dma(E("dma_start", out=cvo_d[l][:, 88:88 * (1 + NSS)], in_=cvo[:].rearrange("p a b c -> p (a b c)")),
                          reads=[b_cvo], chan="cvout")
                elif last_prompt:
                    S.dma(E("dma_start", out=cvo_d[l][:, 0:88], in_=cvc[:, l, :, :].rearrange("p a c -> p (a c)")),
                          reads=[b_cvc[l]], chan="cvout")
                for cb in range(8):
                    slot, bslot = ws_next(l, NPIECE_IN + 4 + NJ + cb)
                    bk = nbank()
                    for j in range(NJ):
                        S.pe(E("matmul", ps[:, bk, 0:T], lhsT=slot[:, j * 128:(j + 1) * 128], rhs=hT[:, j, 0:T],
                                                                       start=(j == 0), stop=(j == NJ - 1)),
                             reads=[bslot, b_hT], writes=[b_ps[bk]])
                    if hpf:
                        bkh = nbank()
                        for jh in range(3):
                            njj = min(8, NJ - jh * 8)
                            w, bw = hp_load(l, NPIECE_IN + 4 + NJ + cb, ("jh", jh, njj))
                            for jj in range(njj):
                                j = jh * 8 + jj
                                S.pe(E("matmul", ps[:, bkh, 0:2], lhsT=w[:, jj * 128:(jj + 1) * 128], rhs=h32[:, j, 0:2],
                                       start=(j == 0), stop=(j == NJ - 1)), reads=[bw, b_h32], writes=[b_ps[bkh]])
                        S.dve(E("scalar_tensor_tensor", out=x32[:, cb, 0:2], in0=x32[:, cb, 0:2], scalar=ALPHA,
                                in1=ps[:, bkh, 0:2], op0=ALU.mult, op1=ALU.add), reads=[b_ps[bkh], b_x32], writes=[b_x32])
                    S.dve(E("scalar_tensor_tensor", out=x32[:, cb, clo:T], in0=x32[:, cb, clo:T], scalar=ALPHA,
                                                                         in1=ps[:, bk, clo:T], op0=ALU.mult, op1=ALU.add),
                          reads=[b_ps[bk], b_x32], writes=[b_x32])
                stage(7)
                ln_epilogue("ln2g", "ln2b")

            S.dma(E("dma_start", out=yT_d[:, :, tok0:tok0 + T], in_=x32[:, :, 0:T]), reads=[b_x32], chan="yout")


    try:
        _tile_loop()
        assert ws["used"] == len(plan)
    except _Stop:
        pass
    S.run_block()
    return nc, S


_CACHE = {}


def _prep_inputs(inp, cfg):
    f = np.float32
    W = _prep_weights(inp)
    vm = [_prep_layer_params(inp, l) for l in range(NL)]
    vecs = np.stack([v for v, _ in vm])
    mats = np.stack([m for _, m in vm])
    cst = _consts()
    cosT, sinT = _rope_tables()
    in_maps = []
    for c in range(8):
        xp = inp["x_prompt"][c]
        xs = inp["x_sample"][c * NSS:(c + 1) * NSS].reshape(NSS * LS, D)
        x = np.concatenate([xp, xs], axis=0)
        xT = np.ascontiguousarray(x.T.reshape(8, 128, -1).transpose(1, 0, 2))
        sst = np.zeros((NL, NSS, 16, 64, 64), f)
        sl = slice(c * NSS, (c + 1) * NSS)
        sst[:, :, 0:6] = inp["state_rwkv"][:, sl].transpose(0, 1, 2, 4, 3)
        sst[:, :, 6:11, 0:32] = inp["state_gla"][:, sl]
        sst[:, :, 11:16] = inp["state_ret"][:, sl]
        sst = sst.reshape(NL, NSS, 8, 2, 64, 64).transpose(0, 1, 3, 4, 2, 5).reshape(NL, NSS, 128, 512)
        sh = inp["state_shift"][:, sl]
        shp = np.zeros((NL, NSS, 12 * 128), f)
        shp[:, :, 0:1152] = sh[:, :, 0:1152]
        shp[:, :, 1152:1408] = sh[:, :, 1152:1408]
        shp[:, :, 1408:1440] = sh[:, :, 1408:1440]
        ssh = shp.reshape(NL, NSS, 12, 128).transpose(0, 3, 2, 1).reshape(NL, 128, 12 * NSS)
        cvs = inp["state_conv"][:, sl]
        scv = cvs.reshape(NL, NSS, 2, 44, 128).transpose(0, 4, 3, 1, 2).reshape(NL, 128, 44 * NSS * 2)
        in_maps.append({"xT": xT, "wts": W, "vecs": vecs, "mats": mats, "cst": cst, "ropec": cosT, "ropes": sinT,
                        "sst": np.ascontiguousarray(sst), "ssh": np.ascontiguousarray(ssh), "scv": np.ascontiguousarray(scv)})
    return in_maps


def _assemble(results):
    f = np.float32
    y_p = np.zeros((8, SEQ, D), f); y_s = np.zeros((8 * NSS, LS, D), f)
    rw_p = np.zeros((NL, 8, 6, 64, 64), f); rw_s = np.zeros((NL, 8 * NSS, 6, 64, 64), f)
    sh_p = np.zeros((NL, 8, 1440), f); sh_s = np.zeros((NL, 8 * NSS, 1440), f)
    gl_p = np.zeros((NL, 8, 5, 32, 64), f); gl_s = np.zeros((NL, 8 * NSS, 5, 32, 64), f)
    rt_p = np.zeros((NL, 8, 5, 64, 64), f); rt_s = np.zeros((NL, 8 * NSS, 5, 64, 64), f)
    cv_p = np.zeros((NL, 8, 2, 2 * DFF), f); cv_s = np.zeros((NL, 8 * NSS, 2, 2 * DFF), f)
    for c, r in enumerate(results):
        yT = r["yT"]
        y = yT.transpose(1, 0, 2).reshape(D, -1).T
        y_p[c] = y[0:SEQ]
        y_s[c * NSS:(c + 1) * NSS] = y[SEQ:].reshape(NSS, LS, D)
        st = r["sto"].reshape(NL, 1 + NSS, 2, 64, 8, 64).transpose(0, 1, 4, 2, 3, 5).reshape(NL, 1 + NSS, 16, 64, 64)
        rw = st[:, :, 0:6].transpose(0, 1, 2, 4, 3)
        gl = st[:, :, 6:11, 0:32]
        rt = st[:, :, 11:16]
        rw_p[:, c] = rw[:, 0]; gl_p[:, c] = gl[:, 0]; rt_p[:, c] = rt[:, 0]
        sl = slice(c * NSS, (c + 1) * NSS)
        rw_s[:, sl] = rw[:, 1:]; gl_s[:, sl] = gl[:, 1:]; rt_s[:, sl] = rt[:, 1:]
        sho = r["sho"].reshape(NL, 128, 12 * (1 + NSS))
        p = sho[:, :, 0:12].transpose(0, 2, 1).reshape(NL, 12 * 128)
        s = sho[:, :, 12:].reshape(NL, 128, 12, NSS).transpose(0, 3, 2, 1).reshape(NL, NSS, 12 * 128)
        sh_p[:, c] = p[:, 0:1440]
        sh_s[:, sl] = s[:, :, 0:1440]
        cvo = r["cvo"].reshape(NL, 128, 44 * (1 + NSS) * 2)
        p = cvo[:, :, 0:88].reshape(NL, 128, 44, 2).transpose(0, 3, 2, 1).reshape(NL, 2, 2 * DFF)
        s = cvo[:, :, 88:].reshape(NL, 128, 44, NSS, 2).transpose(0, 3, 4, 2, 1).reshape(NL, NSS, 2, 2 * DFF)
        cv_p[:, c] = p
        cv_s[:, sl] = s
    return (y_p, y_s, rw_p, sh_p, gl_p, rt_p, cv_p, rw_s, sh_s, gl_s, rt_s, cv_s)


def kernel(**inputs):
    inp = {k: np.asarray(v) for k, v in inputs.items()}
    cfg = CFG_FULL
    if "nc" not in _CACHE:
        _CACHE["nc"] = build(cfg)[0]
    nc = _CACHE["nc"]
    in_maps = _prep_inputs(inp, cfg)
    res = run_bass_kernel_spmd(nc, in_maps, core_ids=list(range(8)))
    return _assemble(res.results)
```

```python
import bisect
import math
from contextlib import ExitStack

import numpy as np
import concourse.bass as bass
import concourse.mybir as mybir
from concourse.bass_utils import run_bass_kernel_spmd

F32 = mybir.dt.float32
F32R = mybir.dt.float32r
BF16 = mybir.dt.bfloat16
AF = mybir.ActivationFunctionType
ALU = mybir.AluOpType
AX = mybir.AxisListType

NL = 4
D = 1024
SEQ = 2048
NSS = 16
LS = 4
TT = 256
PAST = 16384
DFF = 2816
NJ = 22
ALPHA = (2 * NL) ** 0.25
NBLK = 38
NPIECE_IN = 19
SLOT = 2816
NSLOT = 4
RW0, GL0, RT0 = 0, 1440, 2416


class _Stop(Exception):
    pass


class Buf:
    __slots__ = ("name", "last_write", "reads")

    def __init__(self, name):
        self.name = name
        self.last_write = None
        self.reads = []


class Op:
    __slots__ = ("eng", "fn", "deps", "idx", "needs_inc", "cnt", "chan", "dma_cnt")

    def __init__(self, eng, fn, deps, idx, chan=None):
        self.eng = eng
        self.fn = fn
        self.deps = deps
        self.idx = idx
        self.needs_inc = False
        self.cnt = None
        self.chan = chan
        self.dma_cnt = None


COMPUTE = ("pe", "act", "dve", "pool")
DMAQ = ("sp", "actq", "poolq")
STREAM = {"pe": "pe", "act": "act", "dve": "dve", "pool": "pool", "sp": "sp", "actq": "act", "poolq": "pool"}
SAME_SYNC = {"pe": False, "act": True, "dve": True, "pool": True}


class Sched:
    def __init__(self, nc):
        self.nc = nc
        self.ops = []

    def op(self, eng, fn, reads=(), writes=(), chan=None):
        idx = len(self.ops)
        deps = set()
        for b in reads:
            if b.last_write is not None:
                deps.add(b.last_write)
        for b in writes:
            if b.last_write is not None:
                deps.add(b.last_write)
            deps.update(b.reads)
        o = Op(eng, fn, deps, idx, chan)
        self.ops.append(o)
        for b in reads:
            b.reads.append(idx)
        for b in writes:
            b.last_write = idx
            b.reads = []
        return o

    def pe(self, fn, reads=(), writes=()):
        return self.op("pe", fn, reads, writes)

    def act(self, fn, reads=(), writes=()):
        return self.op("act", fn, reads, writes)

    def dve(self, fn, reads=(), writes=()):
        return self.op("dve", fn, reads, writes)

    def pool(self, fn, reads=(), writes=()):
        return self.op("pool", fn, reads, writes)

    def dma(self, fn, reads=(), writes=(), chan=None, q="sp"):
        return self.op(q, fn, reads, writes, chan=chan)

    def _skip(self, p, so):
        sp = STREAM[p.eng]
        return sp == so and p.eng in COMPUTE and not SAME_SYNC[p.eng]

    def run_block(self):
        nc = self.nc
        ops = self.ops
        for o in ops:
            so = STREAM[o.eng]
            for d in o.deps:
                p = ops[d]
                if p.eng in DMAQ or self._skip(p, so):
                    continue
                p.needs_inc = True
        cnt = {e: 0 for e in COMPUTE}
        chan_cnt = {}
        chan_hist = {}
        for o in ops:
            if o.eng in DMAQ:
                c = chan_cnt.get(o.chan, 0) + 16
                chan_cnt[o.chan] = c
                o.dma_cnt = c
                chan_hist.setdefault(o.chan, []).append((o.idx, c))
            elif o.needs_inc:
                cnt[o.eng] += 1
                o.cnt = cnt[o.eng]
        chan_idx = {c: [i for i, _ in h] for c, h in chan_hist.items()}
        streams = {s: [] for s in ("pe", "act", "dve", "pool", "sp")}
        waited = {s: {} for s in streams}
        for o in ops:
            so = STREAM[o.eng]
            need = {}
            for d in o.deps:
                p = ops[d]
                if p.eng in DMAQ:
                    h = chan_hist[p.chan]
                    k = bisect.bisect_left(chan_idx[p.chan], o.idx) - 1
                    key = ("chan", p.chan)
                    need[key] = max(need.get(key, 0), h[k][1])
                else:
                    if self._skip(p, so):
                        continue
                    key = ("eng", p.eng)
                    need[key] = max(need.get(key, 0), p.cnt)
            w = []
            for key, c in need.items():
                if waited[so].get(key, 0) >= c:
                    continue
                waited[so][key] = c
                w.append((key, c))
            streams[so].append((o, w))
        with ExitStack() as es:
            sems = {}
            for e in COMPUTE:
                sems[("eng", e)] = es.enter_context(nc.semaphore("s_" + e))
            for c in chan_cnt:
                sems[("chan", c)] = es.enter_context(nc.semaphore("c_" + str(c)))
            block = es.enter_context(nc.Block())

            def mk(sname):
                def body(eng):
                    for o, w in streams[sname]:
                        for key, c in w:
                            eng.wait_ge(sems[key], c)
                        ins = o.fn(eng)
                        if o.eng in DMAQ:
                            ins.then_inc(sems[("chan", o.chan)], 16)
                        elif o.needs_inc:
                            ins.then_inc(sems[("eng", o.eng)], 1)
                    if sname == "sp":
                        for c, v in chan_cnt.items():
                            eng.wait_ge(sems[("chan", c)], v)
                        for e in COMPUTE:
                            if cnt[e] > 0:
                                eng.wait_ge(sems[("eng", e)], cnt[e])
                return body

            block.tensor(mk("pe"))
            block.scalar(mk("act"))
            block.vector(mk("dve"))
            block.gpsimd(mk("pool"))
            block.sync(mk("sp"))


def _colmap():
    cm = -np.ones(NBLK * 128, np.int64)
    for g, (rw_off, gl_off, rt_off, pad) in enumerate(((0, 0, 0, True), (384, 160, 320, True), (768, 320, 640, False))):
        base = g * 1024
        cm[base:base + 384] = RW0 + rw_off + np.arange(384)
        for j in range(5):
            if pad:
                cm[base + 384 + j * 64: base + 384 + j * 64 + 32] = GL0 + gl_off + j * 32 + np.arange(32)
            else:
                cm[base + 384 + j * 64: base + 384 + j * 64 + 64] = GL0 + gl_off + j * 64 + np.arange(64)
        cm[base + 704: base + 1024] = RT0 + rt_off + np.arange(320)
    b = 24 * 128
    cm[b:b + 320] = GL0 + 640 + np.arange(320)
    cm[b + 320:b + 640] = RT0 + 960 + np.arange(320)
    b = 29 * 128
    cm[b:b + 128] = RW0 + 1152 + np.arange(128)
    cm[b + 128:b + 256] = RW0 + 1280 + np.arange(128)
    cm[b + 256:b + 288] = RW0 + 1408 + np.arange(32)
    cm[b + 288:b + 304] = GL0 + 960 + np.arange(16)
    for gi, off in enumerate((0, 320)):
        b = (32 + 3 * gi) * 128
        for r in range(704, 1024):
            i = r - 704
            h, d = divmod(i, 64)
            cm[b + (r - 640)] = RT0 + off + h * 64 + ((d + 32) % 64)
    return cm


def _rows_layout(v, nblk):
    return np.ascontiguousarray(v.reshape(nblk, 128).T)


VEC = {}


def _vec_layout():
    off = 0
    for name, n in (("mu", 12), ("w0", 3), ("a0", 3), ("kkv", 3), ("ka", 3), ("rk", 3), ("nw", 8), ("nb", 8),
                    ("bgkb", 3), ("ln1g", 8), ("ln1b", 8), ("ln2g", 8), ("ln2b", 8), ("cw0", 44), ("cw1", 44),
                    ("cw2", 44), ("cb", 44), ("rscale", 8), ("kscale", 8), ("ldret", 8)):
        VEC[name] = (off, n)
        off += n
    return off


NV = _vec_layout()
NM = 4 * 384


def _prep_layer_params(inp, l):
    f = np.float32
    vec = np.zeros((128, NV), f)

    def put(name, arr):
        o, n = VEC[name]
        assert arr.shape == (128, n), (name, arr.shape)
        vec[:, o:o + n] = arr

    mu = inp["rwkv_mu"][l]
    mup = np.zeros(12 * 128, f)
    mup[0:384] = mu[0:384]
    mup[384:768] = mu[384:768]
    mup[768:1152] = mu[768:1152]
    mup[1152:1152 + 128] = mu[1152:1280]
    mup[1280:1408] = mu[1280:1408]
    mup[1408:1440] = mu[1408:1440]
    put("mu", _rows_layout(mup, 12))
    put("w0", _rows_layout(inp["rwkv_w0"][l], 3))
    put("a0", _rows_layout(inp["rwkv_a0"][l], 3))
    put("kkv", _rows_layout(inp["rwkv_kk"][l], 3))
    put("ka", _rows_layout(inp["rwkv_ka"][l], 3))
    put("rk", _rows_layout(inp["rwkv_rk"][l].reshape(-1), 3))
    nw = np.ones(1024, f)
    nb = np.zeros(1024, f)
    nw[0:384] = inp["rwkv_lnw"][l]
    nb[0:384] = inp["rwkv_lnb"][l]
    nw[384:704] = np.tile(inp["gla_norm_w"][l], 5)
    put("nw", _rows_layout(nw, 8))
    put("nb", _rows_layout(nb, 8))
    bb = np.zeros(384, f)
    for j in range(5):
        bb[j * 64:j * 64 + 32] = inp["gla_bgk_b"][l][j * 32:(j + 1) * 32]
    put("bgkb", _rows_layout(bb, 3))
    put("ln1g", _rows_layout(inp["ln1_g"][l], 8))
    put("ln1b", _rows_layout(inp["ln1_b"][l], 8))
    put("ln2g", _rows_layout(inp["ln2_g"][l], 8))
    put("ln2b", _rows_layout(inp["ln2_b"][l], 8))
    cw = inp["ffn_conv_w"][l]
    put("cw0", _rows_layout(cw[0], 44))
    put("cw1", _rows_layout(cw[1], 44))
    put("cw2", _rows_layout(cw[2], 44))
    put("cb", _rows_layout(inp["ffn_conv_b"][l], 44))
    rs = np.ones(1024, f)
    rs[384:704] = 32.0 ** -0.5
    ks = np.ones(1024, f)
    ks[704:1024] = 64.0 ** -0.5
    put("rscale", _rows_layout(rs, 8))
    put("kscale", _rows_layout(ks, 8))
    ldr = np.zeros(1024, np.float64)
    for h in range(5):
        ldr[704 + h * 64:704 + (h + 1) * 64] = math.log(1.0 - 2.0 ** (-5.0 - h))
    put("ldret", _rows_layout(ldr.astype(f), 8))
    mats = np.zeros((128, NM), f)
    mats[0:64, 0:384] = inp["rwkv_bw"][l]
    mats[64:128, 0:384] = inp["rwkv_ba"][l]
    mats[:, 384:768] = inp["rwkv_bg"][l][0:128]
    mats[0:32, 768:1152] = inp["rwkv_bg"][l][128:160]
    bgk = inp["gla_bgk"][l]
    for j in range(5):
        mats[32:48, 1152 + j * 64:1152 + j * 64 + 32] = bgk[:, j * 32:(j + 1) * 32]
    return vec, mats


def _prep_weights(inp):
    cm = _colmap()
    npieces = NPIECE_IN + 4 + NJ + 8
    W = np.zeros((NL, npieces, 128, SLOT), np.float32)
    for l in range(NL):
        win = inp["w_in"][l]
        wp = np.zeros((1024, NBLK * 128), np.float32)
        ok = cm >= 0
        wp[:, ok] = win[:, cm[ok]]
        p = 0
        for i in range(NPIECE_IN):
            blkc = wp[:, i * 256:(i + 1) * 256].reshape(8, 128, 256).transpose(1, 0, 2)
            W[l, p, :, :2048] = blkc.reshape(128, 2048)
            p += 1
        wo = inp["w_out"][l]
        for i in range(4):
            blkc = wo[:, i * 256:(i + 1) * 256].reshape(8, 128, 256).transpose(1, 0, 2)
            W[l, p, :, :2048] = blkc.reshape(128, 2048)
            p += 1
        up = inp["ffn_up"][l]
        for j in range(NJ):
            ab = np.concatenate([up[:, j * 128:(j + 1) * 128], up[:, DFF + j * 128:DFF + (j + 1) * 128]], axis=1)
            blkc = ab.reshape(8, 128, 256).transpose(1, 0, 2)
            W[l, p, :, :2048] = blkc.reshape(128, 2048)
            p += 1
        dn = inp["ffn_down"][l]
        for cb in range(8):
            blkc = dn[:, cb * 128:(cb + 1) * 128].reshape(NJ, 128, 128).transpose(1, 0, 2)
            W[l, p, :, :NJ * 128] = blkc.reshape(128, NJ * 128)
            p += 1
        assert p == npieces
    return W


def _consts():
    ident = np.eye(128, dtype=np.float32)
    sI = np.arange(128)[:, None]
    tI = np.arange(128)[None, :]
    bones = np.zeros((128, 128), np.float32)
    bones[0:64, 0:64] = 1
    bones[64:128, 64:128] = 1
    cst = np.zeros((128, 672), np.float32)
    cst[:, 0:128] = ident
    cst[:, 128:256] = bones
    cst[:, 256:384] = (tI >= sI)
    cst[:, 384:512] = (tI > sI)
    cst[:, 512:640] = (tI < sI)
    tok = np.zeros((128, 32), np.float32)
    tok[:, 0:6] = 1.0 / 64
    tok[:, 16:22] = 64e-5
    tok[:, 22:32] = 1e-6
    cst[:, 640:672] = tok
    return cst


def _rope_tables():
    ntok = SEQ + NSS * LS
    pos = np.concatenate([np.arange(SEQ), np.tile(PAST + np.arange(LS), NSS)]).astype(np.float32)
    inv = (1.0 / (10000.0 ** np.linspace(0.0, 1.0, 32, dtype=np.float32))).astype(np.float32)
    ang = pos[None, :] * inv[:, None]
    cos = np.cos(ang).astype(np.float32)
    sin = np.sin(ang).astype(np.float32)
    C = np.ones((384, ntok), np.float32)
    S = np.zeros((384, ntok), np.float32)
    for h in range(5):
        r0 = 64 + h * 64
        C[r0:r0 + 32] = cos
        C[r0 + 32:r0 + 64] = cos
        S[r0:r0 + 32] = -sin
        S[r0 + 32:r0 + 64] = sin
    C = C.reshape(3, 128, ntok).transpose(1, 0, 2)
    S = S.reshape(3, 128, ntok).transpose(1, 0, 2)
    return np.ascontiguousarray(C), np.ascontiguousarray(S)


CFG_FULL = dict(tiles=[("p", i) for i in range(SEQ // TT)] + [("s", 0)], nl=NL)


def build(cfg):
    nc = bass.Bass("TRN2", target_bir_lowering=False)
    S = Sched(nc)
    NTOK = SEQ + NSS * LS
    npieces = NPIECE_IN + 4 + NJ + 8
    nl = cfg["nl"]
    tiles = cfg["tiles"]

    def din(name, shape, dt=F32):
        return nc.dram_tensor(name, list(shape), dt, kind="ExternalInput").ap()

    def dout(name, shape):
        return nc.dram_tensor(name, list(shape), F32, kind="ExternalOutput").ap()

    xT_d = din("xT", [128, 8, NTOK])
    wts_d = din("wts", [NL, npieces, 128, SLOT])
    vec_d = din("vecs", [NL, 128, NV])
    mat_d = din("mats", [NL, 128, NM])
    cst_d = din("cst", [128, 672])
    cos_d = din("ropec", [128, 3, NTOK])
    sin_d = din("ropes", [128, 3, NTOK])
    sst_d = din("sst", [NL, NSS, 128, 512])
    ssh_d = din("ssh", [NL, 128, 12 * NSS])
    scv_d = din("scv", [NL, 128, 44 * NSS * 2])
    yT_d = dout("yT", [128, 8, NTOK])
    sto_d = dout("sto", [NL, 1 + NSS, 128, 512])
    sho_d = dout("sho", [NL, 128, 12 * (1 + NSS)])
    cvo_d = dout("cvo", [NL, 128, 44 * (1 + NSS) * 2])
    dbg_d = dout("dbg", [128, 8192]) if cfg.get("dbg") else None
    dbg_state = {"col": 0, "items": []}

    def dbg(name, ap, bufs, np_=128):
        if dbg_d is None:
            return
        n = 1
        for d_ in ap.shape[1:]:
            n *= d_
        c0 = dbg_state["col"]
        if c0 + n > 8192:
            return
        dbg_state["col"] = c0 + n
        dbg_state["items"].append((name, c0, n, np_, tuple(ap.shape)))
        flat = dbg_d[0:np_, c0:c0 + n]
        if len(ap.shape) == 3:
            flat = flat.rearrange("p (a b) -> p a b", b=ap.shape[2])
        S.dma(E("dma_start", out=flat, in_=ap), reads=bufs, chan="dbg")
    cfg["_dbg_items"] = dbg_state["items"]

    def sb(name, shape, dt=F32):
        return nc.alloc_sbuf_tensor("sb_" + name, list(shape), dt)

    TM = TT
    x32 = sb("x32", [128, 8, TM]); b_x32 = Buf("x32")
    xTb = sb("xTb", [128, 8, TM], BF16); b_xTb = Buf("xTb")
    un1 = sb("un1", [128, 12 * (TM + NSS)]); b_PRW = Buf("PRW")
    PRW = un1[:].rearrange("p (a b) -> p a b", b=TM + NSS)
    Rg = sb("Rg", [128, 8, TM]); b_R = Buf("R")
    Kg = sb("Kg", [128, 8, TM]); b_K = Buf("K")
    Vg = sb("Vg", [128, 8, TM]); b_V = Buf("V")
    Gg = sb("Gg", [128, 8, TM], BF16); b_G = Buf("G")

    XS = sb("XS", [128, 3, TM]); b_XS = Buf("XS")
    ATt = sb("ATt", [128, 3, TM]); b_AT = Buf("AT")
    BTt = sb("BTt", [128, 3, TM]); b_BT = Buf("BT")
    bonus = sb("bonus", [128, 3, TM]); b_bonus = Buf("bonus")
    ynT = sb("ynT", [128, 8, TM]); b_ynT = Buf("ynT")
    SW = ynT; b_SW = b_ynT
    mixT = sb("mixT", [128, 8, TM], BF16); b_mixT = Buf("mixT")
    hT = un1[:].bitcast(BF16)[:, 0:NJ * TM].rearrange("p (a b) -> p a b", b=TM); b_hT = b_PRW
    WC = sb("WC", [128, 8, 16]); b_WC = Buf("WC")
    NTMP = 9
    tmp = [sb("tmp%d" % i, [128, TM]) for i in range(NTMP)]
    b_tmp = [Buf("tmp%d" % i) for i in range(NTMP)]
    tmpb = [sb("tmpb%d" % i, [128, TM], BF16) for i in range(4)]
    b_tmpb = [Buf("tmpb%d" % i) for i in range(4)]
    ubuf = [sb("ubuf%d" % i, [128, TM + 2 * NSS]) for i in range(4)]
    b_ubuf = [Buf("ubuf%d" % i) for i in range(4)]
    cosT = sb("cosT", [128, 3, TM]); sinT = sb("sinT", [128, 3, TM]); b_rope = Buf("rope")
    cst = sb("cst", [128, 672]); b_cst = Buf("cst")
    onesb = sb("onesb", [128, 128], BF16); b_ones = Buf("onesb")
    bonesr = sb("bonesr", [128, 128]); b_bones = Buf("bonesr")
    rmask = sb("rmask", [128, TM]); b_rmask = Buf("rmask")
    vecs = [sb("vecs%d" % i, [128, NV]) for i in range(2)]; b_vecs = [Buf("vecs%d" % i) for i in range(2)]
    vaux = [sb("vaux%d" % i, [128, 8]) for i in range(2)]; b_vaux = [Buf("vaux%d" % i) for i in range(2)]
    _m0 = sb("mats0", [128, NM], BF16); _bm0 = Buf("mats0")
    mats = [_m0, _m0]; b_mats = [_bm0, _bm0]
    hpw = [sb("hpw%d" % i, [128, 1024]) for i in range(2)]; b_hpw = [Buf("hpw%d" % i) for i in range(2)]
    mix32 = sb("mix32", [128, 8, 2]); b_mix32 = Buf("mix32")
    h32 = sb("h32", [128, NJ, 2]); b_h32 = Buf("h32")
    slots = [sb("slot%d" % i, [128, SLOT], BF16) for i in range(NSLOT)]
    b_slots = [Buf("slot%d" % i) for i in range(NSLOT)]
    Sst = [sb("Sst%d" % l, [128, 8, 64]) for l in range(NL)]; b_Sst = [Buf("Sst%d" % l) for l in range(NL)]
    _ss = sb("Ssm0", [128, 8, 64]); _bss = Buf("Ssm0")
    Ssm = [_ss, _ss]; b_Ssm = [_bss, _bss]
    _s0 = sb("Sld0", [128, 8, 64]); _bs0 = Buf("Sld0")
    Sld = [_s0, _s0]; b_Sld = [_bs0, _bs0]
    shc = sb("shc", [128, NL, 12]); b_shc = [Buf("shc%d" % l) for l in range(NL)]
    shst = sb("shst", [128, 12, NSS]); b_shst = Buf("shst")
    sho = sb("sho", [128, 12, NSS]); b_sho = Buf("sho")
    cvc = sb("cvc", [128, NL, 44, 2]); b_cvc = [Buf("cvc%d" % l) for l in range(NL)]
    cvst = sb("cvst", [128, 44, NSS, 2]); b_cvst = Buf("cvst")
    cvo = sb("cvo", [128, 44, NSS, 2]); b_cvo = Buf("cvo")
    CM = 128
    vtok = sb("vtok", [CM, 1024]); b_vtok = Buf("vtok")
    ktok = sb("ktok", [CM, 1024]); b_ktok = Buf("ktok")
    btok = sb("btok", [CM, 384]); b_btok = Buf("btok")
    ark = sb("ark", [CM, 16, CM]); b_ark = Buf("ark")
    aak = sb("aak", [CM, 6, CM]); b_aak = Buf("aak")
    arb = sb("arb", [CM, 6, CM]); b_arb = Buf("arb")
    _p = sb("Pm0", [CM, 6, CM]); _bp = Buf("Pm0")
    _q = sb("Qm0", [CM, 6, CM]); _bq = Buf("Qm0")
    Pm = [_p, _p]; b_Pm = [_bp, _bp]
    Qm = [_q, _q]; b_Qm = [_bq, _bq]
    Xm = sb("Xm", [CM, 6, CM]); b_Xm = Buf("Xm")
    rhs0 = sb("rhs0", [CM, 384]); b_rhs0 = Buf("rhs0")
    usb = sb("usb", [CM, 384]); b_usb = Buf("usb")
    ysb = sb("ysb", [CM, 1024]); b_ysb = Buf("ysb")
    ysq = ktok; b_ysq = b_ktok
    yst = sb("yst", [CM, 8, 16]); b_yst = Buf("yst")
    ps = nc.alloc_psum_tensor("ps", [128, 8, 512], F32)
    b_ps = [Buf("ps%d" % i) for i in range(8)]
    bank_ctr = [0]

    def nbank():
        b = bank_ctr[0] % 8
        bank_ctr[0] += 1
        return b

    R_ = lambda ap: ap.bitcast(F32R)

    def E(name, *a, **k):
        return lambda e: getattr(e, name)(*a, **k)

    S.dma(E("dma_start", out=cst[:], in_=cst_d), writes=[b_cst], chan="cst")
    ident = cst[:, 0:128]
    m_incl = cst[:, 256:384]
    m_su = cst[:, 384:512]
    m_sl = cst[:, 512:640]
    eye64 = cst[:, 0:128]
    tokc = cst[:, 640:672]
    S.dve(E("memset", onesb[:], 1.0), writes=[b_ones])
    S.dve(E("tensor_copy", out=R_(bonesr[:]), in_=cst[:, 128:256]), reads=[b_cst], writes=[b_bones])
    S.dve(E("memset", Sld[0][:], 0.0), writes=[b_Sld[0]])
    for l in range(NL):
        S.dve(E("tensor_copy", out=R_(Sst[l][:]), in_=Sld[0][:]), reads=[b_Sld[0]], writes=[b_Sst[l]])
        S.dve(E("memset", shc[:, l, :], 0.0), writes=[b_shc[l]])
        S.dve(E("memset", cvc[:, l, :, :], 0.0), writes=[b_cvc[l]])

    plan = []
    for (kind, ti) in tiles:
        for l in range(nl):
            for p in range(npieces):
                plan.append((l, p))
    ws = {"issued": 0, "used": 0, "pending_wr": {}}

    wbf_d = nc.dram_tensor("wbf", [NL, npieces, 128, SLOT], BF16).ap()
    b_wbf = {}

    def ws_issue():
        i = ws["issued"]
        if i >= len(plan):
            return
        l, p = plan[i]
        s = i % NSLOT
        n = 2048 if p < NPIECE_IN + 4 + NJ else NJ * 128
        if (l, p) not in b_wbf:
            S.dma(E("dma_start", out=slots[s][:, 0:n], in_=wts_d[l, p, :, 0:n]),
                  writes=[b_slots[s]], chan="slot%d" % s, q="poolq")
            if len(tiles) > 1:
                b_wbf[(l, p)] = Buf("wbf%d_%d" % (l, p))
                ws["pending_wr"][i] = (l, p, s, n)
        else:
            S.dma(E("dma_start", out=slots[s][:, 0:n], in_=wbf_d[l, p, :, 0:n]), reads=[b_wbf[(l, p)]],
                  writes=[b_slots[s]], chan="slot%d" % s)
        ws["issued"] += 1

    def ws_next(l, p):
        i = ws["used"]
        assert plan[i] == (l, p), (plan[i], l, p)
        while ws["issued"] < min(len(plan), i + NSLOT):
            ws_issue()
        if i in ws["pending_wr"]:
            l_, p_, s_, n_ = ws["pending_wr"].pop(i)
            S.dma(E("dma_start", out=wbf_d[l_, p_, :, 0:n_], in_=slots[s_][:, 0:n_]), reads=[b_slots[s_]],
                  writes=[b_wbf[(l_, p_)]], chan="wbfw")
        ws["used"] += 1
        s = i % NSLOT
        return slots[s], b_slots[s]

    for _ in range(NSLOT - 1):
        ws_issue()

    pv_ctr = [0]
    hp_ctr = [0]

    def hp_load(l, p, part):
        i = hp_ctr[0] % 2
        hp_ctr[0] += 1
        if part[0] == "sub":
            src = wts_d[l, p, :, 0:2048].rearrange("p (kc c) -> p kc c", c=256)[:, :, part[1] * 128:(part[1] + 1) * 128]
            dst = hpw[i][:, 0:1024].rearrange("p (kc c) -> p kc c", c=128)
        else:
            src = wts_d[l, p, :, part[1] * 1024:part[1] * 1024 + part[2] * 128]
            dst = hpw[i][:, 0:part[2] * 128]
        S.dma(E("dma_start", out=dst, in_=src), writes=[b_hpw[i]], chan="hp%d" % i)
        return hpw[i], b_hpw[i]

    def hp_mm8(l, p, sub, rhs_fn, brhs):
        w, bw = hp_load(l, p, ("sub", sub))
        bkh = nbank()
        for kc in range(8):
            S.pe(E("matmul", ps[:, bkh, 0:2], lhsT=w[:, kc * 128:(kc + 1) * 128], rhs=rhs_fn(kc), start=(kc == 0), stop=(kc == 7)),
                 reads=[bw, brhs], writes=[b_ps[bkh]])
        return bkh

    def stage(n):
        if cfg.get("stop") == n:
            raise _Stop()

    def _tile_loop():
        for (kind, ti) in tiles:
            if kind == "p":
                T = TT; nseq = 1; L = TT; C = 128; tok0 = ti * TT; nsteps = 6
            else:
                T = NSS * LS; nseq = NSS; L = LS; C = LS; tok0 = SEQ; nsteps = 1
            nch = T // C
            last_prompt = (kind == "p" and ti == SEQ // TT - 1)
            hpf = (kind == "p" and ti == 0 and cfg.get("hp", True))
            clo = 2 if hpf else 0
            S.dma(E("dma_start", out=x32[:, :, 0:T], in_=xT_d[:, :, tok0:tok0 + T]),
                  writes=[b_x32], chan="x32")
            S.dma(E("dma_start", out=xTb[:, :, 0:T], in_=xT_d[:, :, tok0:tok0 + T]),
                  writes=[b_xTb], chan="xTb", q="poolq")
            S.dma(E("dma_start", out=cosT[:, :, 0:T], in_=cos_d[:, :, tok0:tok0 + T]),
                  writes=[b_rope], chan="rope")
            S.dma(E("dma_start", out=sinT[:, :, 0:T], in_=sin_d[:, :, tok0:tok0 + T]),
                  writes=[b_rope], chan="rope")
            S.dve(E("memset", rmask[:, 0:T], 1.0), writes=[b_rmask])
            S.dve(E("memset", rmask[:, 0:T].rearrange("p (c t) -> p c t", t=C)[:, :, 0:1], 0.0),
                  writes=[b_rmask])

            for l in range(nl):
                pv = pv_ctr[0] % 2
                pv_ctr[0] += 1
                vec, bvec, mat, bmat, vx, bvx = vecs[pv], b_vecs[pv], mats[pv], b_mats[pv], vaux[pv], b_vaux[pv]
                S.dma(E("dma_start", out=vec[:], in_=vec_d[l]), writes=[bvec], chan="vec%d" % pv)
                S.dma(E("dma_start", out=mat[:], in_=mat_d[l]), writes=[bmat], chan="mat0",
                      q="poolq")

                def V_(name, b=0, n=1, vec=vec):
                    o, _ = VEC[name]
                    return vec[:, o + b:o + b + n]

                o_ka = VEC["ka"][0]; o_bg = VEC["bgkb"][0]
                S.dve(E("tensor_scalar", out=vx[:, 0:3], in0=vec[:, o_ka:o_ka + 3], scalar1=-1.0,
                                                                        scalar2=1.0, op0=ALU.mult, op1=ALU.add),
                      reads=[bvec], writes=[bvx])
                S.dve(E("tensor_scalar", out=vx[:, 3:6], in0=vec[:, o_bg:o_bg + 3], scalar1=-1.0,
                                                                        scalar2=None, op0=ALU.mult),
                      reads=[bvec], writes=[bvx])

                if kind == "s":
                    S.dma(E("dma_start", out=shst[:].rearrange("p a b -> p (a b)"), in_=ssh_d[l]),
                          writes=[b_shst], chan="shst")
                    S.dve(E("tensor_copy", out=PRW[:, :, 0:NSS * (LS + 1)].rearrange("p a (s t) -> p a s t", t=LS + 1)[:, :, :, 0],
                                                  in_=shst[:]), reads=[b_shst], writes=[b_PRW])
                else:
                    S.dve(E("tensor_copy", out=PRW[:, :, 0], in_=shc[:, l, :]), reads=[b_shc[l]], writes=[b_PRW])

                def prw_dst(pb, T=T, nseq=nseq, L=L):
                    return PRW[:, pb, 0:nseq * (L + 1)].rearrange("p (s t) -> p s t", t=L + 1)[:, :, 1:L + 1]

                def prw_prev(pb, T=T, nseq=nseq, L=L):
                    return PRW[:, pb, 0:nseq * (L + 1)].rearrange("p (s t) -> p s t", t=L + 1)[:, :, 0:L]

                def t3(ap, nseq=nseq, L=L):
                    return ap.rearrange("p (s t) -> p s t", t=L)

                evi = [0]
                for piece in range(NPIECE_IN):
                    slot, bslot = ws_next(l, piece)
                    for sub in range(2):
                        blk = piece * 2 + sub
                        bk = nbank()
                        for kc in range(8):
                            S.pe(E("matmul",
                                ps[:, bk, 0:T], lhsT=slot[:, kc * 256 + sub * 128: kc * 256 + sub * 128 + 128],
                                rhs=xTb[:, kc, 0:T], start=(kc == 0), stop=(kc == 7)),
                                reads=[bslot, b_xTb], writes=[b_ps[bk]])
                        g, gb = divmod(blk, 8)

                        def proj_evac(srcap, lo, hi, bkb, blk=blk, g=g, gb=gb):
                            if blk < 24 and gb < 3:
                                pb = g * 3 + gb
                            elif 29 <= blk < 32:
                                pb = 9 + blk - 29
                            else:
                                pb = None
                            if pb is not None:
                                if kind == "p":
                                    S.act(E("copy", out=PRW[:, pb, 1 + lo:1 + hi], in_=srcap), reads=[bkb], writes=[b_PRW])
                                else:
                                    S.act(E("copy", out=prw_dst(pb), in_=t3(srcap)), reads=[bkb], writes=[b_PRW])
                            elif blk < 24:
                                tile_, btile = ((Rg, b_R), (Kg, b_K), (Vg, b_V))[g]
                                S.dve(E("tensor_copy", out=R_(tile_[:, gb, lo:hi]), in_=srcap), reads=[bkb], writes=[btile])
                            elif blk < 29:
                                S.act(E("activation", out=Gg[:, 3 + blk - 24, lo:hi], in_=srcap, func=AF.Silu), reads=[bkb],
                                      writes=[b_G])
                            else:
                                S.dve(E("tensor_copy", out=SW[:, blk - 32, lo:hi], in_=srcap), reads=[bkb], writes=[b_SW])

                        proj_evac(ps[:, bk, clo:T], clo, T, b_ps[bk])
                        if hpf:
                            bkh = hp_mm8(l, piece, sub, lambda kc: x32[:, kc, 0:2], b_x32)
                            proj_evac(ps[:, bkh, 0:2], 0, 2, b_ps[bkh])

                stage(1)
                for pb in range(12):
                    g, gb = divmod(pb, 3)
                    tile_, btile = ((Rg, b_R), (Kg, b_K), (Vg, b_V), (XS, b_XS))[g]
                    tb = pb % 8
                    d3 = t3(tmp[tb][:, 0:T])
                    S.dve(E("tensor_tensor", out=d3, in0=prw_prev(pb), in1=prw_dst(pb), op=ALU.subtract),
                          reads=[b_PRW], writes=[b_tmp[tb]])
                    dst3 = t3(tile_[:, gb, 0:T])
                    if g <= 2:
                        dst3 = R_(dst3)
                    S.dve(E("scalar_tensor_tensor",
                        out=dst3, in0=d3, scalar=V_("mu", pb), in1=prw_dst(pb), op0=ALU.mult, op1=ALU.add),
                        reads=[b_PRW, b_tmp[tb], bvec], writes=[btile])
                if l == 0:
                    dbg("PRW0", PRW[:, 0, 0:80], [b_PRW]); dbg("Rsh", Rg[:, 0, 0:T], [b_R]); dbg("shst", shst[:, 0, :], [b_shst])
                lastcol = PRW[:, :, 0:nseq * (L + 1)].rearrange("p a (s t) -> p a s t", t=L + 1)[:, :, :, L]
                if kind == "s":
                    S.act(E("copy", out=sho[:], in_=lastcol), reads=[b_PRW], writes=[b_sho])
                    S.dma(E("dma_start", out=sho_d[l][:, 12:12 * (1 + NSS)], in_=sho[:].rearrange("p a b -> p (a b)")),
                          reads=[b_sho], chan="shout")
                else:
                    S.act(E("copy", out=shc[:, l, :], in_=lastcol[:, :, 0]), reads=[b_PRW],
                          writes=[b_shc[l]])
                    if last_prompt:
                        S.dma(E("dma_start", out=sho_d[l][:, 0:12], in_=shc[:, l, :]), reads=[b_shc[l]],
                              chan="shout")

                stage(2)
                for gi, (tile_, btile) in enumerate(((Rg, b_R), (Kg, b_K))):
                    a = tile_[:, 5:8, 0:T]
                    sw = SW[:, gi * 3:gi * 3 + 3, 0:T]
                    S.dve(E("tensor_tensor", out=R_(a), in0=a, in1=cosT[:, :, 0:T], op=ALU.mult),
                          reads=[b_rope, btile], writes=[btile])
                    S.dve(E("tensor_tensor", out=sw, in0=sw, in1=sinT[:, :, 0:T], op=ALU.mult),
                          reads=[b_rope, b_SW], writes=[b_SW])
                    S.dve(E("tensor_tensor", out=R_(a), in0=a, in1=sw, op=ALU.add),
                          reads=[b_SW, btile], writes=[btile])

                thx, bthx = tmpb[0], b_tmpb[0]
                sgx, bsgx = tmpb[1], b_tmpb[1]
                sgx2, bsgx2 = tmpb[2], b_tmpb[2]
                gkb, bgkb_ = tmpb[3], b_tmpb[3]
                S.act(E("activation", out=thx[0:64, 0:T], in_=XS[0:64, 0, 0:T], func=AF.Tanh), reads=[b_XS], writes=[bthx])
                S.act(E("copy", out=thx[64:128, 0:T], in_=XS[64:128, 0, 0:T]), reads=[b_XS], writes=[bthx])
                S.act(E("activation", out=sgx[:, 0:T], in_=XS[:, 1, 0:T], func=AF.Sigmoid), reads=[b_XS], writes=[bsgx])
                S.act(E("activation", out=sgx2[0:32, 0:T], in_=XS[0:32, 2, 0:T], func=AF.Sigmoid), reads=[b_XS], writes=[bsgx2])
                S.act(E("copy", out=gkb[32:64, 0:T], in_=XS[32:64, 2, 0:T]), reads=[b_XS], writes=[bgkb_])

                for b in range(8):
                    t_ld, t_lw, t_W, t_IW, t_a, t_kk, t_x, t_y, t_r = tmp
                    bl_ld, bl_lw, bl_W, bl_IW, bl_a, bl_kk, bl_x, bl_y, bl_r = b_tmp
                    if b < 3:
                        c0 = b * 128
                        bk = nbank()
                        S.pe(E("matmul", ps[:, bk, 0:T], lhsT=mat[0:64, c0:c0 + 128], rhs=thx[0:64, 0:T],
                                                              start=True, stop=True), reads=[bmat, bthx], writes=[b_ps[bk]])
                        S.act(E("activation", out=t_ld[:, 0:T], in_=ps[:, bk, 0:T], func=AF.Sigmoid,
                                                                 bias=V_("w0", b), scale=1.0),
                              reads=[b_ps[bk], bvec], writes=[bl_ld])
                        S.dve(E("tensor_scalar", out=t_ld[:, 0:T], in0=t_ld[:, 0:T], scalar1=-math.exp(-0.5), scalar2=None,
                                                        op0=ALU.mult), reads=[bl_ld], writes=[bl_ld])
                        bk = nbank()
                        S.pe(E("matmul", ps[:, bk, 0:T], lhsT=mat[64:128, c0:c0 + 128], rhs=thx[64:128, 0:T],
                                                              start=True, stop=True), reads=[bmat, bthx], writes=[b_ps[bk]])
                        S.act(E("activation", out=t_a[:, 0:T], in_=ps[:, bk, 0:T], func=AF.Sigmoid,
                                                                 bias=V_("a0", b), scale=1.0),
                              reads=[b_ps[bk], bvec], writes=[bl_a])
                        bk = nbank()
                        S.pe(E("matmul", ps[:, bk, 0:T], lhsT=mat[:, 384 + c0:384 + c0 + 128], rhs=sgx[:, 0:T],
                                                              start=True, stop=False), reads=[bmat, bsgx], writes=[b_ps[bk]])
                        S.pe(E("matmul", ps[:, bk, 0:T], lhsT=mat[0:32, 768 + c0:768 + c0 + 128],
                                                              rhs=sgx2[0:32, 0:T], start=False, stop=True),
                             reads=[bmat, bsgx2], writes=[b_ps[bk]])
                        S.act(E("copy", out=Gg[:, b, 0:T], in_=ps[:, bk, 0:T]), reads=[b_ps[bk]], writes=[b_G])
                        S.dve(E("tensor_scalar", out=t_kk[:, 0:T], in0=Kg[:, b, 0:T], scalar1=V_("kkv", b), scalar2=None,
                                                             op0=ALU.mult), reads=[b_K, bvec], writes=[bl_kk])
                        S.act(E("activation", out=R_(t_r[:, 0:T]), in_=t_kk[:, 0:T], func=AF.Square), reads=[bl_kk],
                              writes=[bl_r])
                        bk = nbank()
                        S.pe(E("matmul", ps[:, bk, 0:T], lhsT=R_(bonesr[:]), rhs=R_(t_r[:, 0:T]), start=True, stop=True),
                             reads=[b_bones, bl_r], writes=[b_ps[bk]])
                        S.act(E("activation", out=t_x[:, 0:T], in_=ps[:, bk, 0:T], func=AF.Sqrt), reads=[b_ps[bk]],
                              writes=[bl_x])
                        S.dve(E("tensor_scalar", out=t_x[:, 0:T], in0=t_x[:, 0:T], scalar1=1e-12, scalar2=None, op0=ALU.max),
                              reads=[bl_x], writes=[bl_x])
                        S.dve(E("reciprocal", out=t_x[:, 0:T], in_=t_x[:, 0:T]), reads=[bl_x], writes=[bl_x])
                        S.dve(E("tensor_tensor", out=t_kk[:, 0:T], in0=t_kk[:, 0:T], in1=t_x[:, 0:T], op=ALU.mult),
                              reads=[bl_x, bl_kk], writes=[bl_kk])
                        S.dve(E("tensor_scalar", out=t_x[:, 0:T], in0=t_a[:, 0:T], scalar1=V_("ka", b),
                                                             scalar2=vx[:, b:b + 1], op0=ALU.mult, op1=ALU.add),
                              reads=[bl_a, bvec, bvx], writes=[bl_x])
                        S.dve(E("tensor_tensor", out=R_(Kg[:, b, 0:T]), in0=Kg[:, b, 0:T], in1=t_x[:, 0:T], op=ALU.mult),
                              reads=[bl_x, b_K], writes=[b_K])
                        S.dve(E("scalar_tensor_tensor", out=R_(t_r[:, 0:T]), in0=Rg[:, b, 0:T], scalar=V_("rk", b),
                                                                    in1=Kg[:, b, 0:T], op0=ALU.mult, op1=ALU.mult),
                              reads=[b_R, b_K, bvec], writes=[bl_r])
                        bk = nbank()
                        S.pe(E("matmul", ps[:, bk, 0:T], lhsT=R_(bonesr[:]), rhs=R_(t_r[:, 0:T]), start=True, stop=True),
                             reads=[b_bones, bl_r], writes=[b_ps[bk]])
                        S.dve(E("tensor_tensor", out=bonus[:, b, 0:T], in0=ps[:, bk, 0:T], in1=Vg[:, b, 0:T],
                                                                    op=ALU.mult), reads=[b_ps[bk], b_V], writes=[b_bonus])
                    elif b < 6:
                        c0 = (b - 3) * 128
                        np_ = 64 if b == 5 else 128
                        bk = nbank()
                        S.pe(E("matmul", ps[0:np_, bk, 0:T], lhsT=mat[32:48, 1152 + c0:1152 + c0 + np_],
                                                                       rhs=gkb[32:48, 0:T], start=True, stop=True),
                             reads=[bmat, bgkb_], writes=[b_ps[bk]])
                        S.act(E("activation", out=t_ld[0:np_, 0:T], in_=ps[0:np_, bk, 0:T], func=AF.Exp,
                                                                          bias=vx[0:np_, b:b + 1], scale=-1.0),
                              reads=[b_ps[bk], bvx], writes=[bl_ld])
                        S.act(E("activation", out=t_ld[0:np_, 0:T], in_=t_ld[0:np_, 0:T], func=AF.Ln, bias=1.0,
                                                              scale=1.0), reads=[bl_ld], writes=[bl_ld])
                        S.dve(E("tensor_scalar", out=t_ld[0:np_, 0:T], in0=t_ld[0:np_, 0:T], scalar1=-1.0 / 16.0,
                                                                 scalar2=None, op0=ALU.mult), reads=[bl_ld], writes=[bl_ld])
                        if b == 5:
                            S.dve(E("tensor_scalar", out=t_ld[64:128, 0:T], in0=rmask[64:128, 0:T], scalar1=0.0,
                                                                 scalar2=V_("ldret", b)[64:128], op0=ALU.mult, op1=ALU.add),
                                  reads=[b_rmask, bvec], writes=[bl_ld])
                    else:
                        S.dve(E("tensor_scalar", out=t_ld[:, 0:T], in0=rmask[:, 0:T], scalar1=0.0,
                                                             scalar2=V_("ldret", b), op0=ALU.mult, op1=ALU.add),
                              reads=[b_rmask, bvec], writes=[bl_ld])
                    S.dve(E("tensor_tensor_scan", out=t_lw[:, 0:T], data0=rmask[:, 0:T], data1=t_ld[:, 0:T], initial=0.0,
                                                         op0=ALU.mult, op1=ALU.add), reads=[bl_ld, b_rmask], writes=[bl_lw])
                    S.act(E("activation", out=t_W[:, 0:T], in_=t_lw[:, 0:T], func=AF.Exp), reads=[bl_lw], writes=[bl_W])
                    S.act(E("activation", out=t_IW[:, 0:T], in_=t_lw[:, 0:T], func=AF.Exp, scale=-1.0), reads=[bl_lw],
                          writes=[bl_IW])
                    S.act(E("copy", out=WC[:, b, 0:nch],
                                                              in_=t_W[:, 0:T].rearrange("p (c t) -> p c t", t=C)[:, :, C - 1]),
                          reads=[bl_W], writes=[b_WC])
                    if b < 3:
                        S.dve(E("tensor_tensor", out=t_y[:, 0:T], in0=t_lw[:, 0:T], in1=t_ld[:, 0:T], op=ALU.subtract),
                              reads=[bl_lw, bl_ld], writes=[bl_y])
                        S.act(E("activation", out=t_y[:, 0:T], in_=t_y[:, 0:T], func=AF.Exp), reads=[bl_y], writes=[bl_y])
                        S.dve(E("scalar_tensor_tensor", out=R_(ATt[:, b, 0:T]), in0=t_kk[:, 0:T], scalar=-1.0,
                                                                    in1=t_y[:, 0:T], op0=ALU.mult, op1=ALU.mult),
                              reads=[bl_kk, bl_y], writes=[b_AT])
                        S.dve(E("tensor_tensor", out=t_y[:, 0:T], in0=t_kk[:, 0:T], in1=t_a[:, 0:T], op=ALU.mult),
                              reads=[bl_kk, bl_a], writes=[bl_y])
                        S.dve(E("tensor_tensor", out=R_(BTt[:, b, 0:T]), in0=t_y[:, 0:T], in1=t_IW[:, 0:T], op=ALU.mult),
                              reads=[bl_y, bl_IW], writes=[b_BT])
                    S.dve(E("scalar_tensor_tensor", out=R_(Rg[:, b, 0:T]), in0=Rg[:, b, 0:T], scalar=V_("rscale", b),
                                                                in1=t_W[:, 0:T], op0=ALU.mult, op1=ALU.mult),
                          reads=[b_R, bl_W, bvec], writes=[b_R])
                    S.dve(E("scalar_tensor_tensor", out=R_(Kg[:, b, 0:T]), in0=Kg[:, b, 0:T], scalar=V_("kscale", b),
                                                                in1=t_IW[:, 0:T], op0=ALU.mult, op1=ALU.mult),
                          reads=[b_K, bl_IW, bvec], writes=[b_K])

                stage(3)
                for ci in range(nch):
                    t0 = ci * C
                    cs = slice(t0, t0 + C)
                    cfg.setdefault("_marks", []).append((l, ci, len(S.ops)))
                    if kind == "p":
                        St, bSt = Sst[l], b_Sst[l]
                    else:
                        sp_ = ci % 2
                        St, bSt = Ssm[sp_], b_Ssm[sp_]
                        S.dma(E("dma_start", out=Sld[sp_][:].rearrange("p a b -> p (a b)"),
                                                                         in_=sst_d[l, ci]),
                              writes=[b_Sld[sp_]], chan="sld0")
                        S.act(E("copy", out=R_(St[:]), in_=Sld[sp_][:]), reads=[b_Sld[sp_]], writes=[bSt])
                    for (src, bsrc, dst, bdst, nb_) in ((Vg, b_V, vtok, b_vtok, 8), (Kg, b_K, ktok, b_ktok, 8),
                                                        (BTt, b_BT, btok, b_btok, 3)):
                        for half in range((nb_ + 3) // 4):
                            bk = nbank()
                            n_in = min(4, nb_ - half * 4)
                            for bl in range(n_in):
                                blk = half * 4 + bl
                                S.pe(E("transpose",
                                    ps[0:C, bk, bl * 128:(bl + 1) * 128], src[:, blk, cs], ident),
                                    reads=[bsrc, b_cst], writes=[b_ps[bk]])
                            w = n_in * 128
                            S.act(E("copy", out=R_(dst[0:C, half * 512:half * 512 + w]),
                                                                                   in_=ps[0:C, bk, 0:w]),
                                  reads=[b_ps[bk]], writes=[bdst])
                    stage(31)
                    def hp(h):
                        return slice((h % 2) * 64, (h % 2) * 64 + 64), h // 2

                    def pv3(bk, n):
                        return ps[0:C, bk, 0:n * C].rearrange("p (h c) -> p h c", c=C)

                    def bc(m, n):
                        return m[0:C, 0:C].unsqueeze(1).to_broadcast([C, n, C])

                    def par3(t, par, n):
                        return t[0:C, :, 0:C].rearrange("p (a two) c -> p a two c", two=2)[:, :, par, :]

                    hpb = min(8, 512 // C)
                    for (lt, blt, rt_, brt, dst, bdst, msk, nh) in ((Kg, b_K, Rg, b_R, ark, b_ark, m_incl, 16),
                                                                   (Kg, b_K, ATt, b_AT, aak, b_aak, m_su, 6),
                                                                   (BTt, b_BT, Rg, b_R, arb, b_arb, m_incl, 6),
                                                                   (BTt, b_BT, ATt, b_AT, Pm[0], b_Pm[0], m_su, 6),
                                                                   (ATt, b_AT, BTt, b_BT, Qm[0], b_Qm[0], m_sl, 6)):
                        npar = nh // 2
                        ngrp = (npar + hpb - 1) // hpb
                        banks = {}
                        for g_ in range(ngrp):
                            for par in range(2):
                                banks[(par, g_)] = nbank()
                        for h in range(nh):
                            rows, blk = hp(h)
                            hl = h // 2
                            bk = banks[(h % 2, hl // hpb)]
                            hi = hl % hpb
                            S.pe(E("matmul", ps[0:C, bk, hi * C:(hi + 1) * C], lhsT=R_(lt[rows, blk, cs]),
                                   rhs=R_(rt_[rows, blk, cs]), start=True, stop=True), reads=[blt, brt], writes=[b_ps[bk]])
                        for g_ in range(ngrp):
                            n_ = min(hpb, npar - g_ * hpb)
                            for par in range(2):
                                bk = banks[(par, g_)]
                                S.dve(E("tensor_tensor", out=R_(par3(dst, par, nh)[:, g_ * hpb:g_ * hpb + n_, :]), in0=pv3(bk, n_),
                                        in1=bc(msk, n_), op=ALU.mult), reads=[b_ps[bk], b_cst], writes=[bdst])
                    if ci == cfg.get("dbgci", 0) and l == 0:
                        dbg("AT0", ATt[:, 0, cs], [b_AT]); dbg("BT0", BTt[:, 0, cs], [b_BT])
                        dbg("K0", Kg[:, 0, cs], [b_K]); dbg("R0", Rg[:, 0, cs], [b_R]); dbg("V0", Vg[:, 0, cs], [b_V])
                        dbg("WC", WC[:, :, 0:4], [b_WC])
                        dbg("P0", Pm[0][0:C, :, 0:C], [b_Pm[0]], C); dbg("Q0", Qm[0][0:C, :, 0:C], [b_Qm[0]], C)
                        dbg("aak", aak[0:C, :, 0:C], [b_aak], C); dbg("arb", arb[0:C, :, 0:C], [b_arb], C)
                        dbg("btok", btok[0:C, 0:128], [b_btok], C)
                    stage(32)
                    S.dve(E("tensor_tensor", out=R_(Xm[0:C, :, 0:C]), in0=Pm[0][0:C, :, 0:C], in1=bc(eye64, 6), op=ALU.add),
                          reads=[b_Pm[0], b_cst], writes=[b_Xm])
                    ng_ = 1 if 6 * C <= 512 else 2
                    hg_ = 6 // ng_

                    def inv_mm(lhs_t, blhs, rhs_t, brhs):
                        bks = []
                        for g_ in range(ng_):
                            bk = nbank()
                            bks.append(bk)
                            for hi in range(hg_):
                                h = g_ * hg_ + hi
                                S.pe(E("matmul", ps[0:C, bk, hi * C:(hi + 1) * C], lhsT=R_(lhs_t[0:C, h, 0:C]),
                                       rhs=R_(rhs_t[0:C, h, 0:C]), start=True, stop=True), reads=[blhs, brhs], writes=[b_ps[bk]])
                        return bks

                    def gsl(t, g_):
                        return t[0:C, g_ * hg_:(g_ + 1) * hg_, 0:C]

                    def par2(t, par, n):
                        return t[0:C, 0:n * 64].rearrange("p (a two v) -> p a two v", two=2, v=64)[:, :, par, :]

                    def pv64(bk, n):
                        return ps[0:C, bk, 0:n * 64].rearrange("p (h v) -> p h v", v=64)

                    def f_y_state():
                        bkp = (nbank(), nbank())
                        for h in range(16):
                            rows, blk = hp(h)
                            bk = bkp[h % 2]
                            hl = h // 2
                            S.pe(E("matmul", ps[0:C, bk, hl * 64:(hl + 1) * 64], lhsT=R_(Rg[rows, blk, cs]), rhs=R_(St[rows, blk, :]),
                                   start=True, stop=True), reads=[b_R, bSt], writes=[b_ps[bk]])
                        for par in range(2):
                            S.act(E("copy", out=par2(ysb, par, 16), in_=pv64(bkp[par], 8)), reads=[b_ps[bkp[par]]], writes=[b_ysb])

                    def f_rhs_state():
                        bkp = (nbank(), nbank())
                        for h in range(6):
                            rows, blk = hp(h)
                            bk = bkp[h % 2]
                            hl = h // 2
                            S.pe(E("matmul", ps[0:C, bk, hl * 64:(hl + 1) * 64], lhsT=R_(ATt[rows, blk, cs]), rhs=R_(St[rows, blk, :]),
                                   start=True, stop=True), reads=[b_AT, bSt], writes=[b_ps[bk]])
                        for par in range(2):
                            S.act(E("copy", out=R_(par2(rhs0, par, 6)), in_=pv64(bkp[par], 3)), reads=[b_ps[bkp[par]]], writes=[b_rhs0])

                    def f_rhs_tok():
                        bk = nbank()
                        for h in range(6):
                            S.pe(E("matmul", ps[0:C, bk, h * 64:(h + 1) * 64], lhsT=R_(aak[0:C, h, 0:C]),
                                   rhs=R_(vtok[0:C, h * 64:(h + 1) * 64]), start=True, stop=True),
                                 reads=[b_aak, b_vtok], writes=[b_ps[bk]])
                        S.dve(E("tensor_tensor", out=R_(rhs0[0:C, :]), in0=rhs0[0:C, :], in1=ps[0:C, bk, 0:384], op=ALU.add),
                              reads=[b_ps[bk], b_rhs0], writes=[b_rhs0])

                    def f_y_tok(half):
                        bk = nbank()
                        for hl in range(8):
                            h = half * 8 + hl
                            S.pe(E("matmul", ps[0:C, bk, hl * 64:(hl + 1) * 64], lhsT=R_(ark[0:C, h, 0:C]),
                                   rhs=R_(vtok[0:C, h * 64:(h + 1) * 64]), start=True, stop=(h >= 6)),
                                 reads=[b_ark, b_vtok], writes=[b_ps[bk]])
                            if h < 6:
                                S.pe(E("matmul", ps[0:C, bk, hl * 64:(hl + 1) * 64], lhsT=R_(arb[0:C, h, 0:C]),
                                       rhs=R_(usb[0:C, h * 64:(h + 1) * 64]), start=False, stop=True),
                                     reads=[b_arb, b_usb], writes=[b_ps[bk]])
                        S.dve(E("tensor_tensor", out=ysb[0:C, half * 512:(half + 1) * 512], in0=ysb[0:C, half * 512:(half + 1) * 512],
                                in1=ps[0:C, bk, :], op=ALU.add), reads=[b_ps[bk], b_ysb], writes=[b_ysb])

                    fillers = [f_y_state, f_rhs_state, f_rhs_tok, lambda: f_y_tok(1)]
                    for st in range(nsteps):
                        bq = inv_mm(Pm[0], b_Pm[0], Qm[0], b_Qm[0])
                        if st < nsteps - 1:
                            bp = inv_mm(Qm[0], b_Qm[0], Pm[0], b_Pm[0])
                        for g_ in range(ng_):
                            S.act(E("copy", out=R_(gsl(Qm[0], g_)), in_=pv3(bq[g_], hg_)), reads=[b_ps[bq[g_]]], writes=[b_Qm[0]])
                        if st < nsteps - 1:
                            for g_ in range(ng_):
                                S.dve(E("tensor_copy", out=R_(gsl(Pm[0], g_)), in_=pv3(bp[g_], hg_)), reads=[b_ps[bp[g_]]],
                                      writes=[b_Pm[0]])
                        if fillers:
                            fillers.pop(0)()
                        bx = inv_mm(Qm[0], b_Qm[0], Xm, b_Xm)
                        for g_ in range(ng_):
                            S.dve(E("tensor_tensor", out=R_(gsl(Xm, g_)), in0=gsl(Xm, g_), in1=pv3(bx[g_], hg_), op=ALU.add),
                                  reads=[b_ps[bx[g_]], b_Xm], writes=[b_Xm])
                    while fillers:
                        fillers.pop(0)()
                    stage(33)
                    bk = nbank()
                    for h in range(6):
                        S.pe(E("matmul", ps[0:C, bk, h * 64:(h + 1) * 64], lhsT=R_(Xm[0:C, h, 0:C]),
                               rhs=R_(rhs0[0:C, h * 64:(h + 1) * 64]), start=True, stop=True),
                             reads=[b_Xm, b_rhs0], writes=[b_ps[bk]])
                    S.act(E("copy", out=R_(usb[0:C, :]), in_=ps[0:C, bk, 0:384]), reads=[b_ps[bk]], writes=[b_usb])
                    stage(34)
                    f_y_tok(0)
                    stage(35)
                    for half in range(2):
                        bk = nbank()
                        for bl in range(4):
                            blk = half * 4 + bl
                            S.pe(E("matmul", ps[:, bk, bl * 128:(bl + 1) * 128],
                                                                           lhsT=R_(ktok[0:C, blk * 128:(blk + 1) * 128]),
                                                                           rhs=R_(vtok[0:C, blk * 128:(blk + 1) * 128]),
                                                                           start=True, stop=(blk >= 3)),
                                 reads=[b_ktok, b_vtok], writes=[b_ps[bk]])
                            if blk < 3:
                                S.pe(E("matmul", ps[:, bk, bl * 128:(bl + 1) * 128],
                                                                               lhsT=R_(btok[0:C, blk * 128:(blk + 1) * 128]),
                                                                               rhs=R_(usb[0:C, blk * 128:(blk + 1) * 128]),
                                                                               start=False, stop=True),
                                     reads=[b_btok, b_usb], writes=[b_ps[bk]])
                        for hh in range(2):
                            rows = slice(hh * 64, hh * 64 + 64)
                            S.dve(E("tensor_tensor",
                                out=R_(St[rows, half * 4:half * 4 + 4, :]), in0=St[rows, half * 4:half * 4 + 4, :],
                                in1=ps[rows, bk, :].rearrange("p (a b) -> p a b", b=128)[:, :, hh * 64:hh * 64 + 64], op=ALU.add),
                                reads=[b_ps[bk], bSt], writes=[bSt])
                    S.dve(E("tensor_tensor", out=R_(St[:]), in0=St[:], in1=WC[:, :, ci:ci + 1].to_broadcast([128, 8, 64]),
                                                           op=ALU.mult), reads=[b_WC, bSt], writes=[bSt])
                    if kind == "s":
                        S.dma(E("dma_start", out=sto_d[l, 1 + ci], in_=St[:].rearrange("p a b -> p (a b)")),
                              reads=[bSt], chan="stout")
                    elif last_prompt and ci == nch - 1:
                        S.dma(E("dma_start", out=sto_d[l, 0], in_=St[:].rearrange("p a b -> p (a b)")),
                              reads=[bSt], chan="stout")
                    if ci == cfg.get("dbgci", 0) and l == 0:
                        dbg("ysb", ysb[0:C, 0:128], [b_ysb], C); dbg("St", St[:, 0, :], [bSt])
                    stage(36)
                    y3 = ysb[0:C, :].rearrange("p (h v) -> p h v", v=64)
                    q3 = ysq[0:C, :].rearrange("p (h v) -> p h v", v=64)
                    s1, s2, mean, tmpv, rstd = (yst[0:C, i, :] for i in range(5))
                    S.dve(E("tensor_tensor", out=R_(ysq[0:C, :]), in0=ysb[0:C, :], in1=ysb[0:C, :], op=ALU.mult), reads=[b_ysb],
                          writes=[b_ysq])
                    S.dve(E("tensor_reduce", out=s1, in_=y3, axis=AX.X, op=ALU.add), reads=[b_ysb], writes=[b_yst])
                    S.dve(E("tensor_reduce", out=s2, in_=q3, axis=AX.X, op=ALU.add), reads=[b_ysq], writes=[b_yst])
                    S.dve(E("tensor_tensor", out=mean, in0=s1, in1=tokc[0:C, 0:16], op=ALU.mult),
                          reads=[b_yst, b_cst], writes=[b_yst])
                    S.dve(E("tensor_tensor", out=tmpv, in0=mean, in1=mean, op=ALU.mult), reads=[b_yst],
                          writes=[b_yst])
                    S.dve(E("scalar_tensor_tensor", out=tmpv, in0=s2, scalar=1.0 / 64, in1=tmpv, op0=ALU.mult,
                                                                             op1=ALU.subtract), reads=[b_yst], writes=[b_yst])
                    S.dve(E("tensor_tensor", out=tmpv, in0=tmpv, in1=tokc[0:C, 16:32], op=ALU.add),
                          reads=[b_yst, b_cst], writes=[b_yst])
                    S.act(E("activation", out=tmpv, in_=tmpv, func=AF.Sqrt), reads=[b_yst], writes=[b_yst])
                    S.dve(E("reciprocal", out=rstd, in_=tmpv), reads=[b_yst], writes=[b_yst])
                    S.dve(E("tensor_tensor", out=y3, in0=y3, in1=mean.unsqueeze(2).to_broadcast([C, 16, 64]),
                                                                      op=ALU.subtract), reads=[b_yst, b_ysb], writes=[b_ysb])
                    S.dve(E("tensor_tensor", out=y3, in0=y3, in1=rstd.unsqueeze(2).to_broadcast([C, 16, 64]),
                                                                      op=ALU.mult), reads=[b_yst, b_ysb], writes=[b_ysb])
                    bpb = min(8, 512 // C)
                    for g_ in range(8 // bpb):
                        bk = nbank()
                        for bi in range(bpb):
                            blk = g_ * bpb + bi
                            S.pe(E("transpose", ps[:, bk, bi * C:(bi + 1) * C], ysb[0:C, blk * 128:(blk + 1) * 128],
                                   cst[0:C, 0:C]), reads=[b_ysb, b_cst], writes=[b_ps[bk]])
                        S.act(E("copy", out=ynT[:, g_ * bpb:(g_ + 1) * bpb, cs],
                                in_=ps[:, bk, 0:bpb * C].rearrange("p (a c) -> p a c", c=C)), reads=[b_ps[bk]], writes=[b_ynT])

                stage(4)
                for b in range(8):
                    S.dve(E("tensor_scalar", out=ynT[:, b, 0:T], in0=ynT[:, b, 0:T], scalar1=V_("nw", b), scalar2=V_("nb", b),
                                                         op0=ALU.mult, op1=ALU.add), reads=[b_ynT, bvec], writes=[b_ynT])
                    if b < 3:
                        S.dve(E("tensor_tensor", out=ynT[:, b, 0:T], in0=ynT[:, b, 0:T], in1=bonus[:, b, 0:T], op=ALU.add),
                              reads=[b_ynT, b_bonus], writes=[b_ynT])
                    S.dve(E("tensor_tensor", out=mixT[:, b, 0:T], in0=ynT[:, b, 0:T], in1=Gg[:, b, 0:T], op=ALU.mult),
                          reads=[b_ynT, b_G], writes=[b_mixT])
                    if hpf:
                        S.dve(E("tensor_tensor", out=mix32[:, b, :], in0=ynT[:, b, 0:2], in1=Gg[:, b, 0:2], op=ALU.mult),
                              reads=[b_ynT, b_G], writes=[b_mix32])

                def ln_epilogue(gname, bname):
                    r16, br16 = hT[:, 0:8, :], b_hT
                    S.act(E("copy", out=hT[:, 0:8, 0:T], in_=x32[:, :, 0:T]), reads=[b_x32], writes=[b_hT])
                    S.act(E("activation", out=hT[:, 8:16, 0:T], in_=x32[:, :, 0:T], func=AF.Square), reads=[b_x32], writes=[b_hT])
                    bk1 = nbank(); bk2 = nbank()
                    for kc in range(8):
                        S.pe(E("matmul", ps[:, bk1, 0:T], lhsT=onesb[:], rhs=hT[:, kc, 0:T], start=(kc == 0), stop=(kc == 7)),
                             reads=[b_ones, b_hT], writes=[b_ps[bk1]])
                    for kc in range(8):
                        S.pe(E("matmul", ps[:, bk2, 0:T], lhsT=onesb[:], rhs=hT[:, 8 + kc, 0:T], start=(kc == 0),
                                                       stop=(kc == 7)), reads=[b_ones, b_hT], writes=[b_ps[bk2]])
                    mean, var = tmp[0], tmp[1]
                    S.act(E("mul", out=mean[:, 0:T], in_=ps[:, bk1, 0:T], mul=1.0 / D), reads=[b_ps[bk1]], writes=[b_tmp[0]])
                    S.dve(E("tensor_tensor", out=var[:, 0:T], in0=mean[:, 0:T], in1=mean[:, 0:T], op=ALU.mult), reads=[b_tmp[0]],
                          writes=[b_tmp[1]])
                    S.dve(E("scalar_tensor_tensor", out=var[:, 0:T], in0=ps[:, bk2, 0:T], scalar=1.0 / D, in1=var[:, 0:T],
                                                           op0=ALU.mult, op1=ALU.subtract), reads=[b_ps[bk2], b_tmp[1]], writes=[b_tmp[1]])
                    S.dve(E("tensor_scalar", out=var[:, 0:T], in0=var[:, 0:T], scalar1=1e-5, scalar2=None, op0=ALU.add),
                          reads=[b_tmp[1]], writes=[b_tmp[1]])
                    S.act(E("activation", out=var[:, 0:T], in_=var[:, 0:T], func=AF.Sqrt), reads=[b_tmp[1]], writes=[b_tmp[1]])
                    S.dve(E("reciprocal", out=var[:, 0:T], in_=var[:, 0:T]), reads=[b_tmp[1]], writes=[b_tmp[1]])
                    S.dve(E("tensor_tensor", out=x32[:, :, 0:T], in0=x32[:, :, 0:T],
                                                    in1=mean[:, 0:T].unsqueeze(1).to_broadcast([128, 8, T]), op=ALU.subtract),
                          reads=[b_tmp[0], b_x32], writes=[b_x32])
                    S.dve(E("tensor_tensor", out=x32[:, :, 0:T], in0=x32[:, :, 0:T],
                                                    in1=var[:, 0:T].unsqueeze(1).to_broadcast([128, 8, T]), op=ALU.mult),
                          reads=[b_tmp[1], b_x32], writes=[b_x32])
                    for b in range(8):
                        S.dve(E("tensor_scalar", out=x32[:, b, 0:T], in0=x32[:, b, 0:T], scalar1=V_(gname, b),
                                                             scalar2=V_(bname, b), op0=ALU.mult, op1=ALU.add),
                              reads=[b_x32, bvec], writes=[b_x32])
                    S.act(E("copy", out=xTb[:, :, 0:T], in_=x32[:, :, 0:T]), reads=[b_x32], writes=[b_xTb])

                stage(5)
                for piece in range(4):
                    slot, bslot = ws_next(l, NPIECE_IN + piece)
                    for sub in range(2):
                        cb = piece * 2 + sub
                        bk = nbank()
                        for kc in range(8):
                            S.pe(E("matmul",
                                ps[:, bk, 0:T], lhsT=slot[:, kc * 256 + sub * 128: kc * 256 + sub * 128 + 128],
                                rhs=mixT[:, kc, 0:T], start=(kc == 0), stop=(kc == 7)), reads=[bslot, b_mixT], writes=[b_ps[bk]])
                        if hpf:
                            bkh = hp_mm8(l, NPIECE_IN + piece, sub, lambda kc: mix32[:, kc, 0:2], b_mix32)
                            S.dve(E("scalar_tensor_tensor", out=x32[:, cb, 0:2], in0=x32[:, cb, 0:2], scalar=ALPHA,
                                    in1=ps[:, bkh, 0:2], op0=ALU.mult, op1=ALU.add), reads=[b_ps[bkh], b_x32], writes=[b_x32])
                        S.dve(E("scalar_tensor_tensor", out=x32[:, cb, clo:T], in0=x32[:, cb, clo:T], scalar=ALPHA,
                                                                             in1=ps[:, bk, clo:T], op0=ALU.mult, op1=ALU.add),
                              reads=[b_ps[bk], b_x32], writes=[b_x32])
                ln_epilogue("ln1g", "ln1b")

                stage(6)
                if kind == "s":
                    S.dma(E("dma_start", out=cvst[:].rearrange("p a b c -> p (a b c)"), in_=scv_d[l]), writes=[b_cvst],
                          chan="cvst")
                ffn_defer = [None]
                for j in range(NJ):
                    slot, bslot = ws_next(l, NPIECE_IN + 4 + j)
                    cv = []
                    for ab in range(2):
                        cbk = j + NJ * ab
                        bk = nbank()
                        for kc in range(8):
                            S.pe(E("matmul",
                                ps[:, bk, 0:T], lhsT=slot[:, kc * 256 + ab * 128: kc * 256 + ab * 128 + 128],
                                rhs=xTb[:, kc, 0:T], start=(kc == 0), stop=(kc == 7)), reads=[bslot, b_xTb], writes=[b_ps[bk]])
                        ub, bub = ubuf[ab + 2 * (j % 2)], b_ubuf[ab + 2 * (j % 2)]
                        u3 = ub[:, 0:nseq * (L + 2)].rearrange("p (s t) -> p s t", t=L + 2)
                        if kind == "s":
                            S.dve(E("tensor_copy", out=u3[:, :, 0:2], in_=cvst[:, cbk, :, :]), reads=[b_cvst],
                                  writes=[bub])
                        else:
                            S.dve(E("tensor_copy", out=u3[:, 0, 0:2], in_=cvc[:, l, cbk, :]),
                                  reads=[b_cvc[l]], writes=[bub])
                        if hpf:
                            bkh = hp_mm8(l, NPIECE_IN + 4 + j, ab, lambda kc: x32[:, kc, 0:2], b_x32)
                            S.act(E("copy", out=u3[:, 0, 2:4], in_=ps[:, bkh, 0:2]), reads=[b_ps[bkh]], writes=[bub])
                            S.act(E("copy", out=u3[:, 0, 4:L + 2], in_=ps[:, bk, 2:T]), reads=[b_ps[bk]], writes=[bub])
                        else:
                            S.act(E("copy", out=u3[:, :, 2:L + 2], in_=t3(ps[:, bk, 0:T])), reads=[b_ps[bk]], writes=[bub])
                        if kind == "s":
                            S.act(E("copy", out=cvo[:, cbk, :, :], in_=u3[:, :, L:L + 2]), reads=[bub], writes=[b_cvo])
                        else:
                            S.act(E("copy", out=cvc[:, l, cbk, :], in_=u3[:, 0, L:L + 2]), reads=[bub],
                                  writes=[b_cvc[l]])
                        ct, bct = tmp[2 + ab + 2 * (j % 2)], b_tmp[2 + ab + 2 * (j % 2)]
                        c3 = t3(ct[:, 0:T])
                        S.pool(E("tensor_scalar", out=c3, in0=u3[:, :, 0:L], scalar1=V_("cw0", cbk),
                                                                               scalar2=V_("cb", cbk), op0=ALU.mult, op1=ALU.add),
                              reads=[bub, bvec], writes=[bct])
                        S.dve(E("scalar_tensor_tensor", out=c3, in0=u3[:, :, 1:L + 1], scalar=V_("cw1", cbk),
                                                                                      in1=c3, op0=ALU.mult, op1=ALU.add),
                              reads=[bub, bvec, bct], writes=[bct])
                        S.dve(E("scalar_tensor_tensor", out=c3, in0=u3[:, :, 2:L + 2], scalar=V_("cw2", cbk),
                                                                                      in1=c3, op0=ALU.mult, op1=ALU.add),
                              reads=[bub, bvec, bct], writes=[bct])
                        cv.append((ct, bct))
                    (ca, bca), (cb_, bcb) = cv

                    def gelu_h(j=j, ca=ca, bca=bca, cb_=cb_, bcb=bcb):
                        S.act(E("activation", out=ca[:, 0:T], in_=ca[:, 0:T], func=AF.Gelu_apprx_tanh), reads=[bca], writes=[bca])
                        S.dve(E("tensor_tensor", out=hT[:, j, 0:T], in0=ca[:, 0:T], in1=cb_[:, 0:T], op=ALU.mult),
                              reads=[bca, bcb], writes=[b_hT])
                        if hpf:
                            S.dve(E("tensor_tensor", out=h32[:, j, :], in0=ca[:, 0:2], in1=cb_[:, 0:2], op=ALU.mult),
                                  reads=[bca, bcb], writes=[b_h32])

                    if ffn_defer[0] is not None:
                        ffn_defer[0]()
                    ffn_defer[0] = gelu_h
                ffn_defer[0]()
                ffn_defer[0] = None
                if kind == "s":
                    S.dma(E("dma_start", out=cvo_d[l][:, 88:88 * (1 + NSS)], in_=cvo[:].rearrange("p a b c -> p (a b c)")),
                          reads=[b_cvo], chan="cvout")
                elif last_prompt:
                    S.dma(E("dma_start", out=cvo_d[l][:, 0:88], in_=cvc[:, l, :, :].rearrange("p a c -> p (a c)")),
                          reads=[b_cvc[l]], chan="cvout")
                for cb in range(8):
                    slot, bslot = ws_next(l, NPIECE_IN + 4 + NJ + cb)
                    bk = nbank()
                    for j in range(NJ):
                        S.pe(E("matmul", ps[:, bk, 0:T], lhsT=slot[:, j * 128:(j + 1) * 128], rhs=hT[:, j, 0:T],
                                                                       start=(j == 0), stop=(j == NJ - 1)),
                             reads=[bslot, b_hT], writes=[b_ps[bk]])
                    if hpf:
                        bkh = nbank()
                        for jh in range(3):
                            njj = min(8, NJ - jh * 8)
                            w, bw = hp_load(l, NPIECE_IN + 4 + NJ + cb, ("jh", jh, njj))
                            for jj in range(njj):
                                j = jh * 8 + jj
                                S.pe(E("matmul", ps[:, bkh, 0:2], lhsT=w[:, jj * 128:(jj + 1) * 128], rhs=h32[:, j, 0:2],
                                       start=(j == 0), stop=(j == NJ - 1)), reads=[bw, b_h32], writes=[b_ps[bkh]])
                        S.dve(E("scalar_tensor_tensor", out=x32[:, cb, 0:2], in0=x32[:, cb, 0:2], scalar=ALPHA,
                                in1=ps[:, bkh, 0:2], op0=ALU.mult, op1=ALU.add), reads=[b_ps[bkh], b_x32], writes=[b_x32])
                    S.dve(E("scalar_tensor_tensor", out=x32[:, cb, clo:T], in0=x32[:, cb, clo:T], scalar=ALPHA,
                                                                         in1=ps[:, bk, clo:T], op0=ALU.mult, op1=ALU.add),
                          reads=[b_ps[bk], b_x32], writes=[b_x32])
                stage(7)
                ln_epilogue("ln2g", "ln2b")

            S.dma(E("dma_start", out=yT_d[:, :, tok0:tok0 + T], in_=x32[:, :, 0:T]), reads=[b_x32], chan="yout")


    try:
        _tile_loop()
        assert ws["used"] == len(plan)
    except _Stop:
        pass
    S.run_block()
    return nc, S


_CACHE = {}


def _prep_inputs(inp, cfg):
    f = np.float32
    W = _prep_weights(inp)
    vm = [_prep_layer_params(inp, l) for l in range(NL)]
    vecs = np.stack([v for v, _ in vm])
    mats = np.stack([m for _, m in vm])
    cst = _consts()
    cosT, sinT = _rope_tables()
    in_maps = []
    for c in range(8):
        xp = inp["x_prompt"][c]
        xs = inp["x_sample"][c * NSS:(c + 1) * NSS].reshape(NSS * LS, D)
        x = np.concatenate([xp, xs], axis=0)
        xT = np.ascontiguousarray(x.T.reshape(8, 128, -1).transpose(1, 0, 2))
        sst = np.zeros((NL, NSS, 16, 64, 64), f)
        sl = slice(c * NSS, (c + 1) * NSS)
        sst[:, :, 0:6] = inp["state_rwkv"][:, sl].transpose(0, 1, 2, 4, 3)
        sst[:, :, 6:11, 0:32] = inp["state_gla"][:, sl]
        sst[:, :, 11:16] = inp["state_ret"][:, sl]
        sst = sst.reshape(NL, NSS, 8, 2, 64, 64).transpose(0, 1, 3, 4, 2, 5).reshape(NL, NSS, 128, 512)
        sh = inp["state_shift"][:, sl]
        shp = np.zeros((NL, NSS, 12 * 128), f)
        shp[:, :, 0:1152] = sh[:, :, 0:1152]
        shp[:, :, 1152:1408] = sh[:, :, 1152:1408]
        shp[:, :, 1408:1440] = sh[:, :, 1408:1440]
        ssh = shp.reshape(NL, NSS, 12, 128).transpose(0, 3, 2, 1).reshape(NL, 128, 12 * NSS)
        cvs = inp["state_conv"][:, sl]
        scv = cvs.reshape(NL, NSS, 2, 44, 128).transpose(0, 4, 3, 1, 2).reshape(NL, 128, 44 * NSS * 2)
        in_maps.append({"xT": xT, "wts": W, "vecs": vecs, "mats": mats, "cst": cst, "ropec": cosT, "ropes": sinT,
                        "sst": np.ascontiguousarray(sst), "ssh": np.ascontiguousarray(ssh), "scv": np.ascontiguousarray(scv)})
    return in_maps


def _assemble(results):
    f = np.float32
    y_p = np.zeros((8, SEQ, D), f); y_s = np.zeros((8 * NSS, LS, D), f)
    rw_p = np.zeros((NL, 8, 6, 64, 64), f); rw_s = np.zeros((NL, 8 * NSS, 6, 64, 64), f)
    sh_p = np.zeros((NL, 8, 1440), f); sh_s = np.zeros((NL, 8 * NSS, 1440), f)
    gl_p = np.zeros((NL, 8, 5, 32, 64), f); gl_s = np.zeros((NL, 8 * NSS, 5, 32, 64), f)
    rt_p = np.zeros((NL, 8, 5, 64, 64), f); rt_s = np.zeros((NL, 8 * NSS, 5, 64, 64), f)
    cv_p = np.zeros((NL, 8, 2, 2 * DFF), f); cv_s = np.zeros((NL, 8 * NSS, 2, 2 * DFF), f)
    for c, r in enumerate(results):
        yT = r["yT"]
        y = yT.transpose(1, 0, 2).reshape(D, -1).T
        y_p[c] = y[0:SEQ]
        y_s[c * NSS:(c + 1) * NSS] = y[SEQ:].reshape(NSS, LS, D)
        st = r["sto"].reshape(NL, 1 + NSS, 2, 64, 8, 64).transpose(0, 1, 4, 2, 3, 5).reshape(NL, 1 + NSS, 16, 64, 64)
        rw = st[:, :, 0:6].transpose(0, 1, 2, 4, 3)
        gl = st[:, :, 6:11, 0:32]
        rt = st[:, :, 11:16]
        rw_p[:, c] = rw[:, 0]; gl_p[:, c] = gl[:, 0]; rt_p[:, c] = rt[:, 0]
        sl = slice(c * NSS, (c + 1) * NSS)
        rw_s[:, sl] = rw[:, 1:]; gl_s[:, sl] = gl[:, 1:]; rt_s[:, sl] = rt[:, 1:]
        sho = r["sho"].reshape(NL, 128, 12 * (1 + NSS))
        p = sho[:, :, 0:12].transpose(0, 2, 1).reshape(NL, 12 * 128)
        s = sho[:, :, 12:].reshape(NL, 128, 12, NSS).transpose(0, 3, 2, 1).reshape(NL, NSS, 12 * 128)
        sh_p[:, c] = p[:, 0:1440]
        sh_s[:, sl] = s[:, :, 0:1440]
        cvo = r["cvo"].reshape(NL, 128, 44 * (1 + NSS) * 2)
        p = cvo[:, :, 0:88].reshape(NL, 128, 44, 2).transpose(0, 3, 2, 1).reshape(NL, 2, 2 * DFF)
        s = cvo[:, :, 88:].reshape(NL, 128, 44, NSS, 2).transpose(0, 3, 4, 2, 1).reshape(NL, NSS, 2, 2 * DFF)
        cv_p[:, c] = p
        cv_s[:, sl] = s
    return (y_p, y_s, rw_p, sh_p, gl_p, rt_p, cv_p, rw_s, sh_s, gl_s, rt_s, cv_s)


def kernel(**inputs):
    inp = {k: np.asarray(v) for k, v in inputs.items()}
    cfg = CFG_FULL
    if "nc" not in _CACHE:
        _CACHE["nc"] = build(cfg)[0]
    nc = _CACHE["nc"]
    in_maps = _prep_inputs(inp, cfg)
    res = run_bass_kernel_spmd(nc, in_maps, core_ids=list(range(8)))
    return _assemble(res.results)
```

```python
import bisect
import math
from contextlib import ExitStack

import numpy as np
import concourse.bass as bass
import concourse.mybir as mybir
from concourse.bass_utils import run_bass_kernel_spmd

F32 = mybir.dt.float32
F32R = mybir.dt.float32r
BF16 = mybir.dt.bfloat16
AF = mybir.ActivationFunctionType
ALU = mybir.AluOpType
AX = mybir.AxisListType

NL = 4
D = 1024
SEQ = 2048
NSS = 16
LS = 4
TT = 256
PAST = 16384
DFF = 2816
NJ = 22
ALPHA = (2 * NL) ** 0.25
NBLK = 38
NPIECE_IN = 19
SLOT = 2816
NSLOT = 4
RW0, GL0, RT0 = 0, 1440, 2416


class _Stop(Exception):
    pass


class Buf:
    __slots__ = ("name", "last_write", "reads")

    def __init__(self, name):
        self.name = name
        self.last_write = None
        self.reads = []


class Op:
    __slots__ = ("eng", "fn", "deps", "idx", "needs_inc", "cnt", "chan", "dma_cnt")

    def __init__(self, eng, fn, deps, idx, chan=None):
        self.eng = eng
        self.fn = fn
        self.deps = deps
        self.idx = idx
        self.needs_inc = False
        self.cnt = None
        self.chan = chan
        self.dma_cnt = None


COMPUTE = ("pe", "act", "dve", "pool")
DMAQ = ("sp", "actq", "poolq")
STREAM = {"pe": "pe", "act": "act", "dve": "dve", "pool": "pool", "sp": "sp", "actq": "act", "poolq": "pool"}
SAME_SYNC = {"pe": False, "act": True, "dve": True, "pool": True}


class Sched:
    def __init__(self, nc):
        self.nc = nc
        self.ops = []

    def op(self, eng, fn, reads=(), writes=(), chan=None):
        idx = len(self.ops)
        deps = set()
        for b in reads:
            if b.last_write is not None:
                deps.add(b.last_write)
        for b in writes:
            if b.last_write is not None:
                deps.add(b.last_write)
            deps.update(b.reads)
        o = Op(eng, fn, deps, idx, chan)
        self.ops.append(o)
        for b in reads:
            b.reads.append(idx)
        for b in writes:
            b.last_write = idx
            b.reads = []
        return o

    def pe(self, fn, reads=(), writes=()):
        return self.op("pe", fn, reads, writes)

    def act(self, fn, reads=(), writes=()):
        return self.op("act", fn, reads, writes)

    def dve(self, fn, reads=(), writes=()):
        return self.op("dve", fn, reads, writes)

    def pool(self, fn, reads=(), writes=()):
        return self.op("pool", fn, reads, writes)

    def dma(self, fn, reads=(), writes=(), chan=None, q="sp"):
        return self.op(q, fn, reads, writes, chan=chan)

    def _skip(self, p, so):
        sp = STREAM[p.eng]
        return sp == so and p.eng in COMPUTE and not SAME_SYNC[p.eng]

    def run_block(self):
        nc = self.nc
        ops = self.ops
        for o in ops:
            so = STREAM[o.eng]
            for d in o.deps:
                p = ops[d]
                if p.eng in DMAQ or self._skip(p, so):
                    continue
                p.needs_inc = True
        cnt = {e: 0 for e in COMPUTE}
        chan_cnt = {}
        chan_hist = {}
        for o in ops:
            if o.eng in DMAQ:
                c = chan_cnt.get(o.chan, 0) + 16
                chan_cnt[o.chan] = c
                o.dma_cnt = c
                chan_hist.setdefault(o.chan, []).append((o.idx, c))
            elif o.needs_inc:
                cnt[o.eng] += 1
                o.cnt = cnt[o.eng]
        chan_idx = {c: [i for i, _ in h] for c, h in chan_hist.items()}
        streams = {s: [] for s in ("pe", "act", "dve", "pool", "sp")}
        waited = {s: {} for s in streams}
        for o in ops:
            so = STREAM[o.eng]
            need = {}
            for d in o.deps:
                p = ops[d]
                if p.eng in DMAQ:
                    h = chan_hist[p.chan]
                    k = bisect.bisect_left(chan_idx[p.chan], o.idx) - 1
                    key = ("chan", p.chan)
                    need[key] = max(need.get(key, 0), h[k][1])
                else:
                    if self._skip(p, so):
                        continue
                    key = ("eng", p.eng)
                    need[key] = max(need.get(key, 0), p.cnt)
            w = []
            for key, c in need.items():
                if waited[so].get(key, 0) >= c:
                    continue
                waited[so][key] = c
                w.append((key, c))
            streams[so].append((o, w))
        with ExitStack() as es:
            sems = {}
            for e in COMPUTE:
                sems[("eng", e)] = es.enter_context(nc.semaphore("s_" + e))
            for c in chan_cnt:
                sems[("chan", c)] = es.enter_context(nc.semaphore("c_" + str(c)))
            block = es.enter_context(nc.Block())

            def mk(sname):
                def body(eng):
                    for o, w in streams[sname]:
                        for key, c in w:
                            eng.wait_ge(sems[key], c)
                        ins = o.fn(eng)
                        if o.eng in DMAQ:
                            ins.then_inc(sems[("chan", o.chan)], 16)
                        elif o.needs_inc:
                            ins.then_inc(sems[("eng", o.eng)], 1)
                    if sname == "sp":
                        for c, v in chan_cnt.items():
                            eng.wait_ge(sems[("chan", c)], v)
                        for e in COMPUTE:
                            if cnt[e] > 0:
                                eng.wait_ge(sems[("eng", e)], cnt[e])
                return body

            block.tensor(mk("pe"))
            block.scalar(mk("act"))
            block.vector(mk("dve"))
            block.gpsimd(mk("pool"))
            block.sync(mk("sp"))


def _colmap():
    cm = -np.ones(NBLK * 128, np.int64)
    for g, (rw_off, gl_off, rt_off, pad) in enumerate(((0, 0, 0, True), (384, 160, 320, True), (768, 320, 640, False))):
        base = g * 1024
        cm[base:base + 384] = RW0 + rw_off + np.arange(384)
        for j in range(5):
            if pad:
                cm[base + 384 + j * 64: base + 384 + j * 64 + 32] = GL0 + gl_off + j * 32 + np.arange(32)
            else:
                cm[base + 384 + j * 64: base + 384 + j * 64 + 64] = GL0 + gl_off + j * 64 + np.arange(64)
        cm[base + 704: base + 1024] = RT0 + rt_off + np.arange(320)
    b = 24 * 128
    cm[b:b + 320] = GL0 + 640 + np.arange(320)
    cm[b + 320:b + 640] = RT0 + 960 + np.arange(320)
    b = 29 * 128
    cm[b:b + 128] = RW0 + 1152 + np.arange(128)
    cm[b + 128:b + 256] = RW0 + 1280 + np.arange(128)
    cm[b + 256:b + 288] = RW0 + 1408 + np.arange(32)
    cm[b + 288:b + 304] = GL0 + 960 + np.arange(16)
    for gi, off in enumerate((0, 320)):
        b = (32 + 3 * gi) * 128
        for r in range(704, 1024):
            i = r - 704
            h, d = divmod(i, 64)
            cm[b + (r - 640)] = RT0 + off + h * 64 + ((d + 32) % 64)
    return cm


def _rows_layout(v, nblk):
    return np.ascontiguousarray(v.reshape(nblk, 128).T)


VEC = {}


def _vec_layout():
    off = 0
    for name, n in (("mu", 12), ("w0", 3), ("a0", 3), ("kkv", 3), ("ka", 3), ("rk", 3), ("nw", 8), ("nb", 8),
                    ("bgkb", 3), ("ln1g", 8), ("ln1b", 8), ("ln2g", 8), ("ln2b", 8), ("cw0", 44), ("cw1", 44),
                    ("cw2", 44), ("cb", 44), ("rscale", 8), ("kscale", 8), ("ldret", 8)):
        VEC[name] = (off, n)
        off += n
    return off


NV = _vec_layout()
NM = 4 * 384


def _prep_layer_params(inp, l):
    f = np.float32
    vec = np.zeros((128, NV), f)

    def put(name, arr):
        o, n = VEC[name]
        assert arr.shape == (128, n), (name, arr.shape)
        vec[:, o:o + n] = arr

    mu = inp["rwkv_mu"][l]
    mup = np.zeros(12 * 128, f)
    mup[0:384] = mu[0:384]
    mup[384:768] = mu[384:768]
    mup[768:1152] = mu[768:1152]
    mup[1152:1152 + 128] = mu[1152:1280]
    mup[1280:1408] = mu[1280:1408]
    mup[1408:1440] = mu[1408:1440]
    put("mu", _rows_layout(mup, 12))
    put("w0", _rows_layout(inp["rwkv_w0"][l], 3))
    put("a0", _rows_layout(inp["rwkv_a0"][l], 3))
    put("kkv", _rows_layout(inp["rwkv_kk"][l], 3))
    put("ka", _rows_layout(inp["rwkv_ka"][l], 3))
    put("rk", _rows_layout(inp["rwkv_rk"][l].reshape(-1), 3))
    nw = np.ones(1024, f)
    nb = np.zeros(1024, f)
    nw[0:384] = inp["rwkv_lnw"][l]
    nb[0:384] = inp["rwkv_lnb"][l]
    nw[384:704] = np.tile(inp["gla_norm_w"][l], 5)
    put("nw", _rows_layout(nw, 8))
    put("nb", _rows_layout(nb, 8))
    bb = np.zeros(384, f)
    for j in range(5):
        bb[j * 64:j * 64 + 32] = inp["gla_bgk_b"][l][j * 32:(j + 1) * 32]
    put("bgkb", _rows_layout(bb, 3))
    put("ln1g", _rows_layout(inp["ln1_g"][l], 8))
    put("ln1b", _rows_layout(inp["ln1_b"][l], 8))
    put("ln2g", _rows_layout(inp["ln2_g"][l], 8))
    put("ln2b", _rows_layout(inp["ln2_b"][l], 8))
    cw = inp["ffn_conv_w"][l]
    put("cw0", _rows_layout(cw[0], 44))
    put("cw1", _rows_layout(cw[1], 44))
    put("cw2", _rows_layout(cw[2], 44))
    put("cb", _rows_layout(inp["ffn_conv_b"][l], 44))
    rs = np.ones(1024, f)
    rs[384:704] = 32.0 ** -0.5
    ks = np.ones(1024, f)
    ks[704:1024] = 64.0 ** -0.5
    put("rscale", _rows_layout(rs, 8))
    put("kscale", _rows_layout(ks, 8))
    ldr = np.zeros(1024, np.float64)
    for h in range(5):
        ldr[704 + h * 64:704 + (h + 1) * 64] = math.log(1.0 - 2.0 ** (-5.0 - h))
    put("ldret", _rows_layout(ldr.astype(f), 8))
    mats = np.zeros((128, NM), f)
    mats[0:64, 0:384] = inp["rwkv_bw"][l]
    mats[64:128, 0:384] = inp["rwkv_ba"][l]
    mats[:, 384:768] = inp["rwkv_bg"][l][0:128]
    mats[0:32, 768:1152] = inp["rwkv_bg"][l][128:160]
    bgk = inp["gla_bgk"][l]
    for j in range(5):
        mats[32:48, 1152 + j * 64:1152 + j * 64 + 32] = bgk[:, j * 32:(j + 1) * 32]
    return vec, mats


def _prep_weights(inp):
    cm = _colmap()
    npieces = NPIECE_IN + 4 + NJ + 8
    W = np.zeros((NL, npieces, 128, SLOT), np.float32)
    for l in range(NL):
        win = inp["w_in"][l]
        wp = np.zeros((1024, NBLK * 128), np.float32)
        ok = cm >= 0
        wp[:, ok] = win[:, cm[ok]]
        p = 0
        for i in range(NPIECE_IN):
            blkc = wp[:, i * 256:(i + 1) * 256].reshape(8, 128, 256).transpose(1, 0, 2)
            W[l, p, :, :2048] = blkc.reshape(128, 2048)
            p += 1
        wo = inp["w_out"][l]
        for i in range(4):
            blkc = wo[:, i * 256:(i + 1) * 256].reshape(8, 128, 256).transpose(1, 0, 2)
            W[l, p, :, :2048] = blkc.reshape(128, 2048)
            p += 1
        up = inp["ffn_up"][l]
        for j in range(NJ):
            ab = np.concatenate([up[:, j * 128:(j + 1) * 128], up[:, DFF + j * 128:DFF + (j + 1) * 128]], axis=1)
            blkc = ab.reshape(8, 128, 256).transpose(1, 0, 2)
            W[l, p, :, :2048] = blkc.reshape(128, 2048)
            p += 1
        dn = inp["ffn_down"][l]
        for cb in range(8):
            blkc = dn[:, cb * 128:(cb + 1) * 128].reshape(NJ, 128, 128).transpose(1, 0, 2)
            W[l, p, :, :NJ * 128] = blkc.reshape(128, NJ * 128)
            p += 1
        assert p == npieces
    return W


def _consts():
    ident = np.eye(128, dtype=np.float32)
    sI = np.arange(128)[:, None]
    tI = np.arange(128)[None, :]
    bones = np.zeros((128, 128), np.float32)
    bones[0:64, 0:64] = 1
    bones[64:128, 64:128] = 1
    cst = np.zeros((128, 672), np.float32)
    cst[:, 0:128] = ident
    cst[:, 128:256] = bones
    cst[:, 256:384] = (tI >= sI)
    cst[:, 384:512] = (tI > sI)
    cst[:, 512:640] = (tI < sI)
    tok = np.zeros((128, 32), np.float32)
    tok[:, 0:6] = 1.0 / 64
    tok[:, 16:22] = 64e-5
    tok[:, 22:32] = 1e-6
    cst[:, 640:672] = tok
    return cst


def _rope_tables():
    ntok = SEQ + NSS * LS
    pos = np.concatenate([np.arange(SEQ), np.tile(PAST + np.arange(LS), NSS)]).astype(np.float32)
    inv = (1.0 / (10000.0 ** np.linspace(0.0, 1.0, 32, dtype=np.float32))).astype(np.float32)
    ang = pos[None, :] * inv[:, None]
    cos = np.cos(ang).astype(np.float32)
    sin = np.sin(ang).astype(np.float32)
    C = np.ones((384, ntok), np.float32)
    S = np.zeros((384, ntok), np.float32)
    for h in range(5):
        r0 = 64 + h * 64
        C[r0:r0 + 32] = cos
        C[r0 + 32:r0 + 64] = cos
        S[r0:r0 + 32] = -sin
        S[r0 + 32:r0 + 64] = sin
    C = C.reshape(3, 128, ntok).transpose(1, 0, 2)
    S = S.reshape(3, 128, ntok).transpose(1, 0, 2)
    return np.ascontiguousarray(C), np.ascontiguousarray(S)


CFG_FULL = dict(tiles=[("p", i) for i in range(SEQ // TT)] + [("s", 0)], nl=NL)


def build(cfg):
    nc = bass.Bass("TRN2", target_bir_lowering=False)
    S = Sched(nc)
    NTOK = SEQ + NSS * LS
    npieces = NPIECE_IN + 4 + NJ + 8
    nl = cfg["nl"]
    tiles = cfg["tiles"]

    def din(name, shape, dt=F32):
        return nc.dram_tensor(name, list(shape), dt, kind="ExternalInput").ap()

    def dout(name, shape):
        return nc.dram_tensor(name, list(shape), F32, kind="ExternalOutput").ap()

    xT_d = din("xT", [128, 8, NTOK])
    wts_d = din("wts", [NL, npieces, 128, SLOT])
    vec_d = din("vecs", [NL, 128, NV])
    mat_d = din("mats", [NL, 128, NM])
    cst_d = din("cst", [128, 672])
    cos_d = din("ropec", [128, 3, NTOK])
    sin_d = din("ropes", [128, 3, NTOK])
    sst_d = din("sst", [NL, NSS, 128, 512])
    ssh_d = din("ssh", [NL, 128, 12 * NSS])
    scv_d = din("scv", [NL, 128, 44 * NSS * 2])
    yT_d = dout("yT", [128, 8, NTOK])
    sto_d = dout("sto", [NL, 1 + NSS, 128, 512])
    sho_d = dout("sho", [NL, 128, 12 * (1 + NSS)])
    cvo_d = dout("cvo", [NL, 128, 44 * (1 + NSS) * 2])
    dbg_d = dout("dbg", [128, 8192]) if cfg.get("dbg") else None
    dbg_state = {"col": 0, "items": []}

    def dbg(name, ap, bufs, np_=128):
        if dbg_d is None:
            return
        n = 1
        for d_ in ap.shape[1:]:
            n *= d_
        c0 = dbg_state["col"]
        if c0 + n > 8192:
            return
        dbg_state["col"] = c0 + n
        dbg_state["items"].append((name, c0, n, np_, tuple(ap.shape)))
        flat = dbg_d[0:np_, c0:c0 + n]
        if len(ap.shape) == 3:
            flat = flat.rearrange("p (a b) -> p a b", b=ap.shape[2])
        S.dma(E("dma_start", out=flat, in_=ap), reads=bufs, chan="dbg")
    cfg["_dbg_items"] = dbg_state["items"]

    def sb(name, shape, dt=F32):
        return nc.alloc_sbuf_tensor("sb_" + name, list(shape), dt)

    TM = TT
    x32 = sb("x32", [128, 8, TM]); b_x32 = Buf("x32")
    xTb = sb("xTb", [128, 8, TM], BF16); b_xTb = Buf("xTb")
    un1 = sb("un1", [128, 12 * (TM + NSS)]); b_PRW = Buf("PRW")
    PRW = un1[:].rearrange("p (a b) -> p a b", b=TM + NSS)
    Rg = sb("Rg", [128, 8, TM]); b_R = Buf("R")
    Kg = sb("Kg", [128, 8, TM]); b_K = Buf("K")
    Vg = sb("Vg", [128, 8, TM]); b_V = Buf("V")
    Gg = sb("Gg", [128, 8, TM], BF16); b_G = Buf("G")

    XS = sb("XS", [128, 3, TM]); b_XS = Buf("XS")
    ATt = sb("ATt", [128, 3, TM]); b_AT = Buf("AT")
    BTt = sb("BTt", [128, 3, TM]); b_BT = Buf("BT")
    bonus = sb("bonus", [128, 3, TM]); b_bonus = Buf("bonus")
    ynT = sb("ynT", [128, 8, TM]); b_ynT = Buf("ynT")
    SW = ynT; b_SW = b_ynT
    mixT = sb("mixT", [128, 8, TM], BF16); b_mixT = Buf("mixT")
    hT = un1[:].bitcast(BF16)[:, 0:NJ * TM].rearrange("p (a b) -> p a b", b=TM); b_hT = b_PRW
    WC = sb("WC", [128, 8, 16]); b_WC = Buf("WC")
    NTMP = 9
    tmp = [sb("tmp%d" % i, [128, TM]) for i in range(NTMP)]
    b_tmp = [Buf("tmp%d" % i) for i in range(NTMP)]
    tmpb = [sb("tmpb%d" % i, [128, TM], BF16) for i in range(4)]
    b_tmpb = [Buf("tmpb%d" % i) for i in range(4)]
    ubuf = [sb("ubuf%d" % i, [128, TM + 2 * NSS]) for i in range(4)]
    b_ubuf = [Buf("ubuf%d" % i) for i in range(4)]
    cosT = sb("cosT", [128, 3, TM]); sinT = sb("sinT", [128, 3, TM]); b_rope = Buf("rope")
    cst = sb("cst", [128, 672]); b_cst = Buf("cst")
    onesb = sb("onesb", [128, 128], BF16); b_ones = Buf("onesb")
    bonesr = sb("bonesr", [128, 128]); b_bones = Buf("bonesr")
    rmask = sb("rmask", [128, TM]); b_rmask = Buf("rmask")
    vecs = [sb("vecs%d" % i, [128, NV]) for i in range(2)]; b_vecs = [Buf("vecs%d" % i) for i in range(2)]
    vaux = [sb("vaux%d" % i, [128, 8]) for i in range(2)]; b_vaux = [Buf("vaux%d" % i) for i in range(2)]
    _m0 = sb("mats0", [128, NM], BF16); _bm0 = Buf("mats0")
    mats = [_m0, _m0]; b_mats = [_bm0, _bm0]
    hpw = [sb("hpw%d" % i, [128, 1024]) for i in range(2)]; b_hpw = [Buf("hpw%d" % i) for i in range(2)]
    mix32 = sb("mix32", [128, 8, 2]); b_mix32 = Buf("mix32")
    h32 = sb("h32", [128, NJ, 2]); b_h32 = Buf("h32")
    slots = [sb("slot%d" % i, [128, SLOT], BF16) for i in range(NSLOT)]
    b_slots = [Buf("slot%d" % i) for i in range(NSLOT)]
    Sst = [sb("Sst%d" % l, [128, 8, 64]) for l in range(NL)]; b_Sst = [Buf("Sst%d" % l) for l in range(NL)]
    Ssm = [sb("Ssm%d" % i, [128, 8, 64]) for i in range(2)]; b_Ssm = [Buf("Ssm%d" % i) for i in range(2)]
    _s0 = sb("Sld0", [128, 8, 64]); _bs0 = Buf("Sld0")
    Sld = [_s0, _s0]; b_Sld = [_bs0, _bs0]
    shc = sb("shc", [128, NL, 12]); b_shc = [Buf("shc%d" % l) for l in range(NL)]
    shst = sb("shst", [128, 12, NSS]); b_shst = Buf("shst")
    sho = sb("sho", [128, 12, NSS]); b_sho = Buf("sho")
    cvc = sb("cvc", [128, NL, 44, 2]); b_cvc = [Buf("cvc%d" % l) for l in range(NL)]
    cvst = sb("cvst", [128, 44, NSS, 2]); b_cvst = Buf("cvst")
    cvo = sb("cvo", [128, 44, NSS, 2]); b_cvo = Buf("cvo")
    CM = 128
    vtok = sb("vtok", [CM, 1024]); b_vtok = Buf("vtok")
    ktok = sb("ktok", [CM, 1024]); b_ktok = Buf("ktok")
    btok = sb("btok", [CM, 384]); b_btok = Buf("btok")
    ark = sb("ark", [CM, 16, CM]); b_ark = Buf("ark")
    aak = sb("aak", [CM, 6, CM]); b_aak = Buf("aak")
    arb = sb("arb", [CM, 6, CM]); b_arb = Buf("arb")
    _p = sb("Pm0", [CM, 6, CM]); _bp = Buf("Pm0")
    _q = sb("Qm0", [CM, 6, CM]); _bq = Buf("Qm0")
    Pm = [_p, _p]; b_Pm = [_bp, _bp]
    Qm = [_q, _q]; b_Qm = [_bq, _bq]
    Xm = sb("Xm", [CM, 6, CM]); b_Xm = Buf("Xm")
    rhs0 = sb("rhs0", [CM, 384]); b_rhs0 = Buf("rhs0")
    usb = sb("usb", [CM, 384]); b_usb = Buf("usb")
    ysb = sb("ysb", [CM, 1024]); b_ysb = Buf("ysb")
    ysq = ktok; b_ysq = b_ktok
    yst = sb("yst", [CM, 8, 16]); b_yst = Buf("yst")
    ps = nc.alloc_psum_tensor("ps", [128, 8, 512], F32)
    b_ps = [Buf("ps%d" % i) for i in range(8)]
    bank_ctr = [0]

    def nbank():
        b = bank_ctr[0] % 8
        bank_ctr[0] += 1
        return b

    R_ = lambda ap: ap.bitcast(F32R)

    def E(name, *a, **k):
        return lambda e: getattr(e, name)(*a, **k)

    S.dma(E("dma_start", out=cst[:], in_=cst_d), writes=[b_cst], chan="cst")
    ident = cst[:, 0:128]
    m_incl = cst[:, 256:384]
    m_su = cst[:, 384:512]
    m_sl = cst[:, 512:640]
    eye64 = cst[:, 0:128]
    tokc = cst[:, 640:672]
    S.dve(E("memset", onesb[:], 1.0), writes=[b_ones])
    S.dve(E("tensor_copy", out=R_(bonesr[:]), in_=cst[:, 128:256]), reads=[b_cst], writes=[b_bones])
    S.dve(E("memset", Sld[0][:], 0.0), writes=[b_Sld[0]])
    for l in range(NL):
        S.dve(E("tensor_copy", out=R_(Sst[l][:]), in_=Sld[0][:]), reads=[b_Sld[0]], writes=[b_Sst[l]])
        S.dve(E("memset", shc[:, l, :], 0.0), writes=[b_shc[l]])
        S.dve(E("memset", cvc[:, l, :, :], 0.0), writes=[b_cvc[l]])

    plan = []
    for (kind, ti) in tiles:
        for l in range(nl):
            for p in range(npieces):
                plan.append((l, p))
    ws = {"issued": 0, "used": 0, "pending_wr": {}}

    wbf_d = nc.dram_tensor("wbf", [NL, npieces, 128, SLOT], BF16).ap()
    b_wbf = {}

    def ws_issue():
        i = ws["issued"]
        if i >= len(plan):
            return
        l, p = plan[i]
        s = i % NSLOT
        n = 2048 if p < NPIECE_IN + 4 + NJ else NJ * 128
        if (l, p) not in b_wbf:
            S.dma(E("dma_start", out=slots[s][:, 0:n], in_=wts_d[l, p, :, 0:n]),
                  writes=[b_slots[s]], chan="slot%d" % s, q="poolq")
            if len(tiles) > 1:
                b_wbf[(l, p)] = Buf("wbf%d_%d" % (l, p))
                ws["pending_wr"][i] = (l, p, s, n)
        else:
            S.dma(E("dma_start", out=slots[s][:, 0:n], in_=wbf_d[l, p, :, 0:n]), reads=[b_wbf[(l, p)]],
                  writes=[b_slots[s]], chan="slot%d" % s)
        ws["issued"] += 1

    def ws_next(l, p):
        i = ws["used"]
        assert plan[i] == (l, p), (plan[i], l, p)
        while ws["issued"] < min(len(plan), i + NSLOT):
            ws_issue()
        if i in ws["pending_wr"]:
            l_, p_, s_, n_ = ws["pending_wr"].pop(i)
            S.dma(E("dma_start", out=wbf_d[l_, p_, :, 0:n_], in_=slots[s_][:, 0:n_]), reads=[b_slots[s_]],
                  writes=[b_wbf[(l_, p_)]], chan="wbfw")
        ws["used"] += 1
        s = i % NSLOT
        return slots[s], b_slots[s]

    for _ in range(NSLOT - 1):
        ws_issue()

    pv_ctr = [0]
    hp_ctr = [0]

    def hp_load(l, p, part):
        i = hp_ctr[0] % 2
        hp_ctr[0] += 1
        if part[0] == "sub":
            src = wts_d[l, p, :, 0:2048].rearrange("p (kc c) -> p kc c", c=256)[:, :, part[1] * 128:(part[1] + 1) * 128]
            dst = hpw[i][:, 0:1024].rearrange("p (kc c) -> p kc c", c=128)
        else:
            src = wts_d[l, p, :, part[1] * 1024:part[1] * 1024 + part[2] * 128]
            dst = hpw[i][:, 0:part[2] * 128]
        S.dma(E("dma_start", out=dst, in_=src), writes=[b_hpw[i]], chan="hp%d" % i)
        return hpw[i], b_hpw[i]

    def hp_mm8(l, p, sub, rhs_fn, brhs):
        w, bw = hp_load(l, p, ("sub", sub))
        bkh = nbank()
        for kc in range(8):
            S.pe(E("matmul", ps[:, bkh, 0:2], lhsT=w[:, kc * 128:(kc + 1) * 128], rhs=rhs_fn(kc), start=(kc == 0), stop=(kc == 7)),
                 reads=[bw, brhs], writes=[b_ps[bkh]])
        return bkh

    def stage(n):
        if cfg.get("stop") == n:
            raise _Stop()

    def _tile_loop():
        for (kind, ti) in tiles:
            if kind == "p":
                T = TT; nseq = 1; L = TT; C = 128; tok0 = ti * TT; nsteps = 6
            else:
                T = NSS * LS; nseq = NSS; L = LS; C = LS; tok0 = SEQ; nsteps = 1
            nch = T // C
            last_prompt = (kind == "p" and ti == SEQ // TT - 1)
            hpf = (kind == "p" and ti == 0 and cfg.get("hp", True))
            clo = 2 if hpf else 0
            S.dma(E("dma_start", out=x32[:, :, 0:T], in_=xT_d[:, :, tok0:tok0 + T]),
                  writes=[b_x32], chan="x32")
            S.dma(E("dma_start", out=xTb[:, :, 0:T], in_=xT_d[:, :, tok0:tok0 + T]),
                  writes=[b_xTb], chan="xTb", q="poolq")
            S.dma(E("dma_start", out=cosT[:, :, 0:T], in_=cos_d[:, :, tok0:tok0 + T]),
                  writes=[b_rope], chan="rope")
            S.dma(E("dma_start", out=sinT[:, :, 0:T], in_=sin_d[:, :, tok0:tok0 + T]),
                  writes=[b_rope], chan="rope")
            S.dve(E("memset", rmask[:, 0:T], 1.0), writes=[b_rmask])
            S.dve(E("memset", rmask[:, 0:T].rearrange("p (c t) -> p c t", t=C)[:, :, 0:1], 0.0),
                  writes=[b_rmask])

            for l in range(nl):
                pv = pv_ctr[0] % 2
                pv_ctr[0] += 1
                vec, bvec, mat, bmat, vx, bvx = vecs[pv], b_vecs[pv], mats[pv], b_mats[pv], vaux[pv], b_vaux[pv]
                S.dma(E("dma_start", out=vec[:], in_=vec_d[l]), writes=[bvec], chan="vec%d" % pv)
                S.dma(E("dma_start", out=mat[:], in_=mat_d[l]), writes=[bmat], chan="mat0",
                      q="poolq")

                def V_(name, b=0, n=1, vec=vec):
                    o, _ = VEC[name]
                    return vec[:, o + b:o + b + n]

                o_ka = VEC["ka"][0]; o_bg = VEC["bgkb"][0]
                S.dve(E("tensor_scalar", out=vx[:, 0:3], in0=vec[:, o_ka:o_ka + 3], scalar1=-1.0,
                                                                        scalar2=1.0, op0=ALU.mult, op1=ALU.add),
                      reads=[bvec], writes=[bvx])
                S.dve(E("tensor_scalar", out=vx[:, 3:6], in0=vec[:, o_bg:o_bg + 3], scalar1=-1.0,
                                                                        scalar2=None, op0=ALU.mult),
                      reads=[bvec], writes=[bvx])

                if kind == "s":
                    S.dma(E("dma_start", out=shst[:].rearrange("p a b -> p (a b)"), in_=ssh_d[l]),
                          writes=[b_shst], chan="shst")
                    S.dve(E("tensor_copy", out=PRW[:, :, 0:NSS * (LS + 1)].rearrange("p a (s t) -> p a s t", t=LS + 1)[:, :, :, 0],
                                                  in_=shst[:]), reads=[b_shst], writes=[b_PRW])
                else:
                    S.dve(E("tensor_copy", out=PRW[:, :, 0], in_=shc[:, l, :]), reads=[b_shc[l]], writes=[b_PRW])

                def prw_dst(pb, T=T, nseq=nseq, L=L):
                    return PRW[:, pb, 0:nseq * (L + 1)].rearrange("p (s t) -> p s t", t=L + 1)[:, :, 1:L + 1]

                def prw_prev(pb, T=T, nseq=nseq, L=L):
                    return PRW[:, pb, 0:nseq * (L + 1)].rearrange("p (s t) -> p s t", t=L + 1)[:, :, 0:L]

                def t3(ap, nseq=nseq, L=L):
                    return ap.rearrange("p (s t) -> p s t", t=L)

                evi = [0]
                for piece in range(NPIECE_IN):
                    slot, bslot = ws_next(l, piece)
                    for sub in range(2):
                        blk = piece * 2 + sub
                        bk = nbank()
                        for kc in range(8):
                            S.pe(E("matmul",
                                ps[:, bk, 0:T], lhsT=slot[:, kc * 256 + sub * 128: kc * 256 + sub * 128 + 128],
                                rhs=xTb[:, kc, 0:T], start=(kc == 0), stop=(kc == 7)),
                                reads=[bslot, b_xTb], writes=[b_ps[bk]])
                        g, gb = divmod(blk, 8)

                        def proj_evac(srcap, lo, hi, bkb, blk=blk, g=g, gb=gb):
                            if blk < 24 and gb < 3:
                                pb = g * 3 + gb
                            elif 29 <= blk < 32:
                                pb = 9 + blk - 29
                            else:
                                pb = None
                            if pb is not None:
                                if kind == "p":
                                    S.act(E("copy", out=PRW[:, pb, 1 + lo:1 + hi], in_=srcap), reads=[bkb], writes=[b_PRW])
                                else:
                                    S.act(E("copy", out=prw_dst(pb), in_=t3(srcap)), reads=[bkb], writes=[b_PRW])
                            elif blk < 24:
                                tile_, btile = ((Rg, b_R), (Kg, b_K), (Vg, b_V))[g]
                                S.dve(E("tensor_copy", out=R_(tile_[:, gb, lo:hi]), in_=srcap), reads=[bkb], writes=[btile])
                            elif blk < 29:
                                S.act(E("activation", out=Gg[:, 3 + blk - 24, lo:hi], in_=srcap, func=AF.Silu), reads=[bkb],
                                      writes=[b_G])
                            else:
                                S.dve(E("tensor_copy", out=SW[:, blk - 32, lo:hi], in_=srcap), reads=[bkb], writes=[b_SW])

                        proj_evac(ps[:, bk, clo:T], clo, T, b_ps[bk])
                        if hpf:
                            bkh = hp_mm8(l, piece, sub, lambda kc: x32[:, kc, 0:2], b_x32)
                            proj_evac(ps[:, bkh, 0:2], 0, 2, b_ps[bkh])

                stage(1)
                for pb in range(12):
                    g, gb = divmod(pb, 3)
                    tile_, btile = ((Rg, b_R), (Kg, b_K), (Vg, b_V), (XS, b_XS))[g]
                    tb = pb % 8
                    d3 = t3(tmp[tb][:, 0:T])
                    S.dve(E("tensor_tensor", out=d3, in0=prw_prev(pb), in1=prw_dst(pb), op=ALU.subtract),
                          reads=[b_PRW], writes=[b_tmp[tb]])
                    dst3 = t3(tile_[:, gb, 0:T])
                    if g <= 2:
                        dst3 = R_(dst3)
                    S.dve(E("scalar_tensor_tensor",
                        out=dst3, in0=d3, scalar=V_("mu", pb), in1=prw_dst(pb), op0=ALU.mult, op1=ALU.add),
                        reads=[b_PRW, b_tmp[tb], bvec], writes=[btile])
                if l == 0:
                    dbg("PRW0", PRW[:, 0, 0:80], [b_PRW]); dbg("Rsh", Rg[:, 0, 0:T], [b_R]); dbg("shst", shst[:, 0, :], [b_shst])
                lastcol = PRW[:, :, 0:nseq * (L + 1)].rearrange("p a (s t) -> p a s t", t=L + 1)[:, :, :, L]
                if kind == "s":
                    S.act(E("copy", out=sho[:], in_=lastcol), reads=[b_PRW], writes=[b_sho])
                    S.dma(E("dma_start", out=sho_d[l][:, 12:12 * (1 + NSS)], in_=sho[:].rearrange("p a b -> p (a b)")),
                          reads=[b_sho], chan="shout")
                else:
                    S.act(E("copy", out=shc[:, l, :], in_=lastcol[:, :, 0]), reads=[b_PRW],
                          writes=[b_shc[l]])
                    if last_prompt:
                        S.dma(E("dma_start", out=sho_d[l][:, 0:12], in_=shc[:, l, :]), reads=[b_shc[l]],
                              chan="shout")

                stage(2)
                for gi, (tile_, btile) in enumerate(((Rg, b_R), (Kg, b_K))):
                    a = tile_[:, 5:8, 0:T]
                    sw = SW[:, gi * 3:gi * 3 + 3, 0:T]
                    S.dve(E("tensor_tensor", out=R_(a), in0=a, in1=cosT[:, :, 0:T], op=ALU.mult),
                          reads=[b_rope, btile], writes=[btile])
                    S.dve(E("tensor_tensor", out=sw, in0=sw, in1=sinT[:, :, 0:T], op=ALU.mult),
                          reads=[b_rope, b_SW], writes=[b_SW])
                    S.dve(E("tensor_tensor", out=R_(a), in0=a, in1=sw, op=ALU.add),
                          reads=[b_SW, btile], writes=[btile])

                thx, bthx = tmpb[0], b_tmpb[0]
                sgx, bsgx = tmpb[1], b_tmpb[1]
                sgx2, bsgx2 = tmpb[2], b_tmpb[2]
                gkb, bgkb_ = tmpb[3], b_tmpb[3]
                S.act(E("activation", out=thx[0:64, 0:T], in_=XS[0:64, 0, 0:T], func=AF.Tanh), reads=[b_XS], writes=[bthx])
                S.act(E("copy", out=thx[64:128, 0:T], in_=XS[64:128, 0, 0:T]), reads=[b_XS], writes=[bthx])
                S.act(E("activation", out=sgx[:, 0:T], in_=XS[:, 1, 0:T], func=AF.Sigmoid), reads=[b_XS], writes=[bsgx])
                S.act(E("activation", out=sgx2[0:32, 0:T], in_=XS[0:32, 2, 0:T], func=AF.Sigmoid), reads=[b_XS], writes=[bsgx2])
                S.act(E("copy", out=gkb[32:64, 0:T], in_=XS[32:64, 2, 0:T]), reads=[b_XS], writes=[bgkb_])

                for b in range(8):
                    t_ld, t_lw, t_W, t_IW, t_a, t_kk, t_x, t_y, t_r = tmp
                    bl_ld, bl_lw, bl_W, bl_IW, bl_a, bl_kk, bl_x, bl_y, bl_r = b_tmp
                    if b < 3:
                        c0 = b * 128
                        bk = nbank()
                        S.pe(E("matmul", ps[:, bk, 0:T], lhsT=mat[0:64, c0:c0 + 128], rhs=thx[0:64, 0:T],
                                                              start=True, stop=True), reads=[bmat, bthx], writes=[b_ps[bk]])
                        S.act(E("activation", out=t_ld[:, 0:T], in_=ps[:, bk, 0:T], func=AF.Sigmoid,
                                                                 bias=V_("w0", b), scale=1.0),
                              reads=[b_ps[bk], bvec], writes=[bl_ld])
                        S.dve(E("tensor_scalar", out=t_ld[:, 0:T], in0=t_ld[:, 0:T], scalar1=-math.exp(-0.5), scalar2=None,
                                                        op0=ALU.mult), reads=[bl_ld], writes=[bl_ld])
                        bk = nbank()
                        S.pe(E("matmul", ps[:, bk, 0:T], lhsT=mat[64:128, c0:c0 + 128], rhs=thx[64:128, 0:T],
                                                              start=True, stop=True), reads=[bmat, bthx], writes=[b_ps[bk]])
                        S.act(E("activation", out=t_a[:, 0:T], in_=ps[:, bk, 0:T], func=AF.Sigmoid,
                                                                 bias=V_("a0", b), scale=1.0),
                              reads=[b_ps[bk], bvec], writes=[bl_a])
                        bk = nbank()
                        S.pe(E("matmul", ps[:, bk, 0:T], lhsT=mat[:, 384 + c0:384 + c0 + 128], rhs=sgx[:, 0:T],
                                                              start=True, stop=False), reads=[bmat, bsgx], writes=[b_ps[bk]])
                        S.pe(E("matmul", ps[:, bk, 0:T], lhsT=mat[0:32, 768 + c0:768 + c0 + 128],
                                                              rhs=sgx2[0:32, 0:T], start=False, stop=True),
                             reads=[bmat, bsgx2], writes=[b_ps[bk]])
                        S.act(E("copy", out=Gg[:, b, 0:T], in_=ps[:, bk, 0:T]), reads=[b_ps[bk]], writes=[b_G])
                        S.dve(E("tensor_scalar", out=t_kk[:, 0:T], in0=Kg[:, b, 0:T], scalar1=V_("kkv", b), scalar2=None,
                                                             op0=ALU.mult), reads=[b_K, bvec], writes=[bl_kk])
                        S.act(E("activation", out=R_(t_r[:, 0:T]), in_=t_kk[:, 0:T], func=AF.Square), reads=[bl_kk],
                              writes=[bl_r])
                        bk = nbank()
                        S.pe(E("matmul", ps[:, bk, 0:T], lhsT=R_(bonesr[:]), rhs=R_(t_r[:, 0:T]), start=True, stop=True),
                             reads=[b_bones, bl_r], writes=[b_ps[bk]])
                        S.act(E("activation", out=t_x[:, 0:T], in_=ps[:, bk, 0:T], func=AF.Sqrt), reads=[b_ps[bk]],
                              writes=[bl_x])
                        S.dve(E("tensor_scalar", out=t_x[:, 0:T], in0=t_x[:, 0:T], scalar1=1e-12, scalar2=None, op0=ALU.max),
                              reads=[bl_x], writes=[bl_x])
                        S.dve(E("reciprocal", out=t_x[:, 0:T], in_=t_x[:, 0:T]), reads=[bl_x], writes=[bl_x])
                        S.dve(E("tensor_tensor", out=t_kk[:, 0:T], in0=t_kk[:, 0:T], in1=t_x[:, 0:T], op=ALU.mult),
                              reads=[bl_x, bl_kk], writes=[bl_kk])
                        S.dve(E("tensor_scalar", out=t_x[:, 0:T], in0=t_a[:, 0:T], scalar1=V_("ka", b),
                                                             scalar2=vx[:, b:b + 1], op0=ALU.mult, op1=ALU.add),
                              reads=[bl_a, bvec, bvx], writes=[bl_x])
                        S.dve(E("tensor_tensor", out=R_(Kg[:, b, 0:T]), in0=Kg[:, b, 0:T], in1=t_x[:, 0:T], op=ALU.mult),
                              reads=[bl_x, b_K], writes=[b_K])
                        S.dve(E("scalar_tensor_tensor", out=R_(t_r[:, 0:T]), in0=Rg[:, b, 0:T], scalar=V_("rk", b),
                                                                    in1=Kg[:, b, 0:T], op0=ALU.mult, op1=ALU.mult),
                              reads=[b_R, b_K, bvec], writes=[bl_r])
                        bk = nbank()
                        S.pe(E("matmul", ps[:, bk, 0:T], lhsT=R_(bonesr[:]), rhs=R_(t_r[:, 0:T]), start=True, stop=True),
                             reads=[b_bones, bl_r], writes=[b_ps[bk]])
                        S.dve(E("tensor_tensor", out=bonus[:, b, 0:T], in0=ps[:, bk, 0:T], in1=Vg[:, b, 0:T],
                                                                    op=ALU.mult), reads=[b_ps[bk], b_V], writes=[b_bonus])
                    elif b < 6:
                        c0 = (b - 3) * 128
                        np_ = 64 if b == 5 else 128
                        bk = nbank()
                        S.pe(E("matmul", ps[0:np_, bk, 0:T], lhsT=mat[32:48, 1152 + c0:1152 + c0 + np_],
                                                                       rhs=gkb[32:48, 0:T], start=True, stop=True),
                             reads=[bmat, bgkb_], writes=[b_ps[bk]])
                        S.act(E("activation", out=t_ld[0:np_, 0:T], in_=ps[0:np_, bk, 0:T], func=AF.Exp,
                                                                          bias=vx[0:np_, b:b + 1], scale=-1.0),
                              reads=[b_ps[bk], bvx], writes=[bl_ld])
                        S.act(E("activation", out=t_ld[0:np_, 0:T], in_=t_ld[0:np_, 0:T], func=AF.Ln, bias=1.0,
                                                              scale=1.0), reads=[bl_ld], writes=[bl_ld])
                        S.dve(E("tensor_scalar", out=t_ld[0:np_, 0:T], in0=t_ld[0:np_, 0:T], scalar1=-1.0 / 16.0,
                                                                 scalar2=None, op0=ALU.mult), reads=[bl_ld], writes=[bl_ld])
                        if b == 5:
                            S.dve(E("tensor_scalar", out=t_ld[64:128, 0:T], in0=rmask[64:128, 0:T], scalar1=0.0,
                                                                 scalar2=V_("ldret", b)[64:128], op0=ALU.mult, op1=ALU.add),
                                  reads=[b_rmask, bvec], writes=[bl_ld])
                    else:
                        S.dve(E("tensor_scalar", out=t_ld[:, 0:T], in0=rmask[:, 0:T], scalar1=0.0,
                                                             scalar2=V_("ldret", b), op0=ALU.mult, op1=ALU.add),
                              reads=[b_rmask, bvec], writes=[bl_ld])
                    S.dve(E("tensor_tensor_scan", out=t_lw[:, 0:T], data0=rmask[:, 0:T], data1=t_ld[:, 0:T], initial=0.0,
                                                         op0=ALU.mult, op1=ALU.add), reads=[bl_ld, b_rmask], writes=[bl_lw])
                    S.act(E("activation", out=t_W[:, 0:T], in_=t_lw[:, 0:T], func=AF.Exp), reads=[bl_lw], writes=[bl_W])
                    S.act(E("activation", out=t_IW[:, 0:T], in_=t_lw[:, 0:T], func=AF.Exp, scale=-1.0), reads=[bl_lw],
                          writes=[bl_IW])
                    S.act(E("copy", out=WC[:, b, 0:nch],
                                                              in_=t_W[:, 0:T].rearrange("p (c t) -> p c t", t=C)[:, :, C - 1]),
                          reads=[bl_W], writes=[b_WC])
                    if b < 3:
                        S.dve(E("tensor_tensor", out=t_y[:, 0:T], in0=t_lw[:, 0:T], in1=t_ld[:, 0:T], op=ALU.subtract),
                              reads=[bl_lw, bl_ld], writes=[bl_y])
                        S.act(E("activation", out=t_y[:, 0:T], in_=t_y[:, 0:T], func=AF.Exp), reads=[bl_y], writes=[bl_y])
                        S.dve(E("scalar_tensor_tensor", out=R_(ATt[:, b, 0:T]), in0=t_kk[:, 0:T], scalar=-1.0,
                                                                    in1=t_y[:, 0:T], op0=ALU.mult, op1=ALU.mult),
                              reads=[bl_kk, bl_y], writes=[b_AT])
                        S.dve(E("tensor_tensor", out=t_y[:, 0:T], in0=t_kk[:, 0:T], in1=t_a[:, 0:T], op=ALU.mult),
                              reads=[bl_kk, bl_a], writes=[bl_y])
                        S.dve(E("tensor_tensor", out=R_(BTt[:, b, 0:T]), in0=t_y[:, 0:T], in1=t_IW[:, 0:T], op=ALU.mult),
                              reads=[bl_y, bl_IW], writes=[b_BT])
                    S.dve(E("scalar_tensor_tensor", out=R_(Rg[:, b, 0:T]), in0=Rg[:, b, 0:T], scalar=V_("rscale", b),
                                                                in1=t_W[:, 0:T], op0=ALU.mult, op1=ALU.mult),
                          reads=[b_R, bl_W, bvec], writes=[b_R])
                    S.dve(E("scalar_tensor_tensor", out=R_(Kg[:, b, 0:T]), in0=Kg[:, b, 0:T], scalar=V_("kscale", b),
                                                                in1=t_IW[:, 0:T], op0=ALU.mult, op1=ALU.mult),
                          reads=[b_K, bl_IW, bvec], writes=[b_K])

                stage(3)
                for ci in range(nch):
                    t0 = ci * C
                    cs = slice(t0, t0 + C)
                    cfg.setdefault("_marks", []).append((l, ci, len(S.ops)))
                    if kind == "p":
                        St, bSt = Sst[l], b_Sst[l]
                    else:
                        sp_ = ci % 2
                        St, bSt = Ssm[sp_], b_Ssm[sp_]
                        S.dma(E("dma_start", out=Sld[sp_][:].rearrange("p a b -> p (a b)"),
                                                                         in_=sst_d[l, ci]),
                              writes=[b_Sld[sp_]], chan="sld0")
                        S.act(E("copy", out=R_(St[:]), in_=Sld[sp_][:]), reads=[b_Sld[sp_]], writes=[bSt])
                    for (src, bsrc, dst, bdst, nb_) in ((Vg, b_V, vtok, b_vtok, 8), (Kg, b_K, ktok, b_ktok, 8),
                                                        (BTt, b_BT, btok, b_btok, 3)):
                        for half in range((nb_ + 3) // 4):
                            bk = nbank()
                            n_in = min(4, nb_ - half * 4)
                            for bl in range(n_in):
                                blk = half * 4 + bl
                                S.pe(E("transpose",
                                    ps[0:C, bk, bl * 128:(bl + 1) * 128], src[:, blk, cs], ident),
                                    reads=[bsrc, b_cst], writes=[b_ps[bk]])
                            w = n_in * 128
                            S.act(E("copy", out=R_(dst[0:C, half * 512:half * 512 + w]),
                                                                                   in_=ps[0:C, bk, 0:w]),
                                  reads=[b_ps[bk]], writes=[bdst])
                    stage(31)
                    def hp(h):
                        return slice((h % 2) * 64, (h % 2) * 64 + 64), h // 2

                    def pv3(bk, n):
                        return ps[0:C, bk, 0:n * C].rearrange("p (h c) -> p h c", c=C)

                    def bc(m, n):
                        return m[0:C, 0:C].unsqueeze(1).to_broadcast([C, n, C])

                    def par3(t, par, n):
                        return t[0:C, :, 0:C].rearrange("p (a two) c -> p a two c", two=2)[:, :, par, :]

                    hpb = min(8, 512 // C)
                    for (lt, blt, rt_, brt, dst, bdst, msk, nh) in ((Kg, b_K, Rg, b_R, ark, b_ark, m_incl, 16),
                                                                   (Kg, b_K, ATt, b_AT, aak, b_aak, m_su, 6),
                                                                   (BTt, b_BT, Rg, b_R, arb, b_arb, m_incl, 6),
                                                                   (BTt, b_BT, ATt, b_AT, Pm[0], b_Pm[0], m_su, 6),
                                                                   (ATt, b_AT, BTt, b_BT, Qm[0], b_Qm[0], m_sl, 6)):
                        npar = nh // 2
                        ngrp = (npar + hpb - 1) // hpb
                        banks = {}
                        for g_ in range(ngrp):
                            for par in range(2):
                                banks[(par, g_)] = nbank()
                        for h in range(nh):
                            rows, blk = hp(h)
                            hl = h // 2
                            bk = banks[(h % 2, hl // hpb)]
                            hi = hl % hpb
                            S.pe(E("matmul", ps[0:C, bk, hi * C:(hi + 1) * C], lhsT=R_(lt[rows, blk, cs]),
                                   rhs=R_(rt_[rows, blk, cs]), start=True, stop=True), reads=[blt, brt], writes=[b_ps[bk]])
                        for g_ in range(ngrp):
                            n_ = min(hpb, npar - g_ * hpb)
                            for par in range(2):
                                bk = banks[(par, g_)]
                                S.dve(E("tensor_tensor", out=R_(par3(dst, par, nh)[:, g_ * hpb:g_ * hpb + n_, :]), in0=pv3(bk, n_),
                                        in1=bc(msk, n_), op=ALU.mult), reads=[b_ps[bk], b_cst], writes=[bdst])
                    if ci == cfg.get("dbgci", 0) and l == 0:
                        dbg("AT0", ATt[:, 0, cs], [b_AT]); dbg("BT0", BTt[:, 0, cs], [b_BT])
                        dbg("K0", Kg[:, 0, cs], [b_K]); dbg("R0", Rg[:, 0, cs], [b_R]); dbg("V0", Vg[:, 0, cs], [b_V])
                        dbg("WC", WC[:, :, 0:4], [b_WC])
                        dbg("P0", Pm[0][0:C, :, 0:C], [b_Pm[0]], C); dbg("Q0", Qm[0][0:C, :, 0:C], [b_Qm[0]], C)
                        dbg("aak", aak[0:C, :, 0:C], [b_aak], C); dbg("arb", arb[0:C, :, 0:C], [b_arb], C)
                        dbg("btok", btok[0:C, 0:128], [b_btok], C)
                    stage(32)
                    S.dve(E("tensor_tensor", out=R_(Xm[0:C, :, 0:C]), in0=Pm[0][0:C, :, 0:C], in1=bc(eye64, 6), op=ALU.add),
                          reads=[b_Pm[0], b_cst], writes=[b_Xm])
                    ng_ = 1 if 6 * C <= 512 else 2
                    hg_ = 6 // ng_

                    def inv_mm(lhs_t, blhs, rhs_t, brhs):
                        bks = []
                        for g_ in range(ng_):
                            bk = nbank()
                            bks.append(bk)
                            for hi in range(hg_):
                                h = g_ * hg_ + hi
                                S.pe(E("matmul", ps[0:C, bk, hi * C:(hi + 1) * C], lhsT=R_(lhs_t[0:C, h, 0:C]),
                                       rhs=R_(rhs_t[0:C, h, 0:C]), start=True, stop=True), reads=[blhs, brhs], writes=[b_ps[bk]])
                        return bks

                    def gsl(t, g_):
                        return t[0:C, g_ * hg_:(g_ + 1) * hg_, 0:C]

                    def par2(t, par, n):
                        return t[0:C, 0:n * 64].rearrange("p (a two v) -> p a two v", two=2, v=64)[:, :, par, :]

                    def pv64(bk, n):
                        return ps[0:C, bk, 0:n * 64].rearrange("p (h v) -> p h v", v=64)

                    def f_y_state():
                        bkp = (nbank(), nbank())
                        for h in range(16):
                            rows, blk = hp(h)
                            bk = bkp[h % 2]
                            hl = h // 2
                            S.pe(E("matmul", ps[0:C, bk, hl * 64:(hl + 1) * 64], lhsT=R_(Rg[rows, blk, cs]), rhs=R_(St[rows, blk, :]),
                                   start=True, stop=True), reads=[b_R, bSt], writes=[b_ps[bk]])
                        for par in range(2):
                            S.act(E("copy", out=par2(ysb, par, 16), in_=pv64(bkp[par], 8)), reads=[b_ps[bkp[par]]], writes=[b_ysb])

                    def f_rhs_state():
                        bkp = (nbank(), nbank())
                        for h in range(6):
                            rows, blk = hp(h)
                            bk = bkp[h % 2]
                            hl = h // 2
                            S.pe(E("matmul", ps[0:C, bk, hl * 64:(hl + 1) * 64], lhsT=R_(ATt[rows, blk, cs]), rhs=R_(St[rows, blk, :]),
                                   start=True, stop=True), reads=[b_AT, bSt], writes=[b_ps[bk]])
                        for par in range(2):
                            S.act(E("copy", out=R_(par2(rhs0, par, 6)), in_=pv64(bkp[par], 3)), reads=[b_ps[bkp[par]]], writes=[b_rhs0])

                    def f_rhs_tok():
                        bk = nbank()
                        for h in range(6):
                            S.pe(E("matmul", ps[0:C, bk, h * 64:(h + 1) * 64], lhsT=R_(aak[0:C, h, 0:C]),
                                   rhs=R_(vtok[0:C, h * 64:(h + 1) * 64]), start=True, stop=True),
                                 reads=[b_aak, b_vtok], writes=[b_ps[bk]])
                        S.dve(E("tensor_tensor", out=R_(rhs0[0:C, :]), in0=rhs0[0:C, :], in1=ps[0:C, bk, 0:384], op=ALU.add),
                              reads=[b_ps[bk], b_rhs0], writes=[b_rhs0])

                    def f_y_tok(half):
                        bk = nbank()
                        for hl in range(8):
                            h = half * 8 + hl
                            S.pe(E("matmul", ps[0:C, bk, hl * 64:(hl + 1) * 64], lhsT=R_(ark[0:C, h, 0:C]),
                                   rhs=R_(vtok[0:C, h * 64:(h + 1) * 64]), start=True, stop=(h >= 6)),
                                 reads=[b_ark, b_vtok], writes=[b_ps[bk]])
                            if h < 6:
                                S.pe(E("matmul", ps[0:C, bk, hl * 64:(hl + 1) * 64], lhsT=R_(arb[0:C, h, 0:C]),
                                       rhs=R_(usb[0:C, h * 64:(h + 1) * 64]), start=False, stop=True),
                                     reads=[b_arb, b_usb], writes=[b_ps[bk]])
                        S.dve(E("tensor_tensor", out=ysb[0:C, half * 512:(half + 1) * 512], in0=ysb[0:C, half * 512:(half + 1) * 512],
                                in1=ps[0:C, bk, :], op=ALU.add), reads=[b_ps[bk], b_ysb], writes=[b_ysb])

                    fillers = [f_y_state, f_rhs_state, f_rhs_tok, lambda: f_y_tok(1)]
                    for st in range(nsteps):
                        bq = inv_mm(Pm[0], b_Pm[0], Qm[0], b_Qm[0])
                        if st < nsteps - 1:
                            bp = inv_mm(Qm[0], b_Qm[0], Pm[0], b_Pm[0])
                        for g_ in range(ng_):
                            S.act(E("copy", out=R_(gsl(Qm[0], g_)), in_=pv3(bq[g_], hg_)), reads=[b_ps[bq[g_]]], writes=[b_Qm[0]])
                        if st < nsteps - 1:
                            for g_ in range(ng_):
                                S.dve(E("tensor_copy", out=R_(gsl(Pm[0], g_)), in_=pv3(bp[g_], hg_)), reads=[b_ps[bp[g_]]],
                                      writes=[b_Pm[0]])
                        if fillers:
                            fillers.pop(0)()
                        bx = inv_mm(Qm[0], b_Qm[0], Xm, b_Xm)
                        for g_ in range(ng_):
                            S.dve(E("tensor_tensor", out=R_(gsl(Xm, g_)), in0=gsl(Xm, g_), in1=pv3(bx[g_], hg_), op=ALU.add),
                                  reads=[b_ps[bx[g_]], b_Xm], writes=[b_Xm])
                    while fillers:
                        fillers.pop(0)()
                    stage(33)
                    bk = nbank()
                    for h in range(6):
                        S.pe(E("matmul", ps[0:C, bk, h * 64:(h + 1) * 64], lhsT=R_(Xm[0:C, h, 0:C]),
                               rhs=R_(rhs0[0:C, h * 64:(h + 1) * 64]), start=True, stop=True),
                             reads=[b_Xm, b_rhs0], writes=[b_ps[bk]])
                    S.act(E("copy", out=R_(usb[0:C, :]), in_=ps[0:C, bk, 0:384]), reads=[b_ps[bk]], writes=[b_usb])
                    stage(34)
                    f_y_tok(0)
                    stage(35)
                    for half in range(2):
                        bk = nbank()
                        for bl in range(4):
                            blk = half * 4 + bl
                            S.pe(E("matmul", ps[:, bk, bl * 128:(bl + 1) * 128],
                                                                           lhsT=R_(ktok[0:C, blk * 128:(blk + 1) * 128]),
                                                                           rhs=R_(vtok[0:C, blk * 128:(blk + 1) * 128]),
                                                                           start=True, stop=(blk >= 3)),
                                 reads=[b_ktok, b_vtok], writes=[b_ps[bk]])
                            if blk < 3:
                                S.pe(E("matmul", ps[:, bk, bl * 128:(bl + 1) * 128],
                                                                               lhsT=R_(btok[0:C, blk * 128:(blk + 1) * 128]),
                                                                               rhs=R_(usb[0:C, blk * 128:(blk + 1) * 128]),
                                                                               start=False, stop=True),
                                     reads=[b_btok, b_usb], writes=[b_ps[bk]])
                        for hh in range(2):
                            rows = slice(hh * 64, hh * 64 + 64)
                            S.dve(E("tensor_tensor",
                                out=R_(St[rows, half * 4:half * 4 + 4, :]), in0=St[rows, half * 4:half * 4 + 4, :],
                                in1=ps[rows, bk, :].rearrange("p (a b) -> p a b", b=128)[:, :, hh * 64:hh * 64 + 64], op=ALU.add),
                                reads=[b_ps[bk], bSt], writes=[bSt])
                    S.dve(E("tensor_tensor", out=R_(St[:]), in0=St[:], in1=WC[:, :, ci:ci + 1].to_broadcast([128, 8, 64]),
                                                           op=ALU.mult), reads=[b_WC, bSt], writes=[bSt])
                    if kind == "s":
                        S.dma(E("dma_start", out=sto_d[l, 1 + ci], in_=St[:].rearrange("p a b -> p (a b)")),
                              reads=[bSt], chan="stout")
                    elif last_prompt and ci == nch - 1:
                        S.dma(E("dma_start", out=sto_d[l, 0], in_=St[:].rearrange("p a b -> p (a b)")),
                              reads=[bSt], chan="stout")
                    if ci == cfg.get("dbgci", 0) and l == 0:
                        dbg("ysb", ysb[0:C, 0:128], [b_ysb], C); dbg("St", St[:, 0, :], [bSt])
                    stage(36)
                    y3 = ysb[0:C, :].rearrange("p (h v) -> p h v", v=64)
                    q3 = ysq[0:C, :].rearrange("p (h v) -> p h v", v=64)
                    s1, s2, mean, tmpv, rstd = (yst[0:C, i, :] for i in range(5))
                    S.dve(E("tensor_tensor", out=R_(ysq[0:C, :]), in0=ysb[0:C, :], in1=ysb[0:C, :], op=ALU.mult), reads=[b_ysb],
                          writes=[b_ysq])
                    S.dve(E("tensor_reduce", out=s1, in_=y3, axis=AX.X, op=ALU.add), reads=[b_ysb], writes=[b_yst])
                    S.dve(E("tensor_reduce", out=s2, in_=q3, axis=AX.X, op=ALU.add), reads=[b_ysq], writes=[b_yst])
                    S.dve(E("tensor_tensor", out=mean, in0=s1, in1=tokc[0:C, 0:16], op=ALU.mult),
                          reads=[b_yst, b_cst], writes=[b_yst])
                    S.dve(E("tensor_tensor", out=tmpv, in0=mean, in1=mean, op=ALU.mult), reads=[b_yst],
                          writes=[b_yst])
                    S.dve(E("scalar_tensor_tensor", out=tmpv, in0=s2, scalar=1.0 / 64, in1=tmpv, op0=ALU.mult,
                                                                             op1=ALU.subtract), reads=[b_yst], writes=[b_yst])
                    S.dve(E("tensor_tensor", out=tmpv, in0=tmpv, in1=tokc[0:C, 16:32], op=ALU.add),
                          reads=[b_yst, b_cst], writes=[b_yst])
                    S.act(E("activation", out=tmpv, in_=tmpv, func=AF.Sqrt), reads=[b_yst], writes=[b_yst])
                    S.dve(E("reciprocal", out=rstd, in_=tmpv), reads=[b_yst], writes=[b_yst])
                    S.dve(E("tensor_tensor", out=y3, in0=y3, in1=mean.unsqueeze(2).to_broadcast([C, 16, 64]),
                                                                      op=ALU.subtract), reads=[b_yst, b_ysb], writes=[b_ysb])
                    S.dve(E("tensor_tensor", out=y3, in0=y3, in1=rstd.unsqueeze(2).to_broadcast([C, 16, 64]),
                                                                      op=ALU.mult), reads=[b_yst, b_ysb], writes=[b_ysb])
                    bpb = min(8, 512 // C)
                    for g_ in range(8 // bpb):
                        bk = nbank()
                        for bi in range(bpb):
                            blk = g_ * bpb + bi
                            S.pe(E("transpose", ps[:, bk, bi * C:(bi + 1) * C], ysb[0:C, blk * 128:(blk + 1) * 128],
                                   cst[0:C, 0:C]), reads=[b_ysb, b_cst], writes=[b_ps[bk]])
                        S.act(E("copy", out=ynT[:, g_ * bpb:(g_ + 1) * bpb, cs],
                                in_=ps[:, bk, 0:bpb * C].rearrange("p (a c) -> p a c", c=C)), reads=[b_ps[bk]], writes=[b_ynT])

                stage(4)
                for b in range(8):
                    S.dve(E("tensor_scalar", out=ynT[:, b, 0:T], in0=ynT[:, b, 0:T], scalar1=V_("nw", b), scalar2=V_("nb", b),
                                                         op0=ALU.mult, op1=ALU.add), reads=[b_ynT, bvec], writes=[b_ynT])
                    if b < 3:
                        S.dve(E("tensor_tensor", out=ynT[:, b, 0:T], in0=ynT[:, b, 0:T], in1=bonus[:, b, 0:T], op=ALU.add),
                              reads=[b_ynT, b_bonus], writes=[b_ynT])
                    S.dve(E("tensor_tensor", out=mixT[:, b, 0:T], in0=ynT[:, b, 0:T], in1=Gg[:, b, 0:T], op=ALU.mult),
                          reads=[b_ynT, b_G], writes=[b_mixT])
                    if hpf:
                        S.dve(E("tensor_tensor", out=mix32[:, b, :], in0=ynT[:, b, 0:2], in1=Gg[:, b, 0:2], op=ALU.mult),
                              reads=[b_ynT, b_G], writes=[b_mix32])

                def ln_epilogue(gname, bname):
                    r16, br16 = hT[:, 0:8, :], b_hT
                    S.act(E("copy", out=hT[:, 0:8, 0:T], in_=x32[:, :, 0:T]), reads=[b_x32], writes=[b_hT])
                    S.act(E("activation", out=hT[:, 8:16, 0:T], in_=x32[:, :, 0:T], func=AF.Square), reads=[b_x32], writes=[b_hT])
                    bk1 = nbank(); bk2 = nbank()
                    for kc in range(8):
                        S.pe(E("matmul", ps[:, bk1, 0:T], lhsT=onesb[:], rhs=hT[:, kc, 0:T], start=(kc == 0), stop=(kc == 7)),
                             reads=[b_ones, b_hT], writes=[b_ps[bk1]])
                    for kc in range(8):
                        S.pe(E("matmul", ps[:, bk2, 0:T], lhsT=onesb[:], rhs=hT[:, 8 + kc, 0:T], start=(kc == 0),
                                                       stop=(kc == 7)), reads=[b_ones, b_hT], writes=[b_ps[bk2]])
                    mean, var = tmp[0], tmp[1]
                    S.act(E("mul", out=mean[:, 0:T], in_=ps[:, bk1, 0:T], mul=1.0 / D), reads=[b_ps[bk1]], writes=[b_tmp[0]])
                    S.dve(E("tensor_tensor", out=var[:, 0:T], in0=mean[:, 0:T], in1=mean[:, 0:T], op=ALU.mult), reads=[b_tmp[0]],
                          writes=[b_tmp[1]])
                    S.dve(E("scalar_tensor_tensor", out=var[:, 0:T], in0=ps[:, bk2, 0:T], scalar=1.0 / D, in1=var[:, 0:T],
                                                           op0=ALU.mult, op1=ALU.subtract), reads=[b_ps[bk2], b_tmp[1]], writes=[b_tmp[1]])
                    S.dve(E("tensor_scalar", out=var[:, 0:T], in0=var[:, 0:T], scalar1=1e-5, scalar2=None, op0=ALU.add),
                          reads=[b_tmp[1]], writes=[b_tmp[1]])
                    S.act(E("activation", out=var[:, 0:T], in_=var[:, 0:T], func=AF.Sqrt), reads=[b_tmp[1]], writes=[b_tmp[1]])
                    S.dve(E("reciprocal", out=var[:, 0:T], in_=var[:, 0:T]), reads=[b_tmp[1]], writes=[b_tmp[1]])
                    S.dve(E("tensor_tensor", out=x32[:, :, 0:T], in0=x32[:, :, 0:T],
                                                    in1=mean[:, 0:T].unsqueeze(1).to_broadcast([128, 8, T]), op=ALU.subtract),
                          reads=[b_tmp[0], b_x32], writes=[b_x32])
                    S.dve(E("tensor_tensor", out=x32[:, :, 0:T], in0=x32[:, :, 0:T],
                                                    in1=var[:, 0:T].unsqueeze(1).to_broadcast([128, 8, T]), op=ALU.mult),
                          reads=[b_tmp[1], b_x32], writes=[b_x32])
                    for b in range(8):
                        S.dve(E("tensor_scalar", out=x32[:, b, 0:T], in0=x32[:, b, 0:T], scalar1=V_(gname, b),
                                                             scalar2=V_(bname, b), op0=ALU.mult, op1=ALU.add),
                              reads=[b_x32, bvec], writes=[b_x32])
                    S.act(E("copy", out=xTb[:, :, 0:T], in_=x32[:, :, 0:T]), reads=[b_x32], writes=[b_xTb])

                stage(5)
                for piece in range(4):
                    slot, bslot = ws_next(l, NPIECE_IN + piece)
                    for sub in range(2):
                        cb = piece * 2 + sub
                        bk = nbank()
                        for kc in range(8):
                            S.pe(E("matmul",
                                ps[:, bk, 0:T], lhsT=slot[:, kc * 256 + sub * 128: kc * 256 + sub * 128 + 128],
                                rhs=mixT[:, kc, 0:T], start=(kc == 0), stop=(kc == 7)), reads=[bslot, b_mixT], writes=[b_ps[bk]])
                        if hpf:
                            bkh = hp_mm8(l, NPIECE_IN + piece, sub, lambda kc: mix32[:, kc, 0:2], b_mix32)
                            S.dve(E("scalar_tensor_tensor", out=x32[:, cb, 0:2], in0=x32[:, cb, 0:2], scalar=ALPHA,
                                    in1=ps[:, bkh, 0:2], op0=ALU.mult, op1=ALU.add), reads=[b_ps[bkh], b_x32], writes=[b_x32])
                        S.dve(E("scalar_tensor_tensor", out=x32[:, cb, clo:T], in0=x32[:, cb, clo:T], scalar=ALPHA,
                                                                             in1=ps[:, bk, clo:T], op0=ALU.mult, op1=ALU.add),
                              reads=[b_ps[bk], b_x32], writes=[b_x32])
                ln_epilogue("ln1g", "ln1b")

                stage(6)
                if kind == "s":
                    S.dma(E("dma_start", out=cvst[:].rearrange("p a b c -> p (a b c)"), in_=scv_d[l]), writes=[b_cvst],
                          chan="cvst")
                ffn_defer = [None]
                for j in range(NJ):
                    slot, bslot = ws_next(l, NPIECE_IN + 4 + j)
                    cv = []
                    for ab in range(2):
                        cbk = j + NJ * ab
                        bk = nbank()
                        for kc in range(8):
                            S.pe(E("matmul",
                                ps[:, bk, 0:T], lhsT=slot[:, kc * 256 + ab * 128: kc * 256 + ab * 128 + 128],
                                rhs=xTb[:, kc, 0:T], start=(kc == 0), stop=(kc == 7)), reads=[bslot, b_xTb], writes=[b_ps[bk]])
                        ub, bub = ubuf[ab + 2 * (j % 2)], b_ubuf[ab + 2 * (j % 2)]
                        u3 = ub[:, 0:nseq * (L + 2)].rearrange("p (s t) -> p s t", t=L + 2)
                        if kind == "s":
                            S.act(E("copy", out=u3[:, :, 0:2], in_=cvst[:, cbk, :, :]), reads=[b_cvst],
                                  writes=[bub])
                        else:
                            S.act(E("copy", out=u3[:, 0, 0:2], in_=cvc[:, l, cbk, :]),
                                  reads=[b_cvc[l]], writes=[bub])
                        if hpf:
                            bkh = hp_mm8(l, NPIECE_IN + 4 + j, ab, lambda kc: x32[:, kc, 0:2], b_x32)
                            S.act(E("copy", out=u3[:, 0, 2:4], in_=ps[:, bkh, 0:2]), reads=[b_ps[bkh]], writes=[bub])
                            S.act(E("copy", out=u3[:, 0, 4:L + 2], in_=ps[:, bk, 2:T]), reads=[b_ps[bk]], writes=[bub])
                        else:
                            S.act(E("copy", out=u3[:, :, 2:L + 2], in_=t3(ps[:, bk, 0:T])), reads=[b_ps[bk]], writes=[bub])
                        if kind == "s":
                            S.act(E("copy", out=cvo[:, cbk, :, :], in_=u3[:, :, L:L + 2]), reads=[bub], writes=[b_cvo])
                        else:
                            S.act(E("copy", out=cvc[:, l, cbk, :], in_=u3[:, 0, L:L + 2]), reads=[bub],
                                  writes=[b_cvc[l]])
                        ct, bct = tmp[2 + ab + 2 * (j % 2)], b_tmp[2 + ab + 2 * (j % 2)]
                        c3 = t3(ct[:, 0:T])
                        S.pool(E("tensor_scalar", out=c3, in0=u3[:, :, 0:L], scalar1=V_("cw0", cbk),
                                                                               scalar2=V_("cb", cbk), op0=ALU.mult, op1=ALU.add),
                              reads=[bub, bvec], writes=[bct])
                        S.dve(E("scalar_tensor_tensor", out=c3, in0=u3[:, :, 1:L + 1], scalar=V_("cw1", cbk),
                                                                                      in1=c3, op0=ALU.mult, op1=ALU.add),
                              reads=[bub, bvec, bct], writes=[bct])
                        S.dve(E("scalar_tensor_tensor", out=c3, in0=u3[:, :, 2:L + 2], scalar=V_("cw2", cbk),
                                                                                      in1=c3, op0=ALU.mult, op1=ALU.add),
                              reads=[bub, bvec, bct], writes=[bct])
                        cv.append((ct, bct))
                    (ca, bca), (cb_, bcb) = cv

                    def gelu_h(j=j, ca=ca, bca=bca, cb_=cb_, bcb=bcb):
                        S.act(E("activation", out=ca[:, 0:T], in_=ca[:, 0:T], func=AF.Gelu_apprx_tanh), reads=[bca], writes=[bca])
                        S.dve(E("tensor_tensor", out=hT[:, j, 0:T], in0=ca[:, 0:T], in1=cb_[:, 0:T], op=ALU.mult),
                              reads=[bca, bcb], writes=[b_hT])
                        if hpf:
                            S.dve(E("tensor_tensor", out=h32[:, j, :], in0=ca[:, 0:2], in1=cb_[:, 0:2], op=ALU.mult),
                                  reads=[bca, bcb], writes=[b_h32])

                    if ffn_defer[0] is not None:
                        ffn_defer[0]()
                    ffn_defer[0] = gelu_h
                ffn_defer[0]()
                ffn_defer[0] = None
                if kind == "s":
                    S.dma(E("dma_start", out=cvo_d[l][:, 88:88 * (1 + NSS)], in_=cvo[:].rearrange("p a b c -> p (a b c)")),
                          reads=[b_cvo], chan="cvout")
                elif last_prompt:
                    S.dma(E("dma_start", out=cvo_d[l][:, 0:88], in_=cvc[:, l, :, :].rearrange("p a c -> p (a c)")),
                          reads=[b_cvc[l]], chan="cvout")
                for cb in range(8):
                    slot, bslot = ws_next(l, NPIECE_IN + 4 + NJ + cb)
                    bk = nbank()
                    for j in range(NJ):
                        S.pe(E("matmul", ps[:, bk, 0:T], lhsT=slot[:, j * 128:(j + 1) * 128], rhs=hT[:, j, 0:T],
                                                                       start=(j == 0), stop=(j == NJ - 1)),
                             reads=[bslot, b_hT], writes=[b_ps[bk]])
                    if hpf:
                        bkh = nbank()
                        for jh in range(3):
                            njj = min(8, NJ - jh * 8)
                            w, bw = hp_load(l, NPIECE_IN + 4 + NJ + cb, ("jh", jh, njj))
                            for jj in range(njj):
                                j = jh * 8 + jj
                                S.pe(E("matmul", ps[:, bkh, 0:2], lhsT=w[:, jj * 128:(jj + 1) * 128], rhs=h32[:, j, 0:2],
                                       start=(j == 0), stop=(j == NJ - 1)), reads=[bw, b_h32], writes=[b_ps[bkh]])
                        S.dve(E("scalar_tensor_tensor", out=x32[:, cb, 0:2], in0=x32[:, cb, 0:2], scalar=ALPHA,
                                in1=ps[:, bkh, 0:2], op0=ALU.mult, op1=ALU.add), reads=[b_ps[bkh], b_x32], writes=[b_x32])
                    S.dve(E("scalar_tensor_tensor", out=x32[:, cb, clo:T], in0=x32[:, cb, clo:T], scalar=ALPHA,
                                                                         in1=ps[:, bk, clo:T], op0=ALU.mult, op1=ALU.add),
                          reads=[b_ps[bk], b_x32], writes=[b_x32])
                stage(7)
                ln_epilogue("ln2g", "ln2b")

            S.dma(E("dma_start", out=yT_d[:, :, tok0:tok0 + T], in_=x32[:, :, 0:T]), reads=[b_x32], chan="yout")


    try:
        _tile_loop()
        assert ws["used"] == len(plan)
    except _Stop:
        pass
    S.run_block()
    return nc, S


_CACHE = {}


def _prep_inputs(inp, cfg):
    f = np.float32
    W = _prep_weights(inp)
    vm = [_prep_layer_params(inp, l) for l in range(NL)]
    vecs = np.stack([v for v, _ in vm])
    mats = np.stack([m for _, m in vm])
    cst = _consts()
    cosT, sinT = _rope_tables()
    in_maps = []
    for c in range(8):
        xp = inp["x_prompt"][c]
        xs = inp["x_sample"][c * NSS:(c + 1) * NSS].reshape(NSS * LS, D)
        x = np.concatenate([xp, xs], axis=0)
        xT = np.ascontiguousarray(x.T.reshape(8, 128, -1).transpose(1, 0, 2))
        sst = np.zeros((NL, NSS, 16, 64, 64), f)
        sl = slice(c * NSS, (c + 1) * NSS)
        sst[:, :, 0:6] = inp["state_rwkv"][:, sl].transpose(0, 1, 2, 4, 3)
        sst[:, :, 6:11, 0:32] = inp["state_gla"][:, sl]
        sst[:, :, 11:16] = inp["state_ret"][:, sl]
        sst = sst.reshape(NL, NSS, 8, 2, 64, 64).transpose(0, 1, 3, 4, 2, 5).reshape(NL, NSS, 128, 512)
        sh = inp["state_shift"][:, sl]
        shp = np.zeros((NL, NSS, 12 * 128), f)
        shp[:, :, 0:1152] = sh[:, :, 0:1152]
        shp[:, :, 1152:1408] = sh[:, :, 1152:1408]
        shp[:, :, 1408:1440] = sh[:, :, 1408:1440]
        ssh = shp.reshape(NL, NSS, 12, 128).transpose(0, 3, 2, 1).reshape(NL, 128, 12 * NSS)
        cvs = inp["state_conv"][:, sl]
        scv = cvs.reshape(NL, NSS, 2, 44, 128).transpose(0, 4, 3, 1, 2).reshape(NL, 128, 44 * NSS * 2)
        in_maps.append({"xT": xT, "wts": W, "vecs": vecs, "mats": mats, "cst": cst, "ropec": cosT, "ropes": sinT,
                        "sst": np.ascontiguousarray(sst), "ssh": np.ascontiguousarray(ssh), "scv": np.ascontiguousarray(scv)})
    return in_maps


def _assemble(results):
    f = np.float32
    y_p = np.zeros((8, SEQ, D), f); y_s = np.zeros((8 * NSS, LS, D), f)
    rw_p = np.zeros((NL, 8, 6, 64, 64), f); rw_s = np.zeros((NL, 8 * NSS, 6, 64, 64), f)
    sh_p = np.zeros((NL, 8, 1440), f); sh_s = np.zeros((NL, 8 * NSS, 1440), f)
    gl_p = np.zeros((NL, 8, 5, 32, 64), f); gl_s = np.zeros((NL, 8 * NSS, 5, 32, 64), f)
    rt_p = np.zeros((NL, 8, 5, 64, 64), f); rt_s = np.zeros((NL, 8 * NSS, 5, 64, 64), f)
    cv_p = np.zeros((NL, 8, 2, 2 * DFF), f); cv_s = np.zeros((NL, 8 * NSS, 2, 2 * DFF), f)
    for c, r in enumerate(results):
        yT = r["yT"]
        y = yT.transpose(1, 0, 2).reshape(D, -1).T
        y_p[c] = y[0:SEQ]
        y_s[c * NSS:(c + 1) * NSS] = y[SEQ:].reshape(NSS, LS, D)
        st = r["sto"].reshape(NL, 1 + NSS, 2, 64, 8, 64).transpose(0, 1, 4, 2, 3, 5).reshape(NL, 1 + NSS, 16, 64, 64)
        rw = st[:, :, 0:6].transpose(0, 1, 2, 4, 3)
        gl = st[:, :, 6:11, 0:32]
        rt = st[:, :, 11:16]
        rw_p[:, c] = rw[:, 0]; gl_p[:, c] = gl[:, 0]; rt_p[:, c] = rt[:, 0]
        sl = slice(c * NSS, (c + 1) * NSS)
        rw_s[:, sl] = rw[:, 1:]; gl_s[:, sl] = gl[:, 1:]; rt_s[:, sl] = rt[:, 1:]
        sho = r["sho"].reshape(NL, 128, 12 * (1 + NSS))
        p = sho[:, :, 0:12].transpose(0, 2, 1).reshape(NL, 12 * 128)
        s = sho[:, :, 12:].reshape(NL, 128, 12, NSS).transpose(0, 3, 2, 1).reshape(NL, NSS, 12 * 128)
        sh_p[:, c] = p[:, 0:1440]
        sh_s[:, sl] = s[:, :, 0:1440]
        cvo = r["cvo"].reshape(NL, 128, 44 * (1 + NSS) * 2)
        p = cvo[:, :, 0:88].reshape(NL, 128, 44, 2).transpose(0, 3, 2, 1).reshape(NL, 2, 2 * DFF)
        s = cvo[:, :, 88:].reshape(NL, 128, 44, NSS, 2).transpose(0, 3, 4, 2, 1).reshape(NL, NSS, 2, 2 * DFF)
        cv_p[:, c] = p
        cv_s[:, sl] = s
    return (y_p, y_s, rw_p, sh_p, gl_p, rt_p, cv_p, rw_s, sh_s, gl_s, rt_s, cv_s)


def kernel(**inputs):
    inp = {k: np.asarray(v) for k, v in inputs.items()}
    cfg = CFG_FULL
    if "nc" not in _CACHE:
        _CACHE["nc"] = build(cfg)[0]
    nc = _CACHE["nc"]
    in_maps = _prep_inputs(inp, cfg)
    res = run_bass_kernel_spmd(nc, in_maps, core_ids=list(range(8)))
    return _assemble(res.results)
```

```python
import bisect
import math
from contextlib import ExitStack

import numpy as np
import concourse.bass as bass
import concourse.mybir as mybir
from concourse.bass_utils import run_bass_kernel_spmd

F32 = mybir.dt.float32
F32R = mybir.dt.float32r
BF16 = mybir.dt.bfloat16
AF = mybir.ActivationFunctionType
ALU = mybir.AluOpType
AX = mybir.AxisListType

NL = 4
D = 1024
SEQ = 2048
NSS = 16
LS = 4
TT = 256
PAST = 16384
DFF = 2816
NJ = 22
ALPHA = (2 * NL) ** 0.25
NBLK = 38
NPIECE_IN = 19
SLOT = 2816
NSLOT = 4
RW0, GL0, RT0 = 0, 1440, 2416


class _Stop(Exception):
    pass


class Buf:
    __slots__ = ("name", "last_write", "reads")

    def __init__(self, name):
        self.name = name
        self.last_write = None
        self.reads = []


class Op:
    __slots__ = ("eng", "fn", "deps", "idx", "needs_inc", "cnt", "chan", "dma_cnt")

    def __init__(self, eng, fn, deps, idx, chan=None):
        self.eng = eng
        self.fn = fn
        self.deps = deps
        self.idx = idx
        self.needs_inc = False
        self.cnt = None
        self.chan = chan
        self.dma_cnt = None


COMPUTE = ("pe", "act", "dve", "pool")
DMAQ = ("sp", "actq", "poolq")
STREAM = {"pe": "pe", "act": "act", "dve": "dve", "pool": "pool", "sp": "sp", "actq": "act", "poolq": "pool"}
SAME_SYNC = {"pe": False, "act": True, "dve": True, "pool": True}


class Sched:
    def __init__(self, nc):
        self.nc = nc
        self.ops = []

    def op(self, eng, fn, reads=(), writes=(), chan=None):
        idx = len(self.ops)
        deps = set()
        for b in reads:
            if b.last_write is not None:
                deps.add(b.last_write)
        for b in writes:
            if b.last_write is not None:
                deps.add(b.last_write)
            deps.update(b.reads)
        o = Op(eng, fn, deps, idx, chan)
        self.ops.append(o)
        for b in reads:
            b.reads.append(idx)
        for b in writes:
            b.last_write = idx
            b.reads = []
        return o

    def pe(self, fn, reads=(), writes=()):
        return self.op("pe", fn, reads, writes)

    def act(self, fn, reads=(), writes=()):
        return self.op("act", fn, reads, writes)

    def dve(self, fn, reads=(), writes=()):
        return self.op("dve", fn, reads, writes)

    def pool(self, fn, reads=(), writes=()):
        return self.op("pool", fn, reads, writes)

    def dma(self, fn, reads=(), writes=(), chan=None, q="sp"):
        return self.op(q, fn, reads, writes, chan=chan)

    def _skip(self, p, so):
        sp = STREAM[p.eng]
        return sp == so and p.eng in COMPUTE and not SAME_SYNC[p.eng]

    def run_block(self):
        nc = self.nc
        ops = self.ops
        for o in ops:
            so = STREAM[o.eng]
            for d in o.deps:
                p = ops[d]
                if p.eng in DMAQ or self._skip(p, so):
                    continue
                p.needs_inc = True
        cnt = {e: 0 for e in COMPUTE}
        chan_cnt = {}
        chan_hist = {}
        for o in ops:
            if o.eng in DMAQ:
                c = chan_cnt.get(o.chan, 0) + 16
                chan_cnt[o.chan] = c
                o.dma_cnt = c
                chan_hist.setdefault(o.chan, []).append((o.idx, c))
            elif o.needs_inc:
                cnt[o.eng] += 1
                o.cnt = cnt[o.eng]
        chan_idx = {c: [i for i, _ in h] for c, h in chan_hist.items()}
        streams = {s: [] for s in ("pe", "act", "dve", "pool", "sp")}
        waited = {s: {} for s in streams}
        for o in ops:
            so = STREAM[o.eng]
            need = {}
            for d in o.deps:
                p = ops[d]
                if p.eng in DMAQ:
                    h = chan_hist[p.chan]
                    k = bisect.bisect_left(chan_idx[p.chan], o.idx) - 1
                    key = ("chan", p.chan)
                    need[key] = max(need.get(key, 0), h[k][1])
                else:
                    if self._skip(p, so):
                        continue
                    key = ("eng", p.eng)
                    need[key] = max(need.get(key, 0), p.cnt)
            w = []
            for key, c in need.items():
                if waited[so].get(key, 0) >= c:
                    continue
                waited[so][key] = c
                w.append((key, c))
            streams[so].append((o, w))
        with ExitStack() as es:
            sems = {}
            for e in COMPUTE:
                sems[("eng", e)] = es.enter_context(nc.semaphore("s_" + e))
            for c in chan_cnt:
                sems[("chan", c)] = es.enter_context(nc.semaphore("c_" + str(c)))
            block = es.enter_context(nc.Block())

            def mk(sname):
                def body(eng):
                    for o, w in streams[sname]:
                        for key, c in w:
                            eng.wait_ge(sems[key], c)
                        ins = o.fn(eng)
                        if o.eng in DMAQ:
                            ins.then_inc(sems[("chan", o.chan)], 16)
                        elif o.needs_inc:
                            ins.then_inc(sems[("eng", o.eng)], 1)
                    if sname == "sp":
                        for c, v in chan_cnt.items():
                            eng.wait_ge(sems[("chan", c)], v)
                        for e in COMPUTE:
                            if cnt[e] > 0:
                                eng.wait_ge(sems[("eng", e)], cnt[e])
                return body

            block.tensor(mk("pe"))
            block.scalar(mk("act"))
            block.vector(mk("dve"))
            block.gpsimd(mk("pool"))
            block.sync(mk("sp"))


def _colmap():
    cm = -np.ones(NBLK * 128, np.int64)
    for g, (rw_off, gl_off, rt_off, pad) in enumerate(((0, 0, 0, True), (384, 160, 320, True), (768, 320, 640, False))):
        base = g * 1024
        cm[base:base + 384] = RW0 + rw_off + np.arange(384)
        for j in range(5):
            if pad:
                cm[base + 384 + j * 64: base + 384 + j * 64 + 32] = GL0 + gl_off + j * 32 + np.arange(32)
            else:
                cm[base + 384 + j * 64: base + 384 + j * 64 + 64] = GL0 + gl_off + j * 64 + np.arange(64)
        cm[base + 704: base + 1024] = RT0 + rt_off + np.arange(320)
    b = 24 * 128
    cm[b:b + 320] = GL0 + 640 + np.arange(320)
    cm[b + 320:b + 640] = RT0 + 960 + np.arange(320)
    b = 29 * 128
    cm[b:b + 128] = RW0 + 1152 + np.arange(128)
    cm[b + 128:b + 256] = RW0 + 1280 + np.arange(128)
    cm[b + 256:b + 288] = RW0 + 1408 + np.arange(32)
    cm[b + 288:b + 304] = GL0 + 960 + np.arange(16)
    for gi, off in enumerate((0, 320)):
        b = (32 + 3 * gi) * 128
        for r in range(704, 1024):
            i = r - 704
            h, d = divmod(i, 64)
            cm[b + (r - 640)] = RT0 + off + h * 64 + ((d + 32) % 64)
    return cm


def _rows_layout(v, nblk):
    return np.ascontiguousarray(v.reshape(nblk, 128).T)


VEC = {}


def _vec_layout():
    off = 0
    for name, n in (("mu", 12), ("w0", 3), ("a0", 3), ("kkv", 3), ("ka", 3), ("rk", 3), ("nw", 8), ("nb", 8),
                    ("bgkb", 3), ("ln1g", 8), ("ln1b", 8), ("ln2g", 8), ("ln2b", 8), ("cw0", 44), ("cw1", 44),
                    ("cw2", 44), ("cb", 44), ("rscale", 8), ("kscale", 8), ("ldret", 8)):
        VEC[name] = (off, n)
        off += n
    return off


NV = _vec_layout()
NM = 4 * 384


def _prep_layer_params(inp, l):
    f = np.float32
    vec = np.zeros((128, NV), f)

    def put(name, arr):
        o, n = VEC[name]
        assert arr.shape == (128, n), (name, arr.shape)
        vec[:, o:o + n] = arr

    mu = inp["rwkv_mu"][l]
    mup = np.zeros(12 * 128, f)
    mup[0:384] = mu[0:384]
    mup[384:768] = mu[384:768]
    mup[768:1152] = mu[768:1152]
    mup[1152:1152 + 128] = mu[1152:1280]
    mup[1280:1408] = mu[1280:1408]
    mup[1408:1440] = mu[1408:1440]
    put("mu", _rows_layout(mup, 12))
    put("w0", _rows_layout(inp["rwkv_w0"][l], 3))
    put("a0", _rows_layout(inp["rwkv_a0"][l], 3))
    put("kkv", _rows_layout(inp["rwkv_kk"][l], 3))
    put("ka", _rows_layout(inp["rwkv_ka"][l], 3))
    put("rk", _rows_layout(inp["rwkv_rk"][l].reshape(-1), 3))
    nw = np.ones(1024, f)
    nb = np.zeros(1024, f)
    nw[0:384] = inp["rwkv_lnw"][l]
    nb[0:384] = inp["rwkv_lnb"][l]
    nw[384:704] = np.tile(inp["gla_norm_w"][l], 5)
    put("nw", _rows_layout(nw, 8))
    put("nb", _rows_layout(nb, 8))
    bb = np.zeros(384, f)
    for j in range(5):
        bb[j * 64:j * 64 + 32] = inp["gla_bgk_b"][l][j * 32:(j + 1) * 32]
    put("bgkb", _rows_layout(bb, 3))
    put("ln1g", _rows_layout(inp["ln1_g"][l], 8))
    put("ln1b", _rows_layout(inp["ln1_b"][l], 8))
    put("ln2g", _rows_layout(inp["ln2_g"][l], 8))
    put("ln2b", _rows_layout(inp["ln2_b"][l], 8))
    cw = inp["ffn_conv_w"][l]
    put("cw0", _rows_layout(cw[0], 44))
    put("cw1", _rows_layout(cw[1], 44))
    put("cw2", _rows_layout(cw[2], 44))
    put("cb", _rows_layout(inp["ffn_conv_b"][l], 44))
    rs = np.ones(1024, f)
    rs[384:704] = 32.0 ** -0.5
    ks = np.ones(1024, f)
    ks[704:1024] = 64.0 ** -0.5
    put("rscale", _rows_layout(rs, 8))
    put("kscale", _rows_layout(ks, 8))
    ldr = np.zeros(1024, np.float64)
    for h in range(5):
        ldr[704 + h * 64:704 + (h + 1) * 64] = math.log(1.0 - 2.0 ** (-5.0 - h))
    put("ldret", _rows_layout(ldr.astype(f), 8))
    mats = np.zeros((128, NM), f)
    mats[0:64, 0:384] = inp["rwkv_bw"][l]
    mats[64:128, 0:384] = inp["rwkv_ba"][l]
    mats[:, 384:768] = inp["rwkv_bg"][l][0:128]
    mats[0:32, 768:1152] = inp["rwkv_bg"][l][128:160]
    bgk = inp["gla_bgk"][l]
    for j in range(5):
        mats[32:48, 1152 + j * 64:1152 + j * 64 + 32] = bgk[:, j * 32:(j + 1) * 32]
    return vec, mats


def _prep_weights(inp):
    cm = _colmap()
    npieces = NPIECE_IN + 4 + NJ + 8
    W = np.zeros((NL, npieces, 128, SLOT), np.float32)
    for l in range(NL):
        win = inp["w_in"][l]
        wp = np.zeros((1024, NBLK * 128), np.float32)
        ok = cm >= 0
        wp[:, ok] = win[:, cm[ok]]
        p = 0
        for i in range(NPIECE_IN):
            blkc = wp[:, i * 256:(i + 1) * 256].reshape(8, 128, 256).transpose(1, 0, 2)
            W[l, p, :, :2048] = blkc.reshape(128, 2048)
            p += 1
        wo = inp["w_out"][l]
        for i in range(4):
            blkc = wo[:, i * 256:(i + 1) * 256].reshape(8, 128, 256).transpose(1, 0, 2)
            W[l, p, :, :2048] = blkc.reshape(128, 2048)
            p += 1
        up = inp["ffn_up"][l]
        for j in range(NJ):
            ab = np.concatenate([up[:, j * 128:(j + 1) * 128], up[:, DFF + j * 128:DFF + (j + 1) * 128]], axis=1)
            blkc = ab.reshape(8, 128, 256).transpose(1, 0, 2)
            W[l, p, :, :2048] = blkc.reshape(128, 2048)
            p += 1
        dn = inp["ffn_down"][l]
        for cb in range(8):
            blkc = dn[:, cb * 128:(cb + 1) * 128].reshape(NJ, 128, 128).transpose(1, 0, 2)
            W[l, p, :, :NJ * 128] = blkc.reshape(128, NJ * 128)
            p += 1
        assert p == npieces
    return W


def _consts():
    ident = np.eye(128, dtype=np.float32)
    sI = np.arange(128)[:, None]
    tI = np.arange(128)[None, :]
    bones = np.zeros((128, 128), np.float32)
    bones[0:64, 0:64] = 1
    bones[64:128, 64:128] = 1
    cst = np.zeros((128, 672), np.float32)
    cst[:, 0:128] = ident
    cst[:, 128:256] = bones
    cst[:, 256:384] = (tI >= sI)
    cst[:, 384:512] = (tI > sI)
    cst[:, 512:640] = (tI < sI)
    tok = np.zeros((128, 32), np.float32)
    tok[:, 0:6] = 1.0 / 64
    tok[:, 16:22] = 64e-5
    tok[:, 22:32] = 1e-6
    cst[:, 640:672] = tok
    return cst


def _rope_tables():
    ntok = SEQ + NSS * LS
    pos = np.concatenate([np.arange(SEQ), np.tile(PAST + np.arange(LS), NSS)]).astype(np.float32)
    inv = (1.0 / (10000.0 ** np.linspace(0.0, 1.0, 32, dtype=np.float32))).astype(np.float32)
    ang = pos[None, :] * inv[:, None]
    cos = np.cos(ang).astype(np.float32)
    sin = np.sin(ang).astype(np.float32)
    C = np.ones((384, ntok), np.float32)
    S = np.zeros((384, ntok), np.float32)
    for h in range(5):
        r0 = 64 + h * 64
        C[r0:r0 + 32] = cos
        C[r0 + 32:r0 + 64] = cos
        S[r0:r0 + 32] = -sin
        S[r0 + 32:r0 + 64] = sin
    C = C.reshape(3, 128, ntok).transpose(1, 0, 2)
    S = S.reshape(3, 128, ntok).transpose(1, 0, 2)
    return np.ascontiguousarray(C), np.ascontiguousarray(S)


CFG_FULL = dict(tiles=[("p", i) for i in range(SEQ // TT)] + [("s", 0)], nl=NL)


def build(cfg):
    nc = bass.Bass("TRN2", target_bir_lowering=False)
    S = Sched(nc)
    NTOK = SEQ + NSS * LS
    npieces = NPIECE_IN + 4 + NJ + 8
    nl = cfg["nl"]
    tiles = cfg["tiles"]

    def din(name, shape, dt=F32):
        return nc.dram_tensor(name, list(shape), dt, kind="ExternalInput").ap()

    def dout(name, shape):
        return nc.dram_tensor(name, list(shape), F32, kind="ExternalOutput").ap()

    xT_d = din("xT", [128, 8, NTOK])
    wts_d = din("wts", [NL, npieces, 128, SLOT])
    vec_d = din("vecs", [NL, 128, NV])
    mat_d = din("mats", [NL, 128, NM])
    cst_d = din("cst", [128, 672])
    cos_d = din("ropec", [128, 3, NTOK])
    sin_d = din("ropes", [128, 3, NTOK])
    sst_d = din("sst", [NL, NSS, 128, 512])
    ssh_d = din("ssh", [NL, 128, 12 * NSS])
    scv_d = din("scv", [NL, 128, 44 * NSS * 2])
    yT_d = dout("yT", [128, 8, NTOK])
    sto_d = dout("sto", [NL, 1 + NSS, 128, 512])
    sho_d = dout("sho", [NL, 128, 12 * (1 + NSS)])
    cvo_d = dout("cvo", [NL, 128, 44 * (1 + NSS) * 2])
    dbg_d = dout("dbg", [128, 8192]) if cfg.get("dbg") else None
    dbg_state = {"col": 0, "items": []}

    def dbg(name, ap, bufs, np_=128):
        if dbg_d is None:
            return
        n = 1
        for d_ in ap.shape[1:]:
            n *= d_
        c0 = dbg_state["col"]
        if c0 + n > 8192:
            return
        dbg_state["col"] = c0 + n
        dbg_state["items"].append((name, c0, n, np_, tuple(ap.shape)))
        flat = dbg_d[0:np_, c0:c0 + n]
        if len(ap.shape) == 3:
            flat = flat.rearrange("p (a b) -> p a b", b=ap.shape[2])
        S.dma(E("dma_start", out=flat, in_=ap), reads=bufs, chan="dbg")
    cfg["_dbg_items"] = dbg_state["items"]

    def sb(name, shape, dt=F32):
        return nc.alloc_sbuf_tensor("sb_" + name, list(shape), dt)

    TM = TT
    x32 = sb("x32", [128, 8, TM]); b_x32 = Buf("x32")
    xTb = sb("xTb", [128, 8, TM], BF16); b_xTb = Buf("xTb")
    un1 = sb("un1", [128, 12 * (TM + NSS)]); b_PRW = Buf("PRW")
    PRW = un1[:].rearrange("p (a b) -> p a b", b=TM + NSS)
    Rg = sb("Rg", [128, 8, TM]); b_R = Buf("R")
    Kg = sb("Kg", [128, 8, TM]); b_K = Buf("K")
    Vg = sb("Vg", [128, 8, TM]); b_V = Buf("V")
    Gg = sb("Gg", [128, 8, TM], BF16); b_G = Buf("G")

    XS = sb("XS", [128, 3, TM]); b_XS = Buf("XS")
    ATt = sb("ATt", [128, 3, TM]); b_AT = Buf("AT")
    BTt = sb("BTt", [128, 3, TM]); b_BT = Buf("BT")
    bonus = sb("bonus", [128, 3, TM]); b_bonus = Buf("bonus")
    ynT = sb("ynT", [128, 8, TM]); b_ynT = Buf("ynT")
    SW = ynT; b_SW = b_ynT
    mixT = sb("mixT", [128, 8, TM], BF16); b_mixT = Buf("mixT")
    hT = un1[:].bitcast(BF16)[:, 0:NJ * TM].rearrange("p (a b) -> p a b", b=TM); b_hT = b_PRW
    WC = sb("WC", [128, 8, 16]); b_WC = Buf("WC")
    NTMP = 9
    tmp = [sb("tmp%d" % i, [128, TM]) for i in range(NTMP)]
    b_tmp = [Buf("tmp%d" % i) for i in range(NTMP)]
    tmpb = [sb("tmpb%d" % i, [128, TM], BF16) for i in range(4)]
    b_tmpb = [Buf("tmpb%d" % i) for i in range(4)]
    ubuf = [sb("ubuf%d" % i, [128, TM + 2 * NSS]) for i in range(4)]
    b_ubuf = [Buf("ubuf%d" % i) for i in range(4)]
    cosT = sb("cosT", [128, 3, TM]); sinT = sb("sinT", [128, 3, TM]); b_rope = Buf("rope")
    cst = sb("cst", [128, 672]); b_cst = Buf("cst")
    onesb = sb("onesb", [128, 128], BF16); b_ones = Buf("onesb")
    bonesr = sb("bonesr", [128, 128]); b_bones = Buf("bonesr")
    rmask = sb("rmask", [128, TM]); b_rmask = Buf("rmask")
    vecs = [sb("vecs%d" % i, [128, NV]) for i in range(2)]; b_vecs = [Buf("vecs%d" % i) for i in range(2)]
    vaux = [sb("vaux%d" % i, [128, 8]) for i in range(2)]; b_vaux = [Buf("vaux%d" % i) for i in range(2)]
    _m0 = sb("mats0", [128, NM], BF16); _bm0 = Buf("mats0")
    mats = [_m0, _m0]; b_mats = [_bm0, _bm0]
    hpw = [sb("hpw%d" % i, [128, 1024]) for i in range(2)]; b_hpw = [Buf("hpw%d" % i) for i in range(2)]
    mix32 = sb("mix32", [128, 8, 2]); b_mix32 = Buf("mix32")
    h32 = sb("h32", [128, NJ, 2]); b_h32 = Buf("h32")
    slots = [sb("slot%d" % i, [128, SLOT], BF16) for i in range(NSLOT)]
    b_slots = [Buf("slot%d" % i) for i in range(NSLOT)]
    Sst = [sb("Sst%d" % l, [128, 8, 64]) for l in range(NL)]; b_Sst = [Buf("Sst%d" % l) for l in range(NL)]
    Ssm = [sb("Ssm%d" % i, [128, 8, 64]) for i in range(2)]; b_Ssm = [Buf("Ssm%d" % i) for i in range(2)]
    _s0 = sb("Sld0", [128, 8, 64]); _bs0 = Buf("Sld0")
    Sld = [_s0, _s0]; b_Sld = [_bs0, _bs0]
    shc = sb("shc", [128, NL, 12]); b_shc = [Buf("shc%d" % l) for l in range(NL)]
    shst = sb("shst", [128, 12, NSS]); b_shst = Buf("shst")
    sho = sb("sho", [128, 12, NSS]); b_sho = Buf("sho")
    cvc = sb("cvc", [128, NL, 44, 2]); b_cvc = [Buf("cvc%d" % l) for l in range(NL)]
    cvst = sb("cvst", [128, 44, NSS, 2]); b_cvst = Buf("cvst")
    cvo = sb("cvo", [128, 44, NSS, 2]); b_cvo = Buf("cvo")
    CM = 128
    vtok = sb("vtok", [CM, 1024]); b_vtok = Buf("vtok")
    ktok = sb("ktok", [CM, 1024]); b_ktok = Buf("ktok")
    btok = sb("btok", [CM, 384]); b_btok = Buf("btok")
    ark = sb("ark", [CM, 16, CM]); b_ark = Buf("ark")
    aak = sb("aak", [CM, 6, CM]); b_aak = Buf("aak")
    arb = sb("arb", [CM, 6, CM]); b_arb = Buf("arb")
    _p = sb("Pm0", [CM, 6, CM], BF16); _bp = Buf("Pm0")
    _q = sb("Qm0", [CM, 6, CM], BF16); _bq = Buf("Qm0")
    Xb = sb("Xb", [CM, 6, CM], BF16); b_Xb = Buf("Xb")
    Pm = [_p, _p]; b_Pm = [_bp, _bp]
    Qm = [_q, _q]; b_Qm = [_bq, _bq]
    Xm = sb("Xm", [CM, 6, CM]); b_Xm = Buf("Xm")
    rhs0 = sb("rhs0", [CM, 384]); b_rhs0 = Buf("rhs0")
    usb = sb("usb", [CM, 384]); b_usb = Buf("usb")
    ysb = sb("ysb", [CM, 1024]); b_ysb = Buf("ysb")
    ysq = ktok; b_ysq = b_ktok
    yst = sb("yst", [CM, 8, 16]); b_yst = Buf("yst")
    ps = nc.alloc_psum_tensor("ps", [128, 8, 512], F32)
    b_ps = [Buf("ps%d" % i) for i in range(8)]
    bank_ctr = [0]

    def nbank():
        b = bank_ctr[0] % 8
        bank_ctr[0] += 1
        return b

    R_ = lambda ap: ap.bitcast(F32R)

    def E(name, *a, **k):
        return lambda e: getattr(e, name)(*a, **k)

    S.dma(E("dma_start", out=cst[:], in_=cst_d), writes=[b_cst], chan="cst")
    ident = cst[:, 0:128]
    m_incl = cst[:, 256:384]
    m_su = cst[:, 384:512]
    m_sl = cst[:, 512:640]
    eye64 = cst[:, 0:128]
    tokc = cst[:, 640:672]
    S.dve(E("memset", onesb[:], 1.0), writes=[b_ones])
    S.dve(E("tensor_copy", out=R_(bonesr[:]), in_=cst[:, 128:256]), reads=[b_cst], writes=[b_bones])
    S.dve(E("memset", Sld[0][:], 0.0), writes=[b_Sld[0]])
    for l in range(NL):
        S.dve(E("tensor_copy", out=R_(Sst[l][:]), in_=Sld[0][:]), reads=[b_Sld[0]], writes=[b_Sst[l]])
        S.dve(E("memset", shc[:, l, :], 0.0), writes=[b_shc[l]])
        S.dve(E("memset", cvc[:, l, :, :], 0.0), writes=[b_cvc[l]])

    plan = []
    for (kind, ti) in tiles:
        for l in range(nl):
            for p in range(npieces):
                plan.append((l, p))
    ws = {"issued": 0, "used": 0, "pending_wr": {}}

    wbf_d = nc.dram_tensor("wbf", [NL, npieces, 128, SLOT], BF16).ap()
    b_wbf = {}

    def ws_issue():
        i = ws["issued"]
        if i >= len(plan):
            return
        l, p = plan[i]
        s = i % NSLOT
        n = 2048 if p < NPIECE_IN + 4 + NJ else NJ * 128
        if (l, p) not in b_wbf:
            S.dma(E("dma_start", out=slots[s][:, 0:n], in_=wts_d[l, p, :, 0:n]),
                  writes=[b_slots[s]], chan="slot%d" % s, q="poolq")
            if len(tiles) > 1:
                b_wbf[(l, p)] = Buf("wbf%d_%d" % (l, p))
                ws["pending_wr"][i] = (l, p, s, n)
        else:
            S.dma(E("dma_start", out=slots[s][:, 0:n], in_=wbf_d[l, p, :, 0:n]), reads=[b_wbf[(l, p)]],
                  writes=[b_slots[s]], chan="slot%d" % s)
        ws["issued"] += 1

    def ws_next(l, p):
        i = ws["used"]
        assert plan[i] == (l, p), (plan[i], l, p)
        while ws["issued"] < min(len(plan), i + NSLOT):
            ws_issue()
        if i in ws["pending_wr"]:
            l_, p_, s_, n_ = ws["pending_wr"].pop(i)
            S.dma(E("dma_start", out=wbf_d[l_, p_, :, 0:n_], in_=slots[s_][:, 0:n_]), reads=[b_slots[s_]],
                  writes=[b_wbf[(l_, p_)]], chan="wbfw")
        ws["used"] += 1
        s = i % NSLOT
        return slots[s], b_slots[s]

    for _ in range(NSLOT - 1):
        ws_issue()

    pv_ctr = [0]
    hp_ctr = [0]

    def hp_load(l, p, part):
        i = hp_ctr[0] % 2
        hp_ctr[0] += 1
        if part[0] == "sub":
            src = wts_d[l, p, :, 0:2048].rearrange("p (kc c) -> p kc c", c=256)[:, :, part[1] * 128:(part[1] + 1) * 128]
            dst = hpw[i][:, 0:1024].rearrange("p (kc c) -> p kc c", c=128)
        else:
            src = wts_d[l, p, :, part[1] * 1024:part[1] * 1024 + part[2] * 128]
            dst = hpw[i][:, 0:part[2] * 128]
        S.dma(E("dma_start", out=dst, in_=src), writes=[b_hpw[i]], chan="hp%d" % i)
        return hpw[i], b_hpw[i]

    def hp_mm8(l, p, sub, rhs_fn, brhs):
        w, bw = hp_load(l, p, ("sub", sub))
        bkh = nbank()
        for kc in range(8):
            S.pe(E("matmul", ps[:, bkh, 0:2], lhsT=w[:, kc * 128:(kc + 1) * 128], rhs=rhs_fn(kc), start=(kc == 0), stop=(kc == 7)),
                 reads=[bw, brhs], writes=[b_ps[bkh]])
        return bkh

    def stage(n):
        if cfg.get("stop") == n:
            raise _Stop()

    def _tile_loop():
        for (kind, ti) in tiles:
            if kind == "p":
                T = TT; nseq = 1; L = TT; C = 128; tok0 = ti * TT; nsteps = 6
            else:
                T = NSS * LS; nseq = NSS; L = LS; C = LS; tok0 = SEQ; nsteps = 1
            nch = T // C
            last_prompt = (kind == "p" and ti == SEQ // TT - 1)
            hpf = (kind == "p" and ti == 0 and cfg.get("hp", True))
            clo = 2 if hpf else 0
            S.dma(E("dma_start", out=x32[:, :, 0:T], in_=xT_d[:, :, tok0:tok0 + T]),
                  writes=[b_x32], chan="x32")
            S.dma(E("dma_start", out=xTb[:, :, 0:T], in_=xT_d[:, :, tok0:tok0 + T]),
                  writes=[b_xTb], chan="xTb", q="poolq")
            S.dma(E("dma_start", out=cosT[:, :, 0:T], in_=cos_d[:, :, tok0:tok0 + T]),
                  writes=[b_rope], chan="rope")
            S.dma(E("dma_start", out=sinT[:, :, 0:T], in_=sin_d[:, :, tok0:tok0 + T]),
                  writes=[b_rope], chan="rope")
            S.dve(E("memset", rmask[:, 0:T], 1.0), writes=[b_rmask])
            S.dve(E("memset", rmask[:, 0:T].rearrange("p (c t) -> p c t", t=C)[:, :, 0:1], 0.0),
                  writes=[b_rmask])

            for l in range(nl):
                pv = pv_ctr[0] % 2
                pv_ctr[0] += 1
                vec, bvec, mat, bmat, vx, bvx = vecs[pv], b_vecs[pv], mats[pv], b_mats[pv], vaux[pv], b_vaux[pv]
                S.dma(E("dma_start", out=vec[:], in_=vec_d[l]), writes=[bvec], chan="vec%d" % pv)
                S.dma(E("dma_start", out=mat[:], in_=mat_d[l]), writes=[bmat], chan="mat0",
                      q="poolq")

                def V_(name, b=0, n=1, vec=vec):
                    o, _ = VEC[name]
                    return vec[:, o + b:o + b + n]

                o_ka = VEC["ka"][0]; o_bg = VEC["bgkb"][0]
                S.dve(E("tensor_scalar", out=vx[:, 0:3], in0=vec[:, o_ka:o_ka + 3], scalar1=-1.0,
                                                                        scalar2=1.0, op0=ALU.mult, op1=ALU.add),
                      reads=[bvec], writes=[bvx])
                S.dve(E("tensor_scalar", out=vx[:, 3:6], in0=vec[:, o_bg:o_bg + 3], scalar1=-1.0,
                                                                        scalar2=None, op0=ALU.mult),
                      reads=[bvec], writes=[bvx])

                if kind == "s":
                    S.dma(E("dma_start", out=shst[:].rearrange("p a b -> p (a b)"), in_=ssh_d[l]),
                          writes=[b_shst], chan="shst")
                    S.dve(E("tensor_copy", out=PRW[:, :, 0:NSS * (LS + 1)].rearrange("p a (s t) -> p a s t", t=LS + 1)[:, :, :, 0],
                                                  in_=shst[:]), reads=[b_shst], writes=[b_PRW])
                else:
                    S.dve(E("tensor_copy", out=PRW[:, :, 0], in_=shc[:, l, :]), reads=[b_shc[l]], writes=[b_PRW])

                def prw_dst(pb, T=T, nseq=nseq, L=L):
                    return PRW[:, pb, 0:nseq * (L + 1)].rearrange("p (s t) -> p s t", t=L + 1)[:, :, 1:L + 1]

                def prw_prev(pb, T=T, nseq=nseq, L=L):
                    return PRW[:, pb, 0:nseq * (L + 1)].rearrange("p (s t) -> p s t", t=L + 1)[:, :, 0:L]

                def t3(ap, nseq=nseq, L=L):
                    return ap.rearrange("p (s t) -> p s t", t=L)

                evi = [0]
                for piece in range(NPIECE_IN):
                    slot, bslot = ws_next(l, piece)
                    for sub in range(2):
                        blk = piece * 2 + sub
                        bk = nbank()
                        for kc in range(8):
                            S.pe(E("matmul",
                                ps[:, bk, 0:T], lhsT=slot[:, kc * 256 + sub * 128: kc * 256 + sub * 128 + 128],
                                rhs=xTb[:, kc, 0:T], start=(kc == 0), stop=(kc == 7)),
                                reads=[bslot, b_xTb], writes=[b_ps[bk]])
                        g, gb = divmod(blk, 8)

                        def proj_evac(srcap, lo, hi, bkb, blk=blk, g=g, gb=gb):
                            if blk < 24 and gb < 3:
                                pb = g * 3 + gb
                            elif 29 <= blk < 32:
                                pb = 9 + blk - 29
                            else:
                                pb = None
                            if pb is not None:
                                if kind == "p":
                                    S.act(E("copy", out=PRW[:, pb, 1 + lo:1 + hi], in_=srcap), reads=[bkb], writes=[b_PRW])
                                else:
                                    S.act(E("copy", out=prw_dst(pb), in_=t3(srcap)), reads=[bkb], writes=[b_PRW])
                            elif blk < 24:
                                tile_, btile = ((Rg, b_R), (Kg, b_K), (Vg, b_V))[g]
                                S.dve(E("tensor_copy", out=R_(tile_[:, gb, lo:hi]), in_=srcap), reads=[bkb], writes=[btile])
                            elif blk < 29:
                                S.act(E("activation", out=Gg[:, 3 + blk - 24, lo:hi], in_=srcap, func=AF.Silu), reads=[bkb],
                                      writes=[b_G])
                            else:
                                S.dve(E("tensor_copy", out=SW[:, blk - 32, lo:hi], in_=srcap), reads=[bkb], writes=[b_SW])

                        proj_evac(ps[:, bk, clo:T], clo, T, b_ps[bk])
                        if hpf:
                            bkh = hp_mm8(l, piece, sub, lambda kc: x32[:, kc, 0:2], b_x32)
                            proj_evac(ps[:, bkh, 0:2], 0, 2, b_ps[bkh])

                stage(1)
                for pb in range(12):
                    g, gb = divmod(pb, 3)
                    tile_, btile = ((Rg, b_R), (Kg, b_K), (Vg, b_V), (XS, b_XS))[g]
                    tb = pb % 8
                    d3 = t3(tmp[tb][:, 0:T])
                    S.dve(E("tensor_tensor", out=d3, in0=prw_prev(pb), in1=prw_dst(pb), op=ALU.subtract),
                          reads=[b_PRW], writes=[b_tmp[tb]])
                    dst3 = t3(tile_[:, gb, 0:T])
                    if g <= 2:
                        dst3 = R_(dst3)
                    S.dve(E("scalar_tensor_tensor",
                        out=dst3, in0=d3, scalar=V_("mu", pb), in1=prw_dst(pb), op0=ALU.mult, op1=ALU.add),
                        reads=[b_PRW, b_tmp[tb], bvec], writes=[btile])
                if l == 0:
                    dbg("PRW0", PRW[:, 0, 0:80], [b_PRW]); dbg("Rsh", Rg[:, 0, 0:T], [b_R]); dbg("shst", shst[:, 0, :], [b_shst])
                lastcol = PRW[:, :, 0:nseq * (L + 1)].rearrange("p a (s t) -> p a s t", t=L + 1)[:, :, :, L]
                if kind == "s":
                    S.act(E("copy", out=sho[:], in_=lastcol), reads=[b_PRW], writes=[b_sho])
                    S.dma(E("dma_start", out=sho_d[l][:, 12:12 * (1 + NSS)], in_=sho[:].rearrange("p a b -> p (a b)")),
                          reads=[b_sho], chan="shout")
                else:
                    S.act(E("copy", out=shc[:, l, :], in_=lastcol[:, :, 0]), reads=[b_PRW],
                          writes=[b_shc[l]])
                    if last_prompt:
                        S.dma(E("dma_start", out=sho_d[l][:, 0:12], in_=shc[:, l, :]), reads=[b_shc[l]],
                              chan="shout")

                stage(2)
                for gi, (tile_, btile) in enumerate(((Rg, b_R), (Kg, b_K))):
                    a = tile_[:, 5:8, 0:T]
                    sw = SW[:, gi * 3:gi * 3 + 3, 0:T]
                    S.dve(E("tensor_tensor", out=R_(a), in0=a, in1=cosT[:, :, 0:T], op=ALU.mult),
                          reads=[b_rope, btile], writes=[btile])
                    S.dve(E("tensor_tensor", out=sw, in0=sw, in1=sinT[:, :, 0:T], op=ALU.mult),
                          reads=[b_rope, b_SW], writes=[b_SW])
                    S.dve(E("tensor_tensor", out=R_(a), in0=a, in1=sw, op=ALU.add),
                          reads=[b_SW, btile], writes=[btile])

                thx, bthx = tmpb[0], b_tmpb[0]
                sgx, bsgx = tmpb[1], b_tmpb[1]
                sgx2, bsgx2 = tmpb[2], b_tmpb[2]
                gkb, bgkb_ = tmpb[3], b_tmpb[3]
                S.act(E("activation", out=thx[0:64, 0:T], in_=XS[0:64, 0, 0:T], func=AF.Tanh), reads=[b_XS], writes=[bthx])
                S.act(E("copy", out=thx[64:128, 0:T], in_=XS[64:128, 0, 0:T]), reads=[b_XS], writes=[bthx])
                S.act(E("activation", out=sgx[:, 0:T], in_=XS[:, 1, 0:T], func=AF.Sigmoid), reads=[b_XS], writes=[bsgx])
                S.act(E("activation", out=sgx2[0:32, 0:T], in_=XS[0:32, 2, 0:T], func=AF.Sigmoid), reads=[b_XS], writes=[bsgx2])
                S.act(E("copy", out=gkb[32:64, 0:T], in_=XS[32:64, 2, 0:T]), reads=[b_XS], writes=[bgkb_])

                for b in range(8):
                    t_ld, t_lw, t_W, t_IW, t_a, t_kk, t_x, t_y, t_r = tmp
                    bl_ld, bl_lw, bl_W, bl_IW, bl_a, bl_kk, bl_x, bl_y, bl_r = b_tmp
                    if b < 3:
                        c0 = b * 128
                        bk = nbank()
                        S.pe(E("matmul", ps[:, bk, 0:T], lhsT=mat[0:64, c0:c0 + 128], rhs=thx[0:64, 0:T],
                                                              start=True, stop=True), reads=[bmat, bthx], writes=[b_ps[bk]])
                        S.act(E("activation", out=t_ld[:, 0:T], in_=ps[:, bk, 0:T], func=AF.Sigmoid,
                                                                 bias=V_("w0", b), scale=1.0),
                              reads=[b_ps[bk], bvec], writes=[bl_ld])
                        S.dve(E("tensor_scalar", out=t_ld[:, 0:T], in0=t_ld[:, 0:T], scalar1=-math.exp(-0.5), scalar2=None,
                                                        op0=ALU.mult), reads=[bl_ld], writes=[bl_ld])
                        bk = nbank()
                        S.pe(E("matmul", ps[:, bk, 0:T], lhsT=mat[64:128, c0:c0 + 128], rhs=thx[64:128, 0:T],
                                                              start=True, stop=True), reads=[bmat, bthx], writes=[b_ps[bk]])
                        S.act(E("activation", out=t_a[:, 0:T], in_=ps[:, bk, 0:T], func=AF.Sigmoid,
                                                                 bias=V_("a0", b), scale=1.0),
                              reads=[b_ps[bk], bvec], writes=[bl_a])
                        bk = nbank()
                        S.pe(E("matmul", ps[:, bk, 0:T], lhsT=mat[:, 384 + c0:384 + c0 + 128], rhs=sgx[:, 0:T],
                                                              start=True, stop=False), reads=[bmat, bsgx], writes=[b_ps[bk]])
                        S.pe(E("matmul", ps[:, bk, 0:T], lhsT=mat[0:32, 768 + c0:768 + c0 + 128],
                                                              rhs=sgx2[0:32, 0:T], start=False, stop=True),
                             reads=[bmat, bsgx2], writes=[b_ps[bk]])
                        S.act(E("copy", out=Gg[:, b, 0:T], in_=ps[:, bk, 0:T]), reads=[b_ps[bk]], writes=[b_G])
                        S.dve(E("tensor_scalar", out=t_kk[:, 0:T], in0=Kg[:, b, 0:T], scalar1=V_("kkv", b), scalar2=None,
                                                             op0=ALU.mult), reads=[b_K, bvec], writes=[bl_kk])
                        S.act(E("activation", out=R_(t_r[:, 0:T]), in_=t_kk[:, 0:T], func=AF.Square), reads=[bl_kk],
                              writes=[bl_r])
                        bk = nbank()
                        S.pe(E("matmul", ps[:, bk, 0:T], lhsT=R_(bonesr[:]), rhs=R_(t_r[:, 0:T]), start=True, stop=True),
                             reads=[b_bones, bl_r], writes=[b_ps[bk]])
                        S.act(E("activation", out=t_x[:, 0:T], in_=ps[:, bk, 0:T], func=AF.Sqrt), reads=[b_ps[bk]],
                              writes=[bl_x])
                        S.dve(E("tensor_scalar", out=t_x[:, 0:T], in0=t_x[:, 0:T], scalar1=1e-12, scalar2=None, op0=ALU.max),
                              reads=[bl_x], writes=[bl_x])
                        S.dve(E("reciprocal", out=t_x[:, 0:T], in_=t_x[:, 0:T]), reads=[bl_x], writes=[bl_x])
                        S.dve(E("tensor_tensor", out=t_kk[:, 0:T], in0=t_kk[:, 0:T], in1=t_x[:, 0:T], op=ALU.mult),
                              reads=[bl_x, bl_kk], writes=[bl_kk])
                        S.dve(E("tensor_scalar", out=t_x[:, 0:T], in0=t_a[:, 0:T], scalar1=V_("ka", b),
                                                             scalar2=vx[:, b:b + 1], op0=ALU.mult, op1=ALU.add),
                              reads=[bl_a, bvec, bvx], writes=[bl_x])
                        S.dve(E("tensor_tensor", out=R_(Kg[:, b, 0:T]), in0=Kg[:, b, 0:T], in1=t_x[:, 0:T], op=ALU.mult),
                              reads=[bl_x, b_K], writes=[b_K])
                        S.dve(E("scalar_tensor_tensor", out=R_(t_r[:, 0:T]), in0=Rg[:, b, 0:T], scalar=V_("rk", b),
                                                                    in1=Kg[:, b, 0:T], op0=ALU.mult, op1=ALU.mult),
                              reads=[b_R, b_K, bvec], writes=[bl_r])
                        bk = nbank()
                        S.pe(E("matmul", ps[:, bk, 0:T], lhsT=R_(bonesr[:]), rhs=R_(t_r[:, 0:T]), start=True, stop=True),
                             reads=[b_bones, bl_r], writes=[b_ps[bk]])
                        S.dve(E("tensor_tensor", out=bonus[:, b, 0:T], in0=ps[:, bk, 0:T], in1=Vg[:, b, 0:T],
                                                                    op=ALU.mult), reads=[b_ps[bk], b_V], writes=[b_bonus])
                    elif b < 6:
                        c0 = (b - 3) * 128
                        np_ = 64 if b == 5 else 128
                        bk = nbank()
                        S.pe(E("matmul", ps[0:np_, bk, 0:T], lhsT=mat[32:48, 1152 + c0:1152 + c0 + np_],
                                                                       rhs=gkb[32:48, 0:T], start=True, stop=True),
                             reads=[bmat, bgkb_], writes=[b_ps[bk]])
                        S.act(E("activation", out=t_ld[0:np_, 0:T], in_=ps[0:np_, bk, 0:T], func=AF.Exp,
                                                                          bias=vx[0:np_, b:b + 1], scale=-1.0),
                              reads=[b_ps[bk], bvx], writes=[bl_ld])
                        S.act(E("activation", out=t_ld[0:np_, 0:T], in_=t_ld[0:np_, 0:T], func=AF.Ln, bias=1.0,
                                                              scale=1.0), reads=[bl_ld], writes=[bl_ld])
                        S.dve(E("tensor_scalar", out=t_ld[0:np_, 0:T], in0=t_ld[0:np_, 0:T], scalar1=-1.0 / 16.0,
                                                                 scalar2=None, op0=ALU.mult), reads=[bl_ld], writes=[bl_ld])
                        if b == 5:
                            S.dve(E("tensor_scalar", out=t_ld[64:128, 0:T], in0=rmask[64:128, 0:T], scalar1=0.0,
                                                                 scalar2=V_("ldret", b)[64:128], op0=ALU.mult, op1=ALU.add),
                                  reads=[b_rmask, bvec], writes=[bl_ld])
                    else:
                        S.dve(E("tensor_scalar", out=t_ld[:, 0:T], in0=rmask[:, 0:T], scalar1=0.0,
                                                             scalar2=V_("ldret", b), op0=ALU.mult, op1=ALU.add),
                              reads=[b_rmask, bvec], writes=[bl_ld])
                    S.dve(E("tensor_tensor_scan", out=t_lw[:, 0:T], data0=rmask[:, 0:T], data1=t_ld[:, 0:T], initial=0.0,
                                                         op0=ALU.mult, op1=ALU.add), reads=[bl_ld, b_rmask], writes=[bl_lw])
                    S.act(E("activation", out=t_W[:, 0:T], in_=t_lw[:, 0:T], func=AF.Exp), reads=[bl_lw], writes=[bl_W])
                    S.act(E("activation", out=t_IW[:, 0:T], in_=t_lw[:, 0:T], func=AF.Exp, scale=-1.0), reads=[bl_lw],
                          writes=[bl_IW])
                    S.act(E("copy", out=WC[:, b, 0:nch],
                                                              in_=t_W[:, 0:T].rearrange("p (c t) -> p c t", t=C)[:, :, C - 1]),
                          reads=[bl_W], writes=[b_WC])
                    if b < 3:
                        S.dve(E("tensor_tensor", out=t_y[:, 0:T], in0=t_lw[:, 0:T], in1=t_ld[:, 0:T], op=ALU.subtract),
                              reads=[bl_lw, bl_ld], writes=[bl_y])
                        S.act(E("activation", out=t_y[:, 0:T], in_=t_y[:, 0:T], func=AF.Exp), reads=[bl_y], writes=[bl_y])
                        S.dve(E("scalar_tensor_tensor", out=R_(ATt[:, b, 0:T]), in0=t_kk[:, 0:T], scalar=-1.0,
                                                                    in1=t_y[:, 0:T], op0=ALU.mult, op1=ALU.mult),
                              reads=[bl_kk, bl_y], writes=[b_AT])
                        S.dve(E("tensor_tensor", out=t_y[:, 0:T], in0=t_kk[:, 0:T], in1=t_a[:, 0:T], op=ALU.mult),
                              reads=[bl_kk, bl_a], writes=[bl_y])
                        S.dve(E("tensor_tensor", out=R_(BTt[:, b, 0:T]), in0=t_y[:, 0:T], in1=t_IW[:, 0:T], op=ALU.mult),
                              reads=[bl_y, bl_IW], writes=[b_BT])
                    S.dve(E("scalar_tensor_tensor", out=R_(Rg[:, b, 0:T]), in0=Rg[:, b, 0:T], scalar=V_("rscale", b),
                                                                in1=t_W[:, 0:T], op0=ALU.mult, op1=ALU.mult),
                          reads=[b_R, bl_W, bvec], writes=[b_R])
                    S.dve(E("scalar_tensor_tensor", out=R_(Kg[:, b, 0:T]), in0=Kg[:, b, 0:T], scalar=V_("kscale", b),
                                                                in1=t_IW[:, 0:T], op0=ALU.mult, op1=ALU.mult),
                          reads=[b_K, bl_IW, bvec], writes=[b_K])

                stage(3)
                for ci in range(nch):
                    t0 = ci * C
                    cs = slice(t0, t0 + C)
                    cfg.setdefault("_marks", []).append((l, ci, len(S.ops)))
                    if kind == "p":
                        St, bSt = Sst[l], b_Sst[l]
                    else:
                        sp_ = ci % 2
                        St, bSt = Ssm[sp_], b_Ssm[sp_]
                        S.dma(E("dma_start", out=Sld[sp_][:].rearrange("p a b -> p (a b)"),
                                                                         in_=sst_d[l, ci]),
                              writes=[b_Sld[sp_]], chan="sld0")
                        S.act(E("copy", out=R_(St[:]), in_=Sld[sp_][:]), reads=[b_Sld[sp_]], writes=[bSt])
                    for (src, bsrc, dst, bdst, nb_) in ((Vg, b_V, vtok, b_vtok, 8), (Kg, b_K, ktok, b_ktok, 8),
                                                        (BTt, b_BT, btok, b_btok, 3)):
                        for half in range((nb_ + 3) // 4):
                            bk = nbank()
                            n_in = min(4, nb_ - half * 4)
                            for bl in range(n_in):
                                blk = half * 4 + bl
                                S.pe(E("transpose",
                                    ps[0:C, bk, bl * 128:(bl + 1) * 128], src[:, blk, cs], ident),
                                    reads=[bsrc, b_cst], writes=[b_ps[bk]])
                            w = n_in * 128
                            S.act(E("copy", out=R_(dst[0:C, half * 512:half * 512 + w]),
                                                                                   in_=ps[0:C, bk, 0:w]),
                                  reads=[b_ps[bk]], writes=[bdst])
                    stage(31)
                    def hp(h):
                        return slice((h % 2) * 64, (h % 2) * 64 + 64), h // 2

                    def pv3(bk, n):
                        return ps[0:C, bk, 0:n * C].rearrange("p (h c) -> p h c", c=C)

                    def bc(m, n):
                        return m[0:C, 0:C].unsqueeze(1).to_broadcast([C, n, C])

                    def par3(t, par, n):
                        return t[0:C, :, 0:C].rearrange("p (a two) c -> p a two c", two=2)[:, :, par, :]

                    hpb = min(8, 512 // C)
                    for (lt, blt, rt_, brt, dst, bdst, msk, nh) in ((Kg, b_K, Rg, b_R, ark, b_ark, m_incl, 16),
                                                                   (Kg, b_K, ATt, b_AT, aak, b_aak, m_su, 6),
                                                                   (BTt, b_BT, Rg, b_R, arb, b_arb, m_incl, 6),
                                                                   (BTt, b_BT, ATt, b_AT, Pm[0], b_Pm[0], m_su, 6),
                                                                   (ATt, b_AT, BTt, b_BT, Qm[0], b_Qm[0], m_sl, 6)):
                        npar = nh // 2
                        ngrp = (npar + hpb - 1) // hpb
                        banks = {}
                        for g_ in range(ngrp):
                            for par in range(2):
                                banks[(par, g_)] = nbank()
                        for h in range(nh):
                            rows, blk = hp(h)
                            hl = h // 2
                            bk = banks[(h % 2, hl // hpb)]
                            hi = hl % hpb
                            S.pe(E("matmul", ps[0:C, bk, hi * C:(hi + 1) * C], lhsT=R_(lt[rows, blk, cs]),
                                   rhs=R_(rt_[rows, blk, cs]), start=True, stop=True), reads=[blt, brt], writes=[b_ps[bk]])
                        for g_ in range(ngrp):
                            n_ = min(hpb, npar - g_ * hpb)
                            for par in range(2):
                                bk = banks[(par, g_)]
                                o_ = par3(dst, par, nh)[:, g_ * hpb:g_ * hpb + n_, :]
                                if dst is not Pm[0] and dst is not Qm[0]:
                                    o_ = R_(o_)
                                S.dve(E("tensor_tensor", out=o_, in0=pv3(bk, n_),
                                        in1=bc(msk, n_), op=ALU.mult), reads=[b_ps[bk], b_cst], writes=[bdst])
                    if ci == cfg.get("dbgci", 0) and l == 0:
                        dbg("AT0", ATt[:, 0, cs], [b_AT]); dbg("BT0", BTt[:, 0, cs], [b_BT])
                        dbg("K0", Kg[:, 0, cs], [b_K]); dbg("R0", Rg[:, 0, cs], [b_R]); dbg("V0", Vg[:, 0, cs], [b_V])
                        dbg("WC", WC[:, :, 0:4], [b_WC])
                        dbg("P0", Pm[0][0:C, :, 0:C], [b_Pm[0]], C); dbg("Q0", Qm[0][0:C, :, 0:C], [b_Qm[0]], C)
                        dbg("aak", aak[0:C, :, 0:C], [b_aak], C); dbg("arb", arb[0:C, :, 0:C], [b_arb], C)
                        dbg("btok", btok[0:C, 0:128], [b_btok], C)
                    stage(32)
                    S.dve(E("tensor_tensor", out=R_(Xm[0:C, :, 0:C]), in0=Pm[0][0:C, :, 0:C], in1=bc(eye64, 6), op=ALU.add),
                          reads=[b_Pm[0], b_cst], writes=[b_Xm])
                    S.act(E("copy", out=Xb[0:C, :, 0:C], in_=Xm[0:C, :, 0:C]), reads=[b_Xm], writes=[b_Xb])
                    ng_ = 1 if 6 * C <= 512 else 2
                    hg_ = 6 // ng_

                    def inv_mm(lhs_t, blhs, rhs_t, brhs):
                        bks = []
                        for g_ in range(ng_):
                            bk = nbank()
                            bks.append(bk)
                            for hi in range(hg_):
                                h = g_ * hg_ + hi
                                S.pe(E("matmul", ps[0:C, bk, hi * C:(hi + 1) * C], lhsT=lhs_t[0:C, h, 0:C],
                                       rhs=rhs_t[0:C, h, 0:C], start=True, stop=True), reads=[blhs, brhs], writes=[b_ps[bk]])
                        return bks

                    def gsl(t, g_):
                        return t[0:C, g_ * hg_:(g_ + 1) * hg_, 0:C]

                    def par2(t, par, n):
                        return t[0:C, 0:n * 64].rearrange("p (a two v) -> p a two v", two=2, v=64)[:, :, par, :]

                    def pv64(bk, n):
                        return ps[0:C, bk, 0:n * 64].rearrange("p (h v) -> p h v", v=64)

                    def f_y_state():
                        bkp = (nbank(), nbank())
                        for h in range(16):
                            rows, blk = hp(h)
                            bk = bkp[h % 2]
                            hl = h // 2
                            S.pe(E("matmul", ps[0:C, bk, hl * 64:(hl + 1) * 64], lhsT=R_(Rg[rows, blk, cs]), rhs=R_(St[rows, blk, :]),
                                   start=True, stop=True), reads=[b_R, bSt], writes=[b_ps[bk]])
                        for par in range(2):
                            S.act(E("copy", out=par2(ysb, par, 16), in_=pv64(bkp[par], 8)), reads=[b_ps[bkp[par]]], writes=[b_ysb])

                    def f_rhs_state():
                        bkp = (nbank(), nbank())
                        for h in range(6):
                            rows, blk = hp(h)
                            bk = bkp[h % 2]
                            hl = h // 2
                            S.pe(E("matmul", ps[0:C, bk, hl * 64:(hl + 1) * 64], lhsT=R_(ATt[rows, blk, cs]), rhs=R_(St[rows, blk, :]),
                                   start=True, stop=True), reads=[b_AT, bSt], writes=[b_ps[bk]])
                        for par in range(2):
                            S.act(E("copy", out=R_(par2(rhs0, par, 6)), in_=pv64(bkp[par], 3)), reads=[b_ps[bkp[par]]], writes=[b_rhs0])

                    def f_rhs_tok():
                        bk = nbank()
                        for h in range(6):
                            S.pe(E("matmul", ps[0:C, bk, h * 64:(h + 1) * 64], lhsT=R_(aak[0:C, h, 0:C]),
                                   rhs=R_(vtok[0:C, h * 64:(h + 1) * 64]), start=True, stop=True),
                                 reads=[b_aak, b_vtok], writes=[b_ps[bk]])
                        S.dve(E("tensor_tensor", out=R_(rhs0[0:C, :]), in0=rhs0[0:C, :], in1=ps[0:C, bk, 0:384], op=ALU.add),
                              reads=[b_ps[bk], b_rhs0], writes=[b_rhs0])

                    def f_y_tok(half):
                        bk = nbank()
                        for hl in range(8):
                            h = half * 8 + hl
                            S.pe(E("matmul", ps[0:C, bk, hl * 64:(hl + 1) * 64], lhsT=R_(ark[0:C, h, 0:C]),
                                   rhs=R_(vtok[0:C, h * 64:(h + 1) * 64]), start=True, stop=(h >= 6)),
                                 reads=[b_ark, b_vtok], writes=[b_ps[bk]])
                            if h < 6:
                                S.pe(E("matmul", ps[0:C, bk, hl * 64:(hl + 1) * 64], lhsT=R_(arb[0:C, h, 0:C]),
                                       rhs=R_(usb[0:C, h * 64:(h + 1) * 64]), start=False, stop=True),
                                     reads=[b_arb, b_usb], writes=[b_ps[bk]])
                        S.dve(E("tensor_tensor", out=ysb[0:C, half * 512:(half + 1) * 512], in0=ysb[0:C, half * 512:(half + 1) * 512],
                                in1=ps[0:C, bk, :], op=ALU.add), reads=[b_ps[bk], b_ysb], writes=[b_ysb])

                    fillers = [f_y_state, f_rhs_state, f_rhs_tok, lambda: f_y_tok(1)]
                    for st in range(nsteps):
                        bq = inv_mm(Pm[0], b_Pm[0], Qm[0], b_Qm[0])
                        if st < nsteps - 1:
                            bp = inv_mm(Qm[0], b_Qm[0], Pm[0], b_Pm[0])
                        for g_ in range(ng_):
                            S.act(E("copy", out=gsl(Qm[0], g_), in_=pv3(bq[g_], hg_)), reads=[b_ps[bq[g_]]], writes=[b_Qm[0]])
                        if st < nsteps - 1:
                            for g_ in range(ng_):
                                S.dve(E("tensor_copy", out=gsl(Pm[0], g_), in_=pv3(bp[g_], hg_)), reads=[b_ps[bp[g_]]],
                                      writes=[b_Pm[0]])
                        if fillers:
                            fillers.pop(0)()
                        bx = inv_mm(Qm[0], b_Qm[0], Xb, b_Xb)
                        for g_ in range(ng_):
                            S.dve(E("tensor_tensor", out=R_(gsl(Xm, g_)), in0=gsl(Xm, g_), in1=pv3(bx[g_], hg_), op=ALU.add),
                                  reads=[b_ps[bx[g_]], b_Xm], writes=[b_Xm])
                            if st < nsteps - 1:
                                S.act(E("copy", out=gsl(Xb, g_), in_=gsl(Xm, g_)), reads=[b_Xm], writes=[b_Xb])
                    while fillers:
                        fillers.pop(0)()
                    stage(33)
                    bk = nbank()
                    for h in range(6):
                        S.pe(E("matmul", ps[0:C, bk, h * 64:(h + 1) * 64], lhsT=R_(Xm[0:C, h, 0:C]),
                               rhs=R_(rhs0[0:C, h * 64:(h + 1) * 64]), start=True, stop=True),
                             reads=[b_Xm, b_rhs0], writes=[b_ps[bk]])
                    S.act(E("copy", out=R_(usb[0:C, :]), in_=ps[0:C, bk, 0:384]), reads=[b_ps[bk]], writes=[b_usb])
                    stage(34)
                    f_y_tok(0)
                    stage(35)
                    ds_banks = []
                    for half in range(2):
                        bk = nbank()
                        ds_banks.append(bk)
                        for bl in range(4):
                            blk = half * 4 + bl
                            S.pe(E("matmul", ps[:, bk, bl * 128:(bl + 1) * 128],
                                   lhsT=R_(ktok[0:C, blk * 128:(blk + 1) * 128]),
                                   rhs=R_(vtok[0:C, blk * 128:(blk + 1) * 128]),
                                   start=True, stop=(blk >= 3)), reads=[b_ktok, b_vtok], writes=[b_ps[bk]])
                            if blk < 3:
                                S.pe(E("matmul", ps[:, bk, bl * 128:(bl + 1) * 128],
                                       lhsT=R_(btok[0:C, blk * 128:(blk + 1) * 128]),
                                       rhs=R_(usb[0:C, blk * 128:(blk + 1) * 128]),
                                       start=False, stop=True), reads=[b_btok, b_usb], writes=[b_ps[bk]])
                    stage(36)
                    y3 = ysb[0:C, :].rearrange("p (h v) -> p h v", v=64)
                    q3 = ysq[0:C, :].rearrange("p (h v) -> p h v", v=64)
                    s1, s2, mean, tmpv, rstd = (yst[0:C, i, :] for i in range(5))
                    S.dve(E("tensor_tensor", out=R_(ysq[0:C, :]), in0=ysb[0:C, :], in1=ysb[0:C, :], op=ALU.mult), reads=[b_ysb],
                          writes=[b_ysq])
                    S.dve(E("tensor_reduce", out=s1, in_=y3, axis=AX.X, op=ALU.add), reads=[b_ysb], writes=[b_yst])
                    S.dve(E("tensor_reduce", out=s2, in_=q3, axis=AX.X, op=ALU.add), reads=[b_ysq], writes=[b_yst])
                    S.dve(E("tensor_tensor", out=mean, in0=s1, in1=tokc[0:C, 0:16], op=ALU.mult),
                          reads=[b_yst, b_cst], writes=[b_yst])
                    S.dve(E("tensor_tensor", out=tmpv, in0=mean, in1=mean, op=ALU.mult), reads=[b_yst],
                          writes=[b_yst])
                    S.dve(E("scalar_tensor_tensor", out=tmpv, in0=s2, scalar=1.0 / 64, in1=tmpv, op0=ALU.mult,
                                                                             op1=ALU.subtract), reads=[b_yst], writes=[b_yst])
                    S.dve(E("tensor_tensor", out=tmpv, in0=tmpv, in1=tokc[0:C, 16:32], op=ALU.add),
                          reads=[b_yst, b_cst], writes=[b_yst])
                    S.act(E("activation", out=tmpv, in_=tmpv, func=AF.Sqrt), reads=[b_yst], writes=[b_yst])
                    S.dve(E("reciprocal", out=rstd, in_=tmpv), reads=[b_yst], writes=[b_yst])
                    S.dve(E("tensor_tensor", out=y3, in0=y3, in1=mean.unsqueeze(2).to_broadcast([C, 16, 64]),
                                                                      op=ALU.subtract), reads=[b_yst, b_ysb], writes=[b_ysb])
                    S.dve(E("tensor_tensor", out=y3, in0=y3, in1=rstd.unsqueeze(2).to_broadcast([C, 16, 64]),
                                                                      op=ALU.mult), reads=[b_yst, b_ysb], writes=[b_ysb])
                    bpb = min(8, 512 // C)
                    for g_ in range(8 // bpb):
                        bk = nbank()
                        for bi in range(bpb):
                            blk = g_ * bpb + bi
                            S.pe(E("transpose", ps[:, bk, bi * C:(bi + 1) * C], ysb[0:C, blk * 128:(blk + 1) * 128],
                                   cst[0:C, 0:C]), reads=[b_ysb, b_cst], writes=[b_ps[bk]])
                        S.act(E("copy", out=ynT[:, g_ * bpb:(g_ + 1) * bpb, cs],
                                in_=ps[:, bk, 0:bpb * C].rearrange("p (a c) -> p a c", c=C)), reads=[b_ps[bk]], writes=[b_ynT])

                    for half in range(2):
                        bk = ds_banks[half]
                        for hh in range(2):
                            rows = slice(hh * 64, hh * 64 + 64)
                            S.dve(E("tensor_tensor",
                                out=R_(St[rows, half * 4:half * 4 + 4, :]), in0=St[rows, half * 4:half * 4 + 4, :],
                                in1=ps[rows, bk, :].rearrange("p (a b) -> p a b", b=128)[:, :, hh * 64:hh * 64 + 64], op=ALU.add),
                                reads=[b_ps[bk], bSt], writes=[bSt])
                    S.dve(E("tensor_tensor", out=R_(St[:]), in0=St[:], in1=WC[:, :, ci:ci + 1].to_broadcast([128, 8, 64]),
                                                           op=ALU.mult), reads=[b_WC, bSt], writes=[bSt])
                    if kind == "s":
                        S.dma(E("dma_start", out=sto_d[l, 1 + ci], in_=St[:].rearrange("p a b -> p (a b)")),
                              reads=[bSt], chan="stout")
                    elif last_prompt and ci == nch - 1:
                        S.dma(E("dma_start", out=sto_d[l, 0], in_=St[:].rearrange("p a b -> p (a b)")),
                              reads=[bSt], chan="stout")

                stage(4)
                for b in range(8):
                    S.dve(E("tensor_scalar", out=ynT[:, b, 0:T], in0=ynT[:, b, 0:T], scalar1=V_("nw", b), scalar2=V_("nb", b),
                                                         op0=ALU.mult, op1=ALU.add), reads=[b_ynT, bvec], writes=[b_ynT])
                    if b < 3:
                        S.dve(E("tensor_tensor", out=ynT[:, b, 0:T], in0=ynT[:, b, 0:T], in1=bonus[:, b, 0:T], op=ALU.add),
                              reads=[b_ynT, b_bonus], writes=[b_ynT])
                    S.dve(E("tensor_tensor", out=mixT[:, b, 0:T], in0=ynT[:, b, 0:T], in1=Gg[:, b, 0:T], op=ALU.mult),
                          reads=[b_ynT, b_G], writes=[b_mixT])
                    if hpf:
                        S.dve(E("tensor_tensor", out=mix32[:, b, :], in0=ynT[:, b, 0:2], in1=Gg[:, b, 0:2], op=ALU.mult),
                              reads=[b_ynT, b_G], writes=[b_mix32])

                def ln_epilogue(gname, bname):
                    r16, br16 = hT[:, 0:8, :], b_hT
                    S.act(E("copy", out=hT[:, 0:8, 0:T], in_=x32[:, :, 0:T]), reads=[b_x32], writes=[b_hT])
                    S.act(E("activation", out=hT[:, 8:16, 0:T], in_=x32[:, :, 0:T], func=AF.Square), reads=[b_x32], writes=[b_hT])
                    bk1 = nbank(); bk2 = nbank()
                    for kc in range(8):
                        S.pe(E("matmul", ps[:, bk1, 0:T], lhsT=onesb[:], rhs=hT[:, kc, 0:T], start=(kc == 0), stop=(kc == 7)),
                             reads=[b_ones, b_hT], writes=[b_ps[bk1]])
                    for kc in range(8):
                        S.pe(E("matmul", ps[:, bk2, 0:T], lhsT=onesb[:], rhs=hT[:, 8 + kc, 0:T], start=(kc == 0),
                                                       stop=(kc == 7)), reads=[b_ones, b_hT], writes=[b_ps[bk2]])
                    mean, var = tmp[0], tmp[1]
                    S.act(E("mul", out=mean[:, 0:T], in_=ps[:, bk1, 0:T], mul=1.0 / D), reads=[b_ps[bk1]], writes=[b_tmp[0]])
                    S.dve(E("tensor_tensor", out=var[:, 0:T], in0=mean[:, 0:T], in1=mean[:, 0:T], op=ALU.mult), reads=[b_tmp[0]],
                          writes=[b_tmp[1]])
                    S.dve(E("scalar_tensor_tensor", out=var[:, 0:T], in0=ps[:, bk2, 0:T], scalar=1.0 / D, in1=var[:, 0:T],
                                                           op0=ALU.mult, op1=ALU.subtract), reads=[b_ps[bk2], b_tmp[1]], writes=[b_tmp[1]])
                    S.dve(E("tensor_scalar", out=var[:, 0:T], in0=var[:, 0:T], scalar1=1e-5, scalar2=None, op0=ALU.add),
                          reads=[b_tmp[1]], writes=[b_tmp[1]])
                    S.act(E("activation", out=var[:, 0:T], in_=var[:, 0:T], func=AF.Sqrt), reads=[b_tmp[1]], writes=[b_tmp[1]])
                    S.dve(E("reciprocal", out=var[:, 0:T], in_=var[:, 0:T]), reads=[b_tmp[1]], writes=[b_tmp[1]])
                    S.dve(E("tensor_tensor", out=x32[:, :, 0:T], in0=x32[:, :, 0:T],
                                                    in1=mean[:, 0:T].unsqueeze(1).to_broadcast([128, 8, T]), op=ALU.subtract),
                          reads=[b_tmp[0], b_x32], writes=[b_x32])
                    S.dve(E("tensor_tensor", out=x32[:, :, 0:T], in0=x32[:, :, 0:T],
                                                    in1=var[:, 0:T].unsqueeze(1).to_broadcast([128, 8, T]), op=ALU.mult),
                          reads=[b_tmp[1], b_x32], writes=[b_x32])
                    for b in range(8):
                        S.dve(E("tensor_scalar", out=x32[:, b, 0:T], in0=x32[:, b, 0:T], scalar1=V_(gname, b),
                                                             scalar2=V_(bname, b), op0=ALU.mult, op1=ALU.add),
                              reads=[b_x32, bvec], writes=[b_x32])
                    S.act(E("copy", out=xTb[:, :, 0:T], in_=x32[:, :, 0:T]), reads=[b_x32], writes=[b_xTb])

                stage(5)
                for piece in range(4):
                    slot, bslot = ws_next(l, NPIECE_IN + piece)
                    for sub in range(2):
                        cb = piece * 2 + sub
                        bk = nbank()
                        for kc in range(8):
                            S.pe(E("matmul",
                                ps[:, bk, 0:T], lhsT=slot[:, kc * 256 + sub * 128: kc * 256 + sub * 128 + 128],
                                rhs=mixT[:, kc, 0:T], start=(kc == 0), stop=(kc == 7)), reads=[bslot, b_mixT], writes=[b_ps[bk]])
                        if hpf:
                            bkh = hp_mm8(l, NPIECE_IN + piece, sub, lambda kc: mix32[:, kc, 0:2], b_mix32)
                            S.dve(E("scalar_tensor_tensor", out=x32[:, cb, 0:2], in0=x32[:, cb, 0:2], scalar=ALPHA,
                                    in1=ps[:, bkh, 0:2], op0=ALU.mult, op1=ALU.add), reads=[b_ps[bkh], b_x32], writes=[b_x32])
                        S.dve(E("scalar_tensor_tensor", out=x32[:, cb, clo:T], in0=x32[:, cb, clo:T], scalar=ALPHA,
                                                                             in1=ps[:, bk, clo:T], op0=ALU.mult, op1=ALU.add),
                              reads=[b_ps[bk], b_x32], writes=[b_x32])
                ln_epilogue("ln1g", "ln1b")

                stage(6)
                if kind == "s":
                    S.dma(E("dma_start", out=cvst[:].rearrange("p a b c -> p (a b c)"), in_=scv_d[l]), writes=[b_cvst],
                          chan="cvst")
                ffn_defer = [None]
                for j in range(NJ):
                    slot, bslot = ws_next(l, NPIECE_IN + 4 + j)
                    cv = []
                    for ab in range(2):
                        cbk = j + NJ * ab
                        bk = nbank()
                        for kc in range(8):
                            S.pe(E("matmul",
                                ps[:, bk, 0:T], lhsT=slot[:, kc * 256 + ab * 128: kc * 256 + ab * 128 + 128],
                                rhs=xTb[:, kc, 0:T], start=(kc == 0), stop=(kc == 7)), reads=[bslot, b_xTb], writes=[b_ps[bk]])
                        ub, bub = ubuf[ab + 2 * (j % 2)], b_ubuf[ab + 2 * (j % 2)]
                        u3 = ub[:, 0:nseq * (L + 2)].rearrange("p (s t) -> p s t", t=L + 2)
                        if kind == "s":
                            S.act(E("copy", out=u3[:, :, 0:2], in_=cvst[:, cbk, :, :]), reads=[b_cvst],
                                  writes=[bub])
                        else:
                            S.act(E("copy", out=u3[:, 0, 0:2], in_=cvc[:, l, cbk, :]),
                                  reads=[b_cvc[l]], writes=[bub])
                        if hpf:
                            bkh = hp_mm8(l, NPIECE_IN + 4 + j, ab, lambda kc: x32[:, kc, 0:2], b_x32)
                            S.act(E("copy", out=u3[:, 0, 2:4], in_=ps[:, bkh, 0:2]), reads=[b_ps[bkh]], writes=[bub])
                            S.act(E("copy", out=u3[:, 0, 4:L + 2], in_=ps[:, bk, 2:T]), reads=[b_ps[bk]], writes=[bub])
                        else:
                            S.act(E("copy", out=u3[:, :, 2:L + 2], in_=t3(ps[:, bk, 0:T])), reads=[b_ps[bk]], writes=[bub])
                        if kind == "s":
                            S.act(E("copy", out=cvo[:, cbk, :, :], in_=u3[:, :, L:L + 2]), reads=[bub], writes=[b_cvo])
                        else:
                            S.act(E("copy", out=cvc[:, l, cbk, :], in_=u3[:, 0, L:L + 2]), reads=[bub],
                                  writes=[b_cvc[l]])
                        ct, bct = tmp[2 + ab + 2 * (j % 2)], b_tmp[2 + ab + 2 * (j % 2)]
                        c3 = t3(ct[:, 0:T])
                        S.pool(E("tensor_scalar", out=c3, in0=u3[:, :, 0:L], scalar1=V_("cw0", cbk),
                                                                               scalar2=V_("cb", cbk), op0=ALU.mult, op1=ALU.add),
                              reads=[bub, bvec], writes=[bct])
                        S.dve(E("scalar_tensor_tensor", out=c3, in0=u3[:, :, 1:L + 1], scalar=V_("cw1", cbk),
                                                                                      in1=c3, op0=ALU.mult, op1=ALU.add),
                              reads=[bub, bvec, bct], writes=[bct])
                        S.dve(E("scalar_tensor_tensor", out=c3, in0=u3[:, :, 2:L + 2], scalar=V_("cw2", cbk),
                                                                                      in1=c3, op0=ALU.mult, op1=ALU.add),
                              reads=[bub, bvec, bct], writes=[bct])
                        cv.append((ct, bct))
                    (ca, bca), (cb_, bcb) = cv

                    def gelu_h(j=j, ca=ca, bca=bca, cb_=cb_, bcb=bcb):
                        S.act(E("activation", out=ca[:, 0:T], in_=ca[:, 0:T], func=AF.Gelu_apprx_tanh), reads=[bca], writes=[bca])
                        S.dve(E("tensor_tensor", out=hT[:, j, 0:T], in0=ca[:, 0:T], in1=cb_[:, 0:T], op=ALU.mult),
                              reads=[bca, bcb], writes=[b_hT])
                        if hpf:
                            S.dve(E("tensor_tensor", out=h32[:, j, :], in0=ca[:, 0:2], in1=cb_[:, 0:2], op=ALU.mult),
                                  reads=[bca, bcb], writes=[b_h32])

                    if ffn_defer[0] is not None:
                        ffn_defer[0]()
                    ffn_defer[0] = gelu_h
                ffn_defer[0]()
                ffn_defer[0] = None
                if kind == "s":
                    S.dma(E("dma_start", out=cvo_d[l][:, 88:88 * (1 + NSS)], in_=cvo[:].rearrange("p a b c -> p (a b c)")),
                          reads=[b_cvo], chan="cvout")
                elif last_prompt:
                    S.dma(E("dma_start", out=cvo_d[l][:, 0:88], in_=cvc[:, l, :, :].rearrange("p a c -> p (a c)")),
                          reads=[b_cvc[l]], chan="cvout")
                for cb in range(8):
                    slot, bslot = ws_next(l, NPIECE_IN + 4 + NJ + cb)
                    bk = nbank()
                    for j in range(NJ):
                        S.pe(E("matmul", ps[:, bk, 0:T], lhsT=slot[:, j * 128:(j + 1) * 128], rhs=hT[:, j, 0:T],
                                                                       start=(j == 0), stop=(j == NJ - 1)),
                             reads=[bslot, b_hT], writes=[b_ps[bk]])
                    if hpf:
                        bkh = nbank()
                        for jh in range(3):
                            njj = min(8, NJ - jh * 8)
                            w, bw = hp_load(l, NPIECE_IN + 4 + NJ + cb, ("jh", jh, njj))
                            for jj in range(njj):
                                j = jh * 8 + jj
                                S.pe(E("matmul", ps[:, bkh, 0:2], lhsT=w[:, jj * 128:(jj + 1) * 128], rhs=h32[:, j, 0:2],
                                       start=(j == 0), stop=(j == NJ - 1)), reads=[bw, b_h32], writes=[b_ps[bkh]])
                        S.dve(E("scalar_tensor_tensor", out=x32[:, cb, 0:2], in0=x32[:, cb, 0:2], scalar=ALPHA,
                                in1=ps[:, bkh, 0:2], op0=ALU.mult, op1=ALU.add), reads=[b_ps[bkh], b_x32], writes=[b_x32])
                    S.dve(E("scalar_tensor_tensor", out=x32[:, cb, clo:T], in0=x32[:, cb, clo:T], scalar=ALPHA,
                                                                         in1=ps[:, bk, clo:T], op0=ALU.mult, op1=ALU.add),
                          reads=[b_ps[bk], b_x32], writes=[b_x32])
                stage(7)
                ln_epilogue("ln2g", "ln2b")

            S.dma(E("dma_start", out=yT_d[:, :, tok0:tok0 + T], in_=x32[:, :, 0:T]), reads=[b_x32], chan="yout")


    try:
        _tile_loop()
        assert ws["used"] == len(plan)
    except _Stop:
        pass
    S.run_block()
    return nc, S


_CACHE = {}


def _prep_inputs(inp, cfg):
    f = np.float32
    W = _prep_weights(inp)
    vm = [_prep_layer_params(inp, l) for l in range(NL)]
    vecs = np.stack([v for v, _ in vm])
    mats = np.stack([m for _, m in vm])
    cst = _consts()
    cosT, sinT = _rope_tables()
    in_maps = []
    for c in range(8):
        xp = inp["x_prompt"][c]
        xs = inp["x_sample"][c * NSS:(c + 1) * NSS].reshape(NSS * LS, D)
        x = np.concatenate([xp, xs], axis=0)
        xT = np.ascontiguousarray(x.T.reshape(8, 128, -1).transpose(1, 0, 2))
        sst = np.zeros((NL, NSS, 16, 64, 64), f)
        sl = slice(c * NSS, (c + 1) * NSS)
        sst[:, :, 0:6] = inp["state_rwkv"][:, sl].transpose(0, 1, 2, 4, 3)
        sst[:, :, 6:11, 0:32] = inp["state_gla"][:, sl]
        sst[:, :, 11:16] = inp["state_ret"][:, sl]
        sst = sst.reshape(NL, NSS, 8, 2, 64, 64).transpose(0, 1, 3, 4, 2, 5).reshape(NL, NSS, 128, 512)
        sh = inp["state_shift"][:, sl]
        shp = np.zeros((NL, NSS, 12 * 128), f)
        shp[:, :, 0:1152] = sh[:, :, 0:1152]
        shp[:, :, 1152:1408] = sh[:, :, 1152:1408]
        shp[:, :, 1408:1440] = sh[:, :, 1408:1440]
        ssh = shp.reshape(NL, NSS, 12, 128).transpose(0, 3, 2, 1).reshape(NL, 128, 12 * NSS)
        cvs = inp["state_conv"][:, sl]
        scv = cvs.reshape(NL, NSS, 2, 44, 128).transpose(0, 4, 3, 1, 2).reshape(NL, 128, 44 * NSS * 2)
        in_maps.append({"xT": xT, "wts": W, "vecs": vecs, "mats": mats, "cst": cst, "ropec": cosT, "ropes": sinT,
                        "sst": np.ascontiguousarray(sst), "ssh": np.ascontiguousarray(ssh), "scv": np.ascontiguousarray(scv)})
    return in_maps


def _assemble(results):
    f = np.float32
    y_p = np.zeros((8, SEQ, D), f); y_s = np.zeros((8 * NSS, LS, D), f)
    rw_p = np.zeros((NL, 8, 6, 64, 64), f); rw_s = np.zeros((NL, 8 * NSS, 6, 64, 64), f)
    sh_p = np.zeros((NL, 8, 1440), f); sh_s = np.zeros((NL, 8 * NSS, 1440), f)
    gl_p = np.zeros((NL, 8, 5, 32, 64), f); gl_s = np.zeros((NL, 8 * NSS, 5, 32, 64), f)
    rt_p = np.zeros((NL, 8, 5, 64, 64), f); rt_s = np.zeros((NL, 8 * NSS, 5, 64, 64), f)
    cv_p = np.zeros((NL, 8, 2, 2 * DFF), f); cv_s = np.zeros((NL, 8 * NSS, 2, 2 * DFF), f)
    for c, r in enumerate(results):
        yT = r["yT"]
        y = yT.transpose(1, 0, 2).reshape(D, -1).T
        y_p[c] = y[0:SEQ]
        y_s[c * NSS:(c + 1) * NSS] = y[SEQ:].reshape(NSS, LS, D)
        st = r["sto"].reshape(NL, 1 + NSS, 2, 64, 8, 64).transpose(0, 1, 4, 2, 3, 5).reshape(NL, 1 + NSS, 16, 64, 64)
        rw = st[:, :, 0:6].transpose(0, 1, 2, 4, 3)
        gl = st[:, :, 6:11, 0:32]
        rt = st[:, :, 11:16]
        rw_p[:, c] = rw[:, 0]; gl_p[:, c] = gl[:, 0]; rt_p[:, c] = rt[:, 0]
        sl = slice(c * NSS, (c + 1) * NSS)
        rw_s[:, sl] = rw[:, 1:]; gl_s[:, sl] = gl[:, 1:]; rt_s[:, sl] = rt[:, 1:]
        sho = r["sho"].reshape(NL, 128, 12 * (1 + NSS))
        p = sho[:, :, 0:12].transpose(0, 2, 1).reshape(NL, 12 * 128)
        s = sho[:, :, 12:].reshape(NL, 128, 12, NSS).transpose(0, 3, 2, 1).reshape(NL, NSS, 12 * 128)
        sh_p[:, c] = p[:, 0:1440]
        sh_s[:, sl] = s[:, :, 0:1440]
        cvo = r["cvo"].reshape(NL, 128, 44 * (1 + NSS) * 2)
        p = cvo[:, :, 0:88].reshape(NL, 128, 44, 2).transpose(0, 3, 2, 1).reshape(NL, 2, 2 * DFF)
        s = cvo[:, :, 88:].reshape(NL, 128, 44, NSS, 2).transpose(0, 3, 4, 2, 1).reshape(NL, NSS, 2, 2 * DFF)
        cv_p[:, c] = p
        cv_s[:, sl] = s
    return (y_p, y_s, rw_p, sh_p, gl_p, rt_p, cv_p, rw_s, sh_s, gl_s, rt_s, cv_s)


def kernel(**inputs):
    inp = {k: np.asarray(v) for k, v in inputs.items()}
    cfg = CFG_FULL
    if "nc" not in _CACHE:
        _CACHE["nc"] = build(cfg)[0]
    nc = _CACHE["nc"]
    in_maps = _prep_inputs(inp, cfg)
    res = run_bass_kernel_spmd(nc, in_maps, core_ids=list(range(8)))
    return _assemble(res.results)
```

```python
import bisect
import math
from contextlib import ExitStack

import numpy as np
import concourse.bass as bass
import concourse.mybir as mybir
from concourse.bass_utils import run_bass_kernel_spmd

F32 = mybir.dt.float32
F32R = mybir.dt.float32r
BF16 = mybir.dt.bfloat16
AF = mybir.ActivationFunctionType
ALU = mybir.AluOpType
AX = mybir.AxisListType

NL = 4
D = 1024
SEQ = 2048
NSS = 16
LS = 4
TT = 256
PAST = 16384
DFF = 2816
NJ = 22
ALPHA = (2 * NL) ** 0.25
NBLK = 38
NPIECE_IN = 19
SLOT = 2816
NSLOT = 4
RW0, GL0, RT0 = 0, 1440, 2416


class _Stop(Exception):
    pass


class Buf:
    __slots__ = ("name", "last_write", "reads")

    def __init__(self, name):
        self.name = name
        self.last_write = None
        self.reads = []


class Op:
    __slots__ = ("eng", "fn", "deps", "idx", "needs_inc", "cnt", "chan", "dma_cnt")

    def __init__(self, eng, fn, deps, idx, chan=None):
        self.eng = eng
        self.fn = fn
        self.deps = deps
        self.idx = idx
        self.needs_inc = False
        self.cnt = None
        self.chan = chan
        self.dma_cnt = None


COMPUTE = ("pe", "act", "dve", "pool")
DMAQ = ("sp", "actq", "poolq")
STREAM = {"pe": "pe", "act": "act", "dve": "dve", "pool": "pool", "sp": "sp", "actq": "act", "poolq": "pool"}
SAME_SYNC = {"pe": False, "act": True, "dve": True, "pool": True}


class Sched:
    def __init__(self, nc):
        self.nc = nc
        self.ops = []

    def op(self, eng, fn, reads=(), writes=(), chan=None):
        idx = len(self.ops)
        deps = set()
        for b in reads:
            if b.last_write is not None:
                deps.add(b.last_write)
        for b in writes:
            if b.last_write is not None:
                deps.add(b.last_write)
            deps.update(b.reads)
        o = Op(eng, fn, deps, idx, chan)
        self.ops.append(o)
        for b in reads:
            b.reads.append(idx)
        for b in writes:
            b.last_write = idx
            b.reads = []
        return o

    def pe(self, fn, reads=(), writes=()):
        return self.op("pe", fn, reads, writes)

    def act(self, fn, reads=(), writes=()):
        return self.op("act", fn, reads, writes)

    def dve(self, fn, reads=(), writes=()):
        return self.op("dve", fn, reads, writes)

    def pool(self, fn, reads=(), writes=()):
        return self.op("pool", fn, reads, writes)

    def dma(self, fn, reads=(), writes=(), chan=None, q="sp"):
        return self.op(q, fn, reads, writes, chan=chan)

    def _skip(self, p, so):
        sp = STREAM[p.eng]
        return sp == so and p.eng in COMPUTE and not SAME_SYNC[p.eng]

    def run_block(self):
        nc = self.nc
        ops = self.ops
        for o in ops:
            so = STREAM[o.eng]
            for d in o.deps:
                p = ops[d]
                if p.eng in DMAQ or self._skip(p, so):
                    continue
                p.needs_inc = True
        cnt = {e: 0 for e in COMPUTE}
        chan_cnt = {}
        chan_hist = {}
        for o in ops:
            if o.eng in DMAQ:
                c = chan_cnt.get(o.chan, 0) + 16
                chan_cnt[o.chan] = c
                o.dma_cnt = c
                chan_hist.setdefault(o.chan, []).append((o.idx, c))
            elif o.needs_inc:
                cnt[o.eng] += 1
                o.cnt = cnt[o.eng]
        chan_idx = {c: [i for i, _ in h] for c, h in chan_hist.items()}
        streams = {s: [] for s in ("pe", "act", "dve", "pool", "sp")}
        waited = {s: {} for s in streams}
        for o in ops:
            so = STREAM[o.eng]
            need = {}
            for d in o.deps:
                p = ops[d]
                if p.eng in DMAQ:
                    h = chan_hist[p.chan]
                    k = bisect.bisect_left(chan_idx[p.chan], o.idx) - 1
                    key = ("chan", p.chan)
                    need[key] = max(need.get(key, 0), h[k][1])
                else:
                    if self._skip(p, so):
                        continue
                    key = ("eng", p.eng)
                    need[key] = max(need.get(key, 0), p.cnt)
            w = []
            for key, c in need.items():
                if waited[so].get(key, 0) >= c:
                    continue
                waited[so][key] = c
                w.append((key, c))
            streams[so].append((o, w))
        with ExitStack() as es:
            sems = {}
            for e in COMPUTE:
                sems[("eng", e)] = es.enter_context(nc.semaphore("s_" + e))
            for c in chan_cnt:
                sems[("chan", c)] = es.enter_context(nc.semaphore("c_" + str(c)))
            block = es.enter_context(nc.Block())

            def mk(sname):
                def body(eng):
                    for o, w in streams[sname]:
                        for key, c in w:
                            eng.wait_ge(sems[key], c)
                        ins = o.fn(eng)
                        if o.eng in DMAQ:
                            ins.then_inc(sems[("chan", o.chan)], 16)
                        elif o.needs_inc:
                            ins.then_inc(sems[("eng", o.eng)], 1)
                    if sname == "sp":
                        for c, v in chan_cnt.items():
                            eng.wait_ge(sems[("chan", c)], v)
                        for e in COMPUTE:
                            if cnt[e] > 0:
                                eng.wait_ge(sems[("eng", e)], cnt[e])
                return body

            block.tensor(mk("pe"))
            block.scalar(mk("act"))
            block.vector(mk("dve"))
            block.gpsimd(mk("pool"))
            block.sync(mk("sp"))


def _colmap():
    cm = -np.ones(NBLK * 128, np.int64)
    for g, (rw_off, gl_off, rt_off, pad) in enumerate(((0, 0, 0, True), (384, 160, 320, True), (768, 320, 640, False))):
        base = g * 1024
        cm[base:base + 384] = RW0 + rw_off + np.arange(384)
        for j in range(5):
            if pad:
                cm[base + 384 + j * 64: base + 384 + j * 64 + 32] = GL0 + gl_off + j * 32 + np.arange(32)
            else:
                cm[base + 384 + j * 64: base + 384 + j * 64 + 64] = GL0 + gl_off + j * 64 + np.arange(64)
        cm[base + 704: base + 1024] = RT0 + rt_off + np.arange(320)
    b = 24 * 128
    cm[b:b + 320] = GL0 + 640 + np.arange(320)
    cm[b + 320:b + 640] = RT0 + 960 + np.arange(320)
    b = 29 * 128
    cm[b:b + 128] = RW0 + 1152 + np.arange(128)
    cm[b + 128:b + 256] = RW0 + 1280 + np.arange(128)
    cm[b + 256:b + 288] = RW0 + 1408 + np.arange(32)
    cm[b + 288:b + 304] = GL0 + 960 + np.arange(16)
    for gi, off in enumerate((0, 320)):
        b = (32 + 3 * gi) * 128
        for r in range(704, 1024):
            i = r - 704
            h, d = divmod(i, 64)
            cm[b + (r - 640)] = RT0 + off + h * 64 + ((d + 32) % 64)
    return cm


def _rows_layout(v, nblk):
    return np.ascontiguousarray(v.reshape(nblk, 128).T)


VEC = {}


def _vec_layout():
    off = 0
    for name, n in (("mu", 12), ("w0", 3), ("a0", 3), ("kkv", 3), ("ka", 3), ("rk", 3), ("nw", 8), ("nb", 8),
                    ("bgkb", 3), ("ln1g", 8), ("ln1b", 8), ("ln2g", 8), ("ln2b", 8), ("cw0", 44), ("cw1", 44),
                    ("cw2", 44), ("cb", 44), ("rscale", 8), ("kscale", 8), ("ldret", 8)):
        VEC[name] = (off, n)
        off += n
    return off


NV = _vec_layout()
NM = 4 * 384


def _prep_layer_params(inp, l):
    f = np.float32
    vec = np.zeros((128, NV), f)

    def put(name, arr):
        o, n = VEC[name]
        assert arr.shape == (128, n), (name, arr.shape)
        vec[:, o:o + n] = arr

    mu = inp["rwkv_mu"][l]
    mup = np.zeros(12 * 128, f)
    mup[0:384] = mu[0:384]
    mup[384:768] = mu[384:768]
    mup[768:1152] = mu[768:1152]
    mup[1152:1152 + 128] = mu[1152:1280]
    mup[1280:1408] = mu[1280:1408]
    mup[1408:1440] = mu[1408:1440]
    put("mu", _rows_layout(mup, 12))
    put("w0", _rows_layout(inp["rwkv_w0"][l], 3))
    put("a0", _rows_layout(inp["rwkv_a0"][l], 3))
    put("kkv", _rows_layout(inp["rwkv_kk"][l], 3))
    put("ka", _rows_layout(inp["rwkv_ka"][l], 3))
    put("rk", _rows_layout(inp["rwkv_rk"][l].reshape(-1), 3))
    nw = np.ones(1024, f)
    nb = np.zeros(1024, f)
    nw[0:384] = inp["rwkv_lnw"][l]
    nb[0:384] = inp["rwkv_lnb"][l]
    nw[384:704] = np.tile(inp["gla_norm_w"][l], 5)
    put("nw", _rows_layout(nw, 8))
    put("nb", _rows_layout(nb, 8))
    bb = np.zeros(384, f)
    for j in range(5):
        bb[j * 64:j * 64 + 32] = inp["gla_bgk_b"][l][j * 32:(j + 1) * 32]
    put("bgkb", _rows_layout(bb, 3))
    put("ln1g", _rows_layout(inp["ln1_g"][l], 8))
    put("ln1b", _rows_layout(inp["ln1_b"][l], 8))
    put("ln2g", _rows_layout(inp["ln2_g"][l], 8))
    put("ln2b", _rows_layout(inp["ln2_b"][l], 8))
    cw = inp["ffn_conv_w"][l]
    put("cw0", _rows_layout(cw[0], 44))
    put("cw1", _rows_layout(cw[1], 44))
    put("cw2", _rows_layout(cw[2], 44))
    put("cb", _rows_layout(inp["ffn_conv_b"][l], 44))
    rs = np.ones(1024, f)
    rs[384:704] = 32.0 ** -0.5
    ks = np.ones(1024, f)
    ks[704:1024] = 64.0 ** -0.5
    put("rscale", _rows_layout(rs, 8))
    put("kscale", _rows_layout(ks, 8))
    ldr = np.zeros(1024, np.float64)
    for h in range(5):
        ldr[704 + h * 64:704 + (h + 1) * 64] = math.log(1.0 - 2.0 ** (-5.0 - h))
    put("ldret", _rows_layout(ldr.astype(f), 8))
    mats = np.zeros((128, NM), f)
    mats[0:64, 0:384] = inp["rwkv_bw"][l]
    mats[64:128, 0:384] = inp["rwkv_ba"][l]
    mats[:, 384:768] = inp["rwkv_bg"][l][0:128]
    mats[0:32, 768:1152] = inp["rwkv_bg"][l][128:160]
    bgk = inp["gla_bgk"][l]
    for j in range(5):
        mats[32:48, 1152 + j * 64:1152 + j * 64 + 32] = bgk[:, j * 32:(j + 1) * 32]
    return vec, mats


def _prep_weights(inp):
    cm = _colmap()
    npieces = NPIECE_IN + 4 + NJ + 8
    W = np.zeros((NL, npieces, 128, SLOT), np.float32)
    for l in range(NL):
        win = inp["w_in"][l]
        wp = np.zeros((1024, NBLK * 128), np.float32)
        ok = cm >= 0
        wp[:, ok] = win[:, cm[ok]]
        p = 0
        for i in range(NPIECE_IN):
            blkc = wp[:, i * 256:(i + 1) * 256].reshape(8, 128, 256).transpose(1, 0, 2)
            W[l, p, :, :2048] = blkc.reshape(128, 2048)
            p += 1
        wo = inp["w_out"][l]
        for i in range(4):
            blkc = wo[:, i * 256:(i + 1) * 256].reshape(8, 128, 256).transpose(1, 0, 2)
            W[l, p, :, :2048] = blkc.reshape(128, 2048)
            p += 1
        up = inp["ffn_up"][l]
        for j in range(NJ):
            ab = np.concatenate([up[:, j * 128:(j + 1) * 128], up[:, DFF + j * 128:DFF + (j + 1) * 128]], axis=1)
            blkc = ab.reshape(8, 128, 256).transpose(1, 0, 2)
            W[l, p, :, :2048] = blkc.reshape(128, 2048)
            p += 1
        dn = inp["ffn_down"][l]
        for cb in range(8):
            blkc = dn[:, cb * 128:(cb + 1) * 128].reshape(NJ, 128, 128).transpose(1, 0, 2)
            W[l, p, :, :NJ * 128] = blkc.reshape(128, NJ * 128)
            p += 1
        assert p == npieces
    return W


def _consts():
    ident = np.eye(128, dtype=np.float32)
    sI = np.arange(128)[:, None]
    tI = np.arange(128)[None, :]
    bones = np.zeros((128, 128), np.float32)
    bones[0:64, 0:64] = 1
    bones[64:128, 64:128] = 1
    cst = np.zeros((128, 672), np.float32)
    cst[:, 0:128] = ident
    cst[:, 128:256] = bones
    cst[:, 256:384] = (tI >= sI)
    cst[:, 384:512] = (tI > sI)
    cst[:, 512:640] = (tI < sI)
    tok = np.zeros((128, 32), np.float32)
    tok[:, 0:6] = 1.0 / 64
    tok[:, 16:22] = 64e-5
    tok[:, 22:32] = 1e-6
    cst[:, 640:672] = tok
    return cst


def _rope_tables():
    ntok = SEQ + NSS * LS
    pos = np.concatenate([np.arange(SEQ), np.tile(PAST + np.arange(LS), NSS)]).astype(np.float32)
    inv = (1.0 / (10000.0 ** np.linspace(0.0, 1.0, 32, dtype=np.float32))).astype(np.float32)
    ang = pos[None, :] * inv[:, None]
    cos = np.cos(ang).astype(np.float32)
    sin = np.sin(ang).astype(np.float32)
    C = np.ones((384, ntok), np.float32)
    S = np.zeros((384, ntok), np.float32)
    for h in range(5):
        r0 = 64 + h * 64
        C[r0:r0 + 32] = cos
        C[r0 + 32:r0 + 64] = cos
        S[r0:r0 + 32] = -sin
        S[r0 + 32:r0 + 64] = sin
    C = C.reshape(3, 128, ntok).transpose(1, 0, 2)
    S = S.reshape(3, 128, ntok).transpose(1, 0, 2)
    return np.ascontiguousarray(C), np.ascontiguousarray(S)


CFG_FULL = dict(tiles=[("p", i) for i in range(SEQ // TT)] + [("s", 0)], nl=NL)


def build(cfg):
    nc = bass.Bass("TRN2", target_bir_lowering=False)
    S = Sched(nc)
    NTOK = SEQ + NSS * LS
    npieces = NPIECE_IN + 4 + NJ + 8
    nl = cfg["nl"]
    tiles = cfg["tiles"]

    def din(name, shape, dt=F32):
        return nc.dram_tensor(name, list(shape), dt, kind="ExternalInput").ap()

    def dout(name, shape):
        return nc.dram_tensor(name, list(shape), F32, kind="ExternalOutput").ap()

    xT_d = din("xT", [128, 8, NTOK])
    wts_d = din("wts", [NL, npieces, 128, SLOT])
    vec_d = din("vecs", [NL, 128, NV])
    mat_d = din("mats", [NL, 128, NM])
    cst_d = din("cst", [128, 672])
    cos_d = din("ropec", [128, 3, NTOK])
    sin_d = din("ropes", [128, 3, NTOK])
    sst_d = din("sst", [NL, NSS, 128, 512])
    ssh_d = din("ssh", [NL, 128, 12 * NSS])
    scv_d = din("scv", [NL, 128, 44 * NSS * 2])
    yT_d = dout("yT", [128, 8, NTOK])
    sto_d = dout("sto", [NL, 1 + NSS, 128, 512])
    sho_d = dout("sho", [NL, 128, 12 * (1 + NSS)])
    cvo_d = dout("cvo", [NL, 128, 44 * (1 + NSS) * 2])
    dbg_d = dout("dbg", [128, 8192]) if cfg.get("dbg") else None
    dbg_state = {"col": 0, "items": []}

    def dbg(name, ap, bufs, np_=128):
        if dbg_d is None:
            return
        n = 1
        for d_ in ap.shape[1:]:
            n *= d_
        c0 = dbg_state["col"]
        if c0 + n > 8192:
            return
        dbg_state["col"] = c0 + n
        dbg_state["items"].append((name, c0, n, np_, tuple(ap.shape)))
        flat = dbg_d[0:np_, c0:c0 + n]
        if len(ap.shape) == 3:
            flat = flat.rearrange("p (a b) -> p a b", b=ap.shape[2])
        S.dma(E("dma_start", out=flat, in_=ap), reads=bufs, chan="dbg")
    cfg["_dbg_items"] = dbg_state["items"]

    def sb(name, shape, dt=F32):
        return nc.alloc_sbuf_tensor("sb_" + name, list(shape), dt)

    TM = TT
    x32 = sb("x32", [128, 8, TM]); b_x32 = Buf("x32")
    xTb = sb("xTb", [128, 8, TM], BF16); b_xTb = Buf("xTb")
    un1 = sb("un1", [128, 12 * (TM + NSS)]); b_PRW = Buf("PRW")
    PRW = un1[:].rearrange("p (a b) -> p a b", b=TM + NSS)
    Rg = sb("Rg", [128, 8, TM]); b_R = Buf("R")
    Kg = sb("Kg", [128, 8, TM]); b_K = Buf("K")
    Vg = sb("Vg", [128, 8, TM]); b_V = Buf("V")
    Gg = sb("Gg", [128, 8, TM], BF16); b_G = Buf("G")

    XS = sb("XS", [128, 3, TM]); b_XS = Buf("XS")
    ATt = sb("ATt", [128, 3, TM]); b_AT = Buf("AT")
    BTt = sb("BTt", [128, 3, TM]); b_BT = Buf("BT")
    bonus = sb("bonus", [128, 3, TM]); b_bonus = Buf("bonus")
    ynT = sb("ynT", [128, 8, TM]); b_ynT = Buf("ynT")
    SW = ynT; b_SW = b_ynT
    mixT = sb("mixT", [128, 8, TM], BF16); b_mixT = Buf("mixT")
    hT = un1[:].bitcast(BF16)[:, 0:NJ * TM].rearrange("p (a b) -> p a b", b=TM); b_hT = b_PRW
    WC = sb("WC", [128, 8, 16]); b_WC = Buf("WC")
    NTMP = 9
    tmp = [sb("tmp%d" % i, [128, TM]) for i in range(NTMP)]
    b_tmp = [Buf("tmp%d" % i) for i in range(NTMP)]
    tmpb = [sb("tmpb%d" % i, [128, TM], BF16) for i in range(4)]
    b_tmpb = [Buf("tmpb%d" % i) for i in range(4)]
    ubuf = [sb("ubuf%d" % i, [128, TM + 2 * NSS]) for i in range(4)]
    b_ubuf = [Buf("ubuf%d" % i) for i in range(4)]
    cosT = sb("cosT", [128, 3, TM]); sinT = sb("sinT", [128, 3, TM]); b_rope = Buf("rope")
    cst = sb("cst", [128, 672]); b_cst = Buf("cst")
    onesb = sb("onesb", [128, 128], BF16); b_ones = Buf("onesb")
    bonesr = sb("bonesr", [128, 128]); b_bones = Buf("bonesr")
    rmask = sb("rmask", [128, TM]); b_rmask = Buf("rmask")
    vecs = [sb("vecs%d" % i, [128, NV]) for i in range(2)]; b_vecs = [Buf("vecs%d" % i) for i in range(2)]
    vaux = [sb("vaux%d" % i, [128, 8]) for i in range(2)]; b_vaux = [Buf("vaux%d" % i) for i in range(2)]
    _m0 = sb("mats0", [128, NM], BF16); _bm0 = Buf("mats0")
    mats = [_m0, _m0]; b_mats = [_bm0, _bm0]
    hpw = [sb("hpw%d" % i, [128, 1024]) for i in range(2)]; b_hpw = [Buf("hpw%d" % i) for i in range(2)]
    mix32 = sb("mix32", [128, 8, 2]); b_mix32 = Buf("mix32")
    h32 = sb("h32", [128, NJ, 2]); b_h32 = Buf("h32")
    slots = [sb("slot%d" % i, [128, SLOT], BF16) for i in range(NSLOT)]
    b_slots = [Buf("slot%d" % i) for i in range(NSLOT)]
    Sst = [sb("Sst%d" % l, [128, 8, 64]) for l in range(NL)]; b_Sst = [Buf("Sst%d" % l) for l in range(NL)]
    Ssm = [sb("Ssm%d" % i, [128, 8, 64]) for i in range(2)]; b_Ssm = [Buf("Ssm%d" % i) for i in range(2)]
    _s0 = sb("Sld0", [128, 8, 64]); _bs0 = Buf("Sld0")
    Sld = [_s0, _s0]; b_Sld = [_bs0, _bs0]
    shc = sb("shc", [128, NL, 12]); b_shc = [Buf("shc%d" % l) for l in range(NL)]
    shst = sb("shst", [128, 12, NSS]); b_shst = Buf("shst")
    sho = sb("sho", [128, 12, NSS]); b_sho = Buf("sho")
    cvc = sb("cvc", [128, NL, 44, 2]); b_cvc = [Buf("cvc%d" % l) for l in range(NL)]
    cvst = sb("cvst", [128, 44, NSS, 2]); b_cvst = Buf("cvst")
    cvo = sb("cvo", [128, 44, NSS, 2]); b_cvo = Buf("cvo")
    CM = 128
    vtok = sb("vtok", [CM, 1024]); b_vtok = Buf("vtok")
    ktok = sb("ktok", [CM, 1024]); b_ktok = Buf("ktok")
    btok = sb("btok", [CM, 384]); b_btok = Buf("btok")
    ark = sb("ark", [CM, 16, CM]); b_ark = Buf("ark")
    aak = sb("aak", [CM, 6, CM]); b_aak = Buf("aak")
    arb = sb("arb", [CM, 6, CM]); b_arb = Buf("arb")
    _p = sb("Pm0", [CM, 6, CM], BF16); _bp = Buf("Pm0")
    _q = sb("Qm0", [CM, 6, CM], BF16); _bq = Buf("Qm0")
    Xb = sb("Xb", [CM, 6, CM], BF16); b_Xb = Buf("Xb")
    Pm = [_p, _p]; b_Pm = [_bp, _bp]
    Qm = [_q, _q]; b_Qm = [_bq, _bq]
    Xm = sb("Xm", [CM, 6, CM]); b_Xm = Buf("Xm")
    rhs0 = sb("rhs0", [CM, 384]); b_rhs0 = Buf("rhs0")
    usb = sb("usb", [CM, 384]); b_usb = Buf("usb")
    ysb = sb("ysb", [CM, 1024]); b_ysb = Buf("ysb")
    ysq = ktok; b_ysq = b_ktok
    yst = sb("yst", [CM, 8, 16]); b_yst = Buf("yst")
    ps = nc.alloc_psum_tensor("ps", [128, 8, 512], F32)
    b_ps = [Buf("ps%d" % i) for i in range(8)]
    bank_ctr = [0]

    def nbank():
        b = bank_ctr[0] % 8
        bank_ctr[0] += 1
        return b

    R_ = lambda ap: ap.bitcast(F32R)

    def E(name, *a, **k):
        return lambda e: getattr(e, name)(*a, **k)

    S.dma(E("dma_start", out=cst[:], in_=cst_d), writes=[b_cst], chan="cst")
    ident = cst[:, 0:128]
    m_incl = cst[:, 256:384]
    m_su = cst[:, 384:512]
    m_sl = cst[:, 512:640]
    eye64 = cst[:, 0:128]
    tokc = cst[:, 640:672]
    S.dve(E("memset", onesb[:], 1.0), writes=[b_ones])
    S.dve(E("tensor_copy", out=R_(bonesr[:]), in_=cst[:, 128:256]), reads=[b_cst], writes=[b_bones])
    S.dve(E("memset", Sld[0][:], 0.0), writes=[b_Sld[0]])
    for l in range(NL):
        S.dve(E("tensor_copy", out=R_(Sst[l][:]), in_=Sld[0][:]), reads=[b_Sld[0]], writes=[b_Sst[l]])
        S.dve(E("memset", shc[:, l, :], 0.0), writes=[b_shc[l]])
        S.dve(E("memset", cvc[:, l, :, :], 0.0), writes=[b_cvc[l]])

    plan = []
    for (kind, ti) in tiles:
        for l in range(nl):
            for p in range(npieces):
                plan.append((l, p))
    ws = {"issued": 0, "used": 0, "pending_wr": {}}

    wbf_d = nc.dram_tensor("wbf", [NL, npieces, 128, SLOT], BF16).ap()
    b_wbf = {}

    def ws_issue():
        i = ws["issued"]
        if i >= len(plan):
            return
        l, p = plan[i]
        s = i % NSLOT
        n = 2048 if p < NPIECE_IN + 4 + NJ else NJ * 128
        if (l, p) not in b_wbf:
            S.dma(E("dma_start", out=slots[s][:, 0:n], in_=wts_d[l, p, :, 0:n]),
                  writes=[b_slots[s]], chan="slot%d" % s, q="poolq")
            if len(tiles) > 1:
                b_wbf[(l, p)] = Buf("wbf%d_%d" % (l, p))
                ws["pending_wr"][i] = (l, p, s, n)
        else:
            S.dma(E("dma_start", out=slots[s][:, 0:n], in_=wbf_d[l, p, :, 0:n]), reads=[b_wbf[(l, p)]],
                  writes=[b_slots[s]], chan="slot%d" % s)
        ws["issued"] += 1

    def ws_next(l, p):
        i = ws["used"]
        assert plan[i] == (l, p), (plan[i], l, p)
        while ws["issued"] < min(len(plan), i + NSLOT):
            ws_issue()
        if i in ws["pending_wr"]:
            l_, p_, s_, n_ = ws["pending_wr"].pop(i)
            S.dma(E("dma_start", out=wbf_d[l_, p_, :, 0:n_], in_=slots[s_][:, 0:n_]), reads=[b_slots[s_]],
                  writes=[b_wbf[(l_, p_)]], chan="wbfw")
        ws["used"] += 1
        s = i % NSLOT
        return slots[s], b_slots[s]

    for _ in range(NSLOT - 1):
        ws_issue()

    pv_ctr = [0]
    hp_ctr = [0]

    def hp_load(l, p, part):
        i = hp_ctr[0] % 2
        hp_ctr[0] += 1
        if part[0] == "sub":
            src = wts_d[l, p, :, 0:2048].rearrange("p (kc c) -> p kc c", c=256)[:, :, part[1] * 128:(part[1] + 1) * 128]
            dst = hpw[i][:, 0:1024].rearrange("p (kc c) -> p kc c", c=128)
        else:
            src = wts_d[l, p, :, part[1] * 1024:part[1] * 1024 + part[2] * 128]
            dst = hpw[i][:, 0:part[2] * 128]
        S.dma(E("dma_start", out=dst, in_=src), writes=[b_hpw[i]], chan="hp%d" % i)
        return hpw[i], b_hpw[i]

    def hp_mm8(l, p, sub, rhs_fn, brhs):
        w, bw = hp_load(l, p, ("sub", sub))
        bkh = nbank()
        for kc in range(8):
            S.pe(E("matmul", ps[:, bkh, 0:2], lhsT=w[:, kc * 128:(kc + 1) * 128], rhs=rhs_fn(kc), start=(kc == 0), stop=(kc == 7)),
                 reads=[bw, brhs], writes=[b_ps[bkh]])
        return bkh

    def stage(n):
        if cfg.get("stop") == n:
            raise _Stop()

    def _tile_loop():
        for (kind, ti) in tiles:
            if kind == "p":
                T = TT; nseq = 1; L = TT; C = 128; tok0 = ti * TT; nsteps = 6
            else:
                T = NSS * LS; nseq = NSS; L = LS; C = LS; tok0 = SEQ; nsteps = 1
            nch = T // C
            last_prompt = (kind == "p" and ti == SEQ // TT - 1)
            hpf = (kind == "p" and ti == 0 and cfg.get("hp", True))
            clo = 2 if hpf else 0
            S.dma(E("dma_start", out=x32[:, :, 0:T], in_=xT_d[:, :, tok0:tok0 + T]),
                  writes=[b_x32], chan="x32")
            S.dma(E("dma_start", out=xTb[:, :, 0:T], in_=xT_d[:, :, tok0:tok0 + T]),
                  writes=[b_xTb], chan="xTb", q="poolq")
            S.dma(E("dma_start", out=cosT[:, :, 0:T], in_=cos_d[:, :, tok0:tok0 + T]),
                  writes=[b_rope], chan="rope")
            S.dma(E("dma_start", out=sinT[:, :, 0:T], in_=sin_d[:, :, tok0:tok0 + T]),
                  writes=[b_rope], chan="rope")
            S.dve(E("memset", rmask[:, 0:T], 1.0), writes=[b_rmask])
            S.dve(E("memset", rmask[:, 0:T].rearrange("p (c t) -> p c t", t=C)[:, :, 0:1], 0.0),
                  writes=[b_rmask])

            for l in range(nl):
                pv = pv_ctr[0] % 2
                pv_ctr[0] += 1
                vec, bvec, mat, bmat, vx, bvx = vecs[pv], b_vecs[pv], mats[pv], b_mats[pv], vaux[pv], b_vaux[pv]
                S.dma(E("dma_start", out=vec[:], in_=vec_d[l]), writes=[bvec], chan="vec%d" % pv)
                S.dma(E("dma_start", out=mat[:], in_=mat_d[l]), writes=[bmat], chan="mat0",
                      q="poolq")

                def V_(name, b=0, n=1, vec=vec):
                    o, _ = VEC[name]
                    return vec[:, o + b:o + b + n]

                o_ka = VEC["ka"][0]; o_bg = VEC["bgkb"][0]
                S.dve(E("tensor_scalar", out=vx[:, 0:3], in0=vec[:, o_ka:o_ka + 3], scalar1=-1.0,
                                                                        scalar2=1.0, op0=ALU.mult, op1=ALU.add),
                      reads=[bvec], writes=[bvx])
                S.dve(E("tensor_scalar", out=vx[:, 3:6], in0=vec[:, o_bg:o_bg + 3], scalar1=-1.0,
                                                                        scalar2=None, op0=ALU.mult),
                      reads=[bvec], writes=[bvx])

                if kind == "s":
                    S.dma(E("dma_start", out=shst[:].rearrange("p a b -> p (a b)"), in_=ssh_d[l]),
                          writes=[b_shst], chan="shst")
                    S.dve(E("tensor_copy", out=PRW[:, :, 0:NSS * (LS + 1)].rearrange("p a (s t) -> p a s t", t=LS + 1)[:, :, :, 0],
                                                  in_=shst[:]), reads=[b_shst], writes=[b_PRW])
                else:
                    S.dve(E("tensor_copy", out=PRW[:, :, 0], in_=shc[:, l, :]), reads=[b_shc[l]], writes=[b_PRW])

                def prw_dst(pb, T=T, nseq=nseq, L=L):
                    return PRW[:, pb, 0:nseq * (L + 1)].rearrange("p (s t) -> p s t", t=L + 1)[:, :, 1:L + 1]

                def prw_prev(pb, T=T, nseq=nseq, L=L):
                    return PRW[:, pb, 0:nseq * (L + 1)].rearrange("p (s t) -> p s t", t=L + 1)[:, :, 0:L]

                def t3(ap, nseq=nseq, L=L):
                    return ap.rearrange("p (s t) -> p s t", t=L)

                evi = [0]
                for piece in range(NPIECE_IN):
                    slot, bslot = ws_next(l, piece)
                    for sub in range(2):
                        blk = piece * 2 + sub
                        bk = nbank()
                        for kc in range(8):
                            S.pe(E("matmul",
                                ps[:, bk, 0:T], lhsT=slot[:, kc * 256 + sub * 128: kc * 256 + sub * 128 + 128],
                                rhs=xTb[:, kc, 0:T], start=(kc == 0), stop=(kc == 7)),
                                reads=[bslot, b_xTb], writes=[b_ps[bk]])
                        g, gb = divmod(blk, 8)

                        def proj_evac(srcap, lo, hi, bkb, blk=blk, g=g, gb=gb):
                            if blk < 24 and gb < 3:
                                pb = g * 3 + gb
                            elif 29 <= blk < 32:
                                pb = 9 + blk - 29
                            else:
                                pb = None
                            if pb is not None:
                                if kind == "p":
                                    S.act(E("copy", out=PRW[:, pb, 1 + lo:1 + hi], in_=srcap), reads=[bkb], writes=[b_PRW])
                                else:
                                    S.act(E("copy", out=prw_dst(pb), in_=t3(srcap)), reads=[bkb], writes=[b_PRW])
                            elif blk < 24:
                                tile_, btile = ((Rg, b_R), (Kg, b_K), (Vg, b_V))[g]
                                S.dve(E("tensor_copy", out=R_(tile_[:, gb, lo:hi]), in_=srcap), reads=[bkb], writes=[btile])
                            elif blk < 29:
                                S.act(E("activation", out=Gg[:, 3 + blk - 24, lo:hi], in_=srcap, func=AF.Silu), reads=[bkb],
                                      writes=[b_G])
                            else:
                                S.dve(E("tensor_copy", out=SW[:, blk - 32, lo:hi], in_=srcap), reads=[bkb], writes=[b_SW])

                        proj_evac(ps[:, bk, clo:T], clo, T, b_ps[bk])
                        if hpf:
                            bkh = hp_mm8(l, piece, sub, lambda kc: x32[:, kc, 0:2], b_x32)
                            proj_evac(ps[:, bkh, 0:2], 0, 2, b_ps[bkh])

                stage(1)
                for pb in range(12):
                    g, gb = divmod(pb, 3)
                    tile_, btile = ((Rg, b_R), (Kg, b_K), (Vg, b_V), (XS, b_XS))[g]
                    tb = pb % 8
                    d3 = t3(tmp[tb][:, 0:T])
                    S.dve(E("tensor_tensor", out=d3, in0=prw_prev(pb), in1=prw_dst(pb), op=ALU.subtract),
                          reads=[b_PRW], writes=[b_tmp[tb]])
                    dst3 = t3(tile_[:, gb, 0:T])
                    if g <= 2:
                        dst3 = R_(dst3)
                    S.dve(E("scalar_tensor_tensor",
                        out=dst3, in0=d3, scalar=V_("mu", pb), in1=prw_dst(pb), op0=ALU.mult, op1=ALU.add),
                        reads=[b_PRW, b_tmp[tb], bvec], writes=[btile])
                if l == 0:
                    dbg("PRW0", PRW[:, 0, 0:80], [b_PRW]); dbg("Rsh", Rg[:, 0, 0:T], [b_R]); dbg("shst", shst[:, 0, :], [b_shst])
                lastcol = PRW[:, :, 0:nseq * (L + 1)].rearrange("p a (s t) -> p a s t", t=L + 1)[:, :, :, L]
                if kind == "s":
                    S.act(E("copy", out=sho[:], in_=lastcol), reads=[b_PRW], writes=[b_sho])
                    S.dma(E("dma_start", out=sho_d[l][:, 12:12 * (1 + NSS)], in_=sho[:].rearrange("p a b -> p (a b)")),
                          reads=[b_sho], chan="shout")
                else:
                    S.act(E("copy", out=shc[:, l, :], in_=lastcol[:, :, 0]), reads=[b_PRW],
                          writes=[b_shc[l]])
                    if last_prompt:
                        S.dma(E("dma_start", out=sho_d[l][:, 0:12], in_=shc[:, l, :]), reads=[b_shc[l]],
                              chan="shout")

                stage(2)
                for gi, (tile_, btile) in enumerate(((Rg, b_R), (Kg, b_K))):
                    a = tile_[:, 5:8, 0:T]
                    sw = SW[:, gi * 3:gi * 3 + 3, 0:T]
                    S.dve(E("tensor_tensor", out=R_(a), in0=a, in1=cosT[:, :, 0:T], op=ALU.mult),
                          reads=[b_rope, btile], writes=[btile])
                    S.dve(E("tensor_tensor", out=sw, in0=sw, in1=sinT[:, :, 0:T], op=ALU.mult),
                          reads=[b_rope, b_SW], writes=[b_SW])
                    S.dve(E("tensor_tensor", out=R_(a), in0=a, in1=sw, op=ALU.add),
                          reads=[b_SW, btile], writes=[btile])

                thx, bthx = tmpb[0], b_tmpb[0]
                sgx, bsgx = tmpb[1], b_tmpb[1]
                sgx2, bsgx2 = tmpb[2], b_tmpb[2]
                gkb, bgkb_ = tmpb[3], b_tmpb[3]
                S.act(E("activation", out=thx[0:64, 0:T], in_=XS[0:64, 0, 0:T], func=AF.Tanh), reads=[b_XS], writes=[bthx])
                S.act(E("copy", out=thx[64:128, 0:T], in_=XS[64:128, 0, 0:T]), reads=[b_XS], writes=[bthx])
                S.act(E("activation", out=sgx[:, 0:T], in_=XS[:, 1, 0:T], func=AF.Sigmoid), reads=[b_XS], writes=[bsgx])
                S.act(E("activation", out=sgx2[0:32, 0:T], in_=XS[0:32, 2, 0:T], func=AF.Sigmoid), reads=[b_XS], writes=[bsgx2])
                S.act(E("copy", out=gkb[32:64, 0:T], in_=XS[32:64, 2, 0:T]), reads=[b_XS], writes=[bgkb_])

                for b in range(8):
                    t_ld, t_lw, t_W, t_IW, t_a, t_kk, t_x, t_y, t_r = tmp
                    bl_ld, bl_lw, bl_W, bl_IW, bl_a, bl_kk, bl_x, bl_y, bl_r = b_tmp
                    if b < 3:
                        c0 = b * 128
                        bk = nbank()
                        S.pe(E("matmul", ps[:, bk, 0:T], lhsT=mat[0:64, c0:c0 + 128], rhs=thx[0:64, 0:T],
                                                              start=True, stop=True), reads=[bmat, bthx], writes=[b_ps[bk]])
                        S.act(E("activation", out=t_ld[:, 0:T], in_=ps[:, bk, 0:T], func=AF.Sigmoid,
                                                                 bias=V_("w0", b), scale=1.0),
                              reads=[b_ps[bk], bvec], writes=[bl_ld])
                        S.dve(E("tensor_scalar", out=t_ld[:, 0:T], in0=t_ld[:, 0:T], scalar1=-math.exp(-0.5), scalar2=None,
                                                        op0=ALU.mult), reads=[bl_ld], writes=[bl_ld])
                        bk = nbank()
                        S.pe(E("matmul", ps[:, bk, 0:T], lhsT=mat[64:128, c0:c0 + 128], rhs=thx[64:128, 0:T],
                                                              start=True, stop=True), reads=[bmat, bthx], writes=[b_ps[bk]])
                        S.act(E("activation", out=t_a[:, 0:T], in_=ps[:, bk, 0:T], func=AF.Sigmoid,
                                                                 bias=V_("a0", b), scale=1.0),
                              reads=[b_ps[bk], bvec], writes=[bl_a])
                        bk = nbank()
                        S.pe(E("matmul", ps[:, bk, 0:T], lhsT=mat[:, 384 + c0:384 + c0 + 128], rhs=sgx[:, 0:T],
                                                              start=True, stop=False), reads=[bmat, bsgx], writes=[b_ps[bk]])
                        S.pe(E("matmul", ps[:, bk, 0:T], lhsT=mat[0:32, 768 + c0:768 + c0 + 128],
                                                              rhs=sgx2[0:32, 0:T], start=False, stop=True),
                             reads=[bmat, bsgx2], writes=[b_ps[bk]])
                        S.act(E("copy", out=Gg[:, b, 0:T], in_=ps[:, bk, 0:T]), reads=[b_ps[bk]], writes=[b_G])
                        S.dve(E("tensor_scalar", out=t_kk[:, 0:T], in0=Kg[:, b, 0:T], scalar1=V_("kkv", b), scalar2=None,
                                                             op0=ALU.mult), reads=[b_K, bvec], writes=[bl_kk])
                        S.act(E("activation", out=R_(t_r[:, 0:T]), in_=t_kk[:, 0:T], func=AF.Square), reads=[bl_kk],
                              writes=[bl_r])
                        bk = nbank()
                        S.pe(E("matmul", ps[:, bk, 0:T], lhsT=R_(bonesr[:]), rhs=R_(t_r[:, 0:T]), start=True, stop=True),
                             reads=[b_bones, bl_r], writes=[b_ps[bk]])
                        S.act(E("activation", out=t_x[:, 0:T], in_=ps[:, bk, 0:T], func=AF.Sqrt), reads=[b_ps[bk]],
                              writes=[bl_x])
                        S.dve(E("tensor_scalar", out=t_x[:, 0:T], in0=t_x[:, 0:T], scalar1=1e-12, scalar2=None, op0=ALU.max),
                              reads=[bl_x], writes=[bl_x])
                        S.dve(E("reciprocal", out=t_x[:, 0:T], in_=t_x[:, 0:T]), reads=[bl_x], writes=[bl_x])
                        S.dve(E("tensor_tensor", out=t_kk[:, 0:T], in0=t_kk[:, 0:T], in1=t_x[:, 0:T], op=ALU.mult),
                              reads=[bl_x, bl_kk], writes=[bl_kk])
                        S.dve(E("tensor_scalar", out=t_x[:, 0:T], in0=t_a[:, 0:T], scalar1=V_("ka", b),
                                                             scalar2=vx[:, b:b + 1], op0=ALU.mult, op1=ALU.add),
                              reads=[bl_a, bvec, bvx], writes=[bl_x])
                        S.dve(E("tensor_tensor", out=R_(Kg[:, b, 0:T]), in0=Kg[:, b, 0:T], in1=t_x[:, 0:T], op=ALU.mult),
                              reads=[bl_x, b_K], writes=[b_K])
                        S.dve(E("scalar_tensor_tensor", out=R_(t_r[:, 0:T]), in0=Rg[:, b, 0:T], scalar=V_("rk", b),
                                                                    in1=Kg[:, b, 0:T], op0=ALU.mult, op1=ALU.mult),
                              reads=[b_R, b_K, bvec], writes=[bl_r])
                        bk = nbank()
                        S.pe(E("matmul", ps[:, bk, 0:T], lhsT=R_(bonesr[:]), rhs=R_(t_r[:, 0:T]), start=True, stop=True),
                             reads=[b_bones, bl_r], writes=[b_ps[bk]])
                        S.dve(E("tensor_tensor", out=bonus[:, b, 0:T], in0=ps[:, bk, 0:T], in1=Vg[:, b, 0:T],
                                                                    op=ALU.mult), reads=[b_ps[bk], b_V], writes=[b_bonus])
                    elif b < 6:
                        c0 = (b - 3) * 128
                        np_ = 64 if b == 5 else 128
                        bk = nbank()
                        S.pe(E("matmul", ps[0:np_, bk, 0:T], lhsT=mat[32:48, 1152 + c0:1152 + c0 + np_],
                                                                       rhs=gkb[32:48, 0:T], start=True, stop=True),
                             reads=[bmat, bgkb_], writes=[b_ps[bk]])
                        S.act(E("activation", out=t_ld[0:np_, 0:T], in_=ps[0:np_, bk, 0:T], func=AF.Exp,
                                                                          bias=vx[0:np_, b:b + 1], scale=-1.0),
                              reads=[b_ps[bk], bvx], writes=[bl_ld])
                        S.act(E("activation", out=t_ld[0:np_, 0:T], in_=t_ld[0:np_, 0:T], func=AF.Ln, bias=1.0,
                                                              scale=1.0), reads=[bl_ld], writes=[bl_ld])
                        S.dve(E("tensor_scalar", out=t_ld[0:np_, 0:T], in0=t_ld[0:np_, 0:T], scalar1=-1.0 / 16.0,
                                                                 scalar2=None, op0=ALU.mult), reads=[bl_ld], writes=[bl_ld])
                        if b == 5:
                            S.dve(E("tensor_scalar", out=t_ld[64:128, 0:T], in0=rmask[64:128, 0:T], scalar1=0.0,
                                                                 scalar2=V_("ldret", b)[64:128], op0=ALU.mult, op1=ALU.add),
                                  reads=[b_rmask, bvec], writes=[bl_ld])
                    else:
                        S.dve(E("tensor_scalar", out=t_ld[:, 0:T], in0=rmask[:, 0:T], scalar1=0.0,
                                                             scalar2=V_("ldret", b), op0=ALU.mult, op1=ALU.add),
                              reads=[b_rmask, bvec], writes=[bl_ld])
                    S.dve(E("tensor_tensor_scan", out=t_lw[:, 0:T], data0=rmask[:, 0:T], data1=t_ld[:, 0:T], initial=0.0,
                                                         op0=ALU.mult, op1=ALU.add), reads=[bl_ld, b_rmask], writes=[bl_lw])
                    S.act(E("activation", out=t_W[:, 0:T], in_=t_lw[:, 0:T], func=AF.Exp), reads=[bl_lw], writes=[bl_W])
                    S.act(E("activation", out=t_IW[:, 0:T], in_=t_lw[:, 0:T], func=AF.Exp, scale=-1.0), reads=[bl_lw],
                          writes=[bl_IW])
                    S.act(E("copy", out=WC[:, b, 0:nch],
                                                              in_=t_W[:, 0:T].rearrange("p (c t) -> p c t", t=C)[:, :, C - 1]),
                          reads=[bl_W], writes=[b_WC])
                    if b < 3:
                        S.dve(E("tensor_tensor", out=t_y[:, 0:T], in0=t_lw[:, 0:T], in1=t_ld[:, 0:T], op=ALU.subtract),
                              reads=[bl_lw, bl_ld], writes=[bl_y])
                        S.act(E("activation", out=t_y[:, 0:T], in_=t_y[:, 0:T], func=AF.Exp), reads=[bl_y], writes=[bl_y])
                        S.dve(E("scalar_tensor_tensor", out=R_(ATt[:, b, 0:T]), in0=t_kk[:, 0:T], scalar=-1.0,
                                                                    in1=t_y[:, 0:T], op0=ALU.mult, op1=ALU.mult),
                              reads=[bl_kk, bl_y], writes=[b_AT])
                        S.dve(E("tensor_tensor", out=t_y[:, 0:T], in0=t_kk[:, 0:T], in1=t_a[:, 0:T], op=ALU.mult),
                              reads=[bl_kk, bl_a], writes=[bl_y])
                        S.dve(E("tensor_tensor", out=R_(BTt[:, b, 0:T]), in0=t_y[:, 0:T], in1=t_IW[:, 0:T], op=ALU.mult),
                              reads=[bl_y, bl_IW], writes=[b_BT])
                    S.dve(E("scalar_tensor_tensor", out=R_(Rg[:, b, 0:T]), in0=Rg[:, b, 0:T], scalar=V_("rscale", b),
                                                                in1=t_W[:, 0:T], op0=ALU.mult, op1=ALU.mult),
                          reads=[b_R, bl_W, bvec], writes=[b_R])
                    S.dve(E("scalar_tensor_tensor", out=R_(Kg[:, b, 0:T]), in0=Kg[:, b, 0:T], scalar=V_("kscale", b),
                                                                in1=t_IW[:, 0:T], op0=ALU.mult, op1=ALU.mult),
                          reads=[b_K, bl_IW, bvec], writes=[b_K])

                stage(3)
                for ci in range(nch):
                    t0 = ci * C
                    cs = slice(t0, t0 + C)
                    cfg.setdefault("_marks", []).append((l, ci, len(S.ops)))
                    if kind == "p":
                        St, bSt = Sst[l], b_Sst[l]
                    else:
                        sp_ = ci % 2
                        St, bSt = Ssm[sp_], b_Ssm[sp_]
                        S.dma(E("dma_start", out=Sld[sp_][:].rearrange("p a b -> p (a b)"),
                                                                         in_=sst_d[l, ci]),
                              writes=[b_Sld[sp_]], chan="sld0")
                        S.act(E("copy", out=R_(St[:]), in_=Sld[sp_][:]), reads=[b_Sld[sp_]], writes=[bSt])
                    for (src, bsrc, dst, bdst, nb_) in ((Vg, b_V, vtok, b_vtok, 8), (Kg, b_K, ktok, b_ktok, 8),
                                                        (BTt, b_BT, btok, b_btok, 3)):
                        for half in range((nb_ + 3) // 4):
                            bk = nbank()
                            n_in = min(4, nb_ - half * 4)
                            for bl in range(n_in):
                                blk = half * 4 + bl
                                S.pe(E("transpose",
                                    ps[0:C, bk, bl * 128:(bl + 1) * 128], src[:, blk, cs], ident),
                                    reads=[bsrc, b_cst], writes=[b_ps[bk]])
                            w = n_in * 128
                            S.act(E("copy", out=R_(dst[0:C, half * 512:half * 512 + w]),
                                                                                   in_=ps[0:C, bk, 0:w]),
                                  reads=[b_ps[bk]], writes=[bdst])
                    stage(31)
                    def hp(h):
                        return slice((h % 2) * 64, (h % 2) * 64 + 64), h // 2

                    def pv3(bk, n):
                        return ps[0:C, bk, 0:n * C].rearrange("p (h c) -> p h c", c=C)

                    def bc(m, n):
                        return m[0:C, 0:C].unsqueeze(1).to_broadcast([C, n, C])

                    def par3(t, par, n):
                        return t[0:C, :, 0:C].rearrange("p (a two) c -> p a two c", two=2)[:, :, par, :]

                    hpb = min(8, 512 // C)
                    def a_kind(lt, blt, rt_, brt, dst, bdst, msk, nh):
                        npar = nh // 2
                        ngrp = (npar + hpb - 1) // hpb
                        banks = {}
                        for g_ in range(ngrp):
                            for par in range(2):
                                banks[(par, g_)] = nbank()
                        for h in range(nh):
                            rows, blk = hp(h)
                            hl = h // 2
                            bk = banks[(h % 2, hl // hpb)]
                            hi = hl % hpb
                            S.pe(E("matmul", ps[0:C, bk, hi * C:(hi + 1) * C], lhsT=R_(lt[rows, blk, cs]),
                                   rhs=R_(rt_[rows, blk, cs]), start=True, stop=True), reads=[blt, brt], writes=[b_ps[bk]])
                        for g_ in range(ngrp):
                            n_ = min(hpb, npar - g_ * hpb)
                            for par in range(2):
                                bk = banks[(par, g_)]
                                o_ = par3(dst, par, nh)[:, g_ * hpb:g_ * hpb + n_, :]
                                if dst is not Pm[0] and dst is not Qm[0]:
                                    o_ = R_(o_)
                                S.dve(E("tensor_tensor", out=o_, in0=pv3(bk, n_),
                                        in1=bc(msk, n_), op=ALU.mult), reads=[b_ps[bk], b_cst], writes=[bdst])

                    a_kind(BTt, b_BT, ATt, b_AT, Pm[0], b_Pm[0], m_su, 6)
                    a_kind(ATt, b_AT, BTt, b_BT, Qm[0], b_Qm[0], m_sl, 6)
                    if False:
                        dbg("AT0", ATt[:, 0, cs], [b_AT]); dbg("BT0", BTt[:, 0, cs], [b_BT])
                        dbg("K0", Kg[:, 0, cs], [b_K]); dbg("R0", Rg[:, 0, cs], [b_R]); dbg("V0", Vg[:, 0, cs], [b_V])
                        dbg("WC", WC[:, :, 0:4], [b_WC])
                        dbg("P0", Pm[0][0:C, :, 0:C], [b_Pm[0]], C); dbg("Q0", Qm[0][0:C, :, 0:C], [b_Qm[0]], C)
                        dbg("aak", aak[0:C, :, 0:C], [b_aak], C); dbg("arb", arb[0:C, :, 0:C], [b_arb], C)
                        dbg("btok", btok[0:C, 0:128], [b_btok], C)
                    stage(32)
                    S.dve(E("tensor_tensor", out=R_(Xm[0:C, :, 0:C]), in0=Pm[0][0:C, :, 0:C], in1=bc(eye64, 6), op=ALU.add),
                          reads=[b_Pm[0], b_cst], writes=[b_Xm])
                    S.act(E("copy", out=Xb[0:C, :, 0:C], in_=Xm[0:C, :, 0:C]), reads=[b_Xm], writes=[b_Xb])
                    ng_ = 1 if 6 * C <= 512 else 2
                    hg_ = 6 // ng_

                    def inv_mm(lhs_t, blhs, rhs_t, brhs):
                        bks = []
                        for g_ in range(ng_):
                            bk = nbank()
                            bks.append(bk)
                            for hi in range(hg_):
                                h = g_ * hg_ + hi
                                S.pe(E("matmul", ps[0:C, bk, hi * C:(hi + 1) * C], lhsT=lhs_t[0:C, h, 0:C],
                                       rhs=rhs_t[0:C, h, 0:C], start=True, stop=True), reads=[blhs, brhs], writes=[b_ps[bk]])
                        return bks

                    def gsl(t, g_):
                        return t[0:C, g_ * hg_:(g_ + 1) * hg_, 0:C]

                    def par2(t, par, n):
                        return t[0:C, 0:n * 64].rearrange("p (a two v) -> p a two v", two=2, v=64)[:, :, par, :]

                    def pv64(bk, n):
                        return ps[0:C, bk, 0:n * 64].rearrange("p (h v) -> p h v", v=64)

                    def f_y_state():
                        bkp = (nbank(), nbank())
                        for h in range(16):
                            rows, blk = hp(h)
                            bk = bkp[h % 2]
                            hl = h // 2
                            S.pe(E("matmul", ps[0:C, bk, hl * 64:(hl + 1) * 64], lhsT=R_(Rg[rows, blk, cs]), rhs=R_(St[rows, blk, :]),
                                   start=True, stop=True), reads=[b_R, bSt], writes=[b_ps[bk]])
                        for par in range(2):
                            S.act(E("copy", out=par2(ysb, par, 16), in_=pv64(bkp[par], 8)), reads=[b_ps[bkp[par]]], writes=[b_ysb])

                    def f_rhs_state():
                        bkp = (nbank(), nbank())
                        for h in range(6):
                            rows, blk = hp(h)
                            bk = bkp[h % 2]
                            hl = h // 2
                            S.pe(E("matmul", ps[0:C, bk, hl * 64:(hl + 1) * 64], lhsT=R_(ATt[rows, blk, cs]), rhs=R_(St[rows, blk, :]),
                                   start=True, stop=True), reads=[b_AT, bSt], writes=[b_ps[bk]])
                        for par in range(2):
                            S.act(E("copy", out=R_(par2(rhs0, par, 6)), in_=pv64(bkp[par], 3)), reads=[b_ps[bkp[par]]], writes=[b_rhs0])

                    def f_rhs_tok():
                        bk = nbank()
                        for h in range(6):
                            S.pe(E("matmul", ps[0:C, bk, h * 64:(h + 1) * 64], lhsT=R_(aak[0:C, h, 0:C]),
                                   rhs=R_(vtok[0:C, h * 64:(h + 1) * 64]), start=True, stop=True),
                                 reads=[b_aak, b_vtok], writes=[b_ps[bk]])
                        S.dve(E("tensor_tensor", out=R_(rhs0[0:C, :]), in0=rhs0[0:C, :], in1=ps[0:C, bk, 0:384], op=ALU.add),
                              reads=[b_ps[bk], b_rhs0], writes=[b_rhs0])

                    def f_y_tok(half):
                        bk = nbank()
                        for hl in range(8):
                            h = half * 8 + hl
                            S.pe(E("matmul", ps[0:C, bk, hl * 64:(hl + 1) * 64], lhsT=R_(ark[0:C, h, 0:C]),
                                   rhs=R_(vtok[0:C, h * 64:(h + 1) * 64]), start=True, stop=(h >= 6)),
                                 reads=[b_ark, b_vtok], writes=[b_ps[bk]])
                            if h < 6:
                                S.pe(E("matmul", ps[0:C, bk, hl * 64:(hl + 1) * 64], lhsT=R_(arb[0:C, h, 0:C]),
                                       rhs=R_(usb[0:C, h * 64:(h + 1) * 64]), start=False, stop=True),
                                     reads=[b_arb, b_usb], writes=[b_ps[bk]])
                        S.dve(E("tensor_tensor", out=ysb[0:C, half * 512:(half + 1) * 512], in0=ysb[0:C, half * 512:(half + 1) * 512],
                                in1=ps[0:C, bk, :], op=ALU.add), reads=[b_ps[bk], b_ysb], writes=[b_ysb])

                    fillers = [lambda: a_kind(Kg, b_K, Rg, b_R, ark, b_ark, m_incl, 16),
                               lambda: a_kind(Kg, b_K, ATt, b_AT, aak, b_aak, m_su, 6),
                               f_y_state,
                               lambda: a_kind(BTt, b_BT, Rg, b_R, arb, b_arb, m_incl, 6),
                               f_rhs_state, f_rhs_tok, lambda: f_y_tok(1)]
                    for st in range(nsteps):
                        bq = inv_mm(Pm[0], b_Pm[0], Qm[0], b_Qm[0])
                        if st < nsteps - 1:
                            bp = inv_mm(Qm[0], b_Qm[0], Pm[0], b_Pm[0])
                        for g_ in range(ng_):
                            S.act(E("copy", out=gsl(Qm[0], g_), in_=pv3(bq[g_], hg_)), reads=[b_ps[bq[g_]]], writes=[b_Qm[0]])
                        if st < nsteps - 1:
                            for g_ in range(ng_):
                                S.dve(E("tensor_copy", out=gsl(Pm[0], g_), in_=pv3(bp[g_], hg_)), reads=[b_ps[bp[g_]]],
                                      writes=[b_Pm[0]])
                        if fillers:
                            fillers.pop(0)()
                        bx = inv_mm(Qm[0], b_Qm[0], Xb, b_Xb)
                        for g_ in range(ng_):
                            S.dve(E("tensor_tensor", out=R_(gsl(Xm, g_)), in0=gsl(Xm, g_), in1=pv3(bx[g_], hg_), op=ALU.add),
                                  reads=[b_ps[bx[g_]], b_Xm], writes=[b_Xm])
                            if st < nsteps - 1:
                                S.act(E("copy", out=gsl(Xb, g_), in_=gsl(Xm, g_)), reads=[b_Xm], writes=[b_Xb])
                    while fillers:
                        fillers.pop(0)()
                    stage(33)
                    bk = nbank()
                    for h in range(6):
                        S.pe(E("matmul", ps[0:C, bk, h * 64:(h + 1) * 64], lhsT=R_(Xm[0:C, h, 0:C]),
                               rhs=R_(rhs0[0:C, h * 64:(h + 1) * 64]), start=True, stop=True),
                             reads=[b_Xm, b_rhs0], writes=[b_ps[bk]])
                    S.act(E("copy", out=R_(usb[0:C, :]), in_=ps[0:C, bk, 0:384]), reads=[b_ps[bk]], writes=[b_usb])
                    stage(34)
                    f_y_tok(0)
                    stage(35)
                    ds_banks = []
                    for half in range(2):
                        bk = nbank()
                        ds_banks.append(bk)
                        for bl in range(4):
                            blk = half * 4 + bl
                            S.pe(E("matmul", ps[:, bk, bl * 128:(bl + 1) * 128],
                                   lhsT=R_(ktok[0:C, blk * 128:(blk + 1) * 128]),
                                   rhs=R_(vtok[0:C, blk * 128:(blk + 1) * 128]),
                                   start=True, stop=(blk >= 3)), reads=[b_ktok, b_vtok], writes=[b_ps[bk]])
                            if blk < 3:
                                S.pe(E("matmul", ps[:, bk, bl * 128:(bl + 1) * 128],
                                       lhsT=R_(btok[0:C, blk * 128:(blk + 1) * 128]),
                                       rhs=R_(usb[0:C, blk * 128:(blk + 1) * 128]),
                                       start=False, stop=True), reads=[b_btok, b_usb], writes=[b_ps[bk]])
                    stage(36)
                    y3 = ysb[0:C, :].rearrange("p (h v) -> p h v", v=64)
                    q3 = ysq[0:C, :].rearrange("p (h v) -> p h v", v=64)
                    s1, s2, mean, tmpv, rstd = (yst[0:C, i, :] for i in range(5))
                    S.dve(E("tensor_tensor", out=R_(ysq[0:C, :]), in0=ysb[0:C, :], in1=ysb[0:C, :], op=ALU.mult), reads=[b_ysb],
                          writes=[b_ysq])
                    S.dve(E("tensor_reduce", out=s1, in_=y3, axis=AX.X, op=ALU.add), reads=[b_ysb], writes=[b_yst])
                    S.dve(E("tensor_reduce", out=s2, in_=q3, axis=AX.X, op=ALU.add), reads=[b_ysq], writes=[b_yst])
                    S.dve(E("tensor_tensor", out=mean, in0=s1, in1=tokc[0:C, 0:16], op=ALU.mult),
                          reads=[b_yst, b_cst], writes=[b_yst])
                    S.dve(E("tensor_tensor", out=tmpv, in0=mean, in1=mean, op=ALU.mult), reads=[b_yst],
                          writes=[b_yst])
                    S.dve(E("scalar_tensor_tensor", out=tmpv, in0=s2, scalar=1.0 / 64, in1=tmpv, op0=ALU.mult,
                                                                             op1=ALU.subtract), reads=[b_yst], writes=[b_yst])
                    S.dve(E("tensor_tensor", out=tmpv, in0=tmpv, in1=tokc[0:C, 16:32], op=ALU.add),
                          reads=[b_yst, b_cst], writes=[b_yst])
                    S.act(E("activation", out=tmpv, in_=tmpv, func=AF.Sqrt), reads=[b_yst], writes=[b_yst])
                    S.dve(E("reciprocal", out=rstd, in_=tmpv), reads=[b_yst], writes=[b_yst])
                    S.dve(E("tensor_tensor", out=y3, in0=y3, in1=mean.unsqueeze(2).to_broadcast([C, 16, 64]),
                                                                      op=ALU.subtract), reads=[b_yst, b_ysb], writes=[b_ysb])
                    S.dve(E("tensor_tensor", out=y3, in0=y3, in1=rstd.unsqueeze(2).to_broadcast([C, 16, 64]),
                                                                      op=ALU.mult), reads=[b_yst, b_ysb], writes=[b_ysb])
                    bpb = min(8, 512 // C)
                    for g_ in range(8 // bpb):
                        bk = nbank()
                        for bi in range(bpb):
                            blk = g_ * bpb + bi
                            S.pe(E("transpose", ps[:, bk, bi * C:(bi + 1) * C], ysb[0:C, blk * 128:(blk + 1) * 128],
                                   cst[0:C, 0:C]), reads=[b_ysb, b_cst], writes=[b_ps[bk]])
                        S.act(E("copy", out=ynT[:, g_ * bpb:(g_ + 1) * bpb, cs],
                                in_=ps[:, bk, 0:bpb * C].rearrange("p (a c) -> p a c", c=C)), reads=[b_ps[bk]], writes=[b_ynT])

                    for half in range(2):
                        bk = ds_banks[half]
                        for hh in range(2):
                            rows = slice(hh * 64, hh * 64 + 64)
                            S.dve(E("tensor_tensor",
                                out=R_(St[rows, half * 4:half * 4 + 4, :]), in0=St[rows, half * 4:half * 4 + 4, :],
                                in1=ps[rows, bk, :].rearrange("p (a b) -> p a b", b=128)[:, :, hh * 64:hh * 64 + 64], op=ALU.add),
                                reads=[b_ps[bk], bSt], writes=[bSt])
                    S.dve(E("tensor_tensor", out=R_(St[:]), in0=St[:], in1=WC[:, :, ci:ci + 1].to_broadcast([128, 8, 64]),
                                                           op=ALU.mult), reads=[b_WC, bSt], writes=[bSt])
                    if kind == "s":
                        S.dma(E("dma_start", out=sto_d[l, 1 + ci], in_=St[:].rearrange("p a b -> p (a b)")),
                              reads=[bSt], chan="stout")
                    elif last_prompt and ci == nch - 1:
                        S.dma(E("dma_start", out=sto_d[l, 0], in_=St[:].rearrange("p a b -> p (a b)")),
                              reads=[bSt], chan="stout")

                stage(4)
                for b in range(8):
                    S.dve(E("tensor_scalar", out=ynT[:, b, 0:T], in0=ynT[:, b, 0:T], scalar1=V_("nw", b), scalar2=V_("nb", b),
                                                         op0=ALU.mult, op1=ALU.add), reads=[b_ynT, bvec], writes=[b_ynT])
                    if b < 3:
                        S.dve(E("tensor_tensor", out=ynT[:, b, 0:T], in0=ynT[:, b, 0:T], in1=bonus[:, b, 0:T], op=ALU.add),
                              reads=[b_ynT, b_bonus], writes=[b_ynT])
                    S.dve(E("tensor_tensor", out=mixT[:, b, 0:T], in0=ynT[:, b, 0:T], in1=Gg[:, b, 0:T], op=ALU.mult),
                          reads=[b_ynT, b_G], writes=[b_mixT])
                    if hpf:
                        S.dve(E("tensor_tensor", out=mix32[:, b, :], in0=ynT[:, b, 0:2], in1=Gg[:, b, 0:2], op=ALU.mult),
                              reads=[b_ynT, b_G], writes=[b_mix32])

                def ln_epilogue(gname, bname):
                    r16, br16 = hT[:, 0:8, :], b_hT
                    S.act(E("copy", out=hT[:, 0:8, 0:T], in_=x32[:, :, 0:T]), reads=[b_x32], writes=[b_hT])
                    S.act(E("activation", out=hT[:, 8:16, 0:T], in_=x32[:, :, 0:T], func=AF.Square), reads=[b_x32], writes=[b_hT])
                    bk1 = nbank(); bk2 = nbank()
                    for kc in range(8):
                        S.pe(E("matmul", ps[:, bk1, 0:T], lhsT=onesb[:], rhs=hT[:, kc, 0:T], start=(kc == 0), stop=(kc == 7)),
                             reads=[b_ones, b_hT], writes=[b_ps[bk1]])
                    for kc in range(8):
                        S.pe(E("matmul", ps[:, bk2, 0:T], lhsT=onesb[:], rhs=hT[:, 8 + kc, 0:T], start=(kc == 0),
                                                       stop=(kc == 7)), reads=[b_ones, b_hT], writes=[b_ps[bk2]])
                    mean, var = tmp[0], tmp[1]
                    S.act(E("mul", out=mean[:, 0:T], in_=ps[:, bk1, 0:T], mul=1.0 / D), reads=[b_ps[bk1]], writes=[b_tmp[0]])
                    S.dve(E("tensor_tensor", out=var[:, 0:T], in0=mean[:, 0:T], in1=mean[:, 0:T], op=ALU.mult), reads=[b_tmp[0]],
                          writes=[b_tmp[1]])
                    S.dve(E("scalar_tensor_tensor", out=var[:, 0:T], in0=ps[:, bk2, 0:T], scalar=1.0 / D, in1=var[:, 0:T],
                                                           op0=ALU.mult, op1=ALU.subtract), reads=[b_ps[bk2], b_tmp[1]], writes=[b_tmp[1]])
                    S.dve(E("tensor_scalar", out=var[:, 0:T], in0=var[:, 0:T], scalar1=1e-5, scalar2=None, op0=ALU.add),
                          reads=[b_tmp[1]], writes=[b_tmp[1]])
                    S.act(E("activation", out=var[:, 0:T], in_=var[:, 0:T], func=AF.Sqrt), reads=[b_tmp[1]], writes=[b_tmp[1]])
                    S.dve(E("reciprocal", out=var[:, 0:T], in_=var[:, 0:T]), reads=[b_tmp[1]], writes=[b_tmp[1]])
                    S.dve(E("tensor_tensor", out=x32[:, :, 0:T], in0=x32[:, :, 0:T],
                                                    in1=mean[:, 0:T].unsqueeze(1).to_broadcast([128, 8, T]), op=ALU.subtract),
                          reads=[b_tmp[0], b_x32], writes=[b_x32])
                    S.dve(E("tensor_tensor", out=x32[:, :, 0:T], in0=x32[:, :, 0:T],
                                                    in1=var[:, 0:T].unsqueeze(1).to_broadcast([128, 8, T]), op=ALU.mult),
                          reads=[b_tmp[1], b_x32], writes=[b_x32])
                    for b in range(8):
                        S.dve(E("tensor_scalar", out=x32[:, b, 0:T], in0=x32[:, b, 0:T], scalar1=V_(gname, b),
                                                             scalar2=V_(bname, b), op0=ALU.mult, op1=ALU.add),
                              reads=[b_x32, bvec], writes=[b_x32])
                    S.act(E("copy", out=xTb[:, :, 0:T], in_=x32[:, :, 0:T]), reads=[b_x32], writes=[b_xTb])

                stage(5)
                for piece in range(4):
                    slot, bslot = ws_next(l, NPIECE_IN + piece)
                    for sub in range(2):
                        cb = piece * 2 + sub
                        bk = nbank()
                        for kc in range(8):
                            S.pe(E("matmul",
                                ps[:, bk, 0:T], lhsT=slot[:, kc * 256 + sub * 128: kc * 256 + sub * 128 + 128],
                                rhs=mixT[:, kc, 0:T], start=(kc == 0), stop=(kc == 7)), reads=[bslot, b_mixT], writes=[b_ps[bk]])
                        if hpf:
                            bkh = hp_mm8(l, NPIECE_IN + piece, sub, lambda kc: mix32[:, kc, 0:2], b_mix32)
                            S.dve(E("scalar_tensor_tensor", out=x32[:, cb, 0:2], in0=x32[:, cb, 0:2], scalar=ALPHA,
                                    in1=ps[:, bkh, 0:2], op0=ALU.mult, op1=ALU.add), reads=[b_ps[bkh], b_x32], writes=[b_x32])
                        S.dve(E("scalar_tensor_tensor", out=x32[:, cb, clo:T], in0=x32[:, cb, clo:T], scalar=ALPHA,
                                                                             in1=ps[:, bk, clo:T], op0=ALU.mult, op1=ALU.add),
                              reads=[b_ps[bk], b_x32], writes=[b_x32])
                ln_epilogue("ln1g", "ln1b")

                stage(6)
                if kind == "s":
                    S.dma(E("dma_start", out=cvst[:].rearrange("p a b c -> p (a b c)"), in_=scv_d[l]), writes=[b_cvst],
                          chan="cvst")
                ffn_defer = [None]
                for j in range(NJ):
                    slot, bslot = ws_next(l, NPIECE_IN + 4 + j)
                    cv = []
                    for ab in range(2):
                        cbk = j + NJ * ab
                        bk = nbank()
                        for kc in range(8):
                            S.pe(E("matmul",
                                ps[:, bk, 0:T], lhsT=slot[:, kc * 256 + ab * 128: kc * 256 + ab * 128 + 128],
                                rhs=xTb[:, kc, 0:T], start=(kc == 0), stop=(kc == 7)), reads=[bslot, b_xTb], writes=[b_ps[bk]])
                        ub, bub = ubuf[ab + 2 * (j % 2)], b_ubuf[ab + 2 * (j % 2)]
                        u3 = ub[:, 0:nseq * (L + 2)].rearrange("p (s t) -> p s t", t=L + 2)
                        if kind == "s":
                            S.act(E("copy", out=u3[:, :, 0:2], in_=cvst[:, cbk, :, :]), reads=[b_cvst],
                                  writes=[bub])
                        else:
                            S.act(E("copy", out=u3[:, 0, 0:2], in_=cvc[:, l, cbk, :]),
                                  reads=[b_cvc[l]], writes=[bub])
                        if hpf:
                            bkh = hp_mm8(l, NPIECE_IN + 4 + j, ab, lambda kc: x32[:, kc, 0:2], b_x32)
                            S.act(E("copy", out=u3[:, 0, 2:4], in_=ps[:, bkh, 0:2]), reads=[b_ps[bkh]], writes=[bub])
                            S.act(E("copy", out=u3[:, 0, 4:L + 2], in_=ps[:, bk, 2:T]), reads=[b_ps[bk]], writes=[bub])
                        else:
                            S.act(E("copy", out=u3[:, :, 2:L + 2], in_=t3(ps[:, bk, 0:T])), reads=[b_ps[bk]], writes=[bub])
                        if kind == "s":
                            S.act(E("copy", out=cvo[:, cbk, :, :], in_=u3[:, :, L:L + 2]), reads=[bub], writes=[b_cvo])
                        else:
                            S.act(E("copy", out=cvc[:, l, cbk, :], in_=u3[:, 0, L:L + 2]), reads=[bub],
                                  writes=[b_cvc[l]])
                        ct, bct = tmp[2 + ab + 2 * (j % 2)], b_tmp[2 + ab + 2 * (j % 2)]
                        c3 = t3(ct[:, 0:T])
                        S.pool(E("tensor_scalar", out=c3, in0=u3[:, :, 0:L], scalar1=V_("cw0", cbk),
                                                                               scalar2=V_("cb", cbk), op0=ALU.mult, op1=ALU.add),
                              reads=[bub, bvec], writes=[bct])
                        S.dve(E("scalar_tensor_tensor", out=c3, in0=u3[:, :, 1:L + 1], scalar=V_("cw1", cbk),
                                                                                      in1=c3, op0=ALU.mult, op1=ALU.add),
                              reads=[bub, bvec, bct], writes=[bct])
                        S.dve(E("scalar_tensor_tensor", out=c3, in0=u3[:, :, 2:L + 2], scalar=V_("cw2", cbk),
                                                                                      in1=c3, op0=ALU.mult, op1=ALU.add),
                              reads=[bub, bvec, bct], writes=[bct])
                        cv.append((ct, bct))
                    (ca, bca), (cb_, bcb) = cv

                    def gelu_h(j=j, ca=ca, bca=bca, cb_=cb_, bcb=bcb):
                        S.act(E("activation", out=ca[:, 0:T], in_=ca[:, 0:T], func=AF.Gelu_apprx_tanh), reads=[bca], writes=[bca])
                        S.dve(E("tensor_tensor", out=hT[:, j, 0:T], in0=ca[:, 0:T], in1=cb_[:, 0:T], op=ALU.mult),
                              reads=[bca, bcb], writes=[b_hT])
                        if hpf:
                            S.dve(E("tensor_tensor", out=h32[:, j, :], in0=ca[:, 0:2], in1=cb_[:, 0:2], op=ALU.mult),
                                  reads=[bca, bcb], writes=[b_h32])

                    if ffn_defer[0] is not None:
                        ffn_defer[0]()
                    ffn_defer[0] = gelu_h
                ffn_defer[0]()
                ffn_defer[0] = None
                if kind == "s":
                    S.dma(E("dma_start", out=cvo_d[l][:, 88:88 * (1 + NSS)], in_=cvo[:].rearrange("p a b c -> p (a b c)")),
                          reads=[b_cvo], chan="cvout")
                elif last_prompt:
                    S.dma(E("dma_start", out=cvo_d[l][:, 0:88], in_=cvc[:, l, :, :].rearrange("p a c -> p (a c)")),
                          reads=[b_cvc[l]], chan="cvout")
                for cb in range(8):
                    slot, bslot = ws_next(l, NPIECE_IN + 4 + NJ + cb)
                    bk = nbank()
                    for j in range(NJ):
                        S.pe(E("matmul", ps[:, bk, 0:T], lhsT=slot[:, j * 128:(j + 1) * 128], rhs=hT[:, j, 0:T],
                                                                       start=(j == 0), stop=(j == NJ - 1)),
                             reads=[bslot, b_hT], writes=[b_ps[bk]])
                    if hpf:
                        bkh = nbank()
                        for jh in range(3):
                            njj = min(8, NJ - jh * 8)
                            w, bw = hp_load(l, NPIECE_IN + 4 + NJ + cb, ("jh", jh, njj))
                            for jj in range(njj):
                                j = jh * 8 + jj
                                S.pe(E("matmul", ps[:, bkh, 0:2], lhsT=w[:, jj * 128:(jj + 1) * 128], rhs=h32[:, j, 0:2],
                                       start=(j == 0), stop=(j == NJ - 1)), reads=[bw, b_h32], writes=[b_ps[bkh]])
                        S.dve(E("scalar_tensor_tensor", out=x32[:, cb, 0:2], in0=x32[:, cb, 0:2], scalar=ALPHA,
                                in1=ps[:, bkh, 0:2], op0=ALU.mult, op1=ALU.add), reads=[b_ps[bkh], b_x32], writes=[b_x32])
                    S.dve(E("scalar_tensor_tensor", out=x32[:, cb, clo:T], in0=x32[:, cb, clo:T], scalar=ALPHA,
                                                                         in1=ps[:, bk, clo:T], op0=ALU.mult, op1=ALU.add),
                          reads=[b_ps[bk], b_x32], writes=[b_x32])
                stage(7)
                ln_epilogue("ln2g", "ln2b")

            S.dma(E("dma_start", out=yT_d[:, :, tok0:tok0 + T], in_=x32[:, :, 0:T]), reads=[b_x32], chan="yout")


    try:
        _tile_loop()
        assert ws["used"] == len(plan)
    except _Stop:
        pass
    S.run_block()
    return nc, S


_CACHE = {}


def _prep_inputs(inp, cfg):
    f = np.float32
    W = _prep_weights(inp)
    vm = [_prep_layer_params(inp, l) for l in range(NL)]
    vecs = np.stack([v for v, _ in vm])
    mats = np.stack([m for _, m in vm])
    cst = _consts()
    cosT, sinT = _rope_tables()
    in_maps = []
    for c in range(8):
        xp = inp["x_prompt"][c]
        xs = inp["x_sample"][c * NSS:(c + 1) * NSS].reshape(NSS * LS, D)
        x = np.concatenate([xp, xs], axis=0)
        xT = np.ascontiguousarray(x.T.reshape(8, 128, -1).transpose(1, 0, 2))
        sst = np.zeros((NL, NSS, 16, 64, 64), f)
        sl = slice(c * NSS, (c + 1) * NSS)
        sst[:, :, 0:6] = inp["state_rwkv"][:, sl].transpose(0, 1, 2, 4, 3)
        sst[:, :, 6:11, 0:32] = inp["state_gla"][:, sl]
        sst[:, :, 11:16] = inp["state_ret"][:, sl]
        sst = sst.reshape(NL, NSS, 8, 2, 64, 64).transpose(0, 1, 3, 4, 2, 5).reshape(NL, NSS, 128, 512)
        sh = inp["state_shift"][:, sl]
        shp = np.zeros((NL, NSS, 12 * 128), f)
        shp[:, :, 0:1152] = sh[:, :, 0:1152]
        shp[:, :, 1152:1408] = sh[:, :, 1152:1408]
        shp[:, :, 1408:1440] = sh[:, :, 1408:1440]
        ssh = shp.reshape(NL, NSS, 12, 128).transpose(0, 3, 2, 1).reshape(NL, 128, 12 * NSS)
        cvs = inp["state_conv"][:, sl]
        scv = cvs.reshape(NL, NSS, 2, 44, 128).transpose(0, 4, 3, 1, 2).reshape(NL, 128, 44 * NSS * 2)
        in_maps.append({"xT": xT, "wts": W, "vecs": vecs, "mats": mats, "cst": cst, "ropec": cosT, "ropes": sinT,
                        "sst": np.ascontiguousarray(sst), "ssh": np.ascontiguousarray(ssh), "scv": np.ascontiguousarray(scv)})
    return in_maps


def _assemble(results):
    f = np.float32
    y_p = np.zeros((8, SEQ, D), f); y_s = np.zeros((8 * NSS, LS, D), f)
    rw_p = np.zeros((NL, 8, 6, 64, 64), f); rw_s = np.zeros((NL, 8 * NSS, 6, 64, 64), f)
    sh_p = np.zeros((NL, 8, 1440), f); sh_s = np.zeros((NL, 8 * NSS, 1440), f)
    gl_p = np.zeros((NL, 8, 5, 32, 64), f); gl_s = np.zeros((NL, 8 * NSS, 5, 32, 64), f)
    rt_p = np.zeros((NL, 8, 5, 64, 64), f); rt_s = np.zeros((NL, 8 * NSS, 5, 64, 64), f)
    cv_p = np.zeros((NL, 8, 2, 2 * DFF), f); cv_s = np.zeros((NL, 8 * NSS, 2, 2 * DFF), f)
    for c, r in enumerate(results):
        yT = r["yT"]
        y = yT.transpose(1, 0, 2).reshape(D, -1).T
        y_p[c] = y[0:SEQ]
        y_s[c * NSS:(c + 1) * NSS] = y[SEQ:].reshape(NSS, LS, D)
        st = r["sto"].reshape(NL, 1 + NSS, 2, 64, 8, 64).transpose(0, 1, 4, 2, 3, 5).reshape(NL, 1 + NSS, 16, 64, 64)
        rw = st[:, :, 0:6].transpose(0, 1, 2, 4, 3)
        gl = st[:, :, 6:11, 0:32]
        rt = st[:, :, 11:16]
        rw_p[:, c] = rw[:, 0]; gl_p[:, c] = gl[:, 0]; rt_p[:, c] = rt[:, 0]
        sl = slice(c * NSS, (c + 1) * NSS)
        rw_s[:, sl] = rw[:, 1:]; gl_s[:, sl] = gl[:, 1:]; rt_s[:, sl] = rt[:, 1:]
        sho = r["sho"].reshape(NL, 128, 12 * (1 + NSS))
        p = sho[:, :, 0:12].transpose(0, 2, 1).reshape(NL, 12 * 128)
        s = sho[:, :, 12:].reshape(NL, 128, 12, NSS).transpose(0, 3, 2, 1).reshape(NL, NSS, 12 * 128)
        sh_p[:, c] = p[:, 0:1440]
        sh_s[:, sl] = s[:, :, 0:1440]
        cvo = r["cvo"].reshape(NL, 128, 44 * (1 + NSS) * 2)
        p = cvo[:, :, 0:88].reshape(NL, 128, 44, 2).transpose(0, 3, 2, 1).reshape(NL, 2, 2 * DFF)
        s = cvo[:, :, 88:].reshape(NL, 128, 44, NSS, 2).transpose(0, 3, 4, 2, 1).reshape(NL, NSS, 2, 2 * DFF)
        cv_p[:, c] = p
        cv_s[:, sl] = s
    return (y_p, y_s, rw_p, sh_p, gl_p, rt_p, cv_p, rw_s, sh_s, gl_s, rt_s, cv_s)


def kernel(**inputs):
    inp = {k: np.asarray(v) for k, v in inputs.items()}
    cfg = CFG_FULL
    if "nc" not in _CACHE:
        _CACHE["nc"] = build(cfg)[0]
    nc = _CACHE["nc"]
    in_maps = _prep_inputs(inp, cfg)
    res = run_bass_kernel_spmd(nc, in_maps, core_ids=list(range(8)))
    return _assemble(res.results)
```

```python
import bisect
import math
from contextlib import ExitStack

import numpy as np
import concourse.bass as bass
import concourse.mybir as mybir
from concourse.bass_utils import run_bass_kernel_spmd

F32 = mybir.dt.float32
F32R = mybir.dt.float32r
BF16 = mybir.dt.bfloat16
AF = mybir.ActivationFunctionType
ALU = mybir.AluOpType
AX = mybir.AxisListType

NL = 4
D = 1024
SEQ = 2048
NSS = 16
LS = 4
TT = 256
PAST = 16384
DFF = 2816
NJ = 22
ALPHA = (2 * NL) ** 0.25
NBLK = 38
NPIECE_IN = 19
SLOT = 2816
NSLOT = 4
RW0, GL0, RT0 = 0, 1440, 2416


class _Stop(Exception):
    pass


class Buf:
    __slots__ = ("name", "last_write", "reads")

    def __init__(self, name):
        self.name = name
        self.last_write = None
        self.reads = []


class Op:
    __slots__ = ("eng", "fn", "deps", "idx", "needs_inc", "cnt", "chan", "dma_cnt")

    def __init__(self, eng, fn, deps, idx, chan=None):
        self.eng = eng
        self.fn = fn
        self.deps = deps
        self.idx = idx
        self.needs_inc = False
        self.cnt = None
        self.chan = chan
        self.dma_cnt = None


COMPUTE = ("pe", "act", "dve", "pool")
DMAQ = ("sp", "actq", "poolq")
STREAM = {"pe": "pe", "act": "act", "dve": "dve", "pool": "pool", "sp": "sp", "actq": "act", "poolq": "pool"}
SAME_SYNC = {"pe": False, "act": True, "dve": True, "pool": True}


class Sched:
    def __init__(self, nc):
        self.nc = nc
        self.ops = []

    def op(self, eng, fn, reads=(), writes=(), chan=None):
        idx = len(self.ops)
        deps = set()
        for b in reads:
            if b.last_write is not None:
                deps.add(b.last_write)
        for b in writes:
            if b.last_write is not None:
                deps.add(b.last_write)
            deps.update(b.reads)
        o = Op(eng, fn, deps, idx, chan)
        self.ops.append(o)
        for b in reads:
            b.reads.append(idx)
        for b in writes:
            b.last_write = idx
            b.reads = []
        return o

    def pe(self, fn, reads=(), writes=()):
        return self.op("pe", fn, reads, writes)

    def act(self, fn, reads=(), writes=()):
        return self.op("act", fn, reads, writes)

    def dve(self, fn, reads=(), writes=()):
        return self.op("dve", fn, reads, writes)

    def pool(self, fn, reads=(), writes=()):
        return self.op("pool", fn, reads, writes)

    def dma(self, fn, reads=(), writes=(), chan=None, q="sp"):
        return self.op(q, fn, reads, writes, chan=chan)

    def _skip(self, p, so):
        sp = STREAM[p.eng]
        return sp == so and p.eng in COMPUTE and not SAME_SYNC[p.eng]

    def run_block(self):
        nc = self.nc
        ops = self.ops
        for o in ops:
            so = STREAM[o.eng]
            for d in o.deps:
                p = ops[d]
                if p.eng in DMAQ or self._skip(p, so):
                    continue
                p.needs_inc = True
        cnt = {e: 0 for e in COMPUTE}
        chan_cnt = {}
        chan_hist = {}
        for o in ops:
            if o.eng in DMAQ:
                c = chan_cnt.get(o.chan, 0) + 16
                chan_cnt[o.chan] = c
                o.dma_cnt = c
                chan_hist.setdefault(o.chan, []).append((o.idx, c))
            elif o.needs_inc:
                cnt[o.eng] += 1
                o.cnt = cnt[o.eng]
        chan_idx = {c: [i for i, _ in h] for c, h in chan_hist.items()}
        streams = {s: [] for s in ("pe", "act", "dve", "pool", "sp")}
        waited = {s: {} for s in streams}
        for o in ops:
            so = STREAM[o.eng]
            need = {}
            for d in o.deps:
                p = ops[d]
                if p.eng in DMAQ:
                    h = chan_hist[p.chan]
                    k = bisect.bisect_left(chan_idx[p.chan], o.idx) - 1
                    key = ("chan", p.chan)
                    need[key] = max(need.get(key, 0), h[k][1])
                else:
                    if self._skip(p, so):
                        continue
                    key = ("eng", p.eng)
                    need[key] = max(need.get(key, 0), p.cnt)
            w = []
            for key, c in need.items():
                if waited[so].get(key, 0) >= c:
                    continue
                waited[so][key] = c
                w.append((key, c))
            streams[so].append((o, w))
        with ExitStack() as es:
            sems = {}
            for e in COMPUTE:
                sems[("eng", e)] = es.enter_context(nc.semaphore("s_" + e))
            for c in chan_cnt:
                sems[("chan", c)] = es.enter_context(nc.semaphore("c_" + str(c)))
            block = es.enter_context(nc.Block())

            def mk(sname):
                def body(eng):
                    for o, w in streams[sname]:
                        for key, c in w:
                            eng.wait_ge(sems[key], c)
                        ins = o.fn(eng)
                        if o.eng in DMAQ:
                            ins.then_inc(sems[("chan", o.chan)], 16)
                        elif o.needs_inc:
                            ins.then_inc(sems[("eng", o.eng)], 1)
                    if sname == "sp":
                        for c, v in chan_cnt.items():
                            eng.wait_ge(sems[("chan", c)], v)
                        for e in COMPUTE:
                            if cnt[e] > 0:
                                eng.wait_ge(sems[("eng", e)], cnt[e])
                return body

            block.tensor(mk("pe"))
            block.scalar(mk("act"))
            block.vector(mk("dve"))
            block.gpsimd(mk("pool"))
            block.sync(mk("sp"))


def _colmap():
    cm = -np.ones(NBLK * 128, np.int64)
    for g, (rw_off, gl_off, rt_off, pad) in enumerate(((0, 0, 0, True), (384, 160, 320, True), (768, 320, 640, False))):
        base = g * 1024
        cm[base:base + 384] = RW0 + rw_off + np.arange(384)
        for j in range(5):
            if pad:
                cm[base + 384 + j * 64: base + 384 + j * 64 + 32] = GL0 + gl_off + j * 32 + np.arange(32)
            else:
                cm[base + 384 + j * 64: base + 384 + j * 64 + 64] = GL0 + gl_off + j * 64 + np.arange(64)
        cm[base + 704: base + 1024] = RT0 + rt_off + np.arange(320)
    b = 24 * 128
    cm[b:b + 320] = GL0 + 640 + np.arange(320)
    cm[b + 320:b + 640] = RT0 + 960 + np.arange(320)
    b = 29 * 128
    cm[b:b + 128] = RW0 + 1152 + np.arange(128)
    cm[b + 128:b + 256] = RW0 + 1280 + np.arange(128)
    cm[b + 256:b + 288] = RW0 + 1408 + np.arange(32)
    cm[b + 288:b + 304] = GL0 + 960 + np.arange(16)
    for gi, off in enumerate((0, 320)):
        b = (32 + 3 * gi) * 128
        for r in range(704, 1024):
            i = r - 704
            h, d = divmod(i, 64)
            cm[b + (r - 640)] = RT0 + off + h * 64 + ((d + 32) % 64)
    return cm


def _rows_layout(v, nblk):
    return np.ascontiguousarray(v.reshape(nblk, 128).T)


VEC = {}


def _vec_layout():
    off = 0
    for name, n in (("mu", 12), ("w0", 3), ("a0", 3), ("kkv", 3), ("ka", 3), ("rk", 3), ("nw", 8), ("nb", 8),
                    ("bgkb", 3), ("ln1g", 8), ("ln1b", 8), ("ln2g", 8), ("ln2b", 8), ("cw0", 44), ("cw1", 44),
                    ("cw2", 44), ("cb", 44), ("rscale", 8), ("kscale", 8), ("ldret", 8)):
        VEC[name] = (off, n)
        off += n
    return off


NV = _vec_layout()
NM = 4 * 384


def _prep_layer_params(inp, l):
    f = np.float32
    vec = np.zeros((128, NV), f)

    def put(name, arr):
        o, n = VEC[name]
        assert arr.shape == (128, n), (name, arr.shape)
        vec[:, o:o + n] = arr

    mu = inp["rwkv_mu"][l]
    mup = np.zeros(12 * 128, f)
    mup[0:384] = mu[0:384]
    mup[384:768] = mu[384:768]
    mup[768:1152] = mu[768:1152]
    mup[1152:1152 + 128] = mu[1152:1280]
    mup[1280:1408] = mu[1280:1408]
    mup[1408:1440] = mu[1408:1440]
    put("mu", _rows_layout(mup, 12))
    put("w0", _rows_layout(inp["rwkv_w0"][l], 3))
    put("a0", _rows_layout(inp["rwkv_a0"][l], 3))
    put("kkv", _rows_layout(inp["rwkv_kk"][l], 3))
    put("ka", _rows_layout(inp["rwkv_ka"][l], 3))
    put("rk", _rows_layout(inp["rwkv_rk"][l].reshape(-1), 3))
    nw = np.ones(1024, f)
    nb = np.zeros(1024, f)
    nw[0:384] = inp["rwkv_lnw"][l]
    nb[0:384] = inp["rwkv_lnb"][l]
    nw[384:704] = np.tile(inp["gla_norm_w"][l], 5)
    put("nw", _rows_layout(nw, 8))
    put("nb", _rows_layout(nb, 8))
    bb = np.zeros(384, f)
    for j in range(5):
        bb[j * 64:j * 64 + 32] = inp["gla_bgk_b"][l][j * 32:(j + 1) * 32]
    put("bgkb", _rows_layout(bb, 3))
    put("ln1g", _rows_layout(inp["ln1_g"][l], 8))
    put("ln1b", _rows_layout(inp["ln1_b"][l], 8))
    put("ln2g", _rows_layout(inp["ln2_g"][l], 8))
    put("ln2b", _rows_layout(inp["ln2_b"][l], 8))
    cw = inp["ffn_conv_w"][l]
    put("cw0", _rows_layout(cw[0], 44))
    put("cw1", _rows_layout(cw[1], 44))
    put("cw2", _rows_layout(cw[2], 44))
    put("cb", _rows_layout(inp["ffn_conv_b"][l], 44))
    rs = np.ones(1024, f)
    rs[384:704] = 32.0 ** -0.5
    ks = np.ones(1024, f)
    ks[704:1024] = 64.0 ** -0.5
    put("rscale", _rows_layout(rs, 8))
    put("kscale", _rows_layout(ks, 8))
    ldr = np.zeros(1024, np.float64)
    for h in range(5):
        ldr[704 + h * 64:704 + (h + 1) * 64] = math.log(1.0 - 2.0 ** (-5.0 - h))
    put("ldret", _rows_layout(ldr.astype(f), 8))
    mats = np.zeros((128, NM), f)
    mats[0:64, 0:384] = inp["rwkv_bw"][l]
    mats[64:128, 0:384] = inp["rwkv_ba"][l]
    mats[:, 384:768] = inp["rwkv_bg"][l][0:128]
    mats[0:32, 768:1152] = inp["rwkv_bg"][l][128:160]
    bgk = inp["gla_bgk"][l]
    for j in range(5):
        mats[32:48, 1152 + j * 64:1152 + j * 64 + 32] = bgk[:, j * 32:(j + 1) * 32]
    return vec, mats


def _prep_weights(inp):
    cm = _colmap()
    npieces = NPIECE_IN + 4 + NJ + 8
    W = np.zeros((NL, npieces, 128, SLOT), np.float32)
    for l in range(NL):
        win = inp["w_in"][l]
        wp = np.zeros((1024, NBLK * 128), np.float32)
        ok = cm >= 0
        wp[:, ok] = win[:, cm[ok]]
        p = 0
        for i in range(NPIECE_IN):
            blkc = wp[:, i * 256:(i + 1) * 256].reshape(8, 128, 256).transpose(1, 0, 2)
            W[l, p, :, :2048] = blkc.reshape(128, 2048)
            p += 1
        wo = inp["w_out"][l]
        for i in range(4):
            blkc = wo[:, i * 256:(i + 1) * 256].reshape(8, 128, 256).transpose(1, 0, 2)
            W[l, p, :, :2048] = blkc.reshape(128, 2048)
            p += 1
        up = inp["ffn_up"][l]
        for j in range(NJ):
            ab = np.concatenate([up[:, j * 128:(j + 1) * 128], up[:, DFF + j * 128:DFF + (j + 1) * 128]], axis=1)
            blkc = ab.reshape(8, 128, 256).transpose(1, 0, 2)
            W[l, p, :, :2048] = blkc.reshape(128, 2048)
            p += 1
        dn = inp["ffn_down"][l]
        for cb in range(8):
            blkc = dn[:, cb * 128:(cb + 1) * 128].reshape(NJ, 128, 128).transpose(1, 0, 2)
            W[l, p, :, :NJ * 128] = blkc.reshape(128, NJ * 128)
            p += 1
        assert p == npieces
    return W


def _consts():
    ident = np.eye(128, dtype=np.float32)
    sI = np.arange(128)[:, None]
    tI = np.arange(128)[None, :]
    bones = np.zeros((128, 128), np.float32)
    bones[0:64, 0:64] = 1
    bones[64:128, 64:128] = 1
    cst = np.zeros((128, 672), np.float32)
    cst[:, 0:128] = ident
    cst[:, 128:256] = bones
    cst[:, 256:384] = (tI >= sI)
    cst[:, 384:512] = (tI > sI)
    cst[:, 512:640] = (tI < sI)
    tok = np.zeros((128, 32), np.float32)
    tok[:, 0:6] = 1.0 / 64
    tok[:, 16:22] = 64e-5
    tok[:, 22:32] = 1e-6
    cst[:, 640:672] = tok
    return cst


def _rope_tables():
    ntok = SEQ + NSS * LS
    pos = np.concatenate([np.arange(SEQ), np.tile(PAST + np.arange(LS), NSS)]).astype(np.float32)
    inv = (1.0 / (10000.0 ** np.linspace(0.0, 1.0, 32, dtype=np.float32))).astype(np.float32)
    ang = pos[None, :] * inv[:, None]
    cos = np.cos(ang).astype(np.float32)
    sin = np.sin(ang).astype(np.float32)
    C = np.ones((384, ntok), np.float32)
    S = np.zeros((384, ntok), np.float32)
    for h in range(5):
        r0 = 64 + h * 64
        C[r0:r0 + 32] = cos
        C[r0 + 32:r0 + 64] = cos
        S[r0:r0 + 32] = -sin
        S[r0 + 32:r0 + 64] = sin
    C = C.reshape(3, 128, ntok).transpose(1, 0, 2)
    S = S.reshape(3, 128, ntok).transpose(1, 0, 2)
    return np.ascontiguousarray(C), np.ascontiguousarray(S)


CFG_FULL = dict(tiles=[("p", i) for i in range(SEQ // TT)] + [("s", 0)], nl=NL)


def build(cfg):
    nc = bass.Bass("TRN2", target_bir_lowering=False)
    S = Sched(nc)
    NTOK = SEQ + NSS * LS
    npieces = NPIECE_IN + 4 + NJ + 8
    nl = cfg["nl"]
    tiles = cfg["tiles"]

    def din(name, shape, dt=F32):
        return nc.dram_tensor(name, list(shape), dt, kind="ExternalInput").ap()

    def dout(name, shape):
        return nc.dram_tensor(name, list(shape), F32, kind="ExternalOutput").ap()

    xT_d = din("xT", [128, 8, NTOK])
    wts_d = din("wts", [NL, npieces, 128, SLOT])
    vec_d = din("vecs", [NL, 128, NV])
    mat_d = din("mats", [NL, 128, NM])
    cst_d = din("cst", [128, 672])
    cos_d = din("ropec", [128, 3, NTOK])
    sin_d = din("ropes", [128, 3, NTOK])
    sst_d = din("sst", [NL, NSS, 128, 512])
    ssh_d = din("ssh", [NL, 128, 12 * NSS])
    scv_d = din("scv", [NL, 128, 44 * NSS * 2])
    yT_d = dout("yT", [128, 8, NTOK])
    sto_d = dout("sto", [NL, 1 + NSS, 128, 512])
    sho_d = dout("sho", [NL, 128, 12 * (1 + NSS)])
    cvo_d = dout("cvo", [NL, 128, 44 * (1 + NSS) * 2])
    dbg_d = dout("dbg", [128, 8192]) if cfg.get("dbg") else None
    dbg_state = {"col": 0, "items": []}

    def dbg(name, ap, bufs, np_=128):
        if dbg_d is None:
            return
        n = 1
        for d_ in ap.shape[1:]:
            n *= d_
        c0 = dbg_state["col"]
        if c0 + n > 8192:
            return
        dbg_state["col"] = c0 + n
        dbg_state["items"].append((name, c0, n, np_, tuple(ap.shape)))
        flat = dbg_d[0:np_, c0:c0 + n]
        if len(ap.shape) == 3:
            flat = flat.rearrange("p (a b) -> p a b", b=ap.shape[2])
        S.dma(E("dma_start", out=flat, in_=ap), reads=bufs, chan="dbg")
    cfg["_dbg_items"] = dbg_state["items"]

    def sb(name, shape, dt=F32):
        return nc.alloc_sbuf_tensor("sb_" + name, list(shape), dt)

    TM = TT
    x32 = sb("x32", [128, 8, TM]); b_x32 = Buf("x32")
    xTb = sb("xTb", [128, 8, TM], BF16); b_xTb = Buf("xTb")
    un1 = sb("un1", [128, 12 * (TM + NSS)]); b_PRW = Buf("PRW")
    PRW = un1[:].rearrange("p (a b) -> p a b", b=TM + NSS)
    Rg = sb("Rg", [128, 8, TM]); b_R = Buf("R")
    Kg = sb("Kg", [128, 8, TM]); b_K = Buf("K")
    Vg = sb("Vg", [128, 8, TM]); b_V = Buf("V")
    Gg = sb("Gg", [128, 8, TM], BF16); b_G = Buf("G")

    XS = sb("XS", [128, 3, TM]); b_XS = Buf("XS")
    ATt = sb("ATt", [128, 3, TM]); b_AT = Buf("AT")
    BTt = sb("BTt", [128, 3, TM]); b_BT = Buf("BT")
    bonus = sb("bonus", [128, 3, TM]); b_bonus = Buf("bonus")
    ynT = sb("ynT", [128, 8, TM]); b_ynT = Buf("ynT")
    SW = ynT; b_SW = b_ynT
    mixT = sb("mixT", [128, 8, TM], BF16); b_mixT = Buf("mixT")
    hT = un1[:].bitcast(BF16)[:, 0:NJ * TM].rearrange("p (a b) -> p a b", b=TM); b_hT = b_PRW
    WC = sb("WC", [128, 8, 16]); b_WC = Buf("WC")
    NTMP = 9
    tmp = [sb("tmp%d" % i, [128, TM]) for i in range(NTMP)]
    b_tmp = [Buf("tmp%d" % i) for i in range(NTMP)]
    tmpb = [sb("tmpb%d" % i, [128, TM], BF16) for i in range(4)]
    b_tmpb = [Buf("tmpb%d" % i) for i in range(4)]
    ubuf = [sb("ubuf%d" % i, [128, TM + 2 * NSS]) for i in range(4)]
    b_ubuf = [Buf("ubuf%d" % i) for i in range(4)]
    cosT = sb("cosT", [128, 3, TM]); sinT = sb("sinT", [128, 3, TM]); b_rope = Buf("rope")
    cst = sb("cst", [128, 672]); b_cst = Buf("cst")
    onesb = sb("onesb", [128, 128], BF16); b_ones = Buf("onesb")
    bonesr = sb("bonesr", [128, 128]); b_bones = Buf("bonesr")
    rmask = sb("rmask", [128, TM]); b_rmask = Buf("rmask")
    vecs = [sb("vecs%d" % i, [128, NV]) for i in range(2)]; b_vecs = [Buf("vecs%d" % i) for i in range(2)]
    vaux = [sb("vaux%d" % i, [128, 8]) for i in range(2)]; b_vaux = [Buf("vaux%d" % i) for i in range(2)]
    _m0 = sb("mats0", [128, NM], BF16); _bm0 = Buf("mats0")
    mats = [_m0, _m0]; b_mats = [_bm0, _bm0]
    hpw = [sb("hpw%d" % i, [128, 1024]) for i in range(2)]; b_hpw = [Buf("hpw%d" % i) for i in range(2)]
    mix32 = sb("mix32", [128, 8, 2]); b_mix32 = Buf("mix32")
    h32 = sb("h32", [128, NJ, 2]); b_h32 = Buf("h32")
    slots = [sb("slot%d" % i, [128, SLOT], BF16) for i in range(NSLOT)]
    b_slots = [Buf("slot%d" % i) for i in range(NSLOT)]
    Sst = [sb("Sst%d" % l, [128, 8, 64]) for l in range(NL)]; b_Sst = [Buf("Sst%d" % l) for l in range(NL)]
    Ssm = [sb("Ssm%d" % i, [128, 8, 64]) for i in range(2)]; b_Ssm = [Buf("Ssm%d" % i) for i in range(2)]
    _s0 = sb("Sld0", [128, 8, 64]); _bs0 = Buf("Sld0")
    Sld = [_s0, _s0]; b_Sld = [_bs0, _bs0]
    shc = sb("shc", [128, NL, 12]); b_shc = [Buf("shc%d" % l) for l in range(NL)]
    shst = sb("shst", [128, 12, NSS]); b_shst = Buf("shst")
    sho = sb("sho", [128, 12, NSS]); b_sho = Buf("sho")
    cvc = sb("cvc", [128, NL, 44, 2]); b_cvc = [[Buf("cvc%d_%d" % (l, c_)) for c_ in range(44)] for l in range(NL)]
    b_uh = [Buf("uh%d" % i) for i in range(4)]
    cvst = sb("cvst", [128, 44, NSS, 2]); b_cvst = Buf("cvst")
    cvo = sb("cvo", [128, 44, NSS, 2]); b_cvo = Buf("cvo")
    CM = 128
    vtok = sb("vtok", [CM, 1024]); b_vtok = Buf("vtok")
    ktok = sb("ktok", [CM, 1024]); b_ktok = Buf("ktok")
    btok = sb("btok", [CM, 384]); b_btok = Buf("btok")
    ark = sb("ark", [CM, 16, CM]); b_ark = Buf("ark")
    aak = sb("aak", [CM, 6, CM]); b_aak = Buf("aak")
    arb = sb("arb", [CM, 6, CM]); b_arb = Buf("arb")
    _p = sb("Pm0", [CM, 6, CM], BF16); _bp = Buf("Pm0")
    _q = sb("Qm0", [CM, 6, CM], BF16); _bq = Buf("Qm0")
    Xb = sb("Xb", [CM, 6, CM], BF16); b_Xb = Buf("Xb")
    Pm = [_p, _p]; b_Pm = [_bp, _bp]
    Qm = [_q, _q]; b_Qm = [_bq, _bq]
    Xm = sb("Xm", [CM, 6, CM]); b_Xm = Buf("Xm")
    rhs0 = sb("rhs0", [CM, 384]); b_rhs0 = Buf("rhs0")
    usb = sb("usb", [CM, 384]); b_usb = Buf("usb")
    ysb = sb("ysb", [CM, 1024]); b_ysb = Buf("ysb")
    ysq = ktok; b_ysq = b_ktok
    yst = sb("yst", [CM, 8, 16]); b_yst = Buf("yst")
    ps = nc.alloc_psum_tensor("ps", [128, 8, 512], F32)
    b_ps = [Buf("ps%d" % i) for i in range(8)]
    bank_ctr = [0]

    def nbank():
        b = bank_ctr[0] % 8
        bank_ctr[0] += 1
        return b

    R_ = lambda ap: ap.bitcast(F32R)

    def E(name, *a, **k):
        return lambda e: getattr(e, name)(*a, **k)

    S.dma(E("dma_start", out=cst[:], in_=cst_d), writes=[b_cst], chan="cst")
    ident = cst[:, 0:128]
    m_incl = cst[:, 256:384]
    m_su = cst[:, 384:512]
    m_sl = cst[:, 512:640]
    eye64 = cst[:, 0:128]
    tokc = cst[:, 640:672]
    S.dve(E("memset", onesb[:], 1.0), writes=[b_ones])
    S.dve(E("tensor_copy", out=R_(bonesr[:]), in_=cst[:, 128:256]), reads=[b_cst], writes=[b_bones])
    S.dve(E("memset", Sld[0][:], 0.0), writes=[b_Sld[0]])
    for l in range(NL):
        S.dve(E("tensor_copy", out=R_(Sst[l][:]), in_=Sld[0][:]), reads=[b_Sld[0]], writes=[b_Sst[l]])
        S.dve(E("memset", shc[:, l, :], 0.0), writes=[b_shc[l]])
        S.dve(E("memset", cvc[:, l, :, :], 0.0), writes=b_cvc[l])

    plan = []
    for (kind, ti) in tiles:
        for l in range(nl):
            for p in range(npieces):
                plan.append((l, p))
    ws = {"issued": 0, "used": 0, "pending_wr": {}}

    wbf_d = nc.dram_tensor("wbf", [NL, npieces, 128, SLOT], BF16).ap()
    b_wbf = {}

    def ws_issue():
        i = ws["issued"]
        if i >= len(plan):
            return
        l, p = plan[i]
        s = i % NSLOT
        n = 2048 if p < NPIECE_IN + 4 + NJ else NJ * 128
        if (l, p) not in b_wbf:
            S.dma(E("dma_start", out=slots[s][:, 0:n], in_=wts_d[l, p, :, 0:n]),
                  writes=[b_slots[s]], chan="slot%d" % s, q="poolq")
            if len(tiles) > 1:
                b_wbf[(l, p)] = Buf("wbf%d_%d" % (l, p))
                ws["pending_wr"][i] = (l, p, s, n)
        else:
            S.dma(E("dma_start", out=slots[s][:, 0:n], in_=wbf_d[l, p, :, 0:n]), reads=[b_wbf[(l, p)]],
                  writes=[b_slots[s]], chan="slot%d" % s)
        ws["issued"] += 1

    def ws_next(l, p):
        i = ws["used"]
        assert plan[i] == (l, p), (plan[i], l, p)
        while ws["issued"] < min(len(plan), i + NSLOT):
            ws_issue()
        if i in ws["pending_wr"]:
            l_, p_, s_, n_ = ws["pending_wr"].pop(i)
            S.dma(E("dma_start", out=wbf_d[l_, p_, :, 0:n_], in_=slots[s_][:, 0:n_]), reads=[b_slots[s_]],
                  writes=[b_wbf[(l_, p_)]], chan="wbfw")
        ws["used"] += 1
        s = i % NSLOT
        return slots[s], b_slots[s]

    for _ in range(NSLOT - 1):
        ws_issue()

    pv_ctr = [0]
    hp_ctr = [0]

    def hp_load(l, p, part):
        i = hp_ctr[0] % 2
        hp_ctr[0] += 1
        if part[0] == "sub":
            src = wts_d[l, p, :, 0:2048].rearrange("p (kc c) -> p kc c", c=256)[:, :, part[1] * 128:(part[1] + 1) * 128]
            dst = hpw[i][:, 0:1024].rearrange("p (kc c) -> p kc c", c=128)
        else:
            src = wts_d[l, p, :, part[1] * 1024:part[1] * 1024 + part[2] * 128]
            dst = hpw[i][:, 0:part[2] * 128]
        S.dma(E("dma_start", out=dst, in_=src), writes=[b_hpw[i]], chan="hp%d" % i)
        return hpw[i], b_hpw[i]

    def hp_mm8(l, p, sub, rhs_fn, brhs):
        w, bw = hp_load(l, p, ("sub", sub))
        bkh = nbank()
        for kc in range(8):
            S.pe(E("matmul", ps[:, bkh, 0:2], lhsT=w[:, kc * 128:(kc + 1) * 128], rhs=rhs_fn(kc), start=(kc == 0), stop=(kc == 7)),
                 reads=[bw, brhs], writes=[b_ps[bkh]])
        return bkh

    def stage(n):
        if cfg.get("stop") == n:
            raise _Stop()

    def _tile_loop():
        for (kind, ti) in tiles:
            if kind == "p":
                T = TT; nseq = 1; L = TT; C = 128; tok0 = ti * TT; nsteps = 6
            else:
                T = NSS * LS; nseq = NSS; L = LS; C = LS; tok0 = SEQ; nsteps = 1
            nch = T // C
            last_prompt = (kind == "p" and ti == SEQ // TT - 1)
            hpf = (kind == "p" and ti == 0 and cfg.get("hp", True))
            clo = 2 if hpf else 0
            S.dma(E("dma_start", out=x32[:, :, 0:T], in_=xT_d[:, :, tok0:tok0 + T]),
                  writes=[b_x32], chan="x32")
            S.dma(E("dma_start", out=xTb[:, :, 0:T], in_=xT_d[:, :, tok0:tok0 + T]),
                  writes=[b_xTb], chan="xTb", q="poolq")
            S.dma(E("dma_start", out=cosT[:, :, 0:T], in_=cos_d[:, :, tok0:tok0 + T]),
                  writes=[b_rope], chan="rope")
            S.dma(E("dma_start", out=sinT[:, :, 0:T], in_=sin_d[:, :, tok0:tok0 + T]),
                  writes=[b_rope], chan="rope")
            S.dve(E("memset", rmask[:, 0:T], 1.0), writes=[b_rmask])
            S.dve(E("memset", rmask[:, 0:T].rearrange("p (c t) -> p c t", t=C)[:, :, 0:1], 0.0),
                  writes=[b_rmask])

            for l in range(nl):
                pv = pv_ctr[0] % 2
                pv_ctr[0] += 1
                vec, bvec, mat, bmat, vx, bvx = vecs[pv], b_vecs[pv], mats[pv], b_mats[pv], vaux[pv], b_vaux[pv]
                S.dma(E("dma_start", out=vec[:], in_=vec_d[l]), writes=[bvec], chan="vec%d" % pv)
                S.dma(E("dma_start", out=mat[:], in_=mat_d[l]), writes=[bmat], chan="mat0",
                      q="poolq")

                def V_(name, b=0, n=1, vec=vec):
                    o, _ = VEC[name]
                    return vec[:, o + b:o + b + n]

                o_ka = VEC["ka"][0]; o_bg = VEC["bgkb"][0]
                S.dve(E("tensor_scalar", out=vx[:, 0:3], in0=vec[:, o_ka:o_ka + 3], scalar1=-1.0,
                                                                        scalar2=1.0, op0=ALU.mult, op1=ALU.add),
                      reads=[bvec], writes=[bvx])
                S.dve(E("tensor_scalar", out=vx[:, 3:6], in0=vec[:, o_bg:o_bg + 3], scalar1=-1.0,
                                                                        scalar2=None, op0=ALU.mult),
                      reads=[bvec], writes=[bvx])

                if kind == "s":
                    S.dma(E("dma_start", out=shst[:].rearrange("p a b -> p (a b)"), in_=ssh_d[l]),
                          writes=[b_shst], chan="shst")
                    S.dve(E("tensor_copy", out=PRW[:, :, 0:NSS * (LS + 1)].rearrange("p a (s t) -> p a s t", t=LS + 1)[:, :, :, 0],
                                                  in_=shst[:]), reads=[b_shst], writes=[b_PRW])
                else:
                    S.dve(E("tensor_copy", out=PRW[:, :, 0], in_=shc[:, l, :]), reads=[b_shc[l]], writes=[b_PRW])

                def prw_dst(pb, T=T, nseq=nseq, L=L):
                    return PRW[:, pb, 0:nseq * (L + 1)].rearrange("p (s t) -> p s t", t=L + 1)[:, :, 1:L + 1]

                def prw_prev(pb, T=T, nseq=nseq, L=L):
                    return PRW[:, pb, 0:nseq * (L + 1)].rearrange("p (s t) -> p s t", t=L + 1)[:, :, 0:L]

                def t3(ap, nseq=nseq, L=L):
                    return ap.rearrange("p (s t) -> p s t", t=L)

                evi = [0]
                for piece in range(NPIECE_IN):
                    slot, bslot = ws_next(l, piece)
                    for sub in range(2):
                        blk = piece * 2 + sub
                        bk = nbank()
                        for kc in range(8):
                            S.pe(E("matmul",
                                ps[:, bk, 0:T], lhsT=slot[:, kc * 256 + sub * 128: kc * 256 + sub * 128 + 128],
                                rhs=xTb[:, kc, 0:T], start=(kc == 0), stop=(kc == 7)),
                                reads=[bslot, b_xTb], writes=[b_ps[bk]])
                        g, gb = divmod(blk, 8)

                        def proj_evac(srcap, lo, hi, bkb, blk=blk, g=g, gb=gb):
                            if blk < 24 and gb < 3:
                                pb = g * 3 + gb
                            elif 29 <= blk < 32:
                                pb = 9 + blk - 29
                            else:
                                pb = None
                            if pb is not None:
                                if kind == "p":
                                    S.act(E("copy", out=PRW[:, pb, 1 + lo:1 + hi], in_=srcap), reads=[bkb], writes=[b_PRW])
                                else:
                                    S.act(E("copy", out=prw_dst(pb), in_=t3(srcap)), reads=[bkb], writes=[b_PRW])
                            elif blk < 24:
                                tile_, btile = ((Rg, b_R), (Kg, b_K), (Vg, b_V))[g]
                                S.dve(E("tensor_copy", out=R_(tile_[:, gb, lo:hi]), in_=srcap), reads=[bkb], writes=[btile])
                            elif blk < 29:
                                S.act(E("activation", out=Gg[:, 3 + blk - 24, lo:hi], in_=srcap, func=AF.Silu), reads=[bkb],
                                      writes=[b_G])
                            else:
                                S.dve(E("tensor_copy", out=SW[:, blk - 32, lo:hi], in_=srcap), reads=[bkb], writes=[b_SW])

                        proj_evac(ps[:, bk, clo:T], clo, T, b_ps[bk])
                        if hpf:
                            bkh = hp_mm8(l, piece, sub, lambda kc: x32[:, kc, 0:2], b_x32)
                            proj_evac(ps[:, bkh, 0:2], 0, 2, b_ps[bkh])

                stage(1)
                for pb in range(12):
                    g, gb = divmod(pb, 3)
                    tile_, btile = ((Rg, b_R), (Kg, b_K), (Vg, b_V), (XS, b_XS))[g]
                    tb = pb % 8
                    d3 = t3(tmp[tb][:, 0:T])
                    S.dve(E("tensor_tensor", out=d3, in0=prw_prev(pb), in1=prw_dst(pb), op=ALU.subtract),
                          reads=[b_PRW], writes=[b_tmp[tb]])
                    dst3 = t3(tile_[:, gb, 0:T])
                    if g <= 2:
                        dst3 = R_(dst3)
                    S.dve(E("scalar_tensor_tensor",
                        out=dst3, in0=d3, scalar=V_("mu", pb), in1=prw_dst(pb), op0=ALU.mult, op1=ALU.add),
                        reads=[b_PRW, b_tmp[tb], bvec], writes=[btile])
                if l == 0:
                    dbg("PRW0", PRW[:, 0, 0:80], [b_PRW]); dbg("Rsh", Rg[:, 0, 0:T], [b_R]); dbg("shst", shst[:, 0, :], [b_shst])
                lastcol = PRW[:, :, 0:nseq * (L + 1)].rearrange("p a (s t) -> p a s t", t=L + 1)[:, :, :, L]
                if kind == "s":
                    S.act(E("copy", out=sho[:], in_=lastcol), reads=[b_PRW], writes=[b_sho])
                    S.dma(E("dma_start", out=sho_d[l][:, 12:12 * (1 + NSS)], in_=sho[:].rearrange("p a b -> p (a b)")),
                          reads=[b_sho], chan="shout")
                else:
                    S.act(E("copy", out=shc[:, l, :], in_=lastcol[:, :, 0]), reads=[b_PRW],
                          writes=[b_shc[l]])
                    if last_prompt:
                        S.dma(E("dma_start", out=sho_d[l][:, 0:12], in_=shc[:, l, :]), reads=[b_shc[l]],
                              chan="shout")

                stage(2)
                for gi, (tile_, btile) in enumerate(((Rg, b_R), (Kg, b_K))):
                    a = tile_[:, 5:8, 0:T]
                    sw = SW[:, gi * 3:gi * 3 + 3, 0:T]
                    S.dve(E("tensor_tensor", out=R_(a), in0=a, in1=cosT[:, :, 0:T], op=ALU.mult),
                          reads=[b_rope, btile], writes=[btile])
                    S.dve(E("tensor_tensor", out=sw, in0=sw, in1=sinT[:, :, 0:T], op=ALU.mult),
                          reads=[b_rope, b_SW], writes=[b_SW])
                    S.dve(E("tensor_tensor", out=R_(a), in0=a, in1=sw, op=ALU.add),
                          reads=[b_SW, btile], writes=[btile])

                thx, bthx = tmpb[0], b_tmpb[0]
                sgx, bsgx = tmpb[1], b_tmpb[1]
                sgx2, bsgx2 = tmpb[2], b_tmpb[2]
                gkb, bgkb_ = tmpb[3], b_tmpb[3]
                S.act(E("activation", out=thx[0:64, 0:T], in_=XS[0:64, 0, 0:T], func=AF.Tanh), reads=[b_XS], writes=[bthx])
                S.act(E("copy", out=thx[64:128, 0:T], in_=XS[64:128, 0, 0:T]), reads=[b_XS], writes=[bthx])
                S.act(E("activation", out=sgx[:, 0:T], in_=XS[:, 1, 0:T], func=AF.Sigmoid), reads=[b_XS], writes=[bsgx])
                S.act(E("activation", out=sgx2[0:32, 0:T], in_=XS[0:32, 2, 0:T], func=AF.Sigmoid), reads=[b_XS], writes=[bsgx2])
                S.act(E("copy", out=gkb[32:64, 0:T], in_=XS[32:64, 2, 0:T]), reads=[b_XS], writes=[bgkb_])

                for b in range(8):
                    t_ld, t_lw, t_W, t_IW, t_a, t_kk, t_x, t_y, t_r = tmp
                    bl_ld, bl_lw, bl_W, bl_IW, bl_a, bl_kk, bl_x, bl_y, bl_r = b_tmp
                    if b < 3:
                        c0 = b * 128
                        bk = nbank()
                        S.pe(E("matmul", ps[:, bk, 0:T], lhsT=mat[0:64, c0:c0 + 128], rhs=thx[0:64, 0:T],
                                                              start=True, stop=True), reads=[bmat, bthx], writes=[b_ps[bk]])
                        S.act(E("activation", out=t_ld[:, 0:T], in_=ps[:, bk, 0:T], func=AF.Sigmoid,
                                                                 bias=V_("w0", b), scale=1.0),
                              reads=[b_ps[bk], bvec], writes=[bl_ld])
                        S.dve(E("tensor_scalar", out=t_ld[:, 0:T], in0=t_ld[:, 0:T], scalar1=-math.exp(-0.5), scalar2=None,
                                                        op0=ALU.mult), reads=[bl_ld], writes=[bl_ld])
                        bk = nbank()
                        S.pe(E("matmul", ps[:, bk, 0:T], lhsT=mat[64:128, c0:c0 + 128], rhs=thx[64:128, 0:T],
                                                              start=True, stop=True), reads=[bmat, bthx], writes=[b_ps[bk]])
                        S.act(E("activation", out=t_a[:, 0:T], in_=ps[:, bk, 0:T], func=AF.Sigmoid,
                                                                 bias=V_("a0", b), scale=1.0),
                              reads=[b_ps[bk], bvec], writes=[bl_a])
                        bk = nbank()
                        S.pe(E("matmul", ps[:, bk, 0:T], lhsT=mat[:, 384 + c0:384 + c0 + 128], rhs=sgx[:, 0:T],
                                                              start=True, stop=False), reads=[bmat, bsgx], writes=[b_ps[bk]])
                        S.pe(E("matmul", ps[:, bk, 0:T], lhsT=mat[0:32, 768 + c0:768 + c0 + 128],
                                                              rhs=sgx2[0:32, 0:T], start=False, stop=True),
                             reads=[bmat, bsgx2], writes=[b_ps[bk]])
                        S.act(E("copy", out=Gg[:, b, 0:T], in_=ps[:, bk, 0:T]), reads=[b_ps[bk]], writes=[b_G])
                        S.dve(E("tensor_scalar", out=t_kk[:, 0:T], in0=Kg[:, b, 0:T], scalar1=V_("kkv", b), scalar2=None,
                                                             op0=ALU.mult), reads=[b_K, bvec], writes=[bl_kk])
                        S.act(E("activation", out=R_(t_r[:, 0:T]), in_=t_kk[:, 0:T], func=AF.Square), reads=[bl_kk],
                              writes=[bl_r])
                        bk = nbank()
                        S.pe(E("matmul", ps[:, bk, 0:T], lhsT=R_(bonesr[:]), rhs=R_(t_r[:, 0:T]), start=True, stop=True),
                             reads=[b_bones, bl_r], writes=[b_ps[bk]])
                        S.act(E("activation", out=t_x[:, 0:T], in_=ps[:, bk, 0:T], func=AF.Sqrt), reads=[b_ps[bk]],
                              writes=[bl_x])
                        S.dve(E("tensor_scalar", out=t_x[:, 0:T], in0=t_x[:, 0:T], scalar1=1e-12, scalar2=None, op0=ALU.max),
                              reads=[bl_x], writes=[bl_x])
                        S.dve(E("reciprocal", out=t_x[:, 0:T], in_=t_x[:, 0:T]), reads=[bl_x], writes=[bl_x])
                        S.dve(E("tensor_tensor", out=t_kk[:, 0:T], in0=t_kk[:, 0:T], in1=t_x[:, 0:T], op=ALU.mult),
                              reads=[bl_x, bl_kk], writes=[bl_kk])
                        S.dve(E("tensor_scalar", out=t_x[:, 0:T], in0=t_a[:, 0:T], scalar1=V_("ka", b),
                                                             scalar2=vx[:, b:b + 1], op0=ALU.mult, op1=ALU.add),
                              reads=[bl_a, bvec, bvx], writes=[bl_x])
                        S.dve(E("tensor_tensor", out=R_(Kg[:, b, 0:T]), in0=Kg[:, b, 0:T], in1=t_x[:, 0:T], op=ALU.mult),
                              reads=[bl_x, b_K], writes=[b_K])
                        S.dve(E("scalar_tensor_tensor", out=R_(t_r[:, 0:T]), in0=Rg[:, b, 0:T], scalar=V_("rk", b),
                                                                    in1=Kg[:, b, 0:T], op0=ALU.mult, op1=ALU.mult),
                              reads=[b_R, b_K, bvec], writes=[bl_r])
                        bk = nbank()
                        S.pe(E("matmul", ps[:, bk, 0:T], lhsT=R_(bonesr[:]), rhs=R_(t_r[:, 0:T]), start=True, stop=True),
                             reads=[b_bones, bl_r], writes=[b_ps[bk]])
                        S.dve(E("tensor_tensor", out=bonus[:, b, 0:T], in0=ps[:, bk, 0:T], in1=Vg[:, b, 0:T],
                                                                    op=ALU.mult), reads=[b_ps[bk], b_V], writes=[b_bonus])
                    elif b < 6:
                        c0 = (b - 3) * 128
                        np_ = 64 if b == 5 else 128
                        bk = nbank()
                        S.pe(E("matmul", ps[0:np_, bk, 0:T], lhsT=mat[32:48, 1152 + c0:1152 + c0 + np_],
                                                                       rhs=gkb[32:48, 0:T], start=True, stop=True),
                             reads=[bmat, bgkb_], writes=[b_ps[bk]])
                        S.act(E("activation", out=t_ld[0:np_, 0:T], in_=ps[0:np_, bk, 0:T], func=AF.Exp,
                                                                          bias=vx[0:np_, b:b + 1], scale=-1.0),
                              reads=[b_ps[bk], bvx], writes=[bl_ld])
                        S.act(E("activation", out=t_ld[0:np_, 0:T], in_=t_ld[0:np_, 0:T], func=AF.Ln, bias=1.0,
                                                              scale=1.0), reads=[bl_ld], writes=[bl_ld])
                        S.dve(E("tensor_scalar", out=t_ld[0:np_, 0:T], in0=t_ld[0:np_, 0:T], scalar1=-1.0 / 16.0,
                                                                 scalar2=None, op0=ALU.mult), reads=[bl_ld], writes=[bl_ld])
                        if b == 5:
                            S.dve(E("tensor_scalar", out=t_ld[64:128, 0:T], in0=rmask[64:128, 0:T], scalar1=0.0,
                                                                 scalar2=V_("ldret", b)[64:128], op0=ALU.mult, op1=ALU.add),
                                  reads=[b_rmask, bvec], writes=[bl_ld])
                    else:
                        S.dve(E("tensor_scalar", out=t_ld[:, 0:T], in0=rmask[:, 0:T], scalar1=0.0,
                                                             scalar2=V_("ldret", b), op0=ALU.mult, op1=ALU.add),
                              reads=[b_rmask, bvec], writes=[bl_ld])
                    S.dve(E("tensor_tensor_scan", out=t_lw[:, 0:T], data0=rmask[:, 0:T], data1=t_ld[:, 0:T], initial=0.0,
                                                         op0=ALU.mult, op1=ALU.add), reads=[bl_ld, b_rmask], writes=[bl_lw])
                    S.act(E("activation", out=t_W[:, 0:T], in_=t_lw[:, 0:T], func=AF.Exp), reads=[bl_lw], writes=[bl_W])
                    S.act(E("activation", out=t_IW[:, 0:T], in_=t_lw[:, 0:T], func=AF.Exp, scale=-1.0), reads=[bl_lw],
                          writes=[bl_IW])
                    S.act(E("copy", out=WC[:, b, 0:nch],
                                                              in_=t_W[:, 0:T].rearrange("p (c t) -> p c t", t=C)[:, :, C - 1]),
                          reads=[bl_W], writes=[b_WC])
                    if b < 3:
                        S.dve(E("tensor_tensor", out=t_y[:, 0:T], in0=t_lw[:, 0:T], in1=t_ld[:, 0:T], op=ALU.subtract),
                              reads=[bl_lw, bl_ld], writes=[bl_y])
                        S.act(E("activation", out=t_y[:, 0:T], in_=t_y[:, 0:T], func=AF.Exp), reads=[bl_y], writes=[bl_y])
                        S.dve(E("scalar_tensor_tensor", out=R_(ATt[:, b, 0:T]), in0=t_kk[:, 0:T], scalar=-1.0,
                                                                    in1=t_y[:, 0:T], op0=ALU.mult, op1=ALU.mult),
                              reads=[bl_kk, bl_y], writes=[b_AT])
                        S.dve(E("tensor_tensor", out=t_y[:, 0:T], in0=t_kk[:, 0:T], in1=t_a[:, 0:T], op=ALU.mult),
                              reads=[bl_kk, bl_a], writes=[bl_y])
                        S.dve(E("tensor_tensor", out=R_(BTt[:, b, 0:T]), in0=t_y[:, 0:T], in1=t_IW[:, 0:T], op=ALU.mult),
                              reads=[bl_y, bl_IW], writes=[b_BT])
                    S.dve(E("scalar_tensor_tensor", out=R_(Rg[:, b, 0:T]), in0=Rg[:, b, 0:T], scalar=V_("rscale", b),
                                                                in1=t_W[:, 0:T], op0=ALU.mult, op1=ALU.mult),
                          reads=[b_R, bl_W, bvec], writes=[b_R])
                    S.dve(E("scalar_tensor_tensor", out=R_(Kg[:, b, 0:T]), in0=Kg[:, b, 0:T], scalar=V_("kscale", b),
                                                                in1=t_IW[:, 0:T], op0=ALU.mult, op1=ALU.mult),
                          reads=[b_K, bl_IW, bvec], writes=[b_K])

                stage(3)
                for ci in range(nch):
                    t0 = ci * C
                    cs = slice(t0, t0 + C)
                    cfg.setdefault("_marks", []).append((l, ci, len(S.ops)))
                    if kind == "p":
                        St, bSt = Sst[l], b_Sst[l]
                    else:
                        sp_ = ci % 2
                        St, bSt = Ssm[sp_], b_Ssm[sp_]
                        S.dma(E("dma_start", out=Sld[sp_][:].rearrange("p a b -> p (a b)"),
                                                                         in_=sst_d[l, ci]),
                              writes=[b_Sld[sp_]], chan="sld0")
                        S.act(E("copy", out=R_(St[:]), in_=Sld[sp_][:]), reads=[b_Sld[sp_]], writes=[bSt])
                    for (src, bsrc, dst, bdst, nb_) in ((Vg, b_V, vtok, b_vtok, 8), (Kg, b_K, ktok, b_ktok, 8),
                                                        (BTt, b_BT, btok, b_btok, 3)):
                        for half in range((nb_ + 3) // 4):
                            bk = nbank()
                            n_in = min(4, nb_ - half * 4)
                            for bl in range(n_in):
                                blk = half * 4 + bl
                                S.pe(E("transpose",
                                    ps[0:C, bk, bl * 128:(bl + 1) * 128], src[:, blk, cs], ident),
                                    reads=[bsrc, b_cst], writes=[b_ps[bk]])
                            w = n_in * 128
                            S.act(E("copy", out=R_(dst[0:C, half * 512:half * 512 + w]),
                                                                                   in_=ps[0:C, bk, 0:w]),
                                  reads=[b_ps[bk]], writes=[bdst])
                    stage(31)
                    def hp(h):
                        return slice((h % 2) * 64, (h % 2) * 64 + 64), h // 2

                    def pv3(bk, n):
                        return ps[0:C, bk, 0:n * C].rearrange("p (h c) -> p h c", c=C)

                    def bc(m, n):
                        return m[0:C, 0:C].unsqueeze(1).to_broadcast([C, n, C])

                    def par3(t, par, n):
                        return t[0:C, :, 0:C].rearrange("p (a two) c -> p a two c", two=2)[:, :, par, :]

                    hpb = min(8, 512 // C)
                    def a_kind(lt, blt, rt_, brt, dst, bdst, msk, nh):
                        npar = nh // 2
                        ngrp = (npar + hpb - 1) // hpb
                        banks = {}
                        for g_ in range(ngrp):
                            for par in range(2):
                                banks[(par, g_)] = nbank()
                        for h in range(nh):
                            rows, blk = hp(h)
                            hl = h // 2
                            bk = banks[(h % 2, hl // hpb)]
                            hi = hl % hpb
                            S.pe(E("matmul", ps[0:C, bk, hi * C:(hi + 1) * C], lhsT=R_(lt[rows, blk, cs]),
                                   rhs=R_(rt_[rows, blk, cs]), start=True, stop=True), reads=[blt, brt], writes=[b_ps[bk]])
                        for g_ in range(ngrp):
                            n_ = min(hpb, npar - g_ * hpb)
                            for par in range(2):
                                bk = banks[(par, g_)]
                                o_ = par3(dst, par, nh)[:, g_ * hpb:g_ * hpb + n_, :]
                                if dst is not Pm[0] and dst is not Qm[0]:
                                    o_ = R_(o_)
                                S.dve(E("tensor_tensor", out=o_, in0=pv3(bk, n_),
                                        in1=bc(msk, n_), op=ALU.mult), reads=[b_ps[bk], b_cst], writes=[bdst])

                    a_kind(BTt, b_BT, ATt, b_AT, Pm[0], b_Pm[0], m_su, 6)
                    a_kind(ATt, b_AT, BTt, b_BT, Qm[0], b_Qm[0], m_sl, 6)
                    if False:
                        dbg("AT0", ATt[:, 0, cs], [b_AT]); dbg("BT0", BTt[:, 0, cs], [b_BT])
                        dbg("K0", Kg[:, 0, cs], [b_K]); dbg("R0", Rg[:, 0, cs], [b_R]); dbg("V0", Vg[:, 0, cs], [b_V])
                        dbg("WC", WC[:, :, 0:4], [b_WC])
                        dbg("P0", Pm[0][0:C, :, 0:C], [b_Pm[0]], C); dbg("Q0", Qm[0][0:C, :, 0:C], [b_Qm[0]], C)
                        dbg("aak", aak[0:C, :, 0:C], [b_aak], C); dbg("arb", arb[0:C, :, 0:C], [b_arb], C)
                        dbg("btok", btok[0:C, 0:128], [b_btok], C)
                    stage(32)
                    S.dve(E("tensor_tensor", out=R_(Xm[0:C, :, 0:C]), in0=Pm[0][0:C, :, 0:C], in1=bc(eye64, 6), op=ALU.add),
                          reads=[b_Pm[0], b_cst], writes=[b_Xm])
                    S.act(E("copy", out=Xb[0:C, :, 0:C], in_=Xm[0:C, :, 0:C]), reads=[b_Xm], writes=[b_Xb])
                    ng_ = 1 if 6 * C <= 512 else 2
                    hg_ = 6 // ng_

                    def inv_mm(lhs_t, blhs, rhs_t, brhs):
                        bks = []
                        for g_ in range(ng_):
                            bk = nbank()
                            bks.append(bk)
                            for hi in range(hg_):
                                h = g_ * hg_ + hi
                                S.pe(E("matmul", ps[0:C, bk, hi * C:(hi + 1) * C], lhsT=lhs_t[0:C, h, 0:C],
                                       rhs=rhs_t[0:C, h, 0:C], start=True, stop=True), reads=[blhs, brhs], writes=[b_ps[bk]])
                        return bks

                    def gsl(t, g_):
                        return t[0:C, g_ * hg_:(g_ + 1) * hg_, 0:C]

                    def par2(t, par, n):
                        return t[0:C, 0:n * 64].rearrange("p (a two v) -> p a two v", two=2, v=64)[:, :, par, :]

                    def pv64(bk, n):
                        return ps[0:C, bk, 0:n * 64].rearrange("p (h v) -> p h v", v=64)

                    def f_y_state():
                        bkp = (nbank(), nbank())
                        for h in range(16):
                            rows, blk = hp(h)
                            bk = bkp[h % 2]
                            hl = h // 2
                            S.pe(E("matmul", ps[0:C, bk, hl * 64:(hl + 1) * 64], lhsT=R_(Rg[rows, blk, cs]), rhs=R_(St[rows, blk, :]),
                                   start=True, stop=True), reads=[b_R, bSt], writes=[b_ps[bk]])
                        for par in range(2):
                            S.act(E("copy", out=par2(ysb, par, 16), in_=pv64(bkp[par], 8)), reads=[b_ps[bkp[par]]], writes=[b_ysb])

                    def f_rhs_state():
                        bkp = (nbank(), nbank())
                        for h in range(6):
                            rows, blk = hp(h)
                            bk = bkp[h % 2]
                            hl = h // 2
                            S.pe(E("matmul", ps[0:C, bk, hl * 64:(hl + 1) * 64], lhsT=R_(ATt[rows, blk, cs]), rhs=R_(St[rows, blk, :]),
                                   start=True, stop=True), reads=[b_AT, bSt], writes=[b_ps[bk]])
                        for par in range(2):
                            S.act(E("copy", out=R_(par2(rhs0, par, 6)), in_=pv64(bkp[par], 3)), reads=[b_ps[bkp[par]]], writes=[b_rhs0])

                    def f_rhs_tok():
                        bk = nbank()
                        for h in range(6):
                            S.pe(E("matmul", ps[0:C, bk, h * 64:(h + 1) * 64], lhsT=R_(aak[0:C, h, 0:C]),
                                   rhs=R_(vtok[0:C, h * 64:(h + 1) * 64]), start=True, stop=True),
                                 reads=[b_aak, b_vtok], writes=[b_ps[bk]])
                        S.dve(E("tensor_tensor", out=R_(rhs0[0:C, :]), in0=rhs0[0:C, :], in1=ps[0:C, bk, 0:384], op=ALU.add),
                              reads=[b_ps[bk], b_rhs0], writes=[b_rhs0])

                    def f_y_tok(half):
                        bk = nbank()
                        for hl in range(8):
                            h = half * 8 + hl
                            S.pe(E("matmul", ps[0:C, bk, hl * 64:(hl + 1) * 64], lhsT=R_(ark[0:C, h, 0:C]),
                                   rhs=R_(vtok[0:C, h * 64:(h + 1) * 64]), start=True, stop=(h >= 6)),
                                 reads=[b_ark, b_vtok], writes=[b_ps[bk]])
                            if h < 6:
                                S.pe(E("matmul", ps[0:C, bk, hl * 64:(hl + 1) * 64], lhsT=R_(arb[0:C, h, 0:C]),
                                       rhs=R_(usb[0:C, h * 64:(h + 1) * 64]), start=False, stop=True),
                                     reads=[b_arb, b_usb], writes=[b_ps[bk]])
                        S.dve(E("tensor_tensor", out=ysb[0:C, half * 512:(half + 1) * 512], in0=ysb[0:C, half * 512:(half + 1) * 512],
                                in1=ps[0:C, bk, :], op=ALU.add), reads=[b_ps[bk], b_ysb], writes=[b_ysb])

                    fillers = [lambda: a_kind(Kg, b_K, Rg, b_R, ark, b_ark, m_incl, 16),
                               lambda: a_kind(Kg, b_K, ATt, b_AT, aak, b_aak, m_su, 6),
                               f_y_state,
                               lambda: a_kind(BTt, b_BT, Rg, b_R, arb, b_arb, m_incl, 6),
                               f_rhs_state, f_rhs_tok, lambda: f_y_tok(1)]
                    for st in range(nsteps):
                        bq = inv_mm(Pm[0], b_Pm[0], Qm[0], b_Qm[0])
                        if st < nsteps - 1:
                            bp = inv_mm(Qm[0], b_Qm[0], Pm[0], b_Pm[0])
                        for g_ in range(ng_):
                            S.act(E("copy", out=gsl(Qm[0], g_), in_=pv3(bq[g_], hg_)), reads=[b_ps[bq[g_]]], writes=[b_Qm[0]])
                        if st < nsteps - 1:
                            for g_ in range(ng_):
                                S.dve(E("tensor_copy", out=gsl(Pm[0], g_), in_=pv3(bp[g_], hg_)), reads=[b_ps[bp[g_]]],
                                      writes=[b_Pm[0]])
                        if fillers:
                            fillers.pop(0)()
                        bx = inv_mm(Qm[0], b_Qm[0], Xb, b_Xb)
                        for g_ in range(ng_):
                            S.dve(E("tensor_tensor", out=R_(gsl(Xm, g_)), in0=gsl(Xm, g_), in1=pv3(bx[g_], hg_), op=ALU.add),
                                  reads=[b_ps[bx[g_]], b_Xm], writes=[b_Xm])
                            if st < nsteps - 1:
                                S.act(E("copy", out=gsl(Xb, g_), in_=gsl(Xm, g_)), reads=[b_Xm], writes=[b_Xb])
                    while fillers:
                        fillers.pop(0)()
                    stage(33)
                    bk = nbank()
                    for h in range(6):
                        S.pe(E("matmul", ps[0:C, bk, h * 64:(h + 1) * 64], lhsT=R_(Xm[0:C, h, 0:C]),
                               rhs=R_(rhs0[0:C, h * 64:(h + 1) * 64]), start=True, stop=True),
                             reads=[b_Xm, b_rhs0], writes=[b_ps[bk]])
                    S.act(E("copy", out=R_(usb[0:C, :]), in_=ps[0:C, bk, 0:384]), reads=[b_ps[bk]], writes=[b_usb])
                    stage(34)
                    f_y_tok(0)
                    stage(35)
                    ds_banks = []
                    for half in range(2):
                        bk = nbank()
                        ds_banks.append(bk)
                        for bl in range(4):
                            blk = half * 4 + bl
                            S.pe(E("matmul", ps[:, bk, bl * 128:(bl + 1) * 128],
                                   lhsT=R_(ktok[0:C, blk * 128:(blk + 1) * 128]),
                                   rhs=R_(vtok[0:C, blk * 128:(blk + 1) * 128]),
                                   start=True, stop=(blk >= 3)), reads=[b_ktok, b_vtok], writes=[b_ps[bk]])
                            if blk < 3:
                                S.pe(E("matmul", ps[:, bk, bl * 128:(bl + 1) * 128],
                                       lhsT=R_(btok[0:C, blk * 128:(blk + 1) * 128]),
                                       rhs=R_(usb[0:C, blk * 128:(blk + 1) * 128]),
                                       start=False, stop=True), reads=[b_btok, b_usb], writes=[b_ps[bk]])
                    stage(36)
                    y3 = ysb[0:C, :].rearrange("p (h v) -> p h v", v=64)
                    q3 = ysq[0:C, :].rearrange("p (h v) -> p h v", v=64)
                    s1, s2, mean, tmpv, rstd = (yst[0:C, i, :] for i in range(5))
                    S.dve(E("tensor_tensor", out=R_(ysq[0:C, :]), in0=ysb[0:C, :], in1=ysb[0:C, :], op=ALU.mult), reads=[b_ysb],
                          writes=[b_ysq])
                    S.dve(E("tensor_reduce", out=s1, in_=y3, axis=AX.X, op=ALU.add), reads=[b_ysb], writes=[b_yst])
                    S.dve(E("tensor_reduce", out=s2, in_=q3, axis=AX.X, op=ALU.add), reads=[b_ysq], writes=[b_yst])
                    S.dve(E("tensor_tensor", out=mean, in0=s1, in1=tokc[0:C, 0:16], op=ALU.mult),
                          reads=[b_yst, b_cst], writes=[b_yst])
                    S.dve(E("tensor_tensor", out=tmpv, in0=mean, in1=mean, op=ALU.mult), reads=[b_yst],
                          writes=[b_yst])
                    S.dve(E("scalar_tensor_tensor", out=tmpv, in0=s2, scalar=1.0 / 64, in1=tmpv, op0=ALU.mult,
                                                                             op1=ALU.subtract), reads=[b_yst], writes=[b_yst])
                    S.dve(E("tensor_tensor", out=tmpv, in0=tmpv, in1=tokc[0:C, 16:32], op=ALU.add),
                          reads=[b_yst, b_cst], writes=[b_yst])
                    S.act(E("activation", out=tmpv, in_=tmpv, func=AF.Sqrt), reads=[b_yst], writes=[b_yst])
                    S.dve(E("reciprocal", out=rstd, in_=tmpv), reads=[b_yst], writes=[b_yst])
                    S.dve(E("tensor_tensor", out=y3, in0=y3, in1=mean.unsqueeze(2).to_broadcast([C, 16, 64]),
                                                                      op=ALU.subtract), reads=[b_yst, b_ysb], writes=[b_ysb])
                    S.dve(E("tensor_tensor", out=y3, in0=y3, in1=rstd.unsqueeze(2).to_broadcast([C, 16, 64]),
                                                                      op=ALU.mult), reads=[b_yst, b_ysb], writes=[b_ysb])
                    bpb = min(8, 512 // C)
                    for g_ in range(8 // bpb):
                        bk = nbank()
                        for bi in range(bpb):
                            blk = g_ * bpb + bi
                            S.pe(E("transpose", ps[:, bk, bi * C:(bi + 1) * C], ysb[0:C, blk * 128:(blk + 1) * 128],
                                   cst[0:C, 0:C]), reads=[b_ysb, b_cst], writes=[b_ps[bk]])
                        S.act(E("copy", out=ynT[:, g_ * bpb:(g_ + 1) * bpb, cs],
                                in_=ps[:, bk, 0:bpb * C].rearrange("p (a c) -> p a c", c=C)), reads=[b_ps[bk]], writes=[b_ynT])

                    for half in range(2):
                        bk = ds_banks[half]
                        for hh in range(2):
                            rows = slice(hh * 64, hh * 64 + 64)
                            S.dve(E("tensor_tensor",
                                out=R_(St[rows, half * 4:half * 4 + 4, :]), in0=St[rows, half * 4:half * 4 + 4, :],
                                in1=ps[rows, bk, :].rearrange("p (a b) -> p a b", b=128)[:, :, hh * 64:hh * 64 + 64], op=ALU.add),
                                reads=[b_ps[bk], bSt], writes=[bSt])
                    S.dve(E("tensor_tensor", out=R_(St[:]), in0=St[:], in1=WC[:, :, ci:ci + 1].to_broadcast([128, 8, 64]),
                                                           op=ALU.mult), reads=[b_WC, bSt], writes=[bSt])
                    if kind == "s":
                        S.dma(E("dma_start", out=sto_d[l, 1 + ci], in_=St[:].rearrange("p a b -> p (a b)")),
                              reads=[bSt], chan="stout")
                    elif last_prompt and ci == nch - 1:
                        S.dma(E("dma_start", out=sto_d[l, 0], in_=St[:].rearrange("p a b -> p (a b)")),
                              reads=[bSt], chan="stout")

                stage(4)
                for b in range(8):
                    S.dve(E("tensor_scalar", out=ynT[:, b, 0:T], in0=ynT[:, b, 0:T], scalar1=V_("nw", b), scalar2=V_("nb", b),
                                                         op0=ALU.mult, op1=ALU.add), reads=[b_ynT, bvec], writes=[b_ynT])
                    if b < 3:
                        S.dve(E("tensor_tensor", out=ynT[:, b, 0:T], in0=ynT[:, b, 0:T], in1=bonus[:, b, 0:T], op=ALU.add),
                              reads=[b_ynT, b_bonus], writes=[b_ynT])
                    S.dve(E("tensor_tensor", out=mixT[:, b, 0:T], in0=ynT[:, b, 0:T], in1=Gg[:, b, 0:T], op=ALU.mult),
                          reads=[b_ynT, b_G], writes=[b_mixT])
                    if hpf:
                        S.dve(E("tensor_tensor", out=mix32[:, b, :], in0=ynT[:, b, 0:2], in1=Gg[:, b, 0:2], op=ALU.mult),
                              reads=[b_ynT, b_G], writes=[b_mix32])

                def ln_epilogue(gname, bname):
                    r16, br16 = hT[:, 0:8, :], b_hT
                    S.act(E("copy", out=hT[:, 0:8, 0:T], in_=x32[:, :, 0:T]), reads=[b_x32], writes=[b_hT])
                    S.act(E("activation", out=hT[:, 8:16, 0:T], in_=x32[:, :, 0:T], func=AF.Square), reads=[b_x32], writes=[b_hT])
                    bk1 = nbank(); bk2 = nbank()
                    for kc in range(8):
                        S.pe(E("matmul", ps[:, bk1, 0:T], lhsT=onesb[:], rhs=hT[:, kc, 0:T], start=(kc == 0), stop=(kc == 7)),
                             reads=[b_ones, b_hT], writes=[b_ps[bk1]])
                    for kc in range(8):
                        S.pe(E("matmul", ps[:, bk2, 0:T], lhsT=onesb[:], rhs=hT[:, 8 + kc, 0:T], start=(kc == 0),
                                                       stop=(kc == 7)), reads=[b_ones, b_hT], writes=[b_ps[bk2]])
                    mean, var = tmp[0], tmp[1]
                    S.act(E("mul", out=mean[:, 0:T], in_=ps[:, bk1, 0:T], mul=1.0 / D), reads=[b_ps[bk1]], writes=[b_tmp[0]])
                    S.dve(E("tensor_tensor", out=var[:, 0:T], in0=mean[:, 0:T], in1=mean[:, 0:T], op=ALU.mult), reads=[b_tmp[0]],
                          writes=[b_tmp[1]])
                    S.dve(E("scalar_tensor_tensor", out=var[:, 0:T], in0=ps[:, bk2, 0:T], scalar=1.0 / D, in1=var[:, 0:T],
                                                           op0=ALU.mult, op1=ALU.subtract), reads=[b_ps[bk2], b_tmp[1]], writes=[b_tmp[1]])
                    S.dve(E("tensor_scalar", out=var[:, 0:T], in0=var[:, 0:T], scalar1=1e-5, scalar2=None, op0=ALU.add),
                          reads=[b_tmp[1]], writes=[b_tmp[1]])
                    S.act(E("activation", out=var[:, 0:T], in_=var[:, 0:T], func=AF.Sqrt), reads=[b_tmp[1]], writes=[b_tmp[1]])
                    S.dve(E("reciprocal", out=var[:, 0:T], in_=var[:, 0:T]), reads=[b_tmp[1]], writes=[b_tmp[1]])
                    S.dve(E("tensor_tensor", out=x32[:, :, 0:T], in0=x32[:, :, 0:T],
                                                    in1=mean[:, 0:T].unsqueeze(1).to_broadcast([128, 8, T]), op=ALU.subtract),
                          reads=[b_tmp[0], b_x32], writes=[b_x32])
                    S.dve(E("tensor_tensor", out=x32[:, :, 0:T], in0=x32[:, :, 0:T],
                                                    in1=var[:, 0:T].unsqueeze(1).to_broadcast([128, 8, T]), op=ALU.mult),
                          reads=[b_tmp[1], b_x32], writes=[b_x32])
                    for b in range(8):
                        S.dve(E("tensor_scalar", out=x32[:, b, 0:T], in0=x32[:, b, 0:T], scalar1=V_(gname, b),
                                                             scalar2=V_(bname, b), op0=ALU.mult, op1=ALU.add),
                              reads=[b_x32, bvec], writes=[b_x32])
                    S.act(E("copy", out=xTb[:, :, 0:T], in_=x32[:, :, 0:T]), reads=[b_x32], writes=[b_xTb])

                stage(5)
                for piece in range(4):
                    slot, bslot = ws_next(l, NPIECE_IN + piece)
                    for sub in range(2):
                        cb = piece * 2 + sub
                        bk = nbank()
                        for kc in range(8):
                            S.pe(E("matmul",
                                ps[:, bk, 0:T], lhsT=slot[:, kc * 256 + sub * 128: kc * 256 + sub * 128 + 128],
                                rhs=mixT[:, kc, 0:T], start=(kc == 0), stop=(kc == 7)), reads=[bslot, b_mixT], writes=[b_ps[bk]])
                        if hpf:
                            bkh = hp_mm8(l, NPIECE_IN + piece, sub, lambda kc: mix32[:, kc, 0:2], b_mix32)
                            S.dve(E("scalar_tensor_tensor", out=x32[:, cb, 0:2], in0=x32[:, cb, 0:2], scalar=ALPHA,
                                    in1=ps[:, bkh, 0:2], op0=ALU.mult, op1=ALU.add), reads=[b_ps[bkh], b_x32], writes=[b_x32])
                        S.dve(E("scalar_tensor_tensor", out=x32[:, cb, clo:T], in0=x32[:, cb, clo:T], scalar=ALPHA,
                                                                             in1=ps[:, bk, clo:T], op0=ALU.mult, op1=ALU.add),
                              reads=[b_ps[bk], b_x32], writes=[b_x32])
                ln_epilogue("ln1g", "ln1b")

                stage(6)
                if kind == "s":
                    S.dma(E("dma_start", out=cvst[:].rearrange("p a b c -> p (a b c)"), in_=scv_d[l]), writes=[b_cvst],
                          chan="cvst")
                ffn_defer = [None]
                for j in range(NJ):
                    slot, bslot = ws_next(l, NPIECE_IN + 4 + j)
                    cv = []
                    for ab in range(2):
                        cbk = j + NJ * ab
                        bk = nbank()
                        for kc in range(8):
                            S.pe(E("matmul",
                                ps[:, bk, 0:T], lhsT=slot[:, kc * 256 + ab * 128: kc * 256 + ab * 128 + 128],
                                rhs=xTb[:, kc, 0:T], start=(kc == 0), stop=(kc == 7)), reads=[bslot, b_xTb], writes=[b_ps[bk]])
                        ub, bub = ubuf[ab + 2 * (j % 2)], b_ubuf[ab + 2 * (j % 2)]
                        buh = b_uh[ab + 2 * (j % 2)]
                        u3 = ub[:, 0:nseq * (L + 2)].rearrange("p (s t) -> p s t", t=L + 2)
                        if kind == "s":
                            S.act(E("copy", out=u3[:, :, 0:2], in_=cvst[:, cbk, :, :]), reads=[b_cvst],
                                  writes=[buh])
                        else:
                            S.act(E("copy", out=u3[:, 0, 0:2], in_=cvc[:, l, cbk, :]),
                                  reads=[b_cvc[l][cbk]], writes=[buh])
                        if hpf:
                            bkh = hp_mm8(l, NPIECE_IN + 4 + j, ab, lambda kc: x32[:, kc, 0:2], b_x32)
                            S.act(E("copy", out=u3[:, 0, 2:4], in_=ps[:, bkh, 0:2]), reads=[b_ps[bkh]], writes=[bub])
                            S.act(E("copy", out=u3[:, 0, 4:L + 2], in_=ps[:, bk, 2:T]), reads=[b_ps[bk]], writes=[bub])
                        else:
                            S.act(E("copy", out=u3[:, :, 2:L + 2], in_=t3(ps[:, bk, 0:T])), reads=[b_ps[bk]], writes=[bub])
                        if kind == "s":
                            S.act(E("copy", out=cvo[:, cbk, :, :], in_=t3(ps[:, bk, 0:T])[:, :, L - 2:L]), reads=[b_ps[bk]],
                                  writes=[b_cvo])
                        else:
                            S.act(E("copy", out=cvc[:, l, cbk, :], in_=ps[:, bk, T - 2:T]), reads=[b_ps[bk]],
                                  writes=[b_cvc[l][cbk]])
                        ct, bct = tmp[2 + ab + 2 * (j % 2)], b_tmp[2 + ab + 2 * (j % 2)]
                        c3 = t3(ct[:, 0:T])
                        S.pool(E("tensor_scalar", out=c3, in0=u3[:, :, 0:L], scalar1=V_("cw0", cbk),
                                                                               scalar2=V_("cb", cbk), op0=ALU.mult, op1=ALU.add),
                              reads=[bub, buh, bvec], writes=[bct])
                        S.dve(E("scalar_tensor_tensor", out=c3, in0=u3[:, :, 1:L + 1], scalar=V_("cw1", cbk),
                                                                                      in1=c3, op0=ALU.mult, op1=ALU.add),
                              reads=[bub, buh, bvec, bct], writes=[bct])
                        S.dve(E("scalar_tensor_tensor", out=c3, in0=u3[:, :, 2:L + 2], scalar=V_("cw2", cbk),
                                                                                      in1=c3, op0=ALU.mult, op1=ALU.add),
                              reads=[bub, buh, bvec, bct], writes=[bct])
                        cv.append((ct, bct))
                    (ca, bca), (cb_, bcb) = cv

                    def gelu_h(j=j, ca=ca, bca=bca, cb_=cb_, bcb=bcb):
                        S.act(E("activation", out=ca[:, 0:T], in_=ca[:, 0:T], func=AF.Gelu_apprx_tanh), reads=[bca], writes=[bca])
                        S.dve(E("tensor_tensor", out=hT[:, j, 0:T], in0=ca[:, 0:T], in1=cb_[:, 0:T], op=ALU.mult),
                              reads=[bca, bcb], writes=[b_hT])
                        if hpf:
                            S.dve(E("tensor_tensor", out=h32[:, j, :], in0=ca[:, 0:2], in1=cb_[:, 0:2], op=ALU.mult),
                                  reads=[bca, bcb], writes=[b_h32])

                    if ffn_defer[0] is not None:
                        ffn_defer[0]()
                    ffn_defer[0] = gelu_h
                ffn_defer[0]()
                ffn_defer[0] = None
                if kind == "s":
                    S.dma(E("dma_start", out=cvo_d[l][:, 88:88 * (1 + NSS)], in_=cvo[:].rearrange("p a b c -> p (a b c)")),
                          reads=[b_cvo], chan="cvout")
                elif last_prompt:
                    S.dma(E("dma_start", out=cvo_d[l][:, 0:88], in_=cvc[:, l, :, :].rearrange("p a c -> p (a c)")),
                          reads=b_cvc[l], chan="cvout")
                for cb in range(8):
                    slot, bslot = ws_next(l, NPIECE_IN + 4 + NJ + cb)
                    bk = nbank()
                    for j in range(NJ):
                        S.pe(E("matmul", ps[:, bk, 0:T], lhsT=slot[:, j * 128:(j + 1) * 128], rhs=hT[:, j, 0:T],
                                                                       start=(j == 0), stop=(j == NJ - 1)),
                             reads=[bslot, b_hT], writes=[b_ps[bk]])
                    if hpf:
                        bkh = nbank()
                        for jh in range(3):
                            njj = min(8, NJ - jh * 8)
                            w, bw = hp_load(l, NPIECE_IN + 4 + NJ + cb, ("jh", jh, njj))
                            for jj in range(njj):
                                j = jh * 8 + jj
                                S.pe(E("matmul", ps[:, bkh, 0:2], lhsT=w[:, jj * 128:(jj + 1) * 128], rhs=h32[:, j, 0:2],
                                       start=(j == 0), stop=(j == NJ - 1)), reads=[bw, b_h32], writes=[b_ps[bkh]])
                        S.dve(E("scalar_tensor_tensor", out=x32[:, cb, 0:2], in0=x32[:, cb, 0:2], scalar=ALPHA,
                                in1=ps[:, bkh, 0:2], op0=ALU.mult, op1=ALU.add), reads=[b_ps[bkh], b_x32], writes=[b_x32])
                    S.dve(E("scalar_tensor_tensor", out=x32[:, cb, clo:T], in0=x32[:, cb, clo:T], scalar=ALPHA,
                                                                         in1=ps[:, bk, clo:T], op0=ALU.mult, op1=ALU.add),
                          reads=[b_ps[bk], b_x32], writes=[b_x32])
                stage(7)
                ln_epilogue("ln2g", "ln2b")

            S.dma(E("dma_start", out=yT_d[:, :, tok0:tok0 + T], in_=x32[:, :, 0:T]), reads=[b_x32], chan="yout")


    try:
        _tile_loop()
        assert ws["used"] == len(plan)
    except _Stop:
        pass
    S.run_block()
    return nc, S


_CACHE = {}


def _prep_inputs(inp, cfg):
    f = np.float32
    W = _prep_weights(inp)
    vm = [_prep_layer_params(inp, l) for l in range(NL)]
    vecs = np.stack([v for v, _ in vm])
    mats = np.stack([m for _, m in vm])
    cst = _consts()
    cosT, sinT = _rope_tables()
    in_maps = []
    for c in range(8):
        xp = inp["x_prompt"][c]
        xs = inp["x_sample"][c * NSS:(c + 1) * NSS].reshape(NSS * LS, D)
        x = np.concatenate([xp, xs], axis=0)
        xT = np.ascontiguousarray(x.T.reshape(8, 128, -1).transpose(1, 0, 2))
        sst = np.zeros((NL, NSS, 16, 64, 64), f)
        sl = slice(c * NSS, (c + 1) * NSS)
        sst[:, :, 0:6] = inp["state_rwkv"][:, sl].transpose(0, 1, 2, 4, 3)
        sst[:, :, 6:11, 0:32] = inp["state_gla"][:, sl]
        sst[:, :, 11:16] = inp["state_ret"][:, sl]
        sst = sst.reshape(NL, NSS, 8, 2, 64, 64).transpose(0, 1, 3, 4, 2, 5).reshape(NL, NSS, 128, 512)
        sh = inp["state_shift"][:, sl]
        shp = np.zeros((NL, NSS, 12 * 128), f)
        shp[:, :, 0:1152] = sh[:, :, 0:1152]
        shp[:, :, 1152:1408] = sh[:, :, 1152:1408]
        shp[:, :, 1408:1440] = sh[:, :, 1408:1440]
        ssh = shp.reshape(NL, NSS, 12, 128).transpose(0, 3, 2, 1).reshape(NL, 128, 12 * NSS)
        cvs = inp["state_conv"][:, sl]
        scv = cvs.reshape(NL, NSS, 2, 44, 128).transpose(0, 4, 3, 1, 2).reshape(NL, 128, 44 * NSS * 2)
        in_maps.append({"xT": xT, "wts": W, "vecs": vecs, "mats": mats, "cst": cst, "ropec": cosT, "ropes": sinT,
                        "sst": np.ascontiguousarray(sst), "ssh": np.ascontiguousarray(ssh), "scv": np.ascontiguousarray(scv)})
    return in_maps


def _assemble(results):
    f = np.float32
    y_p = np.zeros((8, SEQ, D), f); y_s = np.zeros((8 * NSS, LS, D), f)
    rw_p = np.zeros((NL, 8, 6, 64, 64), f); rw_s = np.zeros((NL, 8 * NSS, 6, 64, 64), f)
    sh_p = np.zeros((NL, 8, 1440), f); sh_s = np.zeros((NL, 8 * NSS, 1440), f)
    gl_p = np.zeros((NL, 8, 5, 32, 64), f); gl_s = np.zeros((NL, 8 * NSS, 5, 32, 64), f)
    rt_p = np.zeros((NL, 8, 5, 64, 64), f); rt_s = np.zeros((NL, 8 * NSS, 5, 64, 64), f)
    cv_p = np.zeros((NL, 8, 2, 2 * DFF), f); cv_s = np.zeros((NL, 8 * NSS, 2, 2 * DFF), f)
    for c, r in enumerate(results):
        yT = r["yT"]
        y = yT.transpose(1, 0, 2).reshape(D, -1).T
        y_p[c] = y[0:SEQ]
        y_s[c * NSS:(c + 1) * NSS] = y[SEQ:].reshape(NSS, LS, D)
        st = r["sto"].reshape(NL, 1 + NSS, 2, 64, 8, 64).transpose(0, 1, 4, 2, 3, 5).reshape(NL, 1 + NSS, 16, 64, 64)
        rw = st[:, :, 0:6].transpose(0, 1, 2, 4, 3)
        gl = st[:, :, 6:11, 0:32]
        rt = st[:, :, 11:16]
        rw_p[:, c] = rw[:, 0]; gl_p[:, c] = gl[:, 0]; rt_p[:, c] = rt[:, 0]
        sl = slice(c * NSS, (c + 1) * NSS)
        rw_s[:, sl] = rw[:, 1:]; gl_s[:, sl] = gl[:, 1:]; rt_s[:, sl] = rt[:, 1:]
        sho = r["sho"].reshape(NL, 128, 12 * (1 + NSS))
        p = sho[:, :, 0:12].transpose(0, 2, 1).reshape(NL, 12 * 128)
        s = sho[:, :, 12:].reshape(NL, 128, 12, NSS).transpose(0, 3, 2, 1).reshape(NL, NSS, 12 * 128)
        sh_p[:, c] = p[:, 0:1440]
        sh_s[:, sl] = s[:, :, 0:1440]
        cvo = r["cvo"].reshape(NL, 128, 44 * (1 + NSS) * 2)
        p = cvo[:, :, 0:88].reshape(NL, 128, 44, 2).transpose(0, 3, 2, 1).reshape(NL, 2, 2 * DFF)
        s = cvo[:, :, 88:].reshape(NL, 128, 44, NSS, 2).transpose(0, 3, 4, 2, 1).reshape(NL, NSS, 2, 2 * DFF)
        cv_p[:, c] = p
        cv_s[:, sl] = s
    return (y_p, y_s, rw_p, sh_p, gl_p, rt_p, cv_p, rw_s, sh_s, gl_s, rt_s, cv_s)


def kernel(**inputs):
    inp = {k: np.asarray(v) for k, v in inputs.items()}
    cfg = CFG_FULL
    if "nc" not in _CACHE:
        _CACHE["nc"] = build(cfg)[0]
    nc = _CACHE["nc"]
    in_maps = _prep_inputs(inp, cfg)
    res = run_bass_kernel_spmd(nc, in_maps, core_ids=list(range(8)))
    return _assemble(res.results)
```

```python
import bisect
import math
from contextlib import ExitStack

import numpy as np
import concourse.bass as bass
import concourse.mybir as mybir
from concourse.bass_utils import run_bass_kernel_spmd

F32 = mybir.dt.float32
F32R = mybir.dt.float32r
BF16 = mybir.dt.bfloat16
AF = mybir.ActivationFunctionType
ALU = mybir.AluOpType
AX = mybir.AxisListType

NL = 4
D = 1024
SEQ = 2048
NSS = 16
LS = 4
TT = 256
PAST = 16384
DFF = 2816
NJ = 22
ALPHA = (2 * NL) ** 0.25
NBLK = 38
NPIECE_IN = 19
SLOT = 2816
NSLOT = 4
RW0, GL0, RT0 = 0, 1440, 2416


class _Stop(Exception):
    pass


class Buf:
    __slots__ = ("name", "last_write", "reads")

    def __init__(self, name):
        self.name = name
        self.last_write = None
        self.reads = []


class Op:
    __slots__ = ("eng", "fn", "deps", "idx", "needs_inc", "cnt", "chan", "dma_cnt")

    def __init__(self, eng, fn, deps, idx, chan=None):
        self.eng = eng
        self.fn = fn
        self.deps = deps
        self.idx = idx
        self.needs_inc = False
        self.cnt = None
        self.chan = chan
        self.dma_cnt = None


COMPUTE = ("pe", "act", "dve", "pool")
DMAQ = ("sp", "actq", "poolq")
STREAM = {"pe": "pe", "act": "act", "dve": "dve", "pool": "pool", "sp": "sp", "actq": "act", "poolq": "pool"}
SAME_SYNC = {"pe": False, "act": True, "dve": True, "pool": True}


class Sched:
    def __init__(self, nc):
        self.nc = nc
        self.ops = []

    def op(self, eng, fn, reads=(), writes=(), chan=None):
        idx = len(self.ops)
        deps = {}
        for b in reads:
            if b.last_write is not None:
                deps[b.last_write] = True
        for b in writes:
            if b.last_write is not None:
                deps.setdefault(b.last_write, False)
            for r_ in b.reads:
                deps.setdefault(r_, False)
        o = Op(eng, fn, deps, idx, chan)
        self.ops.append(o)
        for b in reads:
            b.reads.append(idx)
        for b in writes:
            b.last_write = idx
            b.reads = []
        return o

    def pe(self, fn, reads=(), writes=()):
        return self.op("pe", fn, reads, writes)

    def act(self, fn, reads=(), writes=()):
        return self.op("act", fn, reads, writes)

    def dve(self, fn, reads=(), writes=()):
        return self.op("dve", fn, reads, writes)

    def pool(self, fn, reads=(), writes=()):
        return self.op("pool", fn, reads, writes)

    def dma(self, fn, reads=(), writes=(), chan=None, q="sp"):
        return self.op(q, fn, reads, writes, chan=chan)

    def _skip(self, p, so, raw=True):
        sp = STREAM[p.eng]
        return sp == so and p.eng in COMPUTE and (not SAME_SYNC[p.eng] or (not raw and p.eng != "pool"))

    def run_block(self):
        nc = self.nc
        ops = self.ops
        for o in ops:
            so = STREAM[o.eng]
            for d, raw in o.deps.items():
                p = ops[d]
                if p.eng in DMAQ or self._skip(p, so, raw):
                    continue
                p.needs_inc = True
        cnt = {e: 0 for e in COMPUTE}
        chan_cnt = {}
        chan_hist = {}
        for o in ops:
            if o.eng in DMAQ:
                c = chan_cnt.get(o.chan, 0) + 16
                chan_cnt[o.chan] = c
                o.dma_cnt = c
                chan_hist.setdefault(o.chan, []).append((o.idx, c))
            elif o.needs_inc:
                cnt[o.eng] += 1
                o.cnt = cnt[o.eng]
        chan_idx = {c: [i for i, _ in h] for c, h in chan_hist.items()}
        streams = {s: [] for s in ("pe", "act", "dve", "pool", "sp")}
        waited = {s: {} for s in streams}
        for o in ops:
            so = STREAM[o.eng]
            need = {}
            for d, raw in o.deps.items():
                p = ops[d]
                if p.eng in DMAQ:
                    h = chan_hist[p.chan]
                    k = bisect.bisect_left(chan_idx[p.chan], o.idx) - 1
                    key = ("chan", p.chan)
                    need[key] = max(need.get(key, 0), h[k][1])
                else:
                    if self._skip(p, so, raw):
                        continue
                    key = ("eng", p.eng)
                    need[key] = max(need.get(key, 0), p.cnt)
            w = []
            for key, c in need.items():
                if waited[so].get(key, 0) >= c:
                    continue
                waited[so][key] = c
                w.append((key, c))
            streams[so].append((o, w))
        with ExitStack() as es:
            sems = {}
            for e in COMPUTE:
                sems[("eng", e)] = es.enter_context(nc.semaphore("s_" + e))
            for c in chan_cnt:
                sems[("chan", c)] = es.enter_context(nc.semaphore("c_" + str(c)))
            block = es.enter_context(nc.Block())

            def mk(sname):
                def body(eng):
                    for o, w in streams[sname]:
                        for key, c in w:
                            eng.wait_ge(sems[key], c)
                        ins = o.fn(eng)
                        if o.eng in DMAQ:
                            ins.then_inc(sems[("chan", o.chan)], 16)
                        elif o.needs_inc:
                            ins.then_inc(sems[("eng", o.eng)], 1)
                    if sname == "sp":
                        for c, v in chan_cnt.items():
                            eng.wait_ge(sems[("chan", c)], v)
                        for e in COMPUTE:
                            if cnt[e] > 0:
                                eng.wait_ge(sems[("eng", e)], cnt[e])
                return body

            block.tensor(mk("pe"))
            block.scalar(mk("act"))
            block.vector(mk("dve"))
            block.gpsimd(mk("pool"))
            block.sync(mk("sp"))


def _colmap():
    cm = -np.ones(NBLK * 128, np.int64)
    for g, (rw_off, gl_off, rt_off, pad) in enumerate(((0, 0, 0, True), (384, 160, 320, True), (768, 320, 640, False))):
        base = g * 1024
        cm[base:base + 384] = RW0 + rw_off + np.arange(384)
        for j in range(5):
            if pad:
                cm[base + 384 + j * 64: base + 384 + j * 64 + 32] = GL0 + gl_off + j * 32 + np.arange(32)
            else:
                cm[base + 384 + j * 64: base + 384 + j * 64 + 64] = GL0 + gl_off + j * 64 + np.arange(64)
        cm[base + 704: base + 1024] = RT0 + rt_off + np.arange(320)
    b = 24 * 128
    cm[b:b + 320] = GL0 + 640 + np.arange(320)
    cm[b + 320:b + 640] = RT0 + 960 + np.arange(320)
    b = 29 * 128
    cm[b:b + 128] = RW0 + 1152 + np.arange(128)
    cm[b + 128:b + 256] = RW0 + 1280 + np.arange(128)
    cm[b + 256:b + 288] = RW0 + 1408 + np.arange(32)
    cm[b + 288:b + 304] = GL0 + 960 + np.arange(16)
    for gi, off in enumerate((0, 320)):
        b = (32 + 3 * gi) * 128
        for r in range(704, 1024):
            i = r - 704
            h, d = divmod(i, 64)
            cm[b + (r - 640)] = RT0 + off + h * 64 + ((d + 32) % 64)
    return cm


def _rows_layout(v, nblk):
    return np.ascontiguousarray(v.reshape(nblk, 128).T)


VEC = {}


def _vec_layout():
    off = 0
    for name, n in (("mu", 12), ("w0", 3), ("a0", 3), ("kkv", 3), ("ka", 3), ("rk", 3), ("nw", 8), ("nb", 8),
                    ("bgkb", 3), ("ln1g", 8), ("ln1b", 8), ("ln2g", 8), ("ln2b", 8), ("cw0", 44), ("cw1", 44),
                    ("cw2", 44), ("cb", 44), ("rscale", 8), ("kscale", 8), ("ldret", 8)):
        VEC[name] = (off, n)
        off += n
    return off


NV = _vec_layout()
NM = 4 * 384


def _prep_layer_params(inp, l):
    f = np.float32
    vec = np.zeros((128, NV), f)

    def put(name, arr):
        o, n = VEC[name]
        assert arr.shape == (128, n), (name, arr.shape)
        vec[:, o:o + n] = arr

    mu = inp["rwkv_mu"][l]
    mup = np.zeros(12 * 128, f)
    mup[0:384] = mu[0:384]
    mup[384:768] = mu[384:768]
    mup[768:1152] = mu[768:1152]
    mup[1152:1152 + 128] = mu[1152:1280]
    mup[1280:1408] = mu[1280:1408]
    mup[1408:1440] = mu[1408:1440]
    put("mu", _rows_layout(mup, 12))
    put("w0", _rows_layout(inp["rwkv_w0"][l], 3))
    put("a0", _rows_layout(inp["rwkv_a0"][l], 3))
    put("kkv", _rows_layout(inp["rwkv_kk"][l], 3))
    put("ka", _rows_layout(inp["rwkv_ka"][l], 3))
    put("rk", _rows_layout(inp["rwkv_rk"][l].reshape(-1), 3))
    nw = np.ones(1024, f)
    nb = np.zeros(1024, f)
    nw[0:384] = inp["rwkv_lnw"][l]
    nb[0:384] = inp["rwkv_lnb"][l]
    nw[384:704] = np.tile(inp["gla_norm_w"][l], 5)
    put("nw", _rows_layout(nw, 8))
    put("nb", _rows_layout(nb, 8))
    bb = np.zeros(384, f)
    for j in range(5):
        bb[j * 64:j * 64 + 32] = inp["gla_bgk_b"][l][j * 32:(j + 1) * 32]
    put("bgkb", _rows_layout(bb, 3))
    put("ln1g", _rows_layout(inp["ln1_g"][l], 8))
    put("ln1b", _rows_layout(inp["ln1_b"][l], 8))
    put("ln2g", _rows_layout(inp["ln2_g"][l], 8))
    put("ln2b", _rows_layout(inp["ln2_b"][l], 8))
    cw = inp["ffn_conv_w"][l]
    put("cw0", _rows_layout(cw[0], 44))
    put("cw1", _rows_layout(cw[1], 44))
    put("cw2", _rows_layout(cw[2], 44))
    put("cb", _rows_layout(inp["ffn_conv_b"][l], 44))
    rs = np.ones(1024, f)
    rs[384:704] = 32.0 ** -0.5
    ks = np.ones(1024, f)
    ks[704:1024] = 64.0 ** -0.5
    put("rscale", _rows_layout(rs, 8))
    put("kscale", _rows_layout(ks, 8))
    ldr = np.zeros(1024, np.float64)
    for h in range(5):
        ldr[704 + h * 64:704 + (h + 1) * 64] = math.log(1.0 - 2.0 ** (-5.0 - h))
    put("ldret", _rows_layout(ldr.astype(f), 8))
    mats = np.zeros((128, NM), f)
    mats[0:64, 0:384] = inp["rwkv_bw"][l]
    mats[64:128, 0:384] = inp["rwkv_ba"][l]
    mats[:, 384:768] = inp["rwkv_bg"][l][0:128]
    mats[0:32, 768:1152] = inp["rwkv_bg"][l][128:160]
    bgk = inp["gla_bgk"][l]
    for j in range(5):
        mats[32:48, 1152 + j * 64:1152 + j * 64 + 32] = bgk[:, j * 32:(j + 1) * 32]
    return vec, mats


def _prep_weights(inp):
    cm = _colmap()
    npieces = NPIECE_IN + 4 + NJ + 8
    W = np.zeros((NL, npieces, 128, SLOT), np.float32)
    for l in range(NL):
        win = inp["w_in"][l]
        wp = np.zeros((1024, NBLK * 128), np.float32)
        ok = cm >= 0
        wp[:, ok] = win[:, cm[ok]]
        p = 0
        for i in range(NPIECE_IN):
            blkc = wp[:, i * 256:(i + 1) * 256].reshape(8, 128, 256).transpose(1, 0, 2)
            W[l, p, :, :2048] = blkc.reshape(128, 2048)
            p += 1
        wo = inp["w_out"][l]
        for i in range(4):
            blkc = wo[:, i * 256:(i + 1) * 256].reshape(8, 128, 256).transpose(1, 0, 2)
            W[l, p, :, :2048] = blkc.reshape(128, 2048)
            p += 1
        up = inp["ffn_up"][l]
        for j in range(NJ):
            ab = np.concatenate([up[:, j * 128:(j + 1) * 128], up[:, DFF + j * 128:DFF + (j + 1) * 128]], axis=1)
            blkc = ab.reshape(8, 128, 256).transpose(1, 0, 2)
            W[l, p, :, :2048] = blkc.reshape(128, 2048)
            p += 1
        dn = inp["ffn_down"][l]
        for cb in range(8):
            blkc = dn[:, cb * 128:(cb + 1) * 128].reshape(NJ, 128, 128).transpose(1, 0, 2)
            W[l, p, :, :NJ * 128] = blkc.reshape(128, NJ * 128)
            p += 1
        assert p == npieces
    return W


def _consts():
    ident = np.eye(128, dtype=np.float32)
    sI = np.arange(128)[:, None]
    tI = np.arange(128)[None, :]
    bones = np.zeros((128, 128), np.float32)
    bones[0:64, 0:64] = 1
    bones[64:128, 64:128] = 1
    cst = np.zeros((128, 672), np.float32)
    cst[:, 0:128] = ident
    cst[:, 128:256] = bones
    cst[:, 256:384] = (tI >= sI)
    cst[:, 384:512] = (tI > sI)
    cst[:, 512:640] = (tI < sI)
    tok = np.zeros((128, 32), np.float32)
    tok[:, 0:6] = 1.0 / 64
    tok[:, 16:22] = 64e-5
    tok[:, 22:32] = 1e-6
    cst[:, 640:672] = tok
    return cst


def _rope_tables():
    ntok = SEQ + NSS * LS
    pos = np.concatenate([np.arange(SEQ), np.tile(PAST + np.arange(LS), NSS)]).astype(np.float32)
    inv = (1.0 / (10000.0 ** np.linspace(0.0, 1.0, 32, dtype=np.float32))).astype(np.float32)
    ang = pos[None, :] * inv[:, None]
    cos = np.cos(ang).astype(np.float32)
    sin = np.sin(ang).astype(np.float32)
    C = np.ones((384, ntok), np.float32)
    S = np.zeros((384, ntok), np.float32)
    for h in range(5):
        r0 = 64 + h * 64
        C[r0:r0 + 32] = cos
        C[r0 + 32:r0 + 64] = cos
        S[r0:r0 + 32] = -sin
        S[r0 + 32:r0 + 64] = sin
    C = C.reshape(3, 128, ntok).transpose(1, 0, 2)
    S = S.reshape(3, 128, ntok).transpose(1, 0, 2)
    return np.ascontiguousarray(C), np.ascontiguousarray(S)


CFG_FULL = dict(tiles=[("p", i) for i in range(SEQ // TT)] + [("s", 0)], nl=NL)


def build(cfg):
    nc = bass.Bass("TRN2", target_bir_lowering=False)
    S = Sched(nc)
    NTOK = SEQ + NSS * LS
    npieces = NPIECE_IN + 4 + NJ + 8
    nl = cfg["nl"]
    tiles = cfg["tiles"]

    def din(name, shape, dt=F32):
        return nc.dram_tensor(name, list(shape), dt, kind="ExternalInput").ap()

    def dout(name, shape):
        return nc.dram_tensor(name, list(shape), F32, kind="ExternalOutput").ap()

    xT_d = din("xT", [128, 8, NTOK])
    wts_d = din("wts", [NL, npieces, 128, SLOT])
    vec_d = din("vecs", [NL, 128, NV])
    mat_d = din("mats", [NL, 128, NM])
    cst_d = din("cst", [128, 672])
    cos_d = din("ropec", [128, 3, NTOK])
    sin_d = din("ropes", [128, 3, NTOK])
    sst_d = din("sst", [NL, NSS, 128, 512])
    ssh_d = din("ssh", [NL, 128, 12 * NSS])
    scv_d = din("scv", [NL, 128, 44 * NSS * 2])
    yT_d = dout("yT", [128, 8, NTOK])
    sto_d = dout("sto", [NL, 1 + NSS, 128, 512])
    sho_d = dout("sho", [NL, 128, 12 * (1 + NSS)])
    cvo_d = dout("cvo", [NL, 128, 44 * (1 + NSS) * 2])
    dbg_d = dout("dbg", [128, 8192]) if cfg.get("dbg") else None
    dbg_state = {"col": 0, "items": []}

    def dbg(name, ap, bufs, np_=128):
        if dbg_d is None:
            return
        n = 1
        for d_ in ap.shape[1:]:
            n *= d_
        c0 = dbg_state["col"]
        if c0 + n > 8192:
            return
        dbg_state["col"] = c0 + n
        dbg_state["items"].append((name, c0, n, np_, tuple(ap.shape)))
        flat = dbg_d[0:np_, c0:c0 + n]
        if len(ap.shape) == 3:
            flat = flat.rearrange("p (a b) -> p a b", b=ap.shape[2])
        S.dma(E("dma_start", out=flat, in_=ap), reads=bufs, chan="dbg")
    cfg["_dbg_items"] = dbg_state["items"]

    def sb(name, shape, dt=F32):
        return nc.alloc_sbuf_tensor("sb_" + name, list(shape), dt)

    TM = TT
    x32 = sb("x32", [128, 8, TM]); b_x32 = Buf("x32")
    xTb = sb("xTb", [128, 8, TM], BF16); b_xTb = Buf("xTb")
    un1 = sb("un1", [128, 12 * (TM + NSS)]); b_PRW = Buf("PRW")
    PRW = un1[:].rearrange("p (a b) -> p a b", b=TM + NSS)
    Rg = sb("Rg", [128, 8, TM]); b_R = Buf("R")
    Kg = sb("Kg", [128, 8, TM]); b_K = Buf("K")
    Vg = sb("Vg", [128, 8, TM]); b_V = Buf("V")
    Gg = sb("Gg", [128, 8, TM], BF16); b_G = Buf("G")

    XS = sb("XS", [128, 3, TM]); b_XS = Buf("XS")
    ATt = sb("ATt", [128, 3, TM]); b_AT = Buf("AT")
    BTt = sb("BTt", [128, 3, TM]); b_BT = Buf("BT")
    bonus = sb("bonus", [128, 3, TM]); b_bonus = Buf("bonus")
    ynT = sb("ynT", [128, 8, TM]); b_ynT = Buf("ynT")
    SW = ynT; b_SW = b_ynT
    mixT = sb("mixT", [128, 8, TM], BF16); b_mixT = Buf("mixT")
    hT = un1[:].bitcast(BF16)[:, 0:NJ * TM].rearrange("p (a b) -> p a b", b=TM); b_hT = b_PRW
    WC = sb("WC", [128, 8, 16]); b_WC = Buf("WC")
    NTMP = 9
    tmp = [sb("tmp%d" % i, [128, TM]) for i in range(NTMP)]
    b_tmp = [Buf("tmp%d" % i) for i in range(NTMP)]
    tmpb = [sb("tmpb%d" % i, [128, TM], BF16) for i in range(4)]
    b_tmpb = [Buf("tmpb%d" % i) for i in range(4)]
    ubuf = [sb("ubuf%d" % i, [128, TM + 2 * NSS]) for i in range(4)]
    b_ubuf = [Buf("ubuf%d" % i) for i in range(4)]
    cosT = sb("cosT", [128, 3, TM]); sinT = sb("sinT", [128, 3, TM]); b_rope = Buf("rope")
    cst = sb("cst", [128, 672]); b_cst = Buf("cst")
    onesb = sb("onesb", [128, 128], BF16); b_ones = Buf("onesb")
    bonesr = sb("bonesr", [128, 128]); b_bones = Buf("bonesr")
    rmask = sb("rmask", [128, TM]); b_rmask = Buf("rmask")
    vecs = [sb("vecs%d" % i, [128, NV]) for i in range(2)]; b_vecs = [Buf("vecs%d" % i) for i in range(2)]
    vaux = [sb("vaux%d" % i, [128, 8]) for i in range(2)]; b_vaux = [Buf("vaux%d" % i) for i in range(2)]
    _m0 = sb("mats0", [128, NM], BF16); _bm0 = Buf("mats0")
    mats = [_m0, _m0]; b_mats = [_bm0, _bm0]
    hpw = [sb("hpw%d" % i, [128, 1024]) for i in range(2)]; b_hpw = [Buf("hpw%d" % i) for i in range(2)]
    mix32 = sb("mix32", [128, 8, 2]); b_mix32 = Buf("mix32")
    h32 = sb("h32", [128, NJ, 2]); b_h32 = Buf("h32")
    slots = [sb("slot%d" % i, [128, SLOT], BF16) for i in range(NSLOT)]
    b_slots = [Buf("slot%d" % i) for i in range(NSLOT)]
    Sst = [sb("Sst%d" % l, [128, 8, 64]) for l in range(NL)]; b_Sst = [Buf("Sst%d" % l) for l in range(NL)]
    Ssm = [sb("Ssm%d" % i, [128, 8, 64]) for i in range(2)]; b_Ssm = [Buf("Ssm%d" % i) for i in range(2)]
    _s0 = sb("Sld0", [128, 8, 64]); _bs0 = Buf("Sld0")
    Sld = [_s0, _s0]; b_Sld = [_bs0, _bs0]
    shc = sb("shc", [128, NL, 12]); b_shc = [Buf("shc%d" % l) for l in range(NL)]
    shst = sb("shst", [128, 12, NSS]); b_shst = Buf("shst")
    sho = sb("sho", [128, 12, NSS]); b_sho = Buf("sho")
    cvc = sb("cvc", [128, NL, 44, 2]); b_cvc = [[Buf("cvc%d_%d" % (l, c_)) for c_ in range(44)] for l in range(NL)]
    b_uh = [Buf("uh%d" % i) for i in range(4)]
    cvst = sb("cvst", [128, 44, NSS, 2]); b_cvst = Buf("cvst")
    cvo = sb("cvo", [128, 44, NSS, 2]); b_cvo = Buf("cvo")
    CM = 128
    vtok = sb("vtok", [CM, 1024]); b_vtok = Buf("vtok")
    ktok = sb("ktok", [CM, 1024]); b_ktok = Buf("ktok")
    btok = sb("btok", [CM, 384]); b_btok = Buf("btok")
    ark = sb("ark", [CM, 16, CM]); b_ark = Buf("ark")
    aak = sb("aak", [CM, 6, CM]); b_aak = Buf("aak")
    arb = sb("arb", [CM, 6, CM]); b_arb = Buf("arb")
    _p = sb("Pm0", [CM, 6, CM], BF16); _bp = Buf("Pm0")
    _q = sb("Qm0", [CM, 6, CM], BF16); _bq = Buf("Qm0")
    Xb = sb("Xb", [CM, 6, CM], BF16); b_Xb = Buf("Xb")
    Pm = [_p, _p]; b_Pm = [_bp, _bp]
    Qm = [_q, _q]; b_Qm = [_bq, _bq]
    Xm = sb("Xm", [CM, 6, CM]); b_Xm = Buf("Xm")
    rhs0 = sb("rhs0", [CM, 384]); b_rhs0 = Buf("rhs0")
    usb = sb("usb", [CM, 384]); b_usb = Buf("usb")
    ysb = sb("ysb", [CM, 1024]); b_ysb = Buf("ysb")
    ysq = ktok; b_ysq = b_ktok
    yst = sb("yst", [CM, 8, 16]); b_yst = Buf("yst")
    ps = nc.alloc_psum_tensor("ps", [128, 8, 512], F32)
    b_ps = [Buf("ps%d" % i) for i in range(8)]
    bank_ctr = [0]

    def nbank():
        b = bank_ctr[0] % 8
        bank_ctr[0] += 1
        return b

    R_ = lambda ap: ap.bitcast(F32R)

    def E(name, *a, **k):
        return lambda e: getattr(e, name)(*a, **k)

    S.dma(E("dma_start", out=cst[:], in_=cst_d), writes=[b_cst], chan="cst")
    ident = cst[:, 0:128]
    m_incl = cst[:, 256:384]
    m_su = cst[:, 384:512]
    m_sl = cst[:, 512:640]
    eye64 = cst[:, 0:128]
    tokc = cst[:, 640:672]
    S.dve(E("memset", onesb[:], 1.0), writes=[b_ones])
    S.dve(E("tensor_copy", out=R_(bonesr[:]), in_=cst[:, 128:256]), reads=[b_cst], writes=[b_bones])
    S.dve(E("memset", Sld[0][:], 0.0), writes=[b_Sld[0]])
    for l in range(NL):
        S.dve(E("tensor_copy", out=R_(Sst[l][:]), in_=Sld[0][:]), reads=[b_Sld[0]], writes=[b_Sst[l]])
        S.dve(E("memset", shc[:, l, :], 0.0), writes=[b_shc[l]])
        S.dve(E("memset", cvc[:, l, :, :], 0.0), writes=b_cvc[l])

    plan = []
    for (kind, ti) in tiles:
        for l in range(nl):
            for p in range(npieces):
                plan.append((l, p))
    ws = {"issued": 0, "used": 0, "pending_wr": {}}

    wbf_d = nc.dram_tensor("wbf", [NL, npieces, 128, SLOT], BF16).ap()
    b_wbf = {}

    def ws_issue():
        i = ws["issued"]
        if i >= len(plan):
            return
        l, p = plan[i]
        s = i % NSLOT
        n = 2048 if p < NPIECE_IN + 4 + NJ else NJ * 128
        if (l, p) not in b_wbf:
            S.dma(E("dma_start", out=slots[s][:, 0:n], in_=wts_d[l, p, :, 0:n]),
                  writes=[b_slots[s]], chan="slot%d" % s, q="poolq")
            if len(tiles) > 1:
                b_wbf[(l, p)] = Buf("wbf%d_%d" % (l, p))
                ws["pending_wr"][i] = (l, p, s, n)
        else:
            S.dma(E("dma_start", out=slots[s][:, 0:n], in_=wbf_d[l, p, :, 0:n]), reads=[b_wbf[(l, p)]],
                  writes=[b_slots[s]], chan="slot%d" % s)
        ws["issued"] += 1

    def ws_next(l, p):
        i = ws["used"]
        assert plan[i] == (l, p), (plan[i], l, p)
        while ws["issued"] < min(len(plan), i + NSLOT):
            ws_issue()
        if i in ws["pending_wr"]:
            l_, p_, s_, n_ = ws["pending_wr"].pop(i)
            S.dma(E("dma_start", out=wbf_d[l_, p_, :, 0:n_], in_=slots[s_][:, 0:n_]), reads=[b_slots[s_]],
                  writes=[b_wbf[(l_, p_)]], chan="wbfw")
        ws["used"] += 1
        s = i % NSLOT
        return slots[s], b_slots[s]

    for _ in range(NSLOT - 1):
        ws_issue()

    pv_ctr = [0]
    hp_ctr = [0]

    def hp_load(l, p, part):
        i = hp_ctr[0] % 2
        hp_ctr[0] += 1
        if part[0] == "sub":
            src = wts_d[l, p, :, 0:2048].rearrange("p (kc c) -> p kc c", c=256)[:, :, part[1] * 128:(part[1] + 1) * 128]
            dst = hpw[i][:, 0:1024].rearrange("p (kc c) -> p kc c", c=128)
        else:
            src = wts_d[l, p, :, part[1] * 1024:part[1] * 1024 + part[2] * 128]
            dst = hpw[i][:, 0:part[2] * 128]
        S.dma(E("dma_start", out=dst, in_=src), writes=[b_hpw[i]], chan="hp%d" % i)
        return hpw[i], b_hpw[i]

    def hp_mm8(l, p, sub, rhs_fn, brhs):
        w, bw = hp_load(l, p, ("sub", sub))
        bkh = nbank()
        for kc in range(8):
            S.pe(E("matmul", ps[:, bkh, 0:2], lhsT=w[:, kc * 128:(kc + 1) * 128], rhs=rhs_fn(kc), start=(kc == 0), stop=(kc == 7)),
                 reads=[bw, brhs], writes=[b_ps[bkh]])
        return bkh

    def stage(n):
        if cfg.get("stop") == n:
            raise _Stop()

    def _tile_loop():
        for (kind, ti) in tiles:
            if kind == "p":
                T = TT; nseq = 1; L = TT; C = 128; tok0 = ti * TT; nsteps = 6
            else:
                T = NSS * LS; nseq = NSS; L = LS; C = LS; tok0 = SEQ; nsteps = 1
            nch = T // C
            last_prompt = (kind == "p" and ti == SEQ // TT - 1)
            hpf = (kind == "p" and ti == 0 and cfg.get("hp", True))
            clo = 2 if hpf else 0
            S.dma(E("dma_start", out=x32[:, :, 0:T], in_=xT_d[:, :, tok0:tok0 + T]),
                  writes=[b_x32], chan="x32")
            S.dma(E("dma_start", out=xTb[:, :, 0:T], in_=xT_d[:, :, tok0:tok0 + T]),
                  writes=[b_xTb], chan="xTb", q="poolq")
            S.dma(E("dma_start", out=cosT[:, :, 0:T], in_=cos_d[:, :, tok0:tok0 + T]),
                  writes=[b_rope], chan="rope")
            S.dma(E("dma_start", out=sinT[:, :, 0:T], in_=sin_d[:, :, tok0:tok0 + T]),
                  writes=[b_rope], chan="rope")
            S.dve(E("memset", rmask[:, 0:T], 1.0), writes=[b_rmask])
            S.dve(E("memset", rmask[:, 0:T].rearrange("p (c t) -> p c t", t=C)[:, :, 0:1], 0.0),
                  writes=[b_rmask])

            for l in range(nl):
                pv = pv_ctr[0] % 2
                pv_ctr[0] += 1
                vec, bvec, mat, bmat, vx, bvx = vecs[pv], b_vecs[pv], mats[pv], b_mats[pv], vaux[pv], b_vaux[pv]
                S.dma(E("dma_start", out=vec[:], in_=vec_d[l]), writes=[bvec], chan="vec%d" % pv)
                S.dma(E("dma_start", out=mat[:], in_=mat_d[l]), writes=[bmat], chan="mat0",
                      q="poolq")

                def V_(name, b=0, n=1, vec=vec):
                    o, _ = VEC[name]
                    return vec[:, o + b:o + b + n]

                o_ka = VEC["ka"][0]; o_bg = VEC["bgkb"][0]
                S.dve(E("tensor_scalar", out=vx[:, 0:3], in0=vec[:, o_ka:o_ka + 3], scalar1=-1.0,
                                                                        scalar2=1.0, op0=ALU.mult, op1=ALU.add),
                      reads=[bvec], writes=[bvx])
                S.dve(E("tensor_scalar", out=vx[:, 3:6], in0=vec[:, o_bg:o_bg + 3], scalar1=-1.0,
                                                                        scalar2=None, op0=ALU.mult),
                      reads=[bvec], writes=[bvx])

                if kind == "s":
                    S.dma(E("dma_start", out=shst[:].rearrange("p a b -> p (a b)"), in_=ssh_d[l]),
                          writes=[b_shst], chan="shst")
                    S.dve(E("tensor_copy", out=PRW[:, :, 0:NSS * (LS + 1)].rearrange("p a (s t) -> p a s t", t=LS + 1)[:, :, :, 0],
                                                  in_=shst[:]), reads=[b_shst], writes=[b_PRW])
                else:
                    S.dve(E("tensor_copy", out=PRW[:, :, 0], in_=shc[:, l, :]), reads=[b_shc[l]], writes=[b_PRW])

                def prw_dst(pb, T=T, nseq=nseq, L=L):
                    return PRW[:, pb, 0:nseq * (L + 1)].rearrange("p (s t) -> p s t", t=L + 1)[:, :, 1:L + 1]

                def prw_prev(pb, T=T, nseq=nseq, L=L):
                    return PRW[:, pb, 0:nseq * (L + 1)].rearrange("p (s t) -> p s t", t=L + 1)[:, :, 0:L]

                def t3(ap, nseq=nseq, L=L):
                    return ap.rearrange("p (s t) -> p s t", t=L)

                evi = [0]
                for piece in range(NPIECE_IN):
                    slot, bslot = ws_next(l, piece)
                    for sub in range(2):
                        blk = piece * 2 + sub
                        bk = nbank()
                        for kc in range(8):
                            S.pe(E("matmul",
                                ps[:, bk, 0:T], lhsT=slot[:, kc * 256 + sub * 128: kc * 256 + sub * 128 + 128],
                                rhs=xTb[:, kc, 0:T], start=(kc == 0), stop=(kc == 7)),
                                reads=[bslot, b_xTb], writes=[b_ps[bk]])
                        g, gb = divmod(blk, 8)

                        def proj_evac(srcap, lo, hi, bkb, blk=blk, g=g, gb=gb):
                            if blk < 24 and gb < 3:
                                pb = g * 3 + gb
                            elif 29 <= blk < 32:
                                pb = 9 + blk - 29
                            else:
                                pb = None
                            if pb is not None:
                                if kind == "p":
                                    S.act(E("copy", out=PRW[:, pb, 1 + lo:1 + hi], in_=srcap), reads=[bkb], writes=[b_PRW])
                                else:
                                    S.act(E("copy", out=prw_dst(pb), in_=t3(srcap)), reads=[bkb], writes=[b_PRW])
                            elif blk < 24:
                                tile_, btile = ((Rg, b_R), (Kg, b_K), (Vg, b_V))[g]
                                S.dve(E("tensor_copy", out=R_(tile_[:, gb, lo:hi]), in_=srcap), reads=[bkb], writes=[btile])
                            elif blk < 29:
                                S.act(E("activation", out=Gg[:, 3 + blk - 24, lo:hi], in_=srcap, func=AF.Silu), reads=[bkb],
                                      writes=[b_G])
                            else:
                                S.dve(E("tensor_copy", out=SW[:, blk - 32, lo:hi], in_=srcap), reads=[bkb], writes=[b_SW])

                        proj_evac(ps[:, bk, clo:T], clo, T, b_ps[bk])
                        if hpf:
                            bkh = hp_mm8(l, piece, sub, lambda kc: x32[:, kc, 0:2], b_x32)
                            proj_evac(ps[:, bkh, 0:2], 0, 2, b_ps[bkh])

                stage(1)
                for pb in range(12):
                    g, gb = divmod(pb, 3)
                    tile_, btile = ((Rg, b_R), (Kg, b_K), (Vg, b_V), (XS, b_XS))[g]
                    tb = pb % 8
                    d3 = t3(tmp[tb][:, 0:T])
                    S.dve(E("tensor_tensor", out=d3, in0=prw_prev(pb), in1=prw_dst(pb), op=ALU.subtract),
                          reads=[b_PRW], writes=[b_tmp[tb]])
                    dst3 = t3(tile_[:, gb, 0:T])
                    if g <= 2:
                        dst3 = R_(dst3)
                    S.dve(E("scalar_tensor_tensor",
                        out=dst3, in0=d3, scalar=V_("mu", pb), in1=prw_dst(pb), op0=ALU.mult, op1=ALU.add),
                        reads=[b_PRW, b_tmp[tb], bvec], writes=[btile])
                if l == 0:
                    dbg("PRW0", PRW[:, 0, 0:80], [b_PRW]); dbg("Rsh", Rg[:, 0, 0:T], [b_R]); dbg("shst", shst[:, 0, :], [b_shst])
                lastcol = PRW[:, :, 0:nseq * (L + 1)].rearrange("p a (s t) -> p a s t", t=L + 1)[:, :, :, L]
                if kind == "s":
                    S.act(E("copy", out=sho[:], in_=lastcol), reads=[b_PRW], writes=[b_sho])
                    S.dma(E("dma_start", out=sho_d[l][:, 12:12 * (1 + NSS)], in_=sho[:].rearrange("p a b -> p (a b)")),
                          reads=[b_sho], chan="shout")
                else:
                    S.act(E("copy", out=shc[:, l, :], in_=lastcol[:, :, 0]), reads=[b_PRW],
                          writes=[b_shc[l]])
                    if last_prompt:
                        S.dma(E("dma_start", out=sho_d[l][:, 0:12], in_=shc[:, l, :]), reads=[b_shc[l]],
                              chan="shout")

                stage(2)
                for gi, (tile_, btile) in enumerate(((Rg, b_R), (Kg, b_K))):
                    a = tile_[:, 5:8, 0:T]
                    sw = SW[:, gi * 3:gi * 3 + 3, 0:T]
                    S.dve(E("tensor_tensor", out=R_(a), in0=a, in1=cosT[:, :, 0:T], op=ALU.mult),
                          reads=[b_rope, btile], writes=[btile])
                    S.dve(E("tensor_tensor", out=sw, in0=sw, in1=sinT[:, :, 0:T], op=ALU.mult),
                          reads=[b_rope, b_SW], writes=[b_SW])
                    S.dve(E("tensor_tensor", out=R_(a), in0=a, in1=sw, op=ALU.add),
                          reads=[b_SW, btile], writes=[btile])

                thx, bthx = tmpb[0], b_tmpb[0]
                sgx, bsgx = tmpb[1], b_tmpb[1]
                sgx2, bsgx2 = tmpb[2], b_tmpb[2]
                gkb, bgkb_ = tmpb[3], b_tmpb[3]
                S.act(E("activation", out=thx[0:64, 0:T], in_=XS[0:64, 0, 0:T], func=AF.Tanh), reads=[b_XS], writes=[bthx])
                S.act(E("copy", out=thx[64:128, 0:T], in_=XS[64:128, 0, 0:T]), reads=[b_XS], writes=[bthx])
                S.act(E("activation", out=sgx[:, 0:T], in_=XS[:, 1, 0:T], func=AF.Sigmoid), reads=[b_XS], writes=[bsgx])
                S.act(E("activation", out=sgx2[0:32, 0:T], in_=XS[0:32, 2, 0:T], func=AF.Sigmoid), reads=[b_XS], writes=[bsgx2])
                S.act(E("copy", out=gkb[32:64, 0:T], in_=XS[32:64, 2, 0:T]), reads=[b_XS], writes=[bgkb_])

                for b in range(8):
                    t_ld, t_lw, t_W, t_IW, t_a, t_kk, t_x, t_y, t_r = tmp
                    bl_ld, bl_lw, bl_W, bl_IW, bl_a, bl_kk, bl_x, bl_y, bl_r = b_tmp
                    if b < 3:
                        c0 = b * 128
                        bk = nbank()
                        S.pe(E("matmul", ps[:, bk, 0:T], lhsT=mat[0:64, c0:c0 + 128], rhs=thx[0:64, 0:T],
                                                              start=True, stop=True), reads=[bmat, bthx], writes=[b_ps[bk]])
                        S.act(E("activation", out=t_ld[:, 0:T], in_=ps[:, bk, 0:T], func=AF.Sigmoid,
                                                                 bias=V_("w0", b), scale=1.0),
                              reads=[b_ps[bk], bvec], writes=[bl_ld])
                        S.dve(E("tensor_scalar", out=t_ld[:, 0:T], in0=t_ld[:, 0:T], scalar1=-math.exp(-0.5), scalar2=None,
                                                        op0=ALU.mult), reads=[bl_ld], writes=[bl_ld])
                        bk = nbank()
                        S.pe(E("matmul", ps[:, bk, 0:T], lhsT=mat[64:128, c0:c0 + 128], rhs=thx[64:128, 0:T],
                                                              start=True, stop=True), reads=[bmat, bthx], writes=[b_ps[bk]])
                        S.act(E("activation", out=t_a[:, 0:T], in_=ps[:, bk, 0:T], func=AF.Sigmoid,
                                                                 bias=V_("a0", b), scale=1.0),
                              reads=[b_ps[bk], bvec], writes=[bl_a])
                        bk = nbank()
                        S.pe(E("matmul", ps[:, bk, 0:T], lhsT=mat[:, 384 + c0:384 + c0 + 128], rhs=sgx[:, 0:T],
                                                              start=True, stop=False), reads=[bmat, bsgx], writes=[b_ps[bk]])
                        S.pe(E("matmul", ps[:, bk, 0:T], lhsT=mat[0:32, 768 + c0:768 + c0 + 128],
                                                              rhs=sgx2[0:32, 0:T], start=False, stop=True),
                             reads=[bmat, bsgx2], writes=[b_ps[bk]])
                        S.act(E("copy", out=Gg[:, b, 0:T], in_=ps[:, bk, 0:T]), reads=[b_ps[bk]], writes=[b_G])
                        S.dve(E("tensor_scalar", out=t_kk[:, 0:T], in0=Kg[:, b, 0:T], scalar1=V_("kkv", b), scalar2=None,
                                                             op0=ALU.mult), reads=[b_K, bvec], writes=[bl_kk])
                        S.act(E("activation", out=R_(t_r[:, 0:T]), in_=t_kk[:, 0:T], func=AF.Square), reads=[bl_kk],
                              writes=[bl_r])
                        bk = nbank()
                        S.pe(E("matmul", ps[:, bk, 0:T], lhsT=R_(bonesr[:]), rhs=R_(t_r[:, 0:T]), start=True, stop=True),
                             reads=[b_bones, bl_r], writes=[b_ps[bk]])
                        S.act(E("activation", out=t_x[:, 0:T], in_=ps[:, bk, 0:T], func=AF.Sqrt), reads=[b_ps[bk]],
                              writes=[bl_x])
                        S.dve(E("tensor_scalar", out=t_x[:, 0:T], in0=t_x[:, 0:T], scalar1=1e-12, scalar2=None, op0=ALU.max),
                              reads=[bl_x], writes=[bl_x])
                        S.dve(E("reciprocal", out=t_x[:, 0:T], in_=t_x[:, 0:T]), reads=[bl_x], writes=[bl_x])
                        S.dve(E("tensor_tensor", out=t_kk[:, 0:T], in0=t_kk[:, 0:T], in1=t_x[:, 0:T], op=ALU.mult),
                              reads=[bl_x, bl_kk], writes=[bl_kk])
                        S.dve(E("tensor_scalar", out=t_x[:, 0:T], in0=t_a[:, 0:T], scalar1=V_("ka", b),
                                                             scalar2=vx[:, b:b + 1], op0=ALU.mult, op1=ALU.add),
                              reads=[bl_a, bvec, bvx], writes=[bl_x])
                        S.dve(E("tensor_tensor", out=R_(Kg[:, b, 0:T]), in0=Kg[:, b, 0:T], in1=t_x[:, 0:T], op=ALU.mult),
                              reads=[bl_x, b_K], writes=[b_K])
                        S.dve(E("scalar_tensor_tensor", out=R_(t_r[:, 0:T]), in0=Rg[:, b, 0:T], scalar=V_("rk", b),
                                                                    in1=Kg[:, b, 0:T], op0=ALU.mult, op1=ALU.mult),
                              reads=[b_R, b_K, bvec], writes=[bl_r])
                        bk = nbank()
                        S.pe(E("matmul", ps[:, bk, 0:T], lhsT=R_(bonesr[:]), rhs=R_(t_r[:, 0:T]), start=True, stop=True),
                             reads=[b_bones, bl_r], writes=[b_ps[bk]])
                        S.dve(E("tensor_tensor", out=bonus[:, b, 0:T], in0=ps[:, bk, 0:T], in1=Vg[:, b, 0:T],
                                                                    op=ALU.mult), reads=[b_ps[bk], b_V], writes=[b_bonus])
                    elif b < 6:
                        c0 = (b - 3) * 128
                        np_ = 64 if b == 5 else 128
                        bk = nbank()
                        S.pe(E("matmul", ps[0:np_, bk, 0:T], lhsT=mat[32:48, 1152 + c0:1152 + c0 + np_],
                                                                       rhs=gkb[32:48, 0:T], start=True, stop=True),
                             reads=[bmat, bgkb_], writes=[b_ps[bk]])
                        S.act(E("activation", out=t_ld[0:np_, 0:T], in_=ps[0:np_, bk, 0:T], func=AF.Exp,
                                                                          bias=vx[0:np_, b:b + 1], scale=-1.0),
                              reads=[b_ps[bk], bvx], writes=[bl_ld])
                        S.act(E("activation", out=t_ld[0:np_, 0:T], in_=t_ld[0:np_, 0:T], func=AF.Ln, bias=1.0,
                                                              scale=1.0), reads=[bl_ld], writes=[bl_ld])
                        S.dve(E("tensor_scalar", out=t_ld[0:np_, 0:T], in0=t_ld[0:np_, 0:T], scalar1=-1.0 / 16.0,
                                                                 scalar2=None, op0=ALU.mult), reads=[bl_ld], writes=[bl_ld])
                        if b == 5:
                            S.dve(E("tensor_scalar", out=t_ld[64:128, 0:T], in0=rmask[64:128, 0:T], scalar1=0.0,
                                                                 scalar2=V_("ldret", b)[64:128], op0=ALU.mult, op1=ALU.add),
                                  reads=[b_rmask, bvec], writes=[bl_ld])
                    else:
                        S.dve(E("tensor_scalar", out=t_ld[:, 0:T], in0=rmask[:, 0:T], scalar1=0.0,
                                                             scalar2=V_("ldret", b), op0=ALU.mult, op1=ALU.add),
                              reads=[b_rmask, bvec], writes=[bl_ld])
                    S.dve(E("tensor_tensor_scan", out=t_lw[:, 0:T], data0=rmask[:, 0:T], data1=t_ld[:, 0:T], initial=0.0,
                                                         op0=ALU.mult, op1=ALU.add), reads=[bl_ld, b_rmask], writes=[bl_lw])
                    S.act(E("activation", out=t_W[:, 0:T], in_=t_lw[:, 0:T], func=AF.Exp), reads=[bl_lw], writes=[bl_W])
                    S.act(E("activation", out=t_IW[:, 0:T], in_=t_lw[:, 0:T], func=AF.Exp, scale=-1.0), reads=[bl_lw],
                          writes=[bl_IW])
                    S.act(E("copy", out=WC[:, b, 0:nch],
                                                              in_=t_W[:, 0:T].rearrange("p (c t) -> p c t", t=C)[:, :, C - 1]),
                          reads=[bl_W], writes=[b_WC])
                    if b < 3:
                        S.dve(E("tensor_tensor", out=t_y[:, 0:T], in0=t_lw[:, 0:T], in1=t_ld[:, 0:T], op=ALU.subtract),
                              reads=[bl_lw, bl_ld], writes=[bl_y])
                        S.act(E("activation", out=t_y[:, 0:T], in_=t_y[:, 0:T], func=AF.Exp), reads=[bl_y], writes=[bl_y])
                        S.dve(E("scalar_tensor_tensor", out=R_(ATt[:, b, 0:T]), in0=t_kk[:, 0:T], scalar=-1.0,
                                                                    in1=t_y[:, 0:T], op0=ALU.mult, op1=ALU.mult),
                              reads=[bl_kk, bl_y], writes=[b_AT])
                        S.dve(E("tensor_tensor", out=t_y[:, 0:T], in0=t_kk[:, 0:T], in1=t_a[:, 0:T], op=ALU.mult),
                              reads=[bl_kk, bl_a], writes=[bl_y])
                        S.dve(E("tensor_tensor", out=R_(BTt[:, b, 0:T]), in0=t_y[:, 0:T], in1=t_IW[:, 0:T], op=ALU.mult),
                              reads=[bl_y, bl_IW], writes=[b_BT])
                    S.dve(E("scalar_tensor_tensor", out=R_(Rg[:, b, 0:T]), in0=Rg[:, b, 0:T], scalar=V_("rscale", b),
                                                                in1=t_W[:, 0:T], op0=ALU.mult, op1=ALU.mult),
                          reads=[b_R, bl_W, bvec], writes=[b_R])
                    S.dve(E("scalar_tensor_tensor", out=R_(Kg[:, b, 0:T]), in0=Kg[:, b, 0:T], scalar=V_("kscale", b),
                                                                in1=t_IW[:, 0:T], op0=ALU.mult, op1=ALU.mult),
                          reads=[b_K, bl_IW, bvec], writes=[b_K])

                stage(3)
                for ci in range(nch):
                    t0 = ci * C
                    cs = slice(t0, t0 + C)
                    cfg.setdefault("_marks", []).append((l, ci, len(S.ops)))
                    if kind == "p":
                        St, bSt = Sst[l], b_Sst[l]
                    else:
                        sp_ = ci % 2
                        St, bSt = Ssm[sp_], b_Ssm[sp_]
                        S.dma(E("dma_start", out=Sld[sp_][:].rearrange("p a b -> p (a b)"),
                                                                         in_=sst_d[l, ci]),
                              writes=[b_Sld[sp_]], chan="sld0")
                        S.act(E("copy", out=R_(St[:]), in_=Sld[sp_][:]), reads=[b_Sld[sp_]], writes=[bSt])
                    for (src, bsrc, dst, bdst, nb_) in ((Vg, b_V, vtok, b_vtok, 8), (Kg, b_K, ktok, b_ktok, 8),
                                                        (BTt, b_BT, btok, b_btok, 3)):
                        for half in range((nb_ + 3) // 4):
                            bk = nbank()
                            n_in = min(4, nb_ - half * 4)
                            for bl in range(n_in):
                                blk = half * 4 + bl
                                S.pe(E("transpose",
                                    ps[0:C, bk, bl * 128:(bl + 1) * 128], src[:, blk, cs], ident),
                                    reads=[bsrc, b_cst], writes=[b_ps[bk]])
                            w = n_in * 128
                            S.act(E("copy", out=R_(dst[0:C, half * 512:half * 512 + w]),
                                                                                   in_=ps[0:C, bk, 0:w]),
                                  reads=[b_ps[bk]], writes=[bdst])
                    stage(31)
                    def hp(h):
                        return slice((h % 2) * 64, (h % 2) * 64 + 64), h // 2

                    def pv3(bk, n):
                        return ps[0:C, bk, 0:n * C].rearrange("p (h c) -> p h c", c=C)

                    def bc(m, n):
                        return m[0:C, 0:C].unsqueeze(1).to_broadcast([C, n, C])

                    def par3(t, par, n):
                        return t[0:C, :, 0:C].rearrange("p (a two) c -> p a two c", two=2)[:, :, par, :]

                    hpb = min(8, 512 // C)
                    def a_kind(lt, blt, rt_, brt, dst, bdst, msk, nh):
                        npar = nh // 2
                        ngrp = (npar + hpb - 1) // hpb
                        banks = {}
                        for g_ in range(ngrp):
                            for par in range(2):
                                banks[(par, g_)] = nbank()
                        for h in range(nh):
                            rows, blk = hp(h)
                            hl = h // 2
                            bk = banks[(h % 2, hl // hpb)]
                            hi = hl % hpb
                            S.pe(E("matmul", ps[0:C, bk, hi * C:(hi + 1) * C], lhsT=R_(lt[rows, blk, cs]),
                                   rhs=R_(rt_[rows, blk, cs]), start=True, stop=True), reads=[blt, brt], writes=[b_ps[bk]])
                        for g_ in range(ngrp):
                            n_ = min(hpb, npar - g_ * hpb)
                            for par in range(2):
                                bk = banks[(par, g_)]
                                o_ = par3(dst, par, nh)[:, g_ * hpb:g_ * hpb + n_, :]
                                if dst is not Pm[0] and dst is not Qm[0]:
                                    o_ = R_(o_)
                                S.dve(E("tensor_tensor", out=o_, in0=pv3(bk, n_),
                                        in1=bc(msk, n_), op=ALU.mult), reads=[b_ps[bk], b_cst], writes=[bdst])

                    a_kind(BTt, b_BT, ATt, b_AT, Pm[0], b_Pm[0], m_su, 6)
                    a_kind(ATt, b_AT, BTt, b_BT, Qm[0], b_Qm[0], m_sl, 6)
                    if False:
                        dbg("AT0", ATt[:, 0, cs], [b_AT]); dbg("BT0", BTt[:, 0, cs], [b_BT])
                        dbg("K0", Kg[:, 0, cs], [b_K]); dbg("R0", Rg[:, 0, cs], [b_R]); dbg("V0", Vg[:, 0, cs], [b_V])
                        dbg("WC", WC[:, :, 0:4], [b_WC])
                        dbg("P0", Pm[0][0:C, :, 0:C], [b_Pm[0]], C); dbg("Q0", Qm[0][0:C, :, 0:C], [b_Qm[0]], C)
                        dbg("aak", aak[0:C, :, 0:C], [b_aak], C); dbg("arb", arb[0:C, :, 0:C], [b_arb], C)
                        dbg("btok", btok[0:C, 0:128], [b_btok], C)
                    stage(32)
                    S.dve(E("tensor_tensor", out=R_(Xm[0:C, :, 0:C]), in0=Pm[0][0:C, :, 0:C], in1=bc(eye64, 6), op=ALU.add),
                          reads=[b_Pm[0], b_cst], writes=[b_Xm])
                    S.act(E("copy", out=Xb[0:C, :, 0:C], in_=Xm[0:C, :, 0:C]), reads=[b_Xm], writes=[b_Xb])
                    ng_ = 1 if 6 * C <= 512 else 2
                    hg_ = 6 // ng_

                    def inv_mm(lhs_t, blhs, rhs_t, brhs):
                        bks = []
                        for g_ in range(ng_):
                            bk = nbank()
                            bks.append(bk)
                            for hi in range(hg_):
                                h = g_ * hg_ + hi
                                S.pe(E("matmul", ps[0:C, bk, hi * C:(hi + 1) * C], lhsT=lhs_t[0:C, h, 0:C],
                                       rhs=rhs_t[0:C, h, 0:C], start=True, stop=True), reads=[blhs, brhs], writes=[b_ps[bk]])
                        return bks

                    def gsl(t, g_):
                        return t[0:C, g_ * hg_:(g_ + 1) * hg_, 0:C]

                    def par2(t, par, n):
                        return t[0:C, 0:n * 64].rearrange("p (a two v) -> p a two v", two=2, v=64)[:, :, par, :]

                    def pv64(bk, n):
                        return ps[0:C, bk, 0:n * 64].rearrange("p (h v) -> p h v", v=64)

                    def f_y_state():
                        bkp = (nbank(), nbank())
                        for h in range(16):
                            rows, blk = hp(h)
                            bk = bkp[h % 2]
                            hl = h // 2
                            S.pe(E("matmul", ps[0:C, bk, hl * 64:(hl + 1) * 64], lhsT=R_(Rg[rows, blk, cs]), rhs=R_(St[rows, blk, :]),
                                   start=True, stop=True), reads=[b_R, bSt], writes=[b_ps[bk]])
                        for par in range(2):
                            S.act(E("copy", out=par2(ysb, par, 16), in_=pv64(bkp[par], 8)), reads=[b_ps[bkp[par]]], writes=[b_ysb])

                    def f_rhs_state():
                        bkp = (nbank(), nbank())
                        for h in range(6):
                            rows, blk = hp(h)
                            bk = bkp[h % 2]
                            hl = h // 2
                            S.pe(E("matmul", ps[0:C, bk, hl * 64:(hl + 1) * 64], lhsT=R_(ATt[rows, blk, cs]), rhs=R_(St[rows, blk, :]),
                                   start=True, stop=True), reads=[b_AT, bSt], writes=[b_ps[bk]])
                        for par in range(2):
                            S.act(E("copy", out=R_(par2(rhs0, par, 6)), in_=pv64(bkp[par], 3)), reads=[b_ps[bkp[par]]], writes=[b_rhs0])

                    def f_rhs_tok():
                        bk = nbank()
                        for h in range(6):
                            S.pe(E("matmul", ps[0:C, bk, h * 64:(h + 1) * 64], lhsT=R_(aak[0:C, h, 0:C]),
                                   rhs=R_(vtok[0:C, h * 64:(h + 1) * 64]), start=True, stop=True),
                                 reads=[b_aak, b_vtok], writes=[b_ps[bk]])
                        S.dve(E("tensor_tensor", out=R_(rhs0[0:C, :]), in0=rhs0[0:C, :], in1=ps[0:C, bk, 0:384], op=ALU.add),
                              reads=[b_ps[bk], b_rhs0], writes=[b_rhs0])

                    def f_y_tok(half):
                        bk = nbank()
                        for hl in range(8):
                            h = half * 8 + hl
                            S.pe(E("matmul", ps[0:C, bk, hl * 64:(hl + 1) * 64], lhsT=R_(ark[0:C, h, 0:C]),
                                   rhs=R_(vtok[0:C, h * 64:(h + 1) * 64]), start=True, stop=(h >= 6)),
                                 reads=[b_ark, b_vtok], writes=[b_ps[bk]])
                            if h < 6:
                                S.pe(E("matmul", ps[0:C, bk, hl * 64:(hl + 1) * 64], lhsT=R_(arb[0:C, h, 0:C]),
                                       rhs=R_(usb[0:C, h * 64:(h + 1) * 64]), start=False, stop=True),
                                     reads=[b_arb, b_usb], writes=[b_ps[bk]])
                        S.dve(E("tensor_tensor", out=ysb[0:C, half * 512:(half + 1) * 512], in0=ysb[0:C, half * 512:(half + 1) * 512],
                                in1=ps[0:C, bk, :], op=ALU.add), reads=[b_ps[bk], b_ysb], writes=[b_ysb])

                    fillers = [lambda: a_kind(Kg, b_K, Rg, b_R, ark, b_ark, m_incl, 16),
                               lambda: a_kind(Kg, b_K, ATt, b_AT, aak, b_aak, m_su, 6),
                               f_y_state,
                               lambda: a_kind(BTt, b_BT, Rg, b_R, arb, b_arb, m_incl, 6),
                               f_rhs_state, f_rhs_tok, lambda: f_y_tok(1)]
                    for st in range(nsteps):
                        bq = inv_mm(Pm[0], b_Pm[0], Qm[0], b_Qm[0])
                        if st < nsteps - 1:
                            bp = inv_mm(Qm[0], b_Qm[0], Pm[0], b_Pm[0])
                        for g_ in range(ng_):
                            S.act(E("copy", out=gsl(Qm[0], g_), in_=pv3(bq[g_], hg_)), reads=[b_ps[bq[g_]]], writes=[b_Qm[0]])
                        if st < nsteps - 1:
                            for g_ in range(ng_):
                                S.dve(E("tensor_copy", out=gsl(Pm[0], g_), in_=pv3(bp[g_], hg_)), reads=[b_ps[bp[g_]]],
                                      writes=[b_Pm[0]])
                        if fillers:
                            fillers.pop(0)()
                        bx = inv_mm(Qm[0], b_Qm[0], Xb, b_Xb)
                        for g_ in range(ng_):
                            S.dve(E("tensor_tensor", out=R_(gsl(Xm, g_)), in0=gsl(Xm, g_), in1=pv3(bx[g_], hg_), op=ALU.add),
                                  reads=[b_ps[bx[g_]], b_Xm], writes=[b_Xm])
                            if st < nsteps - 1:
                                S.act(E("copy", out=gsl(Xb, g_), in_=gsl(Xm, g_)), reads=[b_Xm], writes=[b_Xb])
                    while fillers:
                        fillers.pop(0)()
                    stage(33)
                    bk = nbank()
                    for h in range(6):
                        S.pe(E("matmul", ps[0:C, bk, h * 64:(h + 1) * 64], lhsT=R_(Xm[0:C, h, 0:C]),
                               rhs=R_(rhs0[0:C, h * 64:(h + 1) * 64]), start=True, stop=True),
                             reads=[b_Xm, b_rhs0], writes=[b_ps[bk]])
                    S.act(E("copy", out=R_(usb[0:C, :]), in_=ps[0:C, bk, 0:384]), reads=[b_ps[bk]], writes=[b_usb])
                    stage(34)
                    f_y_tok(0)
                    stage(35)
                    ds_banks = []
                    for half in range(2):
                        bk = nbank()
                        ds_banks.append(bk)
                        for bl in range(4):
                            blk = half * 4 + bl
                            S.pe(E("matmul", ps[:, bk, bl * 128:(bl + 1) * 128],
                                   lhsT=R_(ktok[0:C, blk * 128:(blk + 1) * 128]),
                                   rhs=R_(vtok[0:C, blk * 128:(blk + 1) * 128]),
                                   start=True, stop=(blk >= 3)), reads=[b_ktok, b_vtok], writes=[b_ps[bk]])
                            if blk < 3:
                                S.pe(E("matmul", ps[:, bk, bl * 128:(bl + 1) * 128],
                                       lhsT=R_(btok[0:C, blk * 128:(blk + 1) * 128]),
                                       rhs=R_(usb[0:C, blk * 128:(blk + 1) * 128]),
                                       start=False, stop=True), reads=[b_btok, b_usb], writes=[b_ps[bk]])
                    stage(36)
                    y3 = ysb[0:C, :].rearrange("p (h v) -> p h v", v=64)
                    q3 = ysq[0:C, :].rearrange("p (h v) -> p h v", v=64)
                    s1, s2, mean, tmpv, rstd = (yst[0:C, i, :] for i in range(5))
                    S.dve(E("tensor_tensor", out=R_(ysq[0:C, :]), in0=ysb[0:C, :], in1=ysb[0:C, :], op=ALU.mult), reads=[b_ysb],
                          writes=[b_ysq])
                    S.dve(E("tensor_reduce", out=s1, in_=y3, axis=AX.X, op=ALU.add), reads=[b_ysb], writes=[b_yst])
                    S.dve(E("tensor_reduce", out=s2, in_=q3, axis=AX.X, op=ALU.add), reads=[b_ysq], writes=[b_yst])
                    S.dve(E("tensor_tensor", out=mean, in0=s1, in1=tokc[0:C, 0:16], op=ALU.mult),
                          reads=[b_yst, b_cst], writes=[b_yst])
                    S.dve(E("tensor_tensor", out=tmpv, in0=mean, in1=mean, op=ALU.mult), reads=[b_yst],
                          writes=[b_yst])
                    S.dve(E("scalar_tensor_tensor", out=tmpv, in0=s2, scalar=1.0 / 64, in1=tmpv, op0=ALU.mult,
                                                                             op1=ALU.subtract), reads=[b_yst], writes=[b_yst])
                    S.dve(E("tensor_tensor", out=tmpv, in0=tmpv, in1=tokc[0:C, 16:32], op=ALU.add),
                          reads=[b_yst, b_cst], writes=[b_yst])
                    S.act(E("activation", out=tmpv, in_=tmpv, func=AF.Sqrt), reads=[b_yst], writes=[b_yst])
                    S.dve(E("reciprocal", out=rstd, in_=tmpv), reads=[b_yst], writes=[b_yst])
                    S.dve(E("tensor_tensor", out=y3, in0=y3, in1=mean.unsqueeze(2).to_broadcast([C, 16, 64]),
                                                                      op=ALU.subtract), reads=[b_yst, b_ysb], writes=[b_ysb])
                    S.dve(E("tensor_tensor", out=y3, in0=y3, in1=rstd.unsqueeze(2).to_broadcast([C, 16, 64]),
                                                                      op=ALU.mult), reads=[b_yst, b_ysb], writes=[b_ysb])
                    bpb = min(8, 512 // C)
                    for g_ in range(8 // bpb):
                        bk = nbank()
                        for bi in range(bpb):
                            blk = g_ * bpb + bi
                            S.pe(E("transpose", ps[:, bk, bi * C:(bi + 1) * C], ysb[0:C, blk * 128:(blk + 1) * 128],
                                   cst[0:C, 0:C]), reads=[b_ysb, b_cst], writes=[b_ps[bk]])
                        S.act(E("copy", out=ynT[:, g_ * bpb:(g_ + 1) * bpb, cs],
                                in_=ps[:, bk, 0:bpb * C].rearrange("p (a c) -> p a c", c=C)), reads=[b_ps[bk]], writes=[b_ynT])

                    for half in range(2):
                        bk = ds_banks[half]
                        for hh in range(2):
                            rows = slice(hh * 64, hh * 64 + 64)
                            S.dve(E("tensor_tensor",
                                out=R_(St[rows, half * 4:half * 4 + 4, :]), in0=St[rows, half * 4:half * 4 + 4, :],
                                in1=ps[rows, bk, :].rearrange("p (a b) -> p a b", b=128)[:, :, hh * 64:hh * 64 + 64], op=ALU.add),
                                reads=[b_ps[bk], bSt], writes=[bSt])
                    S.dve(E("tensor_tensor", out=R_(St[:]), in0=St[:], in1=WC[:, :, ci:ci + 1].to_broadcast([128, 8, 64]),
                                                           op=ALU.mult), reads=[b_WC, bSt], writes=[bSt])
                    if kind == "s":
                        S.dma(E("dma_start", out=sto_d[l, 1 + ci], in_=St[:].rearrange("p a b -> p (a b)")),
                              reads=[bSt], chan="stout")
                    elif last_prompt and ci == nch - 1:
                        S.dma(E("dma_start", out=sto_d[l, 0], in_=St[:].rearrange("p a b -> p (a b)")),
                              reads=[bSt], chan="stout")

                stage(4)
                for b in range(8):
                    S.dve(E("tensor_scalar", out=ynT[:, b, 0:T], in0=ynT[:, b, 0:T], scalar1=V_("nw", b), scalar2=V_("nb", b),
                                                         op0=ALU.mult, op1=ALU.add), reads=[b_ynT, bvec], writes=[b_ynT])
                    if b < 3:
                        S.dve(E("tensor_tensor", out=ynT[:, b, 0:T], in0=ynT[:, b, 0:T], in1=bonus[:, b, 0:T], op=ALU.add),
                              reads=[b_ynT, b_bonus], writes=[b_ynT])
                    S.dve(E("tensor_tensor", out=mixT[:, b, 0:T], in0=ynT[:, b, 0:T], in1=Gg[:, b, 0:T], op=ALU.mult),
                          reads=[b_ynT, b_G], writes=[b_mixT])
                    if hpf:
                        S.dve(E("tensor_tensor", out=mix32[:, b, :], in0=ynT[:, b, 0:2], in1=Gg[:, b, 0:2], op=ALU.mult),
                              reads=[b_ynT, b_G], writes=[b_mix32])

                def ln_epilogue(gname, bname):
                    r16, br16 = hT[:, 0:8, :], b_hT
                    S.act(E("copy", out=hT[:, 0:8, 0:T], in_=x32[:, :, 0:T]), reads=[b_x32], writes=[b_hT])
                    S.act(E("activation", out=hT[:, 8:16, 0:T], in_=x32[:, :, 0:T], func=AF.Square), reads=[b_x32], writes=[b_hT])
                    bk1 = nbank(); bk2 = nbank()
                    for kc in range(8):
                        S.pe(E("matmul", ps[:, bk1, 0:T], lhsT=onesb[:], rhs=hT[:, kc, 0:T], start=(kc == 0), stop=(kc == 7)),
                             reads=[b_ones, b_hT], writes=[b_ps[bk1]])
                    for kc in range(8):
                        S.pe(E("matmul", ps[:, bk2, 0:T], lhsT=onesb[:], rhs=hT[:, 8 + kc, 0:T], start=(kc == 0),
                                                       stop=(kc == 7)), reads=[b_ones, b_hT], writes=[b_ps[bk2]])
                    mean, var = tmp[0], tmp[1]
                    S.act(E("mul", out=mean[:, 0:T], in_=ps[:, bk1, 0:T], mul=1.0 / D), reads=[b_ps[bk1]], writes=[b_tmp[0]])
                    S.dve(E("tensor_tensor", out=var[:, 0:T], in0=mean[:, 0:T], in1=mean[:, 0:T], op=ALU.mult), reads=[b_tmp[0]],
                          writes=[b_tmp[1]])
                    S.dve(E("scalar_tensor_tensor", out=var[:, 0:T], in0=ps[:, bk2, 0:T], scalar=1.0 / D, in1=var[:, 0:T],
                                                           op0=ALU.mult, op1=ALU.subtract), reads=[b_ps[bk2], b_tmp[1]], writes=[b_tmp[1]])
                    S.dve(E("tensor_scalar", out=var[:, 0:T], in0=var[:, 0:T], scalar1=1e-5, scalar2=None, op0=ALU.add),
                          reads=[b_tmp[1]], writes=[b_tmp[1]])
                    S.act(E("activation", out=var[:, 0:T], in_=var[:, 0:T], func=AF.Sqrt), reads=[b_tmp[1]], writes=[b_tmp[1]])
                    S.dve(E("reciprocal", out=var[:, 0:T], in_=var[:, 0:T]), reads=[b_tmp[1]], writes=[b_tmp[1]])
                    S.dve(E("tensor_tensor", out=x32[:, :, 0:T], in0=x32[:, :, 0:T],
                                                    in1=mean[:, 0:T].unsqueeze(1).to_broadcast([128, 8, T]), op=ALU.subtract),
                          reads=[b_tmp[0], b_x32], writes=[b_x32])
                    S.dve(E("tensor_tensor", out=x32[:, :, 0:T], in0=x32[:, :, 0:T],
                                                    in1=var[:, 0:T].unsqueeze(1).to_broadcast([128, 8, T]), op=ALU.mult),
                          reads=[b_tmp[1], b_x32], writes=[b_x32])
                    for b in range(8):
                        S.dve(E("tensor_scalar", out=x32[:, b, 0:T], in0=x32[:, b, 0:T], scalar1=V_(gname, b),
                                                             scalar2=V_(bname, b), op0=ALU.mult, op1=ALU.add),
                              reads=[b_x32, bvec], writes=[b_x32])
                    S.act(E("copy", out=xTb[:, :, 0:T], in_=x32[:, :, 0:T]), reads=[b_x32], writes=[b_xTb])

                stage(5)
                for piece in range(4):
                    slot, bslot = ws_next(l, NPIECE_IN + piece)
                    for sub in range(2):
                        cb = piece * 2 + sub
                        bk = nbank()
                        for kc in range(8):
                            S.pe(E("matmul",
                                ps[:, bk, 0:T], lhsT=slot[:, kc * 256 + sub * 128: kc * 256 + sub * 128 + 128],
                                rhs=mixT[:, kc, 0:T], start=(kc == 0), stop=(kc == 7)), reads=[bslot, b_mixT], writes=[b_ps[bk]])
                        if hpf:
                            bkh = hp_mm8(l, NPIECE_IN + piece, sub, lambda kc: mix32[:, kc, 0:2], b_mix32)
                            S.dve(E("scalar_tensor_tensor", out=x32[:, cb, 0:2], in0=x32[:, cb, 0:2], scalar=ALPHA,
                                    in1=ps[:, bkh, 0:2], op0=ALU.mult, op1=ALU.add), reads=[b_ps[bkh], b_x32], writes=[b_x32])
                        S.dve(E("scalar_tensor_tensor", out=x32[:, cb, clo:T], in0=x32[:, cb, clo:T], scalar=ALPHA,
                                                                             in1=ps[:, bk, clo:T], op0=ALU.mult, op1=ALU.add),
                              reads=[b_ps[bk], b_x32], writes=[b_x32])
                ln_epilogue("ln1g", "ln1b")

                stage(6)
                if kind == "s":
                    S.dma(E("dma_start", out=cvst[:].rearrange("p a b c -> p (a b c)"), in_=scv_d[l]), writes=[b_cvst],
                          chan="cvst")
                ffn_defer = [None]
                for j in range(NJ):
                    slot, bslot = ws_next(l, NPIECE_IN + 4 + j)
                    cv = []
                    for ab in range(2):
                        cbk = j + NJ * ab
                        bk = nbank()
                        for kc in range(8):
                            S.pe(E("matmul",
                                ps[:, bk, 0:T], lhsT=slot[:, kc * 256 + ab * 128: kc * 256 + ab * 128 + 128],
                                rhs=xTb[:, kc, 0:T], start=(kc == 0), stop=(kc == 7)), reads=[bslot, b_xTb], writes=[b_ps[bk]])
                        ub, bub = ubuf[ab + 2 * (j % 2)], b_ubuf[ab + 2 * (j % 2)]
                        buh = b_uh[ab + 2 * (j % 2)]
                        u3 = ub[:, 0:nseq * (L + 2)].rearrange("p (s t) -> p s t", t=L + 2)
                        if kind == "s":
                            S.act(E("copy", out=u3[:, :, 0:2], in_=cvst[:, cbk, :, :]), reads=[b_cvst],
                                  writes=[buh])
                        else:
                            S.act(E("copy", out=u3[:, 0, 0:2], in_=cvc[:, l, cbk, :]),
                                  reads=[b_cvc[l][cbk]], writes=[buh])
                        if hpf:
                            bkh = hp_mm8(l, NPIECE_IN + 4 + j, ab, lambda kc: x32[:, kc, 0:2], b_x32)
                            S.act(E("copy", out=u3[:, 0, 2:4], in_=ps[:, bkh, 0:2]), reads=[b_ps[bkh]], writes=[bub])
                            S.act(E("copy", out=u3[:, 0, 4:L + 2], in_=ps[:, bk, 2:T]), reads=[b_ps[bk]], writes=[bub])
                        else:
                            S.act(E("copy", out=u3[:, :, 2:L + 2], in_=t3(ps[:, bk, 0:T])), reads=[b_ps[bk]], writes=[bub])
                        if kind == "s":
                            S.act(E("copy", out=cvo[:, cbk, :, :], in_=t3(ps[:, bk, 0:T])[:, :, L - 2:L]), reads=[b_ps[bk]],
                                  writes=[b_cvo])
                        else:
                            S.act(E("copy", out=cvc[:, l, cbk, :], in_=ps[:, bk, T - 2:T]), reads=[b_ps[bk]],
                                  writes=[b_cvc[l][cbk]])
                        ct, bct = tmp[2 + ab + 2 * (j % 2)], b_tmp[2 + ab + 2 * (j % 2)]
                        c3 = t3(ct[:, 0:T])
                        S.pool(E("tensor_scalar", out=c3, in0=u3[:, :, 0:L], scalar1=V_("cw0", cbk),
                                                                               scalar2=V_("cb", cbk), op0=ALU.mult, op1=ALU.add),
                              reads=[bub, buh, bvec], writes=[bct])
                        S.dve(E("scalar_tensor_tensor", out=c3, in0=u3[:, :, 1:L + 1], scalar=V_("cw1", cbk),
                                                                                      in1=c3, op0=ALU.mult, op1=ALU.add),
                              reads=[bub, buh, bvec, bct], writes=[bct])
                        S.dve(E("scalar_tensor_tensor", out=c3, in0=u3[:, :, 2:L + 2], scalar=V_("cw2", cbk),
                                                                                      in1=c3, op0=ALU.mult, op1=ALU.add),
                              reads=[bub, buh, bvec, bct], writes=[bct])
                        cv.append((ct, bct))
                    (ca, bca), (cb_, bcb) = cv

                    def gelu_h(j=j, ca=ca, bca=bca, cb_=cb_, bcb=bcb):
                        S.act(E("activation", out=ca[:, 0:T], in_=ca[:, 0:T], func=AF.Gelu_apprx_tanh), reads=[bca], writes=[bca])
                        S.dve(E("tensor_tensor", out=hT[:, j, 0:T], in0=ca[:, 0:T], in1=cb_[:, 0:T], op=ALU.mult),
                              reads=[bca, bcb], writes=[b_hT])
                        if hpf:
                            S.dve(E("tensor_tensor", out=h32[:, j, :], in0=ca[:, 0:2], in1=cb_[:, 0:2], op=ALU.mult),
                                  reads=[bca, bcb], writes=[b_h32])

                    if ffn_defer[0] is not None:
                        ffn_defer[0]()
                    ffn_defer[0] = gelu_h
                ffn_defer[0]()
                ffn_defer[0] = None
                if kind == "s":
                    S.dma(E("dma_start", out=cvo_d[l][:, 88:88 * (1 + NSS)], in_=cvo[:].rearrange("p a b c -> p (a b c)")),
                          reads=[b_cvo], chan="cvout")
                elif last_prompt:
                    S.dma(E("dma_start", out=cvo_d[l][:, 0:88], in_=cvc[:, l, :, :].rearrange("p a c -> p (a c)")),
                          reads=b_cvc[l], chan="cvout")
                for cb in range(8):
                    slot, bslot = ws_next(l, NPIECE_IN + 4 + NJ + cb)
                    bk = nbank()
                    for j in range(NJ):
                        S.pe(E("matmul", ps[:, bk, 0:T], lhsT=slot[:, j * 128:(j + 1) * 128], rhs=hT[:, j, 0:T],
                                                                       start=(j == 0), stop=(j == NJ - 1)),
                             reads=[bslot, b_hT], writes=[b_ps[bk]])
                    if hpf:
                        bkh = nbank()
                        for jh in range(3):
                            njj = min(8, NJ - jh * 8)
                            w, bw = hp_load(l, NPIECE_IN + 4 + NJ + cb, ("jh", jh, njj))
                            for jj in range(njj):
                                j = jh * 8 + jj
                                S.pe(E("matmul", ps[:, bkh, 0:2], lhsT=w[:, jj * 128:(jj + 1) * 128], rhs=h32[:, j, 0:2],
                                       start=(j == 0), stop=(j == NJ - 1)), reads=[bw, b_h32], writes=[b_ps[bkh]])
                        S.dve(E("scalar_tensor_tensor", out=x32[:, cb, 0:2], in0=x32[:, cb, 0:2], scalar=ALPHA,
                                in1=ps[:, bkh, 0:2], op0=ALU.mult, op1=ALU.add), reads=[b_ps[bkh], b_x32], writes=[b_x32])
                    S.dve(E("scalar_tensor_tensor", out=x32[:, cb, clo:T], in0=x32[:, cb, clo:T], scalar=ALPHA,
                                                                         in1=ps[:, bk, clo:T], op0=ALU.mult, op1=ALU.add),
                          reads=[b_ps[bk], b_x32], writes=[b_x32])
                stage(7)
                ln_epilogue("ln2g", "ln2b")

            S.dma(E("dma_start", out=yT_d[:, :, tok0:tok0 + T], in_=x32[:, :, 0:T]), reads=[b_x32], chan="yout")


    try:
        _tile_loop()
        assert ws["used"] == len(plan)
    except _Stop:
        pass
    S.run_block()
    return nc, S


_CACHE = {}


def _prep_inputs(inp, cfg):
    f = np.float32
    W = _prep_weights(inp)
    vm = [_prep_layer_params(inp, l) for l in range(NL)]
    vecs = np.stack([v for v, _ in vm])
    mats = np.stack([m for _, m in vm])
    cst = _consts()
    cosT, sinT = _rope_tables()
    in_maps = []
    for c in range(8):
        xp = inp["x_prompt"][c]
        xs = inp["x_sample"][c * NSS:(c + 1) * NSS].reshape(NSS * LS, D)
        x = np.concatenate([xp, xs], axis=0)
        xT = np.ascontiguousarray(x.T.reshape(8, 128, -1).transpose(1, 0, 2))
        sst = np.zeros((NL, NSS, 16, 64, 64), f)
        sl = slice(c * NSS, (c + 1) * NSS)
        sst[:, :, 0:6] = inp["state_rwkv"][:, sl].transpose(0, 1, 2, 4, 3)
        sst[:, :, 6:11, 0:32] = inp["state_gla"][:, sl]
        sst[:, :, 11:16] = inp["state_ret"][:, sl]
        sst = sst.reshape(NL, NSS, 8, 2, 64, 64).transpose(0, 1, 3, 4, 2, 5).reshape(NL, NSS, 128, 512)
        sh = inp["state_shift"][:, sl]
        shp = np.zeros((NL, NSS, 12 * 128), f)
        shp[:, :, 0:1152] = sh[:, :, 0:1152]
        shp[:, :, 1152:1408] = sh[:, :, 1152:1408]
        shp[:, :, 1408:1440] = sh[:, :, 1408:1440]
        ssh = shp.reshape(NL, NSS, 12, 128).transpose(0, 3, 2, 1).reshape(NL, 128, 12 * NSS)
        cvs = inp["state_conv"][:, sl]
        scv = cvs.reshape(NL, NSS, 2, 44, 128).transpose(0, 4, 3, 1, 2).reshape(NL, 128, 44 * NSS * 2)
        in_maps.append({"xT": xT, "wts": W, "vecs": vecs, "mats": mats, "cst": cst, "ropec": cosT, "ropes": sinT,
                        "sst": np.ascontiguousarray(sst), "ssh": np.ascontiguousarray(ssh), "scv": np.ascontiguousarray(scv)})
    return in_maps


def _assemble(results):
    f = np.float32
    y_p = np.zeros((8, SEQ, D), f); y_s = np.zeros((8 * NSS, LS, D), f)
    rw_p = np.zeros((NL, 8, 6, 64, 64), f); rw_s = np.zeros((NL, 8 * NSS, 6, 64, 64), f)
    sh_p = np.zeros((NL, 8, 1440), f); sh_s = np.zeros((NL, 8 * NSS, 1440), f)
    gl_p = np.zeros((NL, 8, 5, 32, 64), f); gl_s = np.zeros((NL, 8 * NSS, 5, 32, 64), f)
    rt_p = np.zeros((NL, 8, 5, 64, 64), f); rt_s = np.zeros((NL, 8 * NSS, 5, 64, 64), f)
    cv_p = np.zeros((NL, 8, 2, 2 * DFF), f); cv_s = np.zeros((NL, 8 * NSS, 2, 2 * DFF), f)
    for c, r in enumerate(results):
        yT = r["yT"]
        y = yT.transpose(1, 0, 2).reshape(D, -1).T
        y_p[c] = y[0:SEQ]
        y_s[c * NSS:(c + 1) * NSS] = y[SEQ:].reshape(NSS, LS, D)
        st = r["sto"].reshape(NL, 1 + NSS, 2, 64, 8, 64).transpose(0, 1, 4, 2, 3, 5).reshape(NL, 1 + NSS, 16, 64, 64)
        rw = st[:, :, 0:6].transpose(0, 1, 2, 4, 3)
        gl = st[:, :, 6:11, 0:32]
        rt = st[:, :, 11:16]
        rw_p[:, c] = rw[:, 0]; gl_p[:, c] = gl[:, 0]; rt_p[:, c] = rt[:, 0]
        sl = slice(c * NSS, (c + 1) * NSS)
        rw_s[:, sl] = rw[:, 1:]; gl_s[:, sl] = gl[:, 1:]; rt_s[:, sl] = rt[:, 1:]
        sho = r["sho"].reshape(NL, 128, 12 * (1 + NSS))
        p = sho[:, :, 0:12].transpose(0, 2, 1).reshape(NL, 12 * 128)
        s = sho[:, :, 12:].reshape(NL, 128, 12, NSS).transpose(0, 3, 2, 1).reshape(NL, NSS, 12 * 128)
        sh_p[:, c] = p[:, 0:1440]
        sh_s[:, sl] = s[:, :, 0:1440]
        cvo = r["cvo"].reshape(NL, 128, 44 * (1 + NSS) * 2)
        p = cvo[:, :, 0:88].reshape(NL, 128, 44, 2).transpose(0, 3, 2, 1).reshape(NL, 2, 2 * DFF)
        s = cvo[:, :, 88:].reshape(NL, 128, 44, NSS, 2).transpose(0, 3, 4, 2, 1).reshape(NL, NSS, 2, 2 * DFF)
        cv_p[:, c] = p
        cv_s[:, sl] = s
    return (y_p, y_s, rw_p, sh_p, gl_p, rt_p, cv_p, rw_s, sh_s, gl_s, rt_s, cv_s)


def kernel(**inputs):
    inp = {k: np.asarray(v) for k, v in inputs.items()}
    cfg = CFG_FULL
    if "nc" not in _CACHE:
        _CACHE["nc"] = build(cfg)[0]
    nc = _CACHE["nc"]
    in_maps = _prep_inputs(inp, cfg)
    res = run_bass_kernel_spmd(nc, in_maps, core_ids=list(range(8)))
    return _assemble(res.results)
```

```python
import bisect
import math
from contextlib import ExitStack

import numpy as np
import concourse.bass as bass
import concourse.mybir as mybir
from concourse.bass_utils import run_bass_kernel_spmd

F32 = mybir.dt.float32
F32R = mybir.dt.float32r
BF16 = mybir.dt.bfloat16
AF = mybir.ActivationFunctionType
ALU = mybir.AluOpType
AX = mybir.AxisListType

NL = 4
D = 1024
SEQ = 2048
NSS = 16
LS = 4
TT = 256
PAST = 16384
DFF = 2816
NJ = 22
ALPHA = (2 * NL) ** 0.25
NBLK = 38
NPIECE_IN = 19
SLOT = 2816
NSLOT = 4
RW0, GL0, RT0 = 0, 1440, 2416


class _Stop(Exception):
    pass


class Buf:
    __slots__ = ("name", "last_write", "reads")

    def __init__(self, name):
        self.name = name
        self.last_write = None
        self.reads = []


class Op:
    __slots__ = ("eng", "fn", "deps", "idx", "needs_inc", "cnt", "chan", "dma_cnt")

    def __init__(self, eng, fn, deps, idx, chan=None):
        self.eng = eng
        self.fn = fn
        self.deps = deps
        self.idx = idx
        self.needs_inc = False
        self.cnt = None
        self.chan = chan
        self.dma_cnt = None


COMPUTE = ("pe", "act", "dve", "pool")
DMAQ = ("sp", "actq", "poolq")
STREAM = {"pe": "pe", "act": "act", "dve": "dve", "pool": "pool", "sp": "sp", "actq": "act", "poolq": "pool"}
SAME_SYNC = {"pe": False, "act": True, "dve": True, "pool": True}


class Sched:
    def __init__(self, nc):
        self.nc = nc
        self.ops = []

    def op(self, eng, fn, reads=(), writes=(), chan=None):
        idx = len(self.ops)
        deps = {}
        for b in reads:
            if b.last_write is not None:
                deps[b.last_write] = True
        for b in writes:
            if b.last_write is not None:
                deps.setdefault(b.last_write, False)
            for r_ in b.reads:
                deps.setdefault(r_, False)
        o = Op(eng, fn, deps, idx, chan)
        self.ops.append(o)
        for b in reads:
            b.reads.append(idx)
        for b in writes:
            b.last_write = idx
            b.reads = []
        return o

    def pe(self, fn, reads=(), writes=()):
        return self.op("pe", fn, reads, writes)

    def act(self, fn, reads=(), writes=()):
        return self.op("act", fn, reads, writes)

    def dve(self, fn, reads=(), writes=()):
        return self.op("dve", fn, reads, writes)

    def pool(self, fn, reads=(), writes=()):
        return self.op("pool", fn, reads, writes)

    def dma(self, fn, reads=(), writes=(), chan=None, q="sp"):
        return self.op(q, fn, reads, writes, chan=chan)

    def _skip(self, p, so, raw=True):
        sp = STREAM[p.eng]
        return sp == so and p.eng in COMPUTE and (not SAME_SYNC[p.eng] or (not raw and p.eng != "pool"))

    def run_block(self):
        nc = self.nc
        ops = self.ops
        for o in ops:
            so = STREAM[o.eng]
            last = {}
            for d, raw in o.deps.items():
                p = ops[d]
                if p.eng in DMAQ or self._skip(p, so, raw):
                    continue
                if p.eng not in last or d > last[p.eng]:
                    last[p.eng] = d
            o_last = set(last.values())
            for d in o_last:
                ops[d].needs_inc = True
            o.deps = {d: r for d, r in o.deps.items() if ops[d].eng in DMAQ or d in o_last}
        cnt = {e: 0 for e in COMPUTE}
        chan_cnt = {}
        chan_hist = {}
        for o in ops:
            if o.eng in DMAQ:
                c = chan_cnt.get(o.chan, 0) + 16
                chan_cnt[o.chan] = c
                o.dma_cnt = c
                chan_hist.setdefault(o.chan, []).append((o.idx, c))
            elif o.needs_inc:
                cnt[o.eng] += 1
                o.cnt = cnt[o.eng]
        chan_idx = {c: [i for i, _ in h] for c, h in chan_hist.items()}
        streams = {s: [] for s in ("pe", "act", "dve", "pool", "sp")}
        waited = {s: {} for s in streams}
        for o in ops:
            so = STREAM[o.eng]
            need = {}
            for d, raw in o.deps.items():
                p = ops[d]
                if p.eng in DMAQ:
                    h = chan_hist[p.chan]
                    k = bisect.bisect_left(chan_idx[p.chan], o.idx) - 1
                    key = ("chan", p.chan)
                    need[key] = max(need.get(key, 0), h[k][1])
                else:
                    if self._skip(p, so, raw):
                        continue
                    key = ("eng", p.eng)
                    need[key] = max(need.get(key, 0), p.cnt)
            w = []
            for key, c in need.items():
                if waited[so].get(key, 0) >= c:
                    continue
                waited[so][key] = c
                w.append((key, c))
            streams[so].append((o, w))
        with ExitStack() as es:
            sems = {}
            for e in COMPUTE:
                sems[("eng", e)] = es.enter_context(nc.semaphore("s_" + e))
            for c in chan_cnt:
                sems[("chan", c)] = es.enter_context(nc.semaphore("c_" + str(c)))
            block = es.enter_context(nc.Block())

            def mk(sname):
                def body(eng):
                    for o, w in streams[sname]:
                        for key, c in w:
                            eng.wait_ge(sems[key], c)
                        ins = o.fn(eng)
                        if o.eng in DMAQ:
                            ins.then_inc(sems[("chan", o.chan)], 16)
                        elif o.needs_inc:
                            ins.then_inc(sems[("eng", o.eng)], 1)
                    if sname == "sp":
                        for c, v in chan_cnt.items():
                            eng.wait_ge(sems[("chan", c)], v)
                        for e in COMPUTE:
                            if cnt[e] > 0:
                                eng.wait_ge(sems[("eng", e)], cnt[e])
                return body

            block.tensor(mk("pe"))
            block.scalar(mk("act"))
            block.vector(mk("dve"))
            block.gpsimd(mk("pool"))
            block.sync(mk("sp"))


def _colmap():
    cm = -np.ones(NBLK * 128, np.int64)
    for g, (rw_off, gl_off, rt_off, pad) in enumerate(((0, 0, 0, True), (384, 160, 320, True), (768, 320, 640, False))):
        base = g * 1024
        cm[base:base + 384] = RW0 + rw_off + np.arange(384)
        for j in range(5):
            if pad:
                cm[base + 384 + j * 64: base + 384 + j * 64 + 32] = GL0 + gl_off + j * 32 + np.arange(32)
            else:
                cm[base + 384 + j * 64: base + 384 + j * 64 + 64] = GL0 + gl_off + j * 64 + np.arange(64)
        cm[base + 704: base + 1024] = RT0 + rt_off + np.arange(320)
    b = 24 * 128
    cm[b:b + 320] = GL0 + 640 + np.arange(320)
    cm[b + 320:b + 640] = RT0 + 960 + np.arange(320)
    b = 29 * 128
    cm[b:b + 128] = RW0 + 1152 + np.arange(128)
    cm[b + 128:b + 256] = RW0 + 1280 + np.arange(128)
    cm[b + 256:b + 288] = RW0 + 1408 + np.arange(32)
    cm[b + 288:b + 304] = GL0 + 960 + np.arange(16)
    for gi, off in enumerate((0, 320)):
        b = (32 + 3 * gi) * 128
        for r in range(704, 1024):
            i = r - 704
            h, d = divmod(i, 64)
            cm[b + (r - 640)] = RT0 + off + h * 64 + ((d + 32) % 64)
    return cm


def _rows_layout(v, nblk):
    return np.ascontiguousarray(v.reshape(nblk, 128).T)


VEC = {}


def _vec_layout():
    off = 0
    for name, n in (("mu", 12), ("w0", 3), ("a0", 3), ("kkv", 3), ("ka", 3), ("rk", 3), ("nw", 8), ("nb", 8),
                    ("bgkb", 3), ("ln1g", 8), ("ln1b", 8), ("ln2g", 8), ("ln2b", 8), ("cw0", 44), ("cw1", 44),
                    ("cw2", 44), ("cb", 44), ("rscale", 8), ("kscale", 8), ("ldret", 8)):
        VEC[name] = (off, n)
        off += n
    return off


NV = _vec_layout()
NM = 4 * 384


def _prep_layer_params(inp, l):
    f = np.float32
    vec = np.zeros((128, NV), f)

    def put(name, arr):
        o, n = VEC[name]
        assert arr.shape == (128, n), (name, arr.shape)
        vec[:, o:o + n] = arr

    mu = inp["rwkv_mu"][l]
    mup = np.zeros(12 * 128, f)
    mup[0:384] = mu[0:384]
    mup[384:768] = mu[384:768]
    mup[768:1152] = mu[768:1152]
    mup[1152:1152 + 128] = mu[1152:1280]
    mup[1280:1408] = mu[1280:1408]
    mup[1408:1440] = mu[1408:1440]
    put("mu", _rows_layout(mup, 12))
    put("w0", _rows_layout(inp["rwkv_w0"][l], 3))
    put("a0", _rows_layout(inp["rwkv_a0"][l], 3))
    put("kkv", _rows_layout(inp["rwkv_kk"][l], 3))
    put("ka", _rows_layout(inp["rwkv_ka"][l], 3))
    put("rk", _rows_layout(inp["rwkv_rk"][l].reshape(-1), 3))
    nw = np.ones(1024, f)
    nb = np.zeros(1024, f)
    nw[0:384] = inp["rwkv_lnw"][l]
    nb[0:384] = inp["rwkv_lnb"][l]
    nw[384:704] = np.tile(inp["gla_norm_w"][l], 5)
    put("nw", _rows_layout(nw, 8))
    put("nb", _rows_layout(nb, 8))
    bb = np.zeros(384, f)
    for j in range(5):
        bb[j * 64:j * 64 + 32] = inp["gla_bgk_b"][l][j * 32:(j + 1) * 32]
    put("bgkb", _rows_layout(bb, 3))
    put("ln1g", _rows_layout(inp["ln1_g"][l], 8))
    put("ln1b", _rows_layout(inp["ln1_b"][l], 8))
    put("ln2g", _rows_layout(inp["ln2_g"][l], 8))
    put("ln2b", _rows_layout(inp["ln2_b"][l], 8))
    cw = inp["ffn_conv_w"][l]
    put("cw0", _rows_layout(cw[0], 44))
    put("cw1", _rows_layout(cw[1], 44))
    put("cw2", _rows_layout(cw[2], 44))
    put("cb", _rows_layout(inp["ffn_conv_b"][l], 44))
    rs = np.ones(1024, f)
    rs[384:704] = 32.0 ** -0.5
    ks = np.ones(1024, f)
    ks[704:1024] = 64.0 ** -0.5
    put("rscale", _rows_layout(rs, 8))
    put("kscale", _rows_layout(ks, 8))
    ldr = np.zeros(1024, np.float64)
    for h in range(5):
        ldr[704 + h * 64:704 + (h + 1) * 64] = math.log(1.0 - 2.0 ** (-5.0 - h))
    put("ldret", _rows_layout(ldr.astype(f), 8))
    mats = np.zeros((128, NM), f)
    mats[0:64, 0:384] = inp["rwkv_bw"][l]
    mats[64:128, 0:384] = inp["rwkv_ba"][l]
    mats[:, 384:768] = inp["rwkv_bg"][l][0:128]
    mats[0:32, 768:1152] = inp["rwkv_bg"][l][128:160]
    bgk = inp["gla_bgk"][l]
    for j in range(5):
        mats[32:48, 1152 + j * 64:1152 + j * 64 + 32] = bgk[:, j * 32:(j + 1) * 32]
    return vec, mats


def _prep_weights(inp):
    cm = _colmap()
    npieces = NPIECE_IN + 4 + NJ + 8
    W = np.zeros((NL, npieces, 128, SLOT), np.float32)
    for l in range(NL):
        win = inp["w_in"][l]
        wp = np.zeros((1024, NBLK * 128), np.float32)
        ok = cm >= 0
        wp[:, ok] = win[:, cm[ok]]
        p = 0
        for i in range(NPIECE_IN):
            blkc = wp[:, i * 256:(i + 1) * 256].reshape(8, 128, 256).transpose(1, 0, 2)
            W[l, p, :, :2048] = blkc.reshape(128, 2048)
            p += 1
        wo = inp["w_out"][l]
        for i in range(4):
            blkc = wo[:, i * 256:(i + 1) * 256].reshape(8, 128, 256).transpose(1, 0, 2)
            W[l, p, :, :2048] = blkc.reshape(128, 2048)
            p += 1
        up = inp["ffn_up"][l]
        for j in range(NJ):
            ab = np.concatenate([up[:, j * 128:(j + 1) * 128], up[:, DFF + j * 128:DFF + (j + 1) * 128]], axis=1)
            blkc = ab.reshape(8, 128, 256).transpose(1, 0, 2)
            W[l, p, :, :2048] = blkc.reshape(128, 2048)
            p += 1
        dn = inp["ffn_down"][l]
        for cb in range(8):
            blkc = dn[:, cb * 128:(cb + 1) * 128].reshape(NJ, 128, 128).transpose(1, 0, 2)
            W[l, p, :, :NJ * 128] = blkc.reshape(128, NJ * 128)
            p += 1
        assert p == npieces
    return W


def _consts():
    ident = np.eye(128, dtype=np.float32)
    sI = np.arange(128)[:, None]
    tI = np.arange(128)[None, :]
    bones = np.zeros((128, 128), np.float32)
    bones[0:64, 0:64] = 1
    bones[64:128, 64:128] = 1
    cst = np.zeros((128, 672), np.float32)
    cst[:, 0:128] = ident
    cst[:, 128:256] = bones
    cst[:, 256:384] = (tI >= sI)
    cst[:, 384:512] = (tI > sI)
    cst[:, 512:640] = (tI < sI)
    tok = np.zeros((128, 32), np.float32)
    tok[:, 0:6] = 1.0 / 64
    tok[:, 16:22] = 64e-5
    tok[:, 22:32] = 1e-6
    cst[:, 640:672] = tok
    return cst


def _rope_tables():
    ntok = SEQ + NSS * LS
    pos = np.concatenate([np.arange(SEQ), np.tile(PAST + np.arange(LS), NSS)]).astype(np.float32)
    inv = (1.0 / (10000.0 ** np.linspace(0.0, 1.0, 32, dtype=np.float32))).astype(np.float32)
    ang = pos[None, :] * inv[:, None]
    cos = np.cos(ang).astype(np.float32)
    sin = np.sin(ang).astype(np.float32)
    C = np.ones((384, ntok), np.float32)
    S = np.zeros((384, ntok), np.float32)
    for h in range(5):
        r0 = 64 + h * 64
        C[r0:r0 + 32] = cos
        C[r0 + 32:r0 + 64] = cos
        S[r0:r0 + 32] = -sin
        S[r0 + 32:r0 + 64] = sin
    C = C.reshape(3, 128, ntok).transpose(1, 0, 2)
    S = S.reshape(3, 128, ntok).transpose(1, 0, 2)
    return np.ascontiguousarray(C), np.ascontiguousarray(S)


CFG_FULL = dict(tiles=[("p", i) for i in range(SEQ // TT)] + [("s", 0)], nl=NL)


def build(cfg):
    nc = bass.Bass("TRN2", target_bir_lowering=False)
    S = Sched(nc)
    NTOK = SEQ + NSS * LS
    npieces = NPIECE_IN + 4 + NJ + 8
    nl = cfg["nl"]
    tiles = cfg["tiles"]

    def din(name, shape, dt=F32):
        return nc.dram_tensor(name, list(shape), dt, kind="ExternalInput").ap()

    def dout(name, shape):
        return nc.dram_tensor(name, list(shape), F32, kind="ExternalOutput").ap()

    xT_d = din("xT", [128, 8, NTOK])
    wts_d = din("wts", [NL, npieces, 128, SLOT])
    vec_d = din("vecs", [NL, 128, NV])
    mat_d = din("mats", [NL, 128, NM])
    cst_d = din("cst", [128, 672])
    cos_d = din("ropec", [128, 3, NTOK])
    sin_d = din("ropes", [128, 3, NTOK])
    sst_d = din("sst", [NL, NSS, 128, 512])
    ssh_d = din("ssh", [NL, 128, 12 * NSS])
    scv_d = din("scv", [NL, 128, 44 * NSS * 2])
    yT_d = dout("yT", [128, 8, NTOK])
    sto_d = dout("sto", [NL, 1 + NSS, 128, 512])
    sho_d = dout("sho", [NL, 128, 12 * (1 + NSS)])
    cvo_d = dout("cvo", [NL, 128, 44 * (1 + NSS) * 2])
    dbg_d = dout("dbg", [128, 8192]) if cfg.get("dbg") else None
    dbg_state = {"col": 0, "items": []}

    def dbg(name, ap, bufs, np_=128):
        if dbg_d is None:
            return
        n = 1
        for d_ in ap.shape[1:]:
            n *= d_
        c0 = dbg_state["col"]
        if c0 + n > 8192:
            return
        dbg_state["col"] = c0 + n
        dbg_state["items"].append((name, c0, n, np_, tuple(ap.shape)))
        flat = dbg_d[0:np_, c0:c0 + n]
        if len(ap.shape) == 3:
            flat = flat.rearrange("p (a b) -> p a b", b=ap.shape[2])
        S.dma(E("dma_start", out=flat, in_=ap), reads=bufs, chan="dbg")
    cfg["_dbg_items"] = dbg_state["items"]

    def sb(name, shape, dt=F32):
        return nc.alloc_sbuf_tensor("sb_" + name, list(shape), dt)

    TM = TT
    x32 = sb("x32", [128, 8, TM]); b_x32 = Buf("x32")
    xTb = sb("xTb", [128, 8, TM], BF16); b_xTb = Buf("xTb")
    un1 = sb("un1", [128, 12 * (TM + NSS)]); b_PRW = Buf("PRW")
    PRW = un1[:].rearrange("p (a b) -> p a b", b=TM + NSS)
    Rg = sb("Rg", [128, 8, TM]); b_R = Buf("R")
    Kg = sb("Kg", [128, 8, TM]); b_K = Buf("K")
    Vg = sb("Vg", [128, 8, TM]); b_V = Buf("V")
    Gg = sb("Gg", [128, 8, TM], BF16); b_G = Buf("G")

    XS = sb("XS", [128, 3, TM]); b_XS = Buf("XS")
    ATt = sb("ATt", [128, 3, TM]); b_AT = Buf("AT")
    BTt = sb("BTt", [128, 3, TM]); b_BT = Buf("BT")
    bonus = sb("bonus", [128, 3, TM]); b_bonus = Buf("bonus")
    ynT = sb("ynT", [128, 8, TM]); b_ynT = Buf("ynT")
    SW = ynT; b_SW = b_ynT
    mixT = sb("mixT", [128, 8, TM], BF16); b_mixT = Buf("mixT")
    hT = un1[:].bitcast(BF16)[:, 0:NJ * TM].rearrange("p (a b) -> p a b", b=TM); b_hT = b_PRW
    WC = sb("WC", [128, 8, 16]); b_WC = Buf("WC")
    NTMP = 9
    tmp = [sb("tmp%d" % i, [128, TM]) for i in range(NTMP)]
    b_tmp = [Buf("tmp%d" % i) for i in range(NTMP)]
    tmpb = [sb("tmpb%d" % i, [128, TM], BF16) for i in range(4)]
    b_tmpb = [Buf("tmpb%d" % i) for i in range(4)]
    ubuf = [sb("ubuf%d" % i, [128, TM + 2 * NSS]) for i in range(4)]
    b_ubuf = [Buf("ubuf%d" % i) for i in range(4)]
    cosT = sb("cosT", [128, 3, TM]); sinT = sb("sinT", [128, 3, TM]); b_rope = Buf("rope")
    cst = sb("cst", [128, 672]); b_cst = Buf("cst")
    onesb = sb("onesb", [128, 128], BF16); b_ones = Buf("onesb")
    bonesr = sb("bonesr", [128, 128]); b_bones = Buf("bonesr")
    rmask = sb("rmask", [128, TM]); b_rmask = Buf("rmask")
    vecs = [sb("vecs%d" % i, [128, NV]) for i in range(2)]; b_vecs = [Buf("vecs%d" % i) for i in range(2)]
    vaux = [sb("vaux%d" % i, [128, 8]) for i in range(2)]; b_vaux = [Buf("vaux%d" % i) for i in range(2)]
    _m0 = sb("mats0", [128, NM], BF16); _bm0 = Buf("mats0")
    mats = [_m0, _m0]; b_mats = [_bm0, _bm0]
    hpw = [sb("hpw%d" % i, [128, 1024]) for i in range(2)]; b_hpw = [Buf("hpw%d" % i) for i in range(2)]
    mix32 = sb("mix32", [128, 8, 2]); b_mix32 = Buf("mix32")
    h32 = sb("h32", [128, NJ, 2]); b_h32 = Buf("h32")
    slots = [sb("slot%d" % i, [128, SLOT], BF16) for i in range(NSLOT)]
    b_slots = [Buf("slot%d" % i) for i in range(NSLOT)]
    Sst = [sb("Sst%d" % l, [128, 8, 64]) for l in range(NL)]; b_Sst = [Buf("Sst%d" % l) for l in range(NL)]
    Ssm = [sb("Ssm%d" % i, [128, 8, 64]) for i in range(2)]; b_Ssm = [Buf("Ssm%d" % i) for i in range(2)]
    _s0 = sb("Sld0", [128, 8, 64]); _bs0 = Buf("Sld0")
    Sld = [_s0, _s0]; b_Sld = [_bs0, _bs0]
    shc = sb("shc", [128, NL, 12]); b_shc = [Buf("shc%d" % l) for l in range(NL)]
    shst = sb("shst", [128, 12, NSS]); b_shst = Buf("shst")
    sho = sb("sho", [128, 12, NSS]); b_sho = Buf("sho")
    cvc = sb("cvc", [128, NL, 44, 2]); b_cvc = [[Buf("cvc%d_%d" % (l, c_)) for c_ in range(44)] for l in range(NL)]
    b_uh = [Buf("uh%d" % i) for i in range(4)]
    cvst = sb("cvst", [128, 44, NSS, 2]); b_cvst = Buf("cvst")
    cvo = sb("cvo", [128, 44, NSS, 2]); b_cvo = Buf("cvo")
    CM = 128
    vtok = sb("vtok", [CM, 1024]); b_vtok = Buf("vtok")
    ktok = sb("ktok", [CM, 1024]); b_ktok = Buf("ktok")
    btok = sb("btok", [CM, 384]); b_btok = Buf("btok")
    ark = sb("ark", [CM, 16, CM]); b_ark = Buf("ark")
    aak = sb("aak", [CM, 6, CM]); b_aak = Buf("aak")
    arb = sb("arb", [CM, 6, CM]); b_arb = Buf("arb")
    _p = sb("Pm0", [CM, 6, CM], BF16); _bp = Buf("Pm0")
    _q = sb("Qm0", [CM, 6, CM], BF16); _bq = Buf("Qm0")
    Xb = sb("Xb", [CM, 6, CM], BF16); b_Xb = Buf("Xb")
    Pm = [_p, _p]; b_Pm = [_bp, _bp]
    Qm = [_q, _q]; b_Qm = [_bq, _bq]
    Xm = sb("Xm", [CM, 6, CM]); b_Xm = Buf("Xm")
    rhs0 = sb("rhs0", [CM, 384]); b_rhs0 = Buf("rhs0")
    usb = sb("usb", [CM, 384]); b_usb = Buf("usb")
    ysb = sb("ysb", [CM, 1024]); b_ysb = Buf("ysb")
    ysq = ktok; b_ysq = b_ktok
    yst = sb("yst", [CM, 8, 16]); b_yst = Buf("yst")
    ps = nc.alloc_psum_tensor("ps", [128, 8, 512], F32)
    b_ps = [Buf("ps%d" % i) for i in range(8)]
    bank_ctr = [0]

    def nbank():
        b = bank_ctr[0] % 8
        bank_ctr[0] += 1
        return b

    R_ = lambda ap: ap.bitcast(F32R)

    def E(name, *a, **k):
        return lambda e: getattr(e, name)(*a, **k)

    S.dma(E("dma_start", out=cst[:], in_=cst_d), writes=[b_cst], chan="cst")
    ident = cst[:, 0:128]
    m_incl = cst[:, 256:384]
    m_su = cst[:, 384:512]
    m_sl = cst[:, 512:640]
    eye64 = cst[:, 0:128]
    tokc = cst[:, 640:672]
    S.dve(E("memset", onesb[:], 1.0), writes=[b_ones])
    S.dve(E("tensor_copy", out=R_(bonesr[:]), in_=cst[:, 128:256]), reads=[b_cst], writes=[b_bones])
    S.dve(E("memset", Sld[0][:], 0.0), writes=[b_Sld[0]])
    for l in range(NL):
        S.dve(E("tensor_copy", out=R_(Sst[l][:]), in_=Sld[0][:]), reads=[b_Sld[0]], writes=[b_Sst[l]])
        S.dve(E("memset", shc[:, l, :], 0.0), writes=[b_shc[l]])
        S.dve(E("memset", cvc[:, l, :, :], 0.0), writes=b_cvc[l])

    plan = []
    for (kind, ti) in tiles:
        for l in range(nl):
            for p in range(npieces):
                plan.append((l, p))
    ws = {"issued": 0, "used": 0, "pending_wr": {}}

    wbf_d = nc.dram_tensor("wbf", [NL, npieces, 128, SLOT], BF16).ap()
    b_wbf = {}

    def ws_issue():
        i = ws["issued"]
        if i >= len(plan):
            return
        l, p = plan[i]
        s = i % NSLOT
        n = 2048 if p < NPIECE_IN + 4 + NJ else NJ * 128
        if (l, p) not in b_wbf:
            S.dma(E("dma_start", out=slots[s][:, 0:n], in_=wts_d[l, p, :, 0:n]),
                  writes=[b_slots[s]], chan="slot%d" % s, q="poolq")
            if len(tiles) > 1:
                b_wbf[(l, p)] = Buf("wbf%d_%d" % (l, p))
                ws["pending_wr"][i] = (l, p, s, n)
        else:
            S.dma(E("dma_start", out=slots[s][:, 0:n], in_=wbf_d[l, p, :, 0:n]), reads=[b_wbf[(l, p)]],
                  writes=[b_slots[s]], chan="slot%d" % s)
        ws["issued"] += 1

    def ws_next(l, p):
        i = ws["used"]
        assert plan[i] == (l, p), (plan[i], l, p)
        while ws["issued"] < min(len(plan), i + NSLOT):
            ws_issue()
        if i in ws["pending_wr"]:
            l_, p_, s_, n_ = ws["pending_wr"].pop(i)
            S.dma(E("dma_start", out=wbf_d[l_, p_, :, 0:n_], in_=slots[s_][:, 0:n_]), reads=[b_slots[s_]],
                  writes=[b_wbf[(l_, p_)]], chan="wbfw")
        ws["used"] += 1
        s = i % NSLOT
        return slots[s], b_slots[s]

    for _ in range(NSLOT - 1):
        ws_issue()

    pv_ctr = [0]
    hp_ctr = [0]

    def hp_load(l, p, part):
        i = hp_ctr[0] % 2
        hp_ctr[0] += 1
        if part[0] == "sub":
            src = wts_d[l, p, :, 0:2048].rearrange("p (kc c) -> p kc c", c=256)[:, :, part[1] * 128:(part[1] + 1) * 128]
            dst = hpw[i][:, 0:1024].rearrange("p (kc c) -> p kc c", c=128)
        else:
            src = wts_d[l, p, :, part[1] * 1024:part[1] * 1024 + part[2] * 128]
            dst = hpw[i][:, 0:part[2] * 128]
        S.dma(E("dma_start", out=dst, in_=src), writes=[b_hpw[i]], chan="hp%d" % i)
        return hpw[i], b_hpw[i]

    def hp_mm8(l, p, sub, rhs_fn, brhs):
        w, bw = hp_load(l, p, ("sub", sub))
        bkh = nbank()
        for kc in range(8):
            S.pe(E("matmul", ps[:, bkh, 0:2], lhsT=w[:, kc * 128:(kc + 1) * 128], rhs=rhs_fn(kc), start=(kc == 0), stop=(kc == 7)),
                 reads=[bw, brhs], writes=[b_ps[bkh]])
        return bkh

    def stage(n):
        if cfg.get("stop") == n:
            raise _Stop()

    def _tile_loop():
        for (kind, ti) in tiles:
            if kind == "p":
                T = TT; nseq = 1; L = TT; C = 128; tok0 = ti * TT; nsteps = 6
            else:
                T = NSS * LS; nseq = NSS; L = LS; C = LS; tok0 = SEQ; nsteps = 1
            nch = T // C
            last_prompt = (kind == "p" and ti == SEQ // TT - 1)
            hpf = (kind == "p" and ti == 0 and cfg.get("hp", True))
            clo = 2 if hpf else 0
            S.dma(E("dma_start", out=x32[:, :, 0:T], in_=xT_d[:, :, tok0:tok0 + T]),
                  writes=[b_x32], chan="x32")
            S.dma(E("dma_start", out=xTb[:, :, 0:T], in_=xT_d[:, :, tok0:tok0 + T]),
                  writes=[b_xTb], chan="xTb", q="poolq")
            S.dma(E("dma_start", out=cosT[:, :, 0:T], in_=cos_d[:, :, tok0:tok0 + T]),
                  writes=[b_rope], chan="rope")
            S.dma(E("dma_start", out=sinT[:, :, 0:T], in_=sin_d[:, :, tok0:tok0 + T]),
                  writes=[b_rope], chan="rope")
            S.dve(E("memset", rmask[:, 0:T], 1.0), writes=[b_rmask])
            S.dve(E("memset", rmask[:, 0:T].rearrange("p (c t) -> p c t", t=C)[:, :, 0:1], 0.0),
                  writes=[b_rmask])

            for l in range(nl):
                pv = pv_ctr[0] % 2
                pv_ctr[0] += 1
                vec, bvec, mat, bmat, vx, bvx = vecs[pv], b_vecs[pv], mats[pv], b_mats[pv], vaux[pv], b_vaux[pv]
                S.dma(E("dma_start", out=vec[:], in_=vec_d[l]), writes=[bvec], chan="vec%d" % pv)
                S.dma(E("dma_start", out=mat[:], in_=mat_d[l]), writes=[bmat], chan="mat0",
                      q="poolq")

                def V_(name, b=0, n=1, vec=vec):
                    o, _ = VEC[name]
                    return vec[:, o + b:o + b + n]

                o_ka = VEC["ka"][0]; o_bg = VEC["bgkb"][0]
                S.dve(E("tensor_scalar", out=vx[:, 0:3], in0=vec[:, o_ka:o_ka + 3], scalar1=-1.0,
                                                                        scalar2=1.0, op0=ALU.mult, op1=ALU.add),
                      reads=[bvec], writes=[bvx])
                S.dve(E("tensor_scalar", out=vx[:, 3:6], in0=vec[:, o_bg:o_bg + 3], scalar1=-1.0,
                                                                        scalar2=None, op0=ALU.mult),
                      reads=[bvec], writes=[bvx])

                if kind == "s":
                    S.dma(E("dma_start", out=shst[:].rearrange("p a b -> p (a b)"), in_=ssh_d[l]),
                          writes=[b_shst], chan="shst")
                    S.dve(E("tensor_copy", out=PRW[:, :, 0:NSS * (LS + 1)].rearrange("p a (s t) -> p a s t", t=LS + 1)[:, :, :, 0],
                                                  in_=shst[:]), reads=[b_shst], writes=[b_PRW])
                else:
                    S.dve(E("tensor_copy", out=PRW[:, :, 0], in_=shc[:, l, :]), reads=[b_shc[l]], writes=[b_PRW])

                def prw_dst(pb, T=T, nseq=nseq, L=L):
                    return PRW[:, pb, 0:nseq * (L + 1)].rearrange("p (s t) -> p s t", t=L + 1)[:, :, 1:L + 1]

                def prw_prev(pb, T=T, nseq=nseq, L=L):
                    return PRW[:, pb, 0:nseq * (L + 1)].rearrange("p (s t) -> p s t", t=L + 1)[:, :, 0:L]

                def t3(ap, nseq=nseq, L=L):
                    return ap.rearrange("p (s t) -> p s t", t=L)

                evi = [0]
                for piece in range(NPIECE_IN):
                    slot, bslot = ws_next(l, piece)
                    for sub in range(2):
                        blk = piece * 2 + sub
                        bk = nbank()
                        for kc in range(8):
                            S.pe(E("matmul",
                                ps[:, bk, 0:T], lhsT=slot[:, kc * 256 + sub * 128: kc * 256 + sub * 128 + 128],
                                rhs=xTb[:, kc, 0:T], start=(kc == 0), stop=(kc == 7)),
                                reads=[bslot, b_xTb], writes=[b_ps[bk]])
                        g, gb = divmod(blk, 8)

                        def proj_evac(srcap, lo, hi, bkb, blk=blk, g=g, gb=gb):
                            if blk < 24 and gb < 3:
                                pb = g * 3 + gb
                            elif 29 <= blk < 32:
                                pb = 9 + blk - 29
                            else:
                                pb = None
                            if pb is not None:
                                if kind == "p":
                                    S.act(E("copy", out=PRW[:, pb, 1 + lo:1 + hi], in_=srcap), reads=[bkb], writes=[b_PRW])
                                else:
                                    S.act(E("copy", out=prw_dst(pb), in_=t3(srcap)), reads=[bkb], writes=[b_PRW])
                            elif blk < 24:
                                tile_, btile = ((Rg, b_R), (Kg, b_K), (Vg, b_V))[g]
                                S.dve(E("tensor_copy", out=R_(tile_[:, gb, lo:hi]), in_=srcap), reads=[bkb], writes=[btile])
                            elif blk < 29:
                                S.act(E("activation", out=Gg[:, 3 + blk - 24, lo:hi], in_=srcap, func=AF.Silu), reads=[bkb],
                                      writes=[b_G])
                            else:
                                S.dve(E("tensor_copy", out=SW[:, blk - 32, lo:hi], in_=srcap), reads=[bkb], writes=[b_SW])

                        proj_evac(ps[:, bk, clo:T], clo, T, b_ps[bk])
                        if hpf:
                            bkh = hp_mm8(l, piece, sub, lambda kc: x32[:, kc, 0:2], b_x32)
                            proj_evac(ps[:, bkh, 0:2], 0, 2, b_ps[bkh])

                stage(1)
                for pb in range(12):
                    g, gb = divmod(pb, 3)
                    tile_, btile = ((Rg, b_R), (Kg, b_K), (Vg, b_V), (XS, b_XS))[g]
                    tb = pb % 8
                    d3 = t3(tmp[tb][:, 0:T])
                    S.dve(E("tensor_tensor", out=d3, in0=prw_prev(pb), in1=prw_dst(pb), op=ALU.subtract),
                          reads=[b_PRW], writes=[b_tmp[tb]])
                    dst3 = t3(tile_[:, gb, 0:T])
                    if g <= 2:
                        dst3 = R_(dst3)
                    S.dve(E("scalar_tensor_tensor",
                        out=dst3, in0=d3, scalar=V_("mu", pb), in1=prw_dst(pb), op0=ALU.mult, op1=ALU.add),
                        reads=[b_PRW, b_tmp[tb], bvec], writes=[btile])
                if l == 0:
                    dbg("PRW0", PRW[:, 0, 0:80], [b_PRW]); dbg("Rsh", Rg[:, 0, 0:T], [b_R]); dbg("shst", shst[:, 0, :], [b_shst])
                lastcol = PRW[:, :, 0:nseq * (L + 1)].rearrange("p a (s t) -> p a s t", t=L + 1)[:, :, :, L]
                if kind == "s":
                    S.act(E("copy", out=sho[:], in_=lastcol), reads=[b_PRW], writes=[b_sho])
                    S.dma(E("dma_start", out=sho_d[l][:, 12:12 * (1 + NSS)], in_=sho[:].rearrange("p a b -> p (a b)")),
                          reads=[b_sho], chan="shout")
                else:
                    S.act(E("copy", out=shc[:, l, :], in_=lastcol[:, :, 0]), reads=[b_PRW],
                          writes=[b_shc[l]])
                    if last_prompt:
                        S.dma(E("dma_start", out=sho_d[l][:, 0:12], in_=shc[:, l, :]), reads=[b_shc[l]],
                              chan="shout")

                stage(2)
                for gi, (tile_, btile) in enumerate(((Rg, b_R), (Kg, b_K))):
                    a = tile_[:, 5:8, 0:T]
                    sw = SW[:, gi * 3:gi * 3 + 3, 0:T]
                    S.dve(E("tensor_tensor", out=R_(a), in0=a, in1=cosT[:, :, 0:T], op=ALU.mult),
                          reads=[b_rope, btile], writes=[btile])
                    S.dve(E("tensor_tensor", out=sw, in0=sw, in1=sinT[:, :, 0:T], op=ALU.mult),
                          reads=[b_rope, b_SW], writes=[b_SW])
                    S.dve(E("tensor_tensor", out=R_(a), in0=a, in1=sw, op=ALU.add),
                          reads=[b_SW, btile], writes=[btile])

                thx, bthx = tmpb[0], b_tmpb[0]
                sgx, bsgx = tmpb[1], b_tmpb[1]
                sgx2, bsgx2 = tmpb[2], b_tmpb[2]
                gkb, bgkb_ = tmpb[3], b_tmpb[3]
                S.act(E("activation", out=thx[0:64, 0:T], in_=XS[0:64, 0, 0:T], func=AF.Tanh), reads=[b_XS], writes=[bthx])
                S.act(E("copy", out=thx[64:128, 0:T], in_=XS[64:128, 0, 0:T]), reads=[b_XS], writes=[bthx])
                S.act(E("activation", out=sgx[:, 0:T], in_=XS[:, 1, 0:T], func=AF.Sigmoid), reads=[b_XS], writes=[bsgx])
                S.act(E("activation", out=sgx2[0:32, 0:T], in_=XS[0:32, 2, 0:T], func=AF.Sigmoid), reads=[b_XS], writes=[bsgx2])
                S.act(E("copy", out=gkb[32:64, 0:T], in_=XS[32:64, 2, 0:T]), reads=[b_XS], writes=[bgkb_])

                for b in range(8):
                    t_ld, t_lw, t_W, t_IW, t_a, t_kk, t_x, t_y, t_r = tmp
                    bl_ld, bl_lw, bl_W, bl_IW, bl_a, bl_kk, bl_x, bl_y, bl_r = b_tmp
                    if b < 3:
                        c0 = b * 128
                        bk = nbank()
                        S.pe(E("matmul", ps[:, bk, 0:T], lhsT=mat[0:64, c0:c0 + 128], rhs=thx[0:64, 0:T],
                                                              start=True, stop=True), reads=[bmat, bthx], writes=[b_ps[bk]])
                        S.act(E("activation", out=t_ld[:, 0:T], in_=ps[:, bk, 0:T], func=AF.Sigmoid,
                                                                 bias=V_("w0", b), scale=1.0),
                              reads=[b_ps[bk], bvec], writes=[bl_ld])
                        S.dve(E("tensor_scalar", out=t_ld[:, 0:T], in0=t_ld[:, 0:T], scalar1=-math.exp(-0.5), scalar2=None,
                                                        op0=ALU.mult), reads=[bl_ld], writes=[bl_ld])
                        bk = nbank()
                        S.pe(E("matmul", ps[:, bk, 0:T], lhsT=mat[64:128, c0:c0 + 128], rhs=thx[64:128, 0:T],
                                                              start=True, stop=True), reads=[bmat, bthx], writes=[b_ps[bk]])
                        S.act(E("activation", out=t_a[:, 0:T], in_=ps[:, bk, 0:T], func=AF.Sigmoid,
                                                                 bias=V_("a0", b), scale=1.0),
                              reads=[b_ps[bk], bvec], writes=[bl_a])
                        bk = nbank()
                        S.pe(E("matmul", ps[:, bk, 0:T], lhsT=mat[:, 384 + c0:384 + c0 + 128], rhs=sgx[:, 0:T],
                                                              start=True, stop=False), reads=[bmat, bsgx], writes=[b_ps[bk]])
                        S.pe(E("matmul", ps[:, bk, 0:T], lhsT=mat[0:32, 768 + c0:768 + c0 + 128],
                                                              rhs=sgx2[0:32, 0:T], start=False, stop=True),
                             reads=[bmat, bsgx2], writes=[b_ps[bk]])
                        S.act(E("copy", out=Gg[:, b, 0:T], in_=ps[:, bk, 0:T]), reads=[b_ps[bk]], writes=[b_G])
                        S.dve(E("tensor_scalar", out=t_kk[:, 0:T], in0=Kg[:, b, 0:T], scalar1=V_("kkv", b), scalar2=None,
                                                             op0=ALU.mult), reads=[b_K, bvec], writes=[bl_kk])
                        S.act(E("activation", out=R_(t_r[:, 0:T]), in_=t_kk[:, 0:T], func=AF.Square), reads=[bl_kk],
                              writes=[bl_r])
                        bk = nbank()
                        S.pe(E("matmul", ps[:, bk, 0:T], lhsT=R_(bonesr[:]), rhs=R_(t_r[:, 0:T]), start=True, stop=True),
                             reads=[b_bones, bl_r], writes=[b_ps[bk]])
                        S.act(E("activation", out=t_x[:, 0:T], in_=ps[:, bk, 0:T], func=AF.Sqrt), reads=[b_ps[bk]],
                              writes=[bl_x])
                        S.dve(E("tensor_scalar", out=t_x[:, 0:T], in0=t_x[:, 0:T], scalar1=1e-12, scalar2=None, op0=ALU.max),
                              reads=[bl_x], writes=[bl_x])
                        S.dve(E("reciprocal", out=t_x[:, 0:T], in_=t_x[:, 0:T]), reads=[bl_x], writes=[bl_x])
                        S.dve(E("tensor_tensor", out=t_kk[:, 0:T], in0=t_kk[:, 0:T], in1=t_x[:, 0:T], op=ALU.mult),
                              reads=[bl_x, bl_kk], writes=[bl_kk])
                        S.dve(E("tensor_scalar", out=t_x[:, 0:T], in0=t_a[:, 0:T], scalar1=V_("ka", b),
                                                             scalar2=vx[:, b:b + 1], op0=ALU.mult, op1=ALU.add),
                              reads=[bl_a, bvec, bvx], writes=[bl_x])
                        S.dve(E("tensor_tensor", out=R_(Kg[:, b, 0:T]), in0=Kg[:, b, 0:T], in1=t_x[:, 0:T], op=ALU.mult),
                              reads=[bl_x, b_K], writes=[b_K])
                        S.dve(E("scalar_tensor_tensor", out=R_(t_r[:, 0:T]), in0=Rg[:, b, 0:T], scalar=V_("rk", b),
                                                                    in1=Kg[:, b, 0:T], op0=ALU.mult, op1=ALU.mult),
                              reads=[b_R, b_K, bvec], writes=[bl_r])
                        bk = nbank()
                        S.pe(E("matmul", ps[:, bk, 0:T], lhsT=R_(bonesr[:]), rhs=R_(t_r[:, 0:T]), start=True, stop=True),
                             reads=[b_bones, bl_r], writes=[b_ps[bk]])
                        S.dve(E("tensor_tensor", out=bonus[:, b, 0:T], in0=ps[:, bk, 0:T], in1=Vg[:, b, 0:T],
                                                                    op=ALU.mult), reads=[b_ps[bk], b_V], writes=[b_bonus])
                    elif b < 6:
                        c0 = (b - 3) * 128
                        np_ = 64 if b == 5 else 128
                        bk = nbank()
                        S.pe(E("matmul", ps[0:np_, bk, 0:T], lhsT=mat[32:48, 1152 + c0:1152 + c0 + np_],
                                                                       rhs=gkb[32:48, 0:T], start=True, stop=True),
                             reads=[bmat, bgkb_], writes=[b_ps[bk]])
                        S.act(E("activation", out=t_ld[0:np_, 0:T], in_=ps[0:np_, bk, 0:T], func=AF.Exp,
                                                                          bias=vx[0:np_, b:b + 1], scale=-1.0),
                              reads=[b_ps[bk], bvx], writes=[bl_ld])
                        S.act(E("activation", out=t_ld[0:np_, 0:T], in_=t_ld[0:np_, 0:T], func=AF.Ln, bias=1.0,
                                                              scale=1.0), reads=[bl_ld], writes=[bl_ld])
                        S.dve(E("tensor_scalar", out=t_ld[0:np_, 0:T], in0=t_ld[0:np_, 0:T], scalar1=-1.0 / 16.0,
                                                                 scalar2=None, op0=ALU.mult), reads=[bl_ld], writes=[bl_ld])
                        if b == 5:
                            S.dve(E("tensor_scalar", out=t_ld[64:128, 0:T], in0=rmask[64:128, 0:T], scalar1=0.0,
                                                                 scalar2=V_("ldret", b)[64:128], op0=ALU.mult, op1=ALU.add),
                                  reads=[b_rmask, bvec], writes=[bl_ld])
                    else:
                        S.dve(E("tensor_scalar", out=t_ld[:, 0:T], in0=rmask[:, 0:T], scalar1=0.0,
                                                             scalar2=V_("ldret", b), op0=ALU.mult, op1=ALU.add),
                              reads=[b_rmask, bvec], writes=[bl_ld])
                    S.dve(E("tensor_tensor_scan", out=t_lw[:, 0:T], data0=rmask[:, 0:T], data1=t_ld[:, 0:T], initial=0.0,
                                                         op0=ALU.mult, op1=ALU.add), reads=[bl_ld, b_rmask], writes=[bl_lw])
                    S.act(E("activation", out=t_W[:, 0:T], in_=t_lw[:, 0:T], func=AF.Exp), reads=[bl_lw], writes=[bl_W])
                    S.act(E("activation", out=t_IW[:, 0:T], in_=t_lw[:, 0:T], func=AF.Exp, scale=-1.0), reads=[bl_lw],
                          writes=[bl_IW])
                    S.act(E("copy", out=WC[:, b, 0:nch],
                                                              in_=t_W[:, 0:T].rearrange("p (c t) -> p c t", t=C)[:, :, C - 1]),
                          reads=[bl_W], writes=[b_WC])
                    if b < 3:
                        S.dve(E("tensor_tensor", out=t_y[:, 0:T], in0=t_lw[:, 0:T], in1=t_ld[:, 0:T], op=ALU.subtract),
                              reads=[bl_lw, bl_ld], writes=[bl_y])
                        S.act(E("activation", out=t_y[:, 0:T], in_=t_y[:, 0:T], func=AF.Exp), reads=[bl_y], writes=[bl_y])
                        S.dve(E("scalar_tensor_tensor", out=R_(ATt[:, b, 0:T]), in0=t_kk[:, 0:T], scalar=-1.0,
                                                                    in1=t_y[:, 0:T], op0=ALU.mult, op1=ALU.mult),
                              reads=[bl_kk, bl_y], writes=[b_AT])
                        S.dve(E("tensor_tensor", out=t_y[:, 0:T], in0=t_kk[:, 0:T], in1=t_a[:, 0:T], op=ALU.mult),
                              reads=[bl_kk, bl_a], writes=[bl_y])
                        S.dve(E("tensor_tensor", out=R_(BTt[:, b, 0:T]), in0=t_y[:, 0:T], in1=t_IW[:, 0:T], op=ALU.mult),
                              reads=[bl_y, bl_IW], writes=[b_BT])
                    S.dve(E("scalar_tensor_tensor", out=R_(Rg[:, b, 0:T]), in0=Rg[:, b, 0:T], scalar=V_("rscale", b),
                                                                in1=t_W[:, 0:T], op0=ALU.mult, op1=ALU.mult),
                          reads=[b_R, bl_W, bvec], writes=[b_R])
                    S.dve(E("scalar_tensor_tensor", out=R_(Kg[:, b, 0:T]), in0=Kg[:, b, 0:T], scalar=V_("kscale", b),
                                                                in1=t_IW[:, 0:T], op0=ALU.mult, op1=ALU.mult),
                          reads=[b_K, bl_IW, bvec], writes=[b_K])

                stage(3)
                for ci in range(nch):
                    t0 = ci * C
                    cs = slice(t0, t0 + C)
                    cfg.setdefault("_marks", []).append((l, ci, len(S.ops)))
                    if kind == "p":
                        St, bSt = Sst[l], b_Sst[l]
                    else:
                        sp_ = ci % 2
                        St, bSt = Ssm[sp_], b_Ssm[sp_]
                        S.dma(E("dma_start", out=Sld[sp_][:].rearrange("p a b -> p (a b)"),
                                                                         in_=sst_d[l, ci]),
                              writes=[b_Sld[sp_]], chan="sld0")
                        S.act(E("copy", out=R_(St[:]), in_=Sld[sp_][:]), reads=[b_Sld[sp_]], writes=[bSt])
                    for (src, bsrc, dst, bdst, nb_) in ((Vg, b_V, vtok, b_vtok, 8), (Kg, b_K, ktok, b_ktok, 8),
                                                        (BTt, b_BT, btok, b_btok, 3)):
                        for half in range((nb_ + 3) // 4):
                            bk = nbank()
                            n_in = min(4, nb_ - half * 4)
                            for bl in range(n_in):
                                blk = half * 4 + bl
                                S.pe(E("transpose",
                                    ps[0:C, bk, bl * 128:(bl + 1) * 128], src[:, blk, cs], ident),
                                    reads=[bsrc, b_cst], writes=[b_ps[bk]])
                            w = n_in * 128
                            S.act(E("copy", out=R_(dst[0:C, half * 512:half * 512 + w]),
                                                                                   in_=ps[0:C, bk, 0:w]),
                                  reads=[b_ps[bk]], writes=[bdst])
                    stage(31)
                    def hp(h):
                        return slice((h % 2) * 64, (h % 2) * 64 + 64), h // 2

                    def pv3(bk, n):
                        return ps[0:C, bk, 0:n * C].rearrange("p (h c) -> p h c", c=C)

                    def bc(m, n):
                        return m[0:C, 0:C].unsqueeze(1).to_broadcast([C, n, C])

                    def par3(t, par, n):
                        return t[0:C, :, 0:C].rearrange("p (a two) c -> p a two c", two=2)[:, :, par, :]

                    hpb = min(8, 512 // C)
                    def a_kind(lt, blt, rt_, brt, dst, bdst, msk, nh):
                        npar = nh // 2
                        ngrp = (npar + hpb - 1) // hpb
                        banks = {}
                        for g_ in range(ngrp):
                            for par in range(2):
                                banks[(par, g_)] = nbank()
                        for h in range(nh):
                            rows, blk = hp(h)
                            hl = h // 2
                            bk = banks[(h % 2, hl // hpb)]
                            hi = hl % hpb
                            S.pe(E("matmul", ps[0:C, bk, hi * C:(hi + 1) * C], lhsT=R_(lt[rows, blk, cs]),
                                   rhs=R_(rt_[rows, blk, cs]), start=True, stop=True), reads=[blt, brt], writes=[b_ps[bk]])
                        for g_ in range(ngrp):
                            n_ = min(hpb, npar - g_ * hpb)
                            for par in range(2):
                                bk = banks[(par, g_)]
                                o_ = par3(dst, par, nh)[:, g_ * hpb:g_ * hpb + n_, :]
                                if dst is not Pm[0] and dst is not Qm[0]:
                                    o_ = R_(o_)
                                S.dve(E("tensor_tensor", out=o_, in0=pv3(bk, n_),
                                        in1=bc(msk, n_), op=ALU.mult), reads=[b_ps[bk], b_cst], writes=[bdst])

                    a_kind(BTt, b_BT, ATt, b_AT, Pm[0], b_Pm[0], m_su, 6)
                    a_kind(ATt, b_AT, BTt, b_BT, Qm[0], b_Qm[0], m_sl, 6)
                    if False:
                        dbg("AT0", ATt[:, 0, cs], [b_AT]); dbg("BT0", BTt[:, 0, cs], [b_BT])
                        dbg("K0", Kg[:, 0, cs], [b_K]); dbg("R0", Rg[:, 0, cs], [b_R]); dbg("V0", Vg[:, 0, cs], [b_V])
                        dbg("WC", WC[:, :, 0:4], [b_WC])
                        dbg("P0", Pm[0][0:C, :, 0:C], [b_Pm[0]], C); dbg("Q0", Qm[0][0:C, :, 0:C], [b_Qm[0]], C)
                        dbg("aak", aak[0:C, :, 0:C], [b_aak], C); dbg("arb", arb[0:C, :, 0:C], [b_arb], C)
                        dbg("btok", btok[0:C, 0:128], [b_btok], C)
                    stage(32)
                    S.dve(E("tensor_tensor", out=R_(Xm[0:C, :, 0:C]), in0=Pm[0][0:C, :, 0:C], in1=bc(eye64, 6), op=ALU.add),
                          reads=[b_Pm[0], b_cst], writes=[b_Xm])
                    S.act(E("copy", out=Xb[0:C, :, 0:C], in_=Xm[0:C, :, 0:C]), reads=[b_Xm], writes=[b_Xb])
                    ng_ = 1 if 6 * C <= 512 else 2
                    hg_ = 6 // ng_

                    def inv_mm(lhs_t, blhs, rhs_t, brhs):
                        bks = []
                        for g_ in range(ng_):
                            bk = nbank()
                            bks.append(bk)
                            for hi in range(hg_):
                                h = g_ * hg_ + hi
                                S.pe(E("matmul", ps[0:C, bk, hi * C:(hi + 1) * C], lhsT=lhs_t[0:C, h, 0:C],
                                       rhs=rhs_t[0:C, h, 0:C], start=True, stop=True), reads=[blhs, brhs], writes=[b_ps[bk]])
                        return bks

                    def gsl(t, g_):
                        return t[0:C, g_ * hg_:(g_ + 1) * hg_, 0:C]

                    def par2(t, par, n):
                        return t[0:C, 0:n * 64].rearrange("p (a two v) -> p a two v", two=2, v=64)[:, :, par, :]

                    def pv64(bk, n):
                        return ps[0:C, bk, 0:n * 64].rearrange("p (h v) -> p h v", v=64)

                    def f_y_state():
                        bkp = (nbank(), nbank())
                        for h in range(16):
                            rows, blk = hp(h)
                            bk = bkp[h % 2]
                            hl = h // 2
                            S.pe(E("matmul", ps[0:C, bk, hl * 64:(hl + 1) * 64], lhsT=R_(Rg[rows, blk, cs]), rhs=R_(St[rows, blk, :]),
                                   start=True, stop=True), reads=[b_R, bSt], writes=[b_ps[bk]])
                        for par in range(2):
                            S.act(E("copy", out=par2(ysb, par, 16), in_=pv64(bkp[par], 8)), reads=[b_ps[bkp[par]]], writes=[b_ysb])

                    def f_rhs_state():
                        bkp = (nbank(), nbank())
                        for h in range(6):
                            rows, blk = hp(h)
                            bk = bkp[h % 2]
                            hl = h // 2
                            S.pe(E("matmul", ps[0:C, bk, hl * 64:(hl + 1) * 64], lhsT=R_(ATt[rows, blk, cs]), rhs=R_(St[rows, blk, :]),
                                   start=True, stop=True), reads=[b_AT, bSt], writes=[b_ps[bk]])
                        for par in range(2):
                            S.act(E("copy", out=R_(par2(rhs0, par, 6)), in_=pv64(bkp[par], 3)), reads=[b_ps[bkp[par]]], writes=[b_rhs0])

                    def f_rhs_tok():
                        bk = nbank()
                        for h in range(6):
                            S.pe(E("matmul", ps[0:C, bk, h * 64:(h + 1) * 64], lhsT=R_(aak[0:C, h, 0:C]),
                                   rhs=R_(vtok[0:C, h * 64:(h + 1) * 64]), start=True, stop=True),
                                 reads=[b_aak, b_vtok], writes=[b_ps[bk]])
                        S.dve(E("tensor_tensor", out=R_(rhs0[0:C, :]), in0=rhs0[0:C, :], in1=ps[0:C, bk, 0:384], op=ALU.add),
                              reads=[b_ps[bk], b_rhs0], writes=[b_rhs0])

                    def f_y_tok(half):
                        bk = nbank()
                        for hl in range(8):
                            h = half * 8 + hl
                            S.pe(E("matmul", ps[0:C, bk, hl * 64:(hl + 1) * 64], lhsT=R_(ark[0:C, h, 0:C]),
                                   rhs=R_(vtok[0:C, h * 64:(h + 1) * 64]), start=True, stop=(h >= 6)),
                                 reads=[b_ark, b_vtok], writes=[b_ps[bk]])
                            if h < 6:
                                S.pe(E("matmul", ps[0:C, bk, hl * 64:(hl + 1) * 64], lhsT=R_(arb[0:C, h, 0:C]),
                                       rhs=R_(usb[0:C, h * 64:(h + 1) * 64]), start=False, stop=True),
                                     reads=[b_arb, b_usb], writes=[b_ps[bk]])
                        S.dve(E("tensor_tensor", out=ysb[0:C, half * 512:(half + 1) * 512], in0=ysb[0:C, half * 512:(half + 1) * 512],
                                in1=ps[0:C, bk, :], op=ALU.add), reads=[b_ps[bk], b_ysb], writes=[b_ysb])

                    fillers = [lambda: a_kind(Kg, b_K, Rg, b_R, ark, b_ark, m_incl, 16),
                               lambda: a_kind(Kg, b_K, ATt, b_AT, aak, b_aak, m_su, 6),
                               f_y_state,
                               lambda: a_kind(BTt, b_BT, Rg, b_R, arb, b_arb, m_incl, 6),
                               f_rhs_state, f_rhs_tok, lambda: f_y_tok(1)]
                    for st in range(nsteps):
                        bq = inv_mm(Pm[0], b_Pm[0], Qm[0], b_Qm[0])
                        if st < nsteps - 1:
                            bp = inv_mm(Qm[0], b_Qm[0], Pm[0], b_Pm[0])
                        for g_ in range(ng_):
                            S.act(E("copy", out=gsl(Qm[0], g_), in_=pv3(bq[g_], hg_)), reads=[b_ps[bq[g_]]], writes=[b_Qm[0]])
                        if st < nsteps - 1:
                            for g_ in range(ng_):
                                S.dve(E("tensor_copy", out=gsl(Pm[0], g_), in_=pv3(bp[g_], hg_)), reads=[b_ps[bp[g_]]],
                                      writes=[b_Pm[0]])
                        if fillers:
                            fillers.pop(0)()
                        bx = inv_mm(Qm[0], b_Qm[0], Xb, b_Xb)
                        for g_ in range(ng_):
                            S.dve(E("tensor_tensor", out=R_(gsl(Xm, g_)), in0=gsl(Xm, g_), in1=pv3(bx[g_], hg_), op=ALU.add),
                                  reads=[b_ps[bx[g_]], b_Xm], writes=[b_Xm])
                            if st < nsteps - 1:
                                S.act(E("copy", out=gsl(Xb, g_), in_=gsl(Xm, g_)), reads=[b_Xm], writes=[b_Xb])
                    while fillers:
                        fillers.pop(0)()
                    stage(33)
                    bk = nbank()
                    for h in range(6):
                        S.pe(E("matmul", ps[0:C, bk, h * 64:(h + 1) * 64], lhsT=R_(Xm[0:C, h, 0:C]),
                               rhs=R_(rhs0[0:C, h * 64:(h + 1) * 64]), start=True, stop=True),
                             reads=[b_Xm, b_rhs0], writes=[b_ps[bk]])
                    S.act(E("copy", out=R_(usb[0:C, :]), in_=ps[0:C, bk, 0:384]), reads=[b_ps[bk]], writes=[b_usb])
                    stage(34)
                    f_y_tok(0)
                    stage(35)
                    ds_banks = []
                    for half in range(2):
                        bk = nbank()
                        ds_banks.append(bk)
                        for bl in range(4):
                            blk = half * 4 + bl
                            S.pe(E("matmul", ps[:, bk, bl * 128:(bl + 1) * 128],
                                   lhsT=R_(ktok[0:C, blk * 128:(blk + 1) * 128]),
                                   rhs=R_(vtok[0:C, blk * 128:(blk + 1) * 128]),
                                   start=True, stop=(blk >= 3)), reads=[b_ktok, b_vtok], writes=[b_ps[bk]])
                            if blk < 3:
                                S.pe(E("matmul", ps[:, bk, bl * 128:(bl + 1) * 128],
                                       lhsT=R_(btok[0:C, blk * 128:(blk + 1) * 128]),
                                       rhs=R_(usb[0:C, blk * 128:(blk + 1) * 128]),
                                       start=False, stop=True), reads=[b_btok, b_usb], writes=[b_ps[bk]])
                    stage(36)
                    y3 = ysb[0:C, :].rearrange("p (h v) -> p h v", v=64)
                    q3 = ysq[0:C, :].rearrange("p (h v) -> p h v", v=64)
                    s1, s2, mean, tmpv, rstd = (yst[0:C, i, :] for i in range(5))
                    S.dve(E("tensor_tensor", out=R_(ysq[0:C, :]), in0=ysb[0:C, :], in1=ysb[0:C, :], op=ALU.mult), reads=[b_ysb],
                          writes=[b_ysq])
                    S.dve(E("tensor_reduce", out=s1, in_=y3, axis=AX.X, op=ALU.add), reads=[b_ysb], writes=[b_yst])
                    S.dve(E("tensor_reduce", out=s2, in_=q3, axis=AX.X, op=ALU.add), reads=[b_ysq], writes=[b_yst])
                    S.dve(E("tensor_tensor", out=mean, in0=s1, in1=tokc[0:C, 0:16], op=ALU.mult),
                          reads=[b_yst, b_cst], writes=[b_yst])
                    S.dve(E("tensor_tensor", out=tmpv, in0=mean, in1=mean, op=ALU.mult), reads=[b_yst],
                          writes=[b_yst])
                    S.dve(E("scalar_tensor_tensor", out=tmpv, in0=s2, scalar=1.0 / 64, in1=tmpv, op0=ALU.mult,
                                                                             op1=ALU.subtract), reads=[b_yst], writes=[b_yst])
                    S.dve(E("tensor_tensor", out=tmpv, in0=tmpv, in1=tokc[0:C, 16:32], op=ALU.add),
                          reads=[b_yst, b_cst], writes=[b_yst])
                    S.act(E("activation", out=tmpv, in_=tmpv, func=AF.Sqrt), reads=[b_yst], writes=[b_yst])
                    S.dve(E("reciprocal", out=rstd, in_=tmpv), reads=[b_yst], writes=[b_yst])
                    S.dve(E("tensor_tensor", out=y3, in0=y3, in1=mean.unsqueeze(2).to_broadcast([C, 16, 64]),
                                                                      op=ALU.subtract), reads=[b_yst, b_ysb], writes=[b_ysb])
                    S.dve(E("tensor_tensor", out=y3, in0=y3, in1=rstd.unsqueeze(2).to_broadcast([C, 16, 64]),
                                                                      op=ALU.mult), reads=[b_yst, b_ysb], writes=[b_ysb])
                    bpb = min(8, 512 // C)
                    for g_ in range(8 // bpb):
                        bk = nbank()
                        for bi in range(bpb):
                            blk = g_ * bpb + bi
                            S.pe(E("transpose", ps[:, bk, bi * C:(bi + 1) * C], ysb[0:C, blk * 128:(blk + 1) * 128],
                                   cst[0:C, 0:C]), reads=[b_ysb, b_cst], writes=[b_ps[bk]])
                        S.act(E("copy", out=ynT[:, g_ * bpb:(g_ + 1) * bpb, cs],
                                in_=ps[:, bk, 0:bpb * C].rearrange("p (a c) -> p a c", c=C)), reads=[b_ps[bk]], writes=[b_ynT])

                    for half in range(2):
                        bk = ds_banks[half]
                        for hh in range(2):
                            rows = slice(hh * 64, hh * 64 + 64)
                            S.dve(E("tensor_tensor",
                                out=R_(St[rows, half * 4:half * 4 + 4, :]), in0=St[rows, half * 4:half * 4 + 4, :],
                                in1=ps[rows, bk, :].rearrange("p (a b) -> p a b", b=128)[:, :, hh * 64:hh * 64 + 64], op=ALU.add),
                                reads=[b_ps[bk], bSt], writes=[bSt])
                    S.dve(E("tensor_tensor", out=R_(St[:]), in0=St[:], in1=WC[:, :, ci:ci + 1].to_broadcast([128, 8, 64]),
                                                           op=ALU.mult), reads=[b_WC, bSt], writes=[bSt])
                    if kind == "s":
                        S.dma(E("dma_start", out=sto_d[l, 1 + ci], in_=St[:].rearrange("p a b -> p (a b)")),
                              reads=[bSt], chan="stout")
                    elif last_prompt and ci == nch - 1:
                        S.dma(E("dma_start", out=sto_d[l, 0], in_=St[:].rearrange("p a b -> p (a b)")),
                              reads=[bSt], chan="stout")

                stage(4)
                for b in range(8):
                    S.dve(E("tensor_scalar", out=ynT[:, b, 0:T], in0=ynT[:, b, 0:T], scalar1=V_("nw", b), scalar2=V_("nb", b),
                                                         op0=ALU.mult, op1=ALU.add), reads=[b_ynT, bvec], writes=[b_ynT])
                    if b < 3:
                        S.dve(E("tensor_tensor", out=ynT[:, b, 0:T], in0=ynT[:, b, 0:T], in1=bonus[:, b, 0:T], op=ALU.add),
                              reads=[b_ynT, b_bonus], writes=[b_ynT])
                    S.dve(E("tensor_tensor", out=mixT[:, b, 0:T], in0=ynT[:, b, 0:T], in1=Gg[:, b, 0:T], op=ALU.mult),
                          reads=[b_ynT, b_G], writes=[b_mixT])
                    if hpf:
                        S.dve(E("tensor_tensor", out=mix32[:, b, :], in0=ynT[:, b, 0:2], in1=Gg[:, b, 0:2], op=ALU.mult),
                              reads=[b_ynT, b_G], writes=[b_mix32])

                def ln_epilogue(gname, bname):
                    r16, br16 = hT[:, 0:8, :], b_hT
                    S.act(E("copy", out=hT[:, 0:8, 0:T], in_=x32[:, :, 0:T]), reads=[b_x32], writes=[b_hT])
                    S.act(E("activation", out=hT[:, 8:16, 0:T], in_=x32[:, :, 0:T], func=AF.Square), reads=[b_x32], writes=[b_hT])
                    bk1 = nbank(); bk2 = nbank()
                    for kc in range(8):
                        S.pe(E("matmul", ps[:, bk1, 0:T], lhsT=onesb[:], rhs=hT[:, kc, 0:T], start=(kc == 0), stop=(kc == 7)),
                             reads=[b_ones, b_hT], writes=[b_ps[bk1]])
                    for kc in range(8):
                        S.pe(E("matmul", ps[:, bk2, 0:T], lhsT=onesb[:], rhs=hT[:, 8 + kc, 0:T], start=(kc == 0),
                                                       stop=(kc == 7)), reads=[b_ones, b_hT], writes=[b_ps[bk2]])
                    mean, var = tmp[0], tmp[1]
                    S.act(E("mul", out=mean[:, 0:T], in_=ps[:, bk1, 0:T], mul=1.0 / D), reads=[b_ps[bk1]], writes=[b_tmp[0]])
                    S.dve(E("tensor_tensor", out=var[:, 0:T], in0=mean[:, 0:T], in1=mean[:, 0:T], op=ALU.mult), reads=[b_tmp[0]],
                          writes=[b_tmp[1]])
                    S.dve(E("scalar_tensor_tensor", out=var[:, 0:T], in0=ps[:, bk2, 0:T], scalar=1.0 / D, in1=var[:, 0:T],
                                                           op0=ALU.mult, op1=ALU.subtract), reads=[b_ps[bk2], b_tmp[1]], writes=[b_tmp[1]])
                    S.dve(E("tensor_scalar", out=var[:, 0:T], in0=var[:, 0:T], scalar1=1e-5, scalar2=None, op0=ALU.add),
                          reads=[b_tmp[1]], writes=[b_tmp[1]])
                    S.act(E("activation", out=var[:, 0:T], in_=var[:, 0:T], func=AF.Sqrt), reads=[b_tmp[1]], writes=[b_tmp[1]])
                    S.dve(E("reciprocal", out=var[:, 0:T], in_=var[:, 0:T]), reads=[b_tmp[1]], writes=[b_tmp[1]])
                    S.dve(E("tensor_tensor", out=x32[:, :, 0:T], in0=x32[:, :, 0:T],
                                                    in1=mean[:, 0:T].unsqueeze(1).to_broadcast([128, 8, T]), op=ALU.subtract),
                          reads=[b_tmp[0], b_x32], writes=[b_x32])
                    S.dve(E("tensor_tensor", out=x32[:, :, 0:T], in0=x32[:, :, 0:T],
                                                    in1=var[:, 0:T].unsqueeze(1).to_broadcast([128, 8, T]), op=ALU.mult),
                          reads=[b_tmp[1], b_x32], writes=[b_x32])
                    for b in range(8):
                        S.dve(E("tensor_scalar", out=x32[:, b, 0:T], in0=x32[:, b, 0:T], scalar1=V_(gname, b),
                                                             scalar2=V_(bname, b), op0=ALU.mult, op1=ALU.add),
                              reads=[b_x32, bvec], writes=[b_x32])
                    S.act(E("copy", out=xTb[:, :, 0:T], in_=x32[:, :, 0:T]), reads=[b_x32], writes=[b_xTb])

                stage(5)
                for piece in range(4):
                    slot, bslot = ws_next(l, NPIECE_IN + piece)
                    for sub in range(2):
                        cb = piece * 2 + sub
                        bk = nbank()
                        for kc in range(8):
                            S.pe(E("matmul",
                                ps[:, bk, 0:T], lhsT=slot[:, kc * 256 + sub * 128: kc * 256 + sub * 128 + 128],
                                rhs=mixT[:, kc, 0:T], start=(kc == 0), stop=(kc == 7)), reads=[bslot, b_mixT], writes=[b_ps[bk]])
                        if hpf:
                            bkh = hp_mm8(l, NPIECE_IN + piece, sub, lambda kc: mix32[:, kc, 0:2], b_mix32)
                            S.dve(E("scalar_tensor_tensor", out=x32[:, cb, 0:2], in0=x32[:, cb, 0:2], scalar=ALPHA,
                                    in1=ps[:, bkh, 0:2], op0=ALU.mult, op1=ALU.add), reads=[b_ps[bkh], b_x32], writes=[b_x32])
                        S.dve(E("scalar_tensor_tensor", out=x32[:, cb, clo:T], in0=x32[:, cb, clo:T], scalar=ALPHA,
                                                                             in1=ps[:, bk, clo:T], op0=ALU.mult, op1=ALU.add),
                              reads=[b_ps[bk], b_x32], writes=[b_x32])
                ln_epilogue("ln1g", "ln1b")

                stage(6)
                if kind == "s":
                    S.dma(E("dma_start", out=cvst[:].rearrange("p a b c -> p (a b c)"), in_=scv_d[l]), writes=[b_cvst],
                          chan="cvst")
                ffn_defer = [None]
                for j in range(NJ):
                    slot, bslot = ws_next(l, NPIECE_IN + 4 + j)
                    cv = []
                    for ab in range(2):
                        cbk = j + NJ * ab
                        bk = nbank()
                        for kc in range(8):
                            S.pe(E("matmul",
                                ps[:, bk, 0:T], lhsT=slot[:, kc * 256 + ab * 128: kc * 256 + ab * 128 + 128],
                                rhs=xTb[:, kc, 0:T], start=(kc == 0), stop=(kc == 7)), reads=[bslot, b_xTb], writes=[b_ps[bk]])
                        ub, bub = ubuf[ab + 2 * (j % 2)], b_ubuf[ab + 2 * (j % 2)]
                        buh = b_uh[ab + 2 * (j % 2)]
                        u3 = ub[:, 0:nseq * (L + 2)].rearrange("p (s t) -> p s t", t=L + 2)
                        if kind == "s":
                            S.act(E("copy", out=u3[:, :, 0:2], in_=cvst[:, cbk, :, :]), reads=[b_cvst],
                                  writes=[buh])
                        else:
                            S.act(E("copy", out=u3[:, 0, 0:2], in_=cvc[:, l, cbk, :]),
                                  reads=[b_cvc[l][cbk]], writes=[buh])
                        if hpf:
                            bkh = hp_mm8(l, NPIECE_IN + 4 + j, ab, lambda kc: x32[:, kc, 0:2], b_x32)
                            S.act(E("copy", out=u3[:, 0, 2:4], in_=ps[:, bkh, 0:2]), reads=[b_ps[bkh]], writes=[bub])
                            S.act(E("copy", out=u3[:, 0, 4:L + 2], in_=ps[:, bk, 2:T]), reads=[b_ps[bk]], writes=[bub])
                        else:
                            S.act(E("copy", out=u3[:, :, 2:L + 2], in_=t3(ps[:, bk, 0:T])), reads=[b_ps[bk]], writes=[bub])
                        if kind == "s":
                            S.act(E("copy", out=cvo[:, cbk, :, :], in_=t3(ps[:, bk, 0:T])[:, :, L - 2:L]), reads=[b_ps[bk]],
                                  writes=[b_cvo])
                        else:
                            S.act(E("copy", out=cvc[:, l, cbk, :], in_=ps[:, bk, T - 2:T]), reads=[b_ps[bk]],
                                  writes=[b_cvc[l][cbk]])
                        ct, bct = tmp[2 + ab + 2 * (j % 2)], b_tmp[2 + ab + 2 * (j % 2)]
                        c3 = t3(ct[:, 0:T])
                        S.pool(E("tensor_scalar", out=c3, in0=u3[:, :, 0:L], scalar1=V_("cw0", cbk),
                                                                               scalar2=V_("cb", cbk), op0=ALU.mult, op1=ALU.add),
                              reads=[bub, buh, bvec], writes=[bct])
                        S.dve(E("scalar_tensor_tensor", out=c3, in0=u3[:, :, 1:L + 1], scalar=V_("cw1", cbk),
                                                                                      in1=c3, op0=ALU.mult, op1=ALU.add),
                              reads=[bub, buh, bvec, bct], writes=[bct])
                        S.dve(E("scalar_tensor_tensor", out=c3, in0=u3[:, :, 2:L + 2], scalar=V_("cw2", cbk),
                                                                                      in1=c3, op0=ALU.mult, op1=ALU.add),
                              reads=[bub, buh, bvec, bct], writes=[bct])
                        cv.append((ct, bct))
                    (ca, bca), (cb_, bcb) = cv

                    def gelu_h(j=j, ca=ca, bca=bca, cb_=cb_, bcb=bcb):
                        S.act(E("activation", out=ca[:, 0:T], in_=ca[:, 0:T], func=AF.Gelu_apprx_tanh), reads=[bca], writes=[bca])
                        S.dve(E("tensor_tensor", out=hT[:, j, 0:T], in0=ca[:, 0:T], in1=cb_[:, 0:T], op=ALU.mult),
                              reads=[bca, bcb], writes=[b_hT])
                        if hpf:
                            S.dve(E("tensor_tensor", out=h32[:, j, :], in0=ca[:, 0:2], in1=cb_[:, 0:2], op=ALU.mult),
                                  reads=[bca, bcb], writes=[b_h32])

                    if ffn_defer[0] is not None:
                        ffn_defer[0]()
                    ffn_defer[0] = gelu_h
                ffn_defer[0]()
                ffn_defer[0] = None
                if kind == "s":
                    S.dma(E("dma_start", out=cvo_d[l][:, 88:88 * (1 + NSS)], in_=cvo[:].rearrange("p a b c -> p (a b c)")),
                          reads=[b_cvo], chan="cvout")
                elif last_prompt:
                    S.dma(E("dma_start", out=cvo_d[l][:, 0:88], in_=cvc[:, l, :, :].rearrange("p a c -> p (a c)")),
                          reads=b_cvc[l], chan="cvout")
                for cb in range(8):
                    slot, bslot = ws_next(l, NPIECE_IN + 4 + NJ + cb)
                    bk = nbank()
                    for j in range(NJ):
                        S.pe(E("matmul", ps[:, bk, 0:T], lhsT=slot[:, j * 128:(j + 1) * 128], rhs=hT[:, j, 0:T],
                                                                       start=(j == 0), stop=(j == NJ - 1)),
                             reads=[bslot, b_hT], writes=[b_ps[bk]])
                    if hpf:
                        bkh = nbank()
                        for jh in range(3):
                            njj = min(8, NJ - jh * 8)
                            w, bw = hp_load(l, NPIECE_IN + 4 + NJ + cb, ("jh", jh, njj))
                            for jj in range(njj):
                                j = jh * 8 + jj
                                S.pe(E("matmul", ps[:, bkh, 0:2], lhsT=w[:, jj * 128:(jj + 1) * 128], rhs=h32[:, j, 0:2],
                                       start=(j == 0), stop=(j == NJ - 1)), reads=[bw, b_h32], writes=[b_ps[bkh]])
                        S.dve(E("scalar_tensor_tensor", out=x32[:, cb, 0:2], in0=x32[:, cb, 0:2], scalar=ALPHA,
                                in1=ps[:, bkh, 0:2], op0=ALU.mult, op1=ALU.add), reads=[b_ps[bkh], b_x32], writes=[b_x32])
                    S.dve(E("scalar_tensor_tensor", out=x32[:, cb, clo:T], in0=x32[:, cb, clo:T], scalar=ALPHA,
                                                                         in1=ps[:, bk, clo:T], op0=ALU.mult, op1=ALU.add),
                          reads=[b_ps[bk], b_x32], writes=[b_x32])
                stage(7)
                ln_epilogue("ln2g", "ln2b")

            S.dma(E("dma_start", out=yT_d[:, :, tok0:tok0 + T], in_=x32[:, :, 0:T]), reads=[b_x32], chan="yout")


    try:
        _tile_loop()
        assert ws["used"] == len(plan)
    except _Stop:
        pass
    S.run_block()
    return nc, S


_CACHE = {}


def _prep_inputs(inp, cfg):
    f = np.float32
    W = _prep_weights(inp)
    vm = [_prep_layer_params(inp, l) for l in range(NL)]
    vecs = np.stack([v for v, _ in vm])
    mats = np.stack([m for _, m in vm])
    cst = _consts()
    cosT, sinT = _rope_tables()
    in_maps = []
    for c in range(8):
        xp = inp["x_prompt"][c]
        xs = inp["x_sample"][c * NSS:(c + 1) * NSS].reshape(NSS * LS, D)
        x = np.concatenate([xp, xs], axis=0)
        xT = np.ascontiguousarray(x.T.reshape(8, 128, -1).transpose(1, 0, 2))
        sst = np.zeros((NL, NSS, 16, 64, 64), f)
        sl = slice(c * NSS, (c + 1) * NSS)
        sst[:, :, 0:6] = inp["state_rwkv"][:, sl].transpose(0, 1, 2, 4, 3)
        sst[:, :, 6:11, 0:32] = inp["state_gla"][:, sl]
        sst[:, :, 11:16] = inp["state_ret"][:, sl]
        sst = sst.reshape(NL, NSS, 8, 2, 64, 64).transpose(0, 1, 3, 4, 2, 5).reshape(NL, NSS, 128, 512)
        sh = inp["state_shift"][:, sl]
        shp = np.zeros((NL, NSS, 12 * 128), f)
        shp[:, :, 0:1152] = sh[:, :, 0:1152]
        shp[:, :, 1152:1408] = sh[:, :, 1152:1408]
        shp[:, :, 1408:1440] = sh[:, :, 1408:1440]
        ssh = shp.reshape(NL, NSS, 12, 128).transpose(0, 3, 2, 1).reshape(NL, 128, 12 * NSS)
        cvs = inp["state_conv"][:, sl]
        scv = cvs.reshape(NL, NSS, 2, 44, 128).transpose(0, 4, 3, 1, 2).reshape(NL, 128, 44 * NSS * 2)
        in_maps.append({"xT": xT, "wts": W, "vecs": vecs, "mats": mats, "cst": cst, "ropec": cosT, "ropes": sinT,
                        "sst": np.ascontiguousarray(sst), "ssh": np.ascontiguousarray(ssh), "scv": np.ascontiguousarray(scv)})
    return in_maps


def _assemble(results):
    f = np.float32
    y_p = np.zeros((8, SEQ, D), f); y_s = np.zeros((8 * NSS, LS, D), f)
    rw_p = np.zeros((NL, 8, 6, 64, 64), f); rw_s = np.zeros((NL, 8 * NSS, 6, 64, 64), f)
    sh_p = np.zeros((NL, 8, 1440), f); sh_s = np.zeros((NL, 8 * NSS, 1440), f)
    gl_p = np.zeros((NL, 8, 5, 32, 64), f); gl_s = np.zeros((NL, 8 * NSS, 5, 32, 64), f)
    rt_p = np.zeros((NL, 8, 5, 64, 64), f); rt_s = np.zeros((NL, 8 * NSS, 5, 64, 64), f)
    cv_p = np.zeros((NL, 8, 2, 2 * DFF), f); cv_s = np.zeros((NL, 8 * NSS, 2, 2 * DFF), f)
    for c, r in enumerate(results):
        yT = r["yT"]
        y = yT.transpose(1, 0, 2).reshape(D, -1).T
        y_p[c] = y[0:SEQ]
        y_s[c * NSS:(c + 1) * NSS] = y[SEQ:].reshape(NSS, LS, D)
        st = r["sto"].reshape(NL, 1 + NSS, 2, 64, 8, 64).transpose(0, 1, 4, 2, 3, 5).reshape(NL, 1 + NSS, 16, 64, 64)
        rw = st[:, :, 0:6].transpose(0, 1, 2, 4, 3)
        gl = st[:, :, 6:11, 0:32]
        rt = st[:, :, 11:16]
        rw_p[:, c] = rw[:, 0]; gl_p[:, c] = gl[:, 0]; rt_p[:, c] = rt[:, 0]
        sl = slice(c * NSS, (c + 1) * NSS)
        rw_s[:, sl] = rw[:, 1:]; gl_s[:, sl] = gl[:, 1:]; rt_s[:, sl] = rt[:, 1:]
        sho = r["sho"].reshape(NL, 128, 12 * (1 + NSS))
        p = sho[:, :, 0:12].transpose(0, 2, 1).reshape(NL, 12 * 128)
        s = sho[:, :, 12:].reshape(NL, 128, 12, NSS).transpose(0, 3, 2, 1).reshape(NL, NSS, 12 * 128)
        sh_p[:, c] = p[:, 0:1440]
        sh_s[:, sl] = s[:, :, 0:1440]
        cvo = r["cvo"].reshape(NL, 128, 44 * (1 + NSS) * 2)
        p = cvo[:, :, 0:88].reshape(NL, 128, 44, 2).transpose(0, 3, 2, 1).reshape(NL, 2, 2 * DFF)
        s = cvo[:, :, 88:].reshape(NL, 128, 44, NSS, 2).transpose(0, 3, 4, 2, 1).reshape(NL, NSS, 2, 2 * DFF)
        cv_p[:, c] = p
        cv_s[:, sl] = s
    return (y_p, y_s, rw_p, sh_p, gl_p, rt_p, cv_p, rw_s, sh_s, gl_s, rt_s, cv_s)


def kernel(**inputs):
    inp = {k: np.asarray(v) for k, v in inputs.items()}
    cfg = CFG_FULL
    if "nc" not in _CACHE:
        _CACHE["nc"] = build(cfg)[0]
    nc = _CACHE["nc"]
    in_maps = _prep_inputs(inp, cfg)
    res = run_bass_kernel_spmd(nc, in_maps, core_ids=list(range(8)))
    return _assemble(res.results)
```
